# Optimizing a Trainium2 kernel written in Bass

```python
import math
import jax, jax.numpy as jnp
from jax import lax
import numpy as np

D_MODEL = 1024
BATCH = 16
SEQ = 4096
DEPTH = 2
DEC_BATCH = 32
DEC_SEQ = 2048
PAST_LEN = 128

GRID_W = 64
D_FF = 2816
C_A = 512
H_A = 8
N_A = 64
R_W = 64
R_A = 64
R_G = 128
LNX_EPS = 64e-5
C_B = 256
H_B = 4
N_B = 64
NA_KH = 8
NA_KW = 16
C_C = 256
H_C = 4
DQ = 32
DV = 64
Q_BLOCK = 128
RMS_EPS = 1e-6
SUBLN_EPS = 1e-5
RW_COLS = 3 * C_A + R_W + R_A + R_G
NA_COLS = 3 * C_B
DF_COLS = 3 * C_C
GATE_COLS = 3 * D_MODEL
IN_COLS = RW_COLS + NA_COLS + DF_COLS + GATE_COLS

kernel_name = 'hybrid_bidir_rwkv7_natten_diffattn_encoder'


def _rmsnorm(x, g, eps=RMS_EPS):
    x32 = x.astype(jnp.float32)
    y = x32 * lax.rsqrt(jnp.mean(x32 * x32, axis=-1, keepdims=True) + eps)
    return (y * g.astype(jnp.float32)).astype(x.dtype)


def _swiglu(x, w_gate, w_up, w_down):
    return (jax.nn.silu(x @ w_gate) * (x @ w_up)) @ w_down


def _wkv7_scan(r, decay, k, v, a, b, reverse):
    B, L, H, N = r.shape
    xs = tuple(jnp.moveaxis(t, 1, 0) for t in (r, decay, k, v, a, b))

    def step(S, inp):
        r_t, d_t, k_t, v_t, a_t, b_t = inp
        sa = jnp.einsum('bhvk,bhk->bhv', S, a_t)
        S = S * d_t[:, :, None, :] + sa[..., None] * b_t[:, :, None, :] + v_t[..., None] * k_t[:, :, None, :]
        return S, jnp.einsum('bhvk,bhk->bhv', S, r_t)

    S0 = jnp.zeros((B, H, N, N), jnp.float32)
    _, y = lax.scan(step, S0, xs, reverse=reverse)
    return jnp.moveaxis(y, 0, 1)


def _rwkv7_branch(z, mu, w0, w2, a0, a2, k_a, r_k, k_k, g2, lnx_g, lnx_b):
    B, L, _ = z.shape
    f32 = jnp.float32
    prev = jnp.pad(z, ((0, 0), (1, 0), (0, 0)))[:, :L]
    nxt = jnp.pad(z, ((0, 0), (0, 1), (0, 0)))[:, 1:]
    z = z + mu[0] * (prev - z) + mu[1] * (nxt - z)
    r, k, v, w_lo, a_lo, g_lo = jnp.split(
        z, [C_A, 2 * C_A, 3 * C_A, 3 * C_A + R_W, 3 * C_A + R_W + R_A], axis=-1)

    def heads(t):
        return t.astype(f32).reshape(B, L, H_A, N_A)

    r, k, v = heads(r), heads(k), heads(v)
    kk = k * k_k.astype(f32).reshape(H_A, N_A)
    kk = kk / jnp.maximum(jnp.sqrt(jnp.sum(kk * kk, axis=-1, keepdims=True)), 1e-12)
    wt = jnp.tanh(w_lo.astype(f32))
    a_lo = a_lo.astype(f32)

    def direction(d, reverse):
        w = -jax.nn.softplus(-(w0[d].astype(f32) + wt @ w2[d].astype(f32))) - 0.5
        decay = heads(jnp.exp(-jnp.exp(w)))
        a = heads(jax.nn.sigmoid(a0[d].astype(f32) + a_lo @ a2[d].astype(f32)))
        kd = k * (1.0 + (a - 1.0) * k_a[d].astype(f32).reshape(H_A, N_A))
        y = _wkv7_scan(r, decay, kd, v, -kk, kk * a, reverse)
        bonus = jnp.sum(r * kd * r_k[d].astype(f32), axis=-1, keepdims=True) * v
        return y, bonus

    y_f, bonus_f = direction(0, False)
    y_b, bonus_b = direction(1, True)
    y = y_f + y_b
    mean = jnp.mean(y, axis=-1, keepdims=True)
    var = jnp.mean(jnp.square(y - mean), axis=-1, keepdims=True)
    y = (y - mean) * lax.rsqrt(var + LNX_EPS)
    y = y.reshape(B, L, C_A) * lnx_g.astype(f32) + lnx_b.astype(f32) + (bonus_f + bonus_b).reshape(B, L, C_A)
    g = jax.nn.sigmoid(g_lo.astype(f32)) @ g2.astype(f32)
    return (y * g).astype(z.dtype)


def _neighbourhood_attention(q, k, v, rpb):
    B, L, H, N = q.shape
    f32 = jnp.float32
    rows = L // GRID_W
    kh = min(NA_KH, rows)
    qg = q.reshape(B, rows, GRID_W, H, N)
    kg = k.reshape(B, rows, GRID_W, H, N)
    vg = v.reshape(B, rows, GRID_W, H, N)
    cols = np.arange(GRID_W)
    col_idx = np.clip(cols - NA_KW // 2, 0, GRID_W - NA_KW)[:, None] + np.arange(NA_KW)[None, :]
    dc = col_idx - cols[:, None] + (NA_KW - 1)
    bias_c = jnp.transpose(rpb.astype(f32)[:, :, dc], (0, 2, 1, 3))
    scale = N ** -0.5

    def one_row(r):
        rs = jnp.clip(r - kh // 2, 0, rows - kh)
        k_win = lax.dynamic_slice_in_dim(kg, rs, kh, axis=1)[:, :, col_idx]
        v_win = lax.dynamic_slice_in_dim(vg, rs, kh, axis=1)[:, :, col_idx]
        q_r = lax.dynamic_index_in_dim(qg, r, axis=1, keepdims=False)
        s = jnp.einsum('bchn,bicjhn->bhcij', q_r, k_win, preferred_element_type=f32) * scale
        dr = rs + jnp.arange(kh) - r + (NA_KH - 1)
        s = s + jnp.take(bias_c, dr, axis=2)[None]
        p = jax.nn.softmax(s.reshape(B, H, GRID_W, kh * NA_KW), axis=-1).reshape(B, H, GRID_W, kh, NA_KW)
        return jnp.einsum('bhcij,bicjhn->bchn', p, v_win.astype(f32))

    out = lax.map(one_row, jnp.arange(rows))
    return jnp.transpose(out, (1, 0, 2, 3, 4)).reshape(B, L, H * N)


def _diff_attention(q, k, v, lam, lam_init, subln_g):
    B, L = q.shape[:2]
    f32 = jnp.float32
    nb = L // Q_BLOCK
    slopes = np.repeat(2.0 ** (-8.0 * np.arange(1, H_C + 1) / H_C), 2).astype(np.float32)
    pos = jnp.arange(L)
    qb = jnp.moveaxis(q.reshape(B, nb, Q_BLOCK, 2 * H_C, DQ), 1, 0)
    v32 = v.astype(f32)
    scale = DQ ** -0.5

    def one_block(args):
        q_blk, i = args
        s = jnp.einsum('bqgd,bkgd->bgqk', q_blk, k, preferred_element_type=f32) * scale
        qpos = i * Q_BLOCK + jnp.arange(Q_BLOCK)
        dist = jnp.abs(qpos[:, None] - pos[None, :]).astype(f32)
        s = s - slopes[:, None, None] * dist[None]
        p = jax.nn.softmax(s, axis=-1).reshape(B, H_C, 2, Q_BLOCK, L)
        attn = p[:, :, 0] - lam * p[:, :, 1]
        return jnp.einsum('bhqk,bkhd->bqhd', attn, v32)

    o = lax.map(one_block, (qb, jnp.arange(nb)))
    o = jnp.moveaxis(o, 0, 1).reshape(B, L, H_C, DV)
    o = o * lax.rsqrt(jnp.mean(o * o, axis=-1, keepdims=True) + SUBLN_EPS) * subln_g.astype(f32) * (1.0 - lam_init)
    return o.reshape(B, L, H_C * DV)


def _token_mixing(u, p, l):
    B, L, _ = u.shape
    f32 = jnp.float32
    proj = u @ p['w_in'][l]
    o1 = RW_COLS
    o2 = o1 + NA_COLS
    o3 = o2 + DF_COLS
    z_a, z_b, z_c, z_g = jnp.split(proj, [o1, o2, o3], axis=-1)
    y_a = _rwkv7_branch(z_a, p['rwkv_mu'][l], p['rwkv_w0'][l], p['rwkv_w2'][l], p['rwkv_a0'][l], p['rwkv_a2'][l],
                        p['rwkv_k_a'][l], p['rwkv_r_k'][l], p['rwkv_k_k'][l], p['rwkv_g2'][l],
                        p['rwkv_lnx_g'][l], p['rwkv_lnx_b'][l])
    qn, kn, vn = (t.reshape(B, L, H_B, N_B) for t in jnp.split(z_b, 3, axis=-1))
    y_b = _neighbourhood_attention(qn, kn, vn, p['na_rpb'][l]).astype(u.dtype)
    qd, kd, vd = jnp.split(z_c, 3, axis=-1)
    lam_init = 0.8 - 0.6 * math.exp(-0.3 * l)
    lp = p['diff_lam'][l].astype(f32)
    lam = jnp.exp(jnp.sum(lp[0] * lp[1])) - jnp.exp(jnp.sum(lp[2] * lp[3])) + lam_init
    y_c = _diff_attention(qd.reshape(B, L, 2 * H_C, DQ), kd.reshape(B, L, 2 * H_C, DQ),
                          vd.reshape(B, L, H_C, DV), lam, lam_init, p['diff_subln_g'][l]).astype(u.dtype)
    g_a, g_b, g_c = jnp.split(jax.nn.sigmoid(z_g), 3, axis=-1)
    m = g_a * (y_a @ p['p_a'][l]) + g_b * (y_b @ p['p_b'][l]) + g_c * (y_c @ p['p_c'][l])
    return m @ p['w_out'][l]


def _trunk(x, p):
    for l in range(DEPTH):
        x = x + 0.5 * _swiglu(_rmsnorm(x, p['ln_ffn1_g'][l]), p['ffn1_w_gate'][l], p['ffn1_w_up'][l], p['ffn1_w_down'][l])
        x = x + _token_mixing(_rmsnorm(x, p['ln_mix_g'][l]), p, l)
        x = x + 0.5 * _swiglu(_rmsnorm(x, p['ln_ffn2_g'][l]), p['ffn2_w_gate'][l], p['ffn2_w_up'][l], p['ffn2_w_down'][l])
    return _rmsnorm(x, p['final_g'])


def setup_inputs(seed: int = 0) -> dict:
    key = jax.random.key(seed)
    ks = list(jax.random.split(key, 48))
    f32 = jnp.float32

    def nrm(shape, scale):
        return scale * jax.random.normal(ks.pop(), shape, f32)

    def uni(shape, lo, hi):
        return jax.random.uniform(ks.pop(), shape, f32, lo, hi)

    return {
        'x_prompt': nrm((BATCH, SEQ, D_MODEL), 1.0),
        'x_sample': nrm((DEC_BATCH, DEC_SEQ, D_MODEL), 1.0),
        'ln_ffn1_g': 1.0 + nrm((DEPTH, D_MODEL), 0.1),
        'ffn1_w_gate': nrm((DEPTH, D_MODEL, D_FF), D_MODEL ** -0.5),
        'ffn1_w_up': nrm((DEPTH, D_MODEL, D_FF), D_MODEL ** -0.5),
        'ffn1_w_down': nrm((DEPTH, D_FF, D_MODEL), D_FF ** -0.5),
        'ln_mix_g': 1.0 + nrm((DEPTH, D_MODEL), 0.1),
        'w_in': nrm((DEPTH, D_MODEL, IN_COLS), D_MODEL ** -0.5),
        'rwkv_mu': uni((DEPTH, 2, RW_COLS), 0.0, 0.5),
        'rwkv_w0': uni((DEPTH, 2, C_A), -6.0, -1.0),
        'rwkv_w2': nrm((DEPTH, 2, R_W, C_A), 0.5 * R_W ** -0.5),
        'rwkv_a0': nrm((DEPTH, 2, C_A), 0.1),
        'rwkv_a2': nrm((DEPTH, 2, R_A, C_A), R_A ** -0.5),
        'rwkv_k_a': 1.0 + nrm((DEPTH, 2, C_A), 0.1),
        'rwkv_r_k': nrm((DEPTH, 2, H_A, N_A), 0.1),
        'rwkv_k_k': 0.85 + nrm((DEPTH, C_A), 0.05),
        'rwkv_g2': nrm((DEPTH, R_G, C_A), R_G ** -0.5),
        'rwkv_lnx_g': 1.0 + nrm((DEPTH, C_A), 0.1),
        'rwkv_lnx_b': nrm((DEPTH, C_A), 0.1),
        'na_rpb': nrm((DEPTH, H_B, 2 * NA_KH - 1, 2 * NA_KW - 1), 0.2),
        'diff_lam': nrm((DEPTH, 4, DQ), 0.1),
        'diff_subln_g': 1.0 + nrm((DEPTH, DV), 0.1),
        'p_a': nrm((DEPTH, C_A, D_MODEL), C_A ** -0.5),
        'p_b': nrm((DEPTH, C_B, D_MODEL), C_B ** -0.5),
        'p_c': nrm((DEPTH, C_C, D_MODEL), C_C ** -0.5),
        'w_out': nrm((DEPTH, D_MODEL, D_MODEL), D_MODEL ** -0.5),
        'ln_ffn2_g': 1.0 + nrm((DEPTH, D_MODEL), 0.1),
        'ffn2_w_gate': nrm((DEPTH, D_MODEL, D_FF), D_MODEL ** -0.5),
        'ffn2_w_up': nrm((DEPTH, D_MODEL, D_FF), D_MODEL ** -0.5),
        'ffn2_w_down': nrm((DEPTH, D_FF, D_MODEL), D_FF ** -0.5),
        'final_g': 1.0 + nrm((D_MODEL,), 0.1),
    }


def reference(x_prompt, x_sample, ln_ffn1_g, ffn1_w_gate, ffn1_w_up, ffn1_w_down, ln_mix_g, w_in,
              rwkv_mu, rwkv_w0, rwkv_w2, rwkv_a0, rwkv_a2, rwkv_k_a, rwkv_r_k, rwkv_k_k, rwkv_g2,
              rwkv_lnx_g, rwkv_lnx_b, na_rpb, diff_lam, diff_subln_g, p_a, p_b, p_c, w_out,
              ln_ffn2_g, ffn2_w_gate, ffn2_w_up, ffn2_w_down, final_g):
    p = dict(ln_ffn1_g=ln_ffn1_g, ffn1_w_gate=ffn1_w_gate, ffn1_w_up=ffn1_w_up, ffn1_w_down=ffn1_w_down,
             ln_mix_g=ln_mix_g, w_in=w_in, rwkv_mu=rwkv_mu, rwkv_w0=rwkv_w0, rwkv_w2=rwkv_w2,
             rwkv_a0=rwkv_a0, rwkv_a2=rwkv_a2, rwkv_k_a=rwkv_k_a, rwkv_r_k=rwkv_r_k, rwkv_k_k=rwkv_k_k,
             rwkv_g2=rwkv_g2, rwkv_lnx_g=rwkv_lnx_g, rwkv_lnx_b=rwkv_lnx_b, na_rpb=na_rpb,
             diff_lam=diff_lam, diff_subln_g=diff_subln_g, p_a=p_a, p_b=p_b, p_c=p_c, w_out=w_out,
             ln_ffn2_g=ln_ffn2_g, ffn2_w_gate=ffn2_w_gate, ffn2_w_up=ffn2_w_up, ffn2_w_down=ffn2_w_down,
             final_g=final_g)
    y_prompt = _trunk(x_prompt, p)
    y_sample = _trunk(x_sample, p)
    return (y_prompt, y_sample)
```

```python
import math
from contextlib import ExitStack
import numpy as np
import concourse.bass as bass
import concourse.mybir as mybir
from concourse.bass_utils import run_bass_kernel_spmd

F32 = mybir.dt.float32
BF16 = mybir.dt.bfloat16
AF = mybir.ActivationFunctionType
ALU = mybir.AluOpType
AX = mybir.AxisListType

D = 1024
DFF = 2816
KC = 8
FC = 22
DEPTH = 2
NCORES = 8
RMS_EPS = 1e-6
SUBLN_EPS = 1e-5
LNX_EPS = 64e-5
NWIN = 52
CH = 64


class Buf:
    __slots__ = ("w", "r", "psum")

    def __init__(self):
        self.w = None
        self.r = {}
        self.psum = False


class Tile:
    def __init__(self, t, buf=None):
        self.t = t
        self.buf = buf if buf is not None else Buf()

    def __getitem__(self, k):
        return self.t[k]


class Prog:
    ENG = ("pe", "dve", "act", "pool", "sp")
    NDS = 24

    def __init__(self, nc):
        self.nc = nc
        self.eng = {"pe": nc.tensor, "dve": nc.vector, "act": nc.scalar, "pool": nc.gpsimd, "sp": nc.sync}
        self.sem = {}
        for e in self.ENG:
            self.sem[e] = nc.semaphore("s_" + e).__enter__()
        for i in range(self.NDS):
            self.sem[("d", i)] = nc.semaphore("d%d" % i).__enter__()
            self.sem[("g", i)] = nc.semaphore("g%d" % i).__enter__()
        self.gnext = 0
        self.cnt = {k: 0 for k in self.sem}
        self.waited = {e: {} for e in self.ENG}
        self.dnext = 0
        self.ninst = 0

    def _need(self, reads, writes, e=None):
        need = {}
        for b in reads:
            b = b.buf if isinstance(b, Tile) else b
            if b.w is not None and need.get(b.w[0], 0) < b.w[1]:
                need[b.w[0]] = b.w[1]
            if b.psum:
                for k, v in b.r.items():
                    if k != e and need.get(k, 0) < v:
                        need[k] = v
        for b in writes:
            b = b.buf if isinstance(b, Tile) else b
            if b.w is not None and need.get(b.w[0], 0) < b.w[1]:
                need[b.w[0]] = b.w[1]
            for k, v in b.r.items():
                if need.get(k, 0) < v:
                    need[k] = v
        return need

    def _wait(self, e, need, skip_self=False):
        eng = self.eng[e]
        wd = self.waited[e]
        for k, v in need.items():
            if skip_self and k == e:
                continue
            if wd.get(k, 0) >= v:
                continue
            eng.wait_ge(self.sem[k], v)
            wd[k] = v
            self.ninst += 1

    def _mark(self, ev, reads, writes):
        for b in reads:
            b = b.buf if isinstance(b, Tile) else b
            if b.r.get(ev[0], 0) < ev[1]:
                b.r[ev[0]] = ev[1]
        for b in writes:
            b = b.buf if isinstance(b, Tile) else b
            b.w = ev
            b.r = {}

    def op(self, e, fn, reads=(), writes=(), skip_self=False):
        self._wait(e, self._need(reads, writes, e), skip_self)
        ins = fn(self.eng[e])
        ins.then_inc(self.sem[e], 1)
        self.cnt[e] += 1
        self.ninst += 1
        self._mark((e, self.cnt[e]), reads, writes)

    def dma(self, q, out, in_, reads=(), writes=()):
        self._wait(q, self._need(reads, writes))
        if q == "pool":
            k = ("g", self.gnext)
            self.gnext = (self.gnext + 1) % self.NDS
        else:
            k = ("d", self.dnext)
            self.dnext = (self.dnext + 1) % self.NDS
        self.eng[q].dma_start(out=out, in_=in_).then_inc(self.sem[k], 16)
        self.cnt[k] += 16
        self.ninst += 1
        self._mark((k, self.cnt[k]), reads, writes)

    def barrier(self):
        for e in self.ENG:
            self._wait(e, dict(self.cnt))

    def mm(self, out_t, out_ap, lhsT_ap, rhs_ap, reads, start=True, stop=True, sgc=False):
        if sgc:
            self.op("pe", lambda g: g.matmul(out_ap, lhsT=lhsT_ap, rhs=rhs_ap, start=start, stop=stop, skip_group_check=True),
                    reads=reads, writes=[out_t], skip_self=True)
        else:
            self.op("pe", lambda g: g.matmul(out_ap, lhsT=lhsT_ap, rhs=rhs_ap, start=start, stop=stop),
                    reads=reads, writes=[out_t], skip_self=True)

    def tr(self, out_t, out_ap, in_ap, ident_ap, reads):
        self.op("pe", lambda g: g.transpose(out_ap, in_ap, ident_ap), reads=reads, writes=[out_t], skip_self=True)

    def act(self, out_ap, in_ap, func, reads, writes, bias=None, scale=1.0, accum=None):
        kw = {}
        if bias is not None:
            kw["bias"] = bias
        if accum is not None:
            kw["accum_out"] = accum
        self.op("act", lambda g: g.activation(out=out_ap, in_=in_ap, func=func, scale=scale, **kw),
                reads=reads, writes=writes)

    def tt(self, e, out_ap, a_ap, b_ap, op, reads, writes):
        self.op(e, lambda g: g.tensor_tensor(out=out_ap, in0=a_ap, in1=b_ap, op=op), reads=reads, writes=writes)

    def stt(self, e, out_ap, a_ap, scalar, b_ap, op0, op1, reads, writes):
        self.op(e, lambda g: g.scalar_tensor_tensor(out=out_ap, in0=a_ap, scalar=scalar, in1=b_ap, op0=op0, op1=op1),
                reads=reads, writes=writes)

    def ts(self, e, out_ap, a_ap, s1, s2, op0, op1, reads, writes):
        if s2 is None:
            self.op(e, lambda g: g.tensor_scalar(out=out_ap, in0=a_ap, scalar1=s1, scalar2=None, op0=op0),
                    reads=reads, writes=writes)
        else:
            self.op(e, lambda g: g.tensor_scalar(out=out_ap, in0=a_ap, scalar1=s1, scalar2=s2, op0=op0, op1=op1),
                    reads=reads, writes=writes)

    def cp(self, e, out_ap, in_ap, reads, writes):
        if e == "act":
            self.op(e, lambda g: g.copy(out=out_ap, in_=in_ap), reads=reads, writes=writes)
        else:
            self.op(e, lambda g: g.tensor_copy(out=out_ap, in_=in_ap), reads=reads, writes=writes)

    def memset(self, e, t, ap, val):
        self.op(e, lambda g: g.memset(ap, val), reads=(), writes=[t])


class Pool_:
    def __init__(self, nc):
        self.nc = nc
        self.st = ExitStack()
        self.n = 0

    def sb(self, shape, dt, name=None):
        self.n += 1
        return Tile(self.st.enter_context(self.nc.sbuf_tensor("%s_%d" % (name or "t", id(self) % 100000 * 1000 + self.n), list(shape), dt)))

    def ps(self, shape, dt=F32, name=None):
        self.n += 1
        return Tile(self.st.enter_context(self.nc.psum_tensor("%s_%d" % (name or "p", id(self) % 100000 * 1000 + self.n), list(shape), dt)))

    def close(self):
        self.st.close()


class Ring:
    def __init__(self, tiles):
        self.tiles = tiles
        self.i = 0

    def next(self):
        t = self.tiles[self.i % len(self.tiles)]
        self.i += 1
        return t


def fm_pieces(W):
    K, N = W.shape
    return np.ascontiguousarray(W.reshape(K // 128, 128, N // 128, 128).transpose(2, 1, 0, 3)).reshape(N // 128, 128, K)


def pcol(v, nchunk):
    return np.ascontiguousarray(np.asarray(v, np.float32).reshape(nchunk, 128).T)


def host_weights(inp):
    f = lambda a: np.asarray(a, np.float32)
    out = {}
    gu1, d1, gu2, d2, win, wv, pabc, wout = [], [], [], [], [], [], [], []
    for l in range(DEPTH):
        for (gl, dl, pre) in ((gu1, d1, "ffn1"), (gu2, d2, "ffn2")):
            g = fm_pieces(f(inp[pre + "_w_gate"][l]))
            u = fm_pieces(f(inp[pre + "_w_up"][l]))
            gl.append(np.stack([g, u], axis=1).reshape(2 * FC, 128, D))
            dl.append(fm_pieces(f(inp[pre + "_w_down"][l])))
        W = f(inp["w_in"][l])
        cols = [W[:, 0:1792], W[:, 1792:2304]]
        for g in range(8):
            blk = np.zeros((D, 128), np.float32)
            blk[:, (g % 4) * 32:(g % 4) * 32 + 32] = W[:, 2560 + g * 32:2560 + g * 32 + 32]
            cols.append(blk)
        cols += [W[:, 2816:3072], W[:, 3328:6400]]
        win.append(fm_pieces(np.concatenate(cols, axis=1)))
        Wv = np.concatenate([W[:, 2304:2560], W[:, 3072:3328]], axis=1)
        wv.append(np.ascontiguousarray(Wv.reshape(KC, 128, 512).transpose(1, 0, 2)).reshape(128, KC * 512))
        pabc.append(fm_pieces(np.concatenate([f(inp["p_a"][l]), f(inp["p_b"][l]), f(inp["p_c"][l])], axis=0)))
        wout.append(fm_pieces(f(inp["w_out"][l])))
    out["wgu1"] = np.stack(gu1); out["wd1"] = np.stack(d1)
    out["wgu2"] = np.stack(gu2); out["wd2"] = np.stack(d2)
    out["win"] = np.stack(win); out["wv"] = np.stack(wv)
    out["wpabc"] = np.stack(pabc); out["wout"] = np.stack(wout)
    gains = []
    for l in range(DEPTH):
        gains += [pcol(inp["ln_ffn1_g"][l], KC), pcol(inp["ln_mix_g"][l], KC), pcol(inp["ln_ffn2_g"][l], KC)]
    gains.append(pcol(inp["final_g"], KC))
    out["gains"] = np.concatenate(gains, axis=1)
    return out


WSHAPES = {"wgu1": [DEPTH, 2 * FC, 128, D], "wd1": [DEPTH, KC, 128, DFF], "wgu2": [DEPTH, 2 * FC, 128, D],
           "wd2": [DEPTH, KC, 128, DFF], "win": [DEPTH, NWIN, 128, D], "wv": [DEPTH, 128, KC * 512],
           "wpabc": [DEPTH, KC, 128, D], "wout": [DEPTH, KC, 128, D]}


def host_consts():
    c = {}
    c["ident"] = np.eye(128, dtype=np.float32)
    c["onesm"] = np.full((128, 128), 1.0 / D, np.float32)
    return c


CSHAPES = {"ident": [128, 128], "onesm": [128, 128]}


_UID = [0]


def _uid(prefix):
    _UID[0] += 1
    return "%s%d" % (prefix, _UID[0])


class Scope:
    def __init__(self, nc):
        self.nc = nc
        self.st = ExitStack()

    def sb(self, shape, dt, name="t"):
        return Tile(self.st.enter_context(self.nc.sbuf_tensor(_uid(name), list(shape), dt)))

    def ps(self, shape, dt=F32, name="p"):
        t = Tile(self.st.enter_context(self.nc.psum_tensor(_uid(name), list(shape), dt)))
        t.buf.psum = True
        return t

    def close(self):
        self.st.close()


class WStream:
    def __init__(self, B, slots, plan):
        self.B = B
        self.slots = slots
        self.plan = plan
        self.loaded = 0
        self.pos = 0

    def _load(self, i):
        src, G, X = self.plan[i]
        slot = self.slots[i % len(self.slots)]
        if G == 0:
            self.B.P.dma("sp", slot.t[:, 0:X], src, writes=[slot])
        else:
            self.B.P.dma("sp", slot.t[:, 0:G * X].rearrange("p (g x) -> p g x", g=G),
                         src.rearrange("g p x -> p g x"), writes=[slot])

    def get(self):
        while self.loaded < len(self.plan) and self.loaded < self.pos + len(self.slots):
            self._load(self.loaded)
            self.loaded += 1
        slot = self.slots[self.pos % len(self.slots)]
        self.pos += 1
        return slot


class Builder:
    def __init__(self, seqs, debug=None):
        self.seqs = list(seqs)
        self.T = sum(self.seqs)
        self.starts = [sum(self.seqs[:i]) for i in range(len(self.seqs))]
        self.TT = 1024 if self.T % 1024 == 0 else 512
        self.NS = self.TT // 512
        self.debug = debug or {}
        nc = bass.Bass("TRN2", target_bir_lowering=False)
        self.nc = nc
        self.P = Prog(nc)
        T = self.T
        dt_in = lambda name, shape: nc.dram_tensor(name, list(shape), F32, kind="ExternalInput").ap()
        self.xin = dt_in("xin", [T, D])
        self.yout = nc.dram_tensor("yout", [T, D], F32, kind="ExternalOutput").ap()
        self.wf = {k: dt_in(k, s) for k, s in WSHAPES.items()}
        self.wb = {k: nc.dram_tensor(k + "_b", list(s), BF16).ap() for k, s in WSHAPES.items()}
        self.cst = {k: dt_in("c_" + k, s) for k, s in CSHAPES.items()}
        self.gains_d = dt_in("gains", [128, 56])
        self.small = {k: dt_in(k, s) for k, s in SMALL_SHAPES.items()}
        scr = lambda name, shape, dt: nc.dram_tensor(name, list(shape), dt).ap()
        self.xT = scr("xT", [KC, 128, T], F32)
        self.zA = scr("zA", [14, 128, T], F32)
        self.zNq = scr("zNq", [2, 128, T], BF16)
        self.zNk = scr("zNk", [2, 128, T], BF16)
        self.zDq = scr("zDq", [8, 128, T], BF16)
        self.zDk = scr("zDk", [2, 128, T], BF16)
        self.zV = scr("zV", [T, 520], BF16)
        self.zG = scr("zG", [24, 128, T], BF16)
        self.yM = scr("yM", [KC, 128, T], BF16)
        self.dbg = {}
        for k, s in self.debug.items():
            if not isinstance(s, (list, tuple)):
                continue
            self.dbg[k] = nc.dram_tensor("dbg_" + k, list(s), F32, kind="ExternalOutput").ap()

    def build(self):
        P = self.P
        nc = self.nc
        G = Scope(nc)
        self.G = G
        self.ident = G.sb([128, 128], F32, "ident")
        self.identb = G.sb([128, 128], BF16, "identb")
        self.onesm = G.sb([128, 128], BF16, "onesm")
        self.gains = G.sb([128, 56], F32, "gains")
        self.epsr = G.sb([128, 4], F32, "eps")
        tmp = G.sb([128, 128], F32, "ctmp")
        P.dma("sp", self.ident.t[:], self.cst["ident"], writes=[self.ident])
        P.dma("sp", tmp.t[:], self.cst["onesm"], writes=[tmp])
        P.dma("sp", self.gains.t[:], self.gains_d, writes=[self.gains])
        P.cp("dve", self.identb.t[:], self.ident.t[:], [self.ident], [self.identb])
        P.cp("dve", self.onesm.t[:], tmp.t[:], [tmp], [self.onesm])
        P.memset("dve", self.epsr, self.epsr.t[:, 0:1], RMS_EPS)
        P.memset("dve", self.epsr, self.epsr.t[:, 1:2], SUBLN_EPS)
        P.memset("dve", self.epsr, self.epsr.t[:, 2:3], LNX_EPS)
        P.memset("dve", self.epsr, self.epsr.t[:, 3:4], 0.0)
        self.prep_weights()
        P.barrier()
        for l in range(DEPTH):
            self.phaseA(l)
            P.barrier()
            self.mixers(l)
            P.barrier()
            self.phaseC(l)
            P.barrier()
        G.close()
        return nc

    def prep_weights(self):
        P = self.P
        S = Scope(self.nc)
        CHK = 8192
        st32 = [S.sb([128, CHK], F32, "w32") for _ in range(3)]
        st16 = [S.sb([128, CHK], BF16, "w16") for _ in range(3)]
        i = 0
        for k, shp in WSHAPES.items():
            tot = int(np.prod(shp))
            per = tot // 128
            src = self.wf[k]
            dst = self.wb[k]
            X = shp[-1]
            s2 = src.rearrange("l j p x -> (l j p) x") if len(shp) == 4 else src.rearrange("l p x -> (l p) x")
            d2 = dst.rearrange("l j p x -> (l j p) x") if len(shp) == 4 else dst.rearrange("l p x -> (l p) x")
            rows = tot // X
            gmax = max(1, CHK // X)
            r = 0
            while r < rows:
                g = min(gmax, (rows - r) // 128)
                a32 = st32[i % 3]
                a16 = st16[i % 3]
                P.dma("sp", a32.t[:, 0:g * X].rearrange("p (g x) -> p g x", g=g),
                      s2[r:r + g * 128, :].rearrange("(g p) x -> p g x", p=128), writes=[a32])
                e = ("dve", "act", "pool")[i % 3]
                P.cp(e, a16.t[:, 0:g * X], a32.t[:, 0:g * X], [a32], [a16])
                P.dma("sp", d2[r:r + g * 128, :].rearrange("(g p) x -> p g x", p=128),
                      a16.t[:, 0:g * X].rearrange("p (g x) -> p g x", g=g), reads=[a16])
                r += g * 128
                i += 1
        S.close()

    def rmsnorm(self, x, sq, u, gcol, pss, rstd, out_f32=False):
        P = self.P
        NS = self.NS
        for s in range(NS):
            sl = slice(s * 512, (s + 1) * 512)
            for c in range(KC):
                P.tt("pool", sq.t[:, c, sl], x.t[:, c, sl], x.t[:, c, sl], ALU.mult, [x], [sq])
            ps = pss.next()
            for c in range(KC):
                P.mm(ps, ps.t[:, 0:512], self.onesm.t[:], sq.t[:, c, sl], [sq, self.onesm], start=(c == 0), stop=(c == KC - 1))
            P.act(rstd.t[:, sl], ps.t[:, 0:512], AF.Sqrt, [ps, self.epsr], [rstd], bias=self.epsr.t[:, 0:1])
            P.op("dve", lambda g, sl=sl: g.reciprocal(out=rstd.t[:, sl], in_=rstd.t[:, sl]), [rstd], [rstd])
            for c in range(KC):
                P.stt("dve", u.t[:, c, sl], x.t[:, c, sl], self.gains.t[:, gcol + c:gcol + c + 1], rstd.t[:, sl],
                      ALU.mult, ALU.mult, [x, rstd, self.gains], [u])

    def ffn(self, ws, x, u, h, pss, tmps):
        P = self.P
        NS = self.NS
        for jp in range(FC // 2):
            w = ws.get()
            for jj in range(2):
                j = jp * 2 + jj
                pg = [pss.next() for _ in range(NS)]
                pu = [pss.next() for _ in range(NS)]
                for gi, pp in ((0, pg), (1, pu)):
                    base = (jj * 2 + gi) * D
                    for c in range(KC):
                        for s in range(NS):
                            P.mm(pp[s], pp[s].t[:, 0:512], w.t[:, base + c * 128:base + (c + 1) * 128],
                                 u.t[:, c, s * 512:(s + 1) * 512], [w, u], start=(c == 0), stop=(c == KC - 1))
                for s in range(NS):
                    tm = tmps.next()
                    P.act(tm.t[:, 0:512], pg[s].t[:, 0:512], AF.Silu, [pg[s]], [tm])
                    P.tt("dve", h.t[:, j, s * 512:(s + 1) * 512], tm.t[:, 0:512], pu[s].t[:, 0:512], ALU.mult, [tm, pu[s]], [h])
        for o in range(KC):
            w = ws.get()
            po = [pss.next() for _ in range(NS)]
            for j in range(FC):
                for s in range(NS):
                    P.mm(po[s], po[s].t[:, 0:512], w.t[:, j * 128:(j + 1) * 128], h.t[:, j, s * 512:(s + 1) * 512],
                         [w, h], start=(j == 0), stop=(j == FC - 1))
            for s in range(NS):
                sl = slice(s * 512, (s + 1) * 512)
                P.stt("dve", x.t[:, o, sl], po[s].t[:, 0:512], 0.5, x.t[:, o, sl], ALU.mult, ALU.add, [po[s], x], [x])

    def ffn_plan(self, key_gu, key_d, l):
        plan = []
        for jp in range(FC // 2):
            plan.append((self.wb[key_gu][l, jp * 4:jp * 4 + 4], 4, D))
        for o in range(KC):
            plan.append((self.wb[key_d][l, o:o + 1], 1, DFF))
        return plan

    def phaseA(self, l):
        P = self.P
        nc = self.nc
        TT, NS, T = self.TT, self.NS, self.T
        S = Scope(nc)
        x = S.sb([128, KC, TT], F32, "x")
        u = S.sb([128, KC, TT], BF16, "u")
        h = S.sb([128, FC, TT], BF16, "h")
        rstd = S.sb([128, TT], F32, "rstd")
        slots = [S.sb([128, 4096], BF16, "wslot") for _ in range(4)]
        tmps = Ring([S.sb([128, 512], F32, "tmp") for _ in range(4)])
        stg = Ring([S.sb([128, 1024], F32, "stg") for _ in range(4)])
        pss = Ring([S.ps([128, 512], F32, "ps") for _ in range(8)])
        vst = Ring([S.sb([128, 8, 65], BF16, "vst") for _ in range(3)])
        for v_ in vst.tiles:
            P.memset("pool", v_, v_.t[:], 1.0)
        ntile = T // TT
        plan = []
        for it in range(ntile):
            plan += self.ffn_plan("wgu1", "wd1", l)
            for jp in range(NWIN // 4):
                plan.append((self.wb["win"][l, jp * 4:jp * 4 + 4], 4, D))
            plan.append((self.wb["wv"][l], 0, KC * 512))
        ws = WStream(self, slots, plan)
        for it in range(ntile):
            t0 = it * TT
            if l == 0:
                for b in range(TT // 128):
                    sg = stg.next()
                    P.dma("sp", sg.t[:, 0:D], self.xin[t0 + b * 128:t0 + (b + 1) * 128, :], writes=[sg])
                    for half in range(2):
                        ps = pss.next()
                        for cc in range(4):
                            c = half * 4 + cc
                            P.tr(ps, ps.t[:, cc * 128:(cc + 1) * 128], sg.t[:, c * 128:(c + 1) * 128], self.ident.t[:], [sg, self.ident])
                        e = "act" if half == 0 else "dve"
                        P.cp(e, x.t[:, half * 4:half * 4 + 4, b * 128:(b + 1) * 128],
                             ps.t[:, 0:512].rearrange("p (c t) -> p c t", c=4), [ps], [x])
            else:
                P.dma("sp", x.t[:], self.xT[:, :, t0:t0 + TT].rearrange("c p t -> p c t"), writes=[x])
            self.rmsnorm(x, h, u, (l * 3 + 0) * KC, pss, rstd)
            self.ffn(ws, x, u, h, pss, tmps)
            P.dma("pool", self.xT[:, :, t0:t0 + TT].rearrange("c p t -> p c t"), x.t[:], reads=[x])
            self.rmsnorm(x, h, u, (l * 3 + 1) * KC, pss, rstd)
            for jp in range(NWIN // 4):
                w = ws.get()
                for jj in range(4):
                    j = jp * 4 + jj
                    pp = [pss.next() for _ in range(NS)]
                    for c in range(KC):
                        for s in range(NS):
                            P.mm(pp[s], pp[s].t[:, 0:512], w.t[:, jj * D + c * 128:jj * D + (c + 1) * 128],
                                 u.t[:, c, s * 512:(s + 1) * 512], [w, u], start=(c == 0), stop=(c == KC - 1))
                    sg = stg.next()
                    if j < 14:
                        dst, view = self.zA[j, :, t0:t0 + TT], sg.t[:, 0:TT]
                        for s in range(NS):
                            P.cp("act" if s == 0 else "dve", view[:, s * 512:(s + 1) * 512], pp[s].t[:, 0:512], [pp[s]], [sg])
                    else:
                        view = sg.t[:].bitcast(BF16)[:, 0:TT]
                        if j < 16:
                            dst, sc, fn = self.zNq[j - 14, :, t0:t0 + TT], 0.125, AF.Copy
                        elif j < 18:
                            dst, sc, fn = self.zNk[j - 16, :, t0:t0 + TT], 1.0, AF.Copy
                        elif j < 26:
                            dst, sc, fn = self.zDq[j - 18, :, t0:t0 + TT], 32.0 ** -0.5, AF.Copy
                        elif j < 28:
                            dst, sc, fn = self.zDk[j - 26, :, t0:t0 + TT], 1.0, AF.Copy
                        else:
                            dst, sc, fn = self.zG[j - 28, :, t0:t0 + TT], 1.0, AF.Sigmoid
                        for s in range(NS):
                            if fn == AF.Sigmoid or s == 0:
                                P.act(view[:, s * 512:(s + 1) * 512], pp[s].t[:, 0:512], fn, [pp[s]], [sg], scale=sc)
                            else:
                                P.ts("dve", view[:, s * 512:(s + 1) * 512], pp[s].t[:, 0:512], sc, None, ALU.mult, None, [pp[s]], [sg])
                    P.dma("pool", dst, view, reads=[sg])
            w = ws.get()
            for b in range(TT // 128):
                ps = pss.next()
                for c in range(KC):
                    P.mm(ps, ps.t[:, 0:512], u.t[:, c, b * 128:(b + 1) * 128], w.t[:, c * 512:(c + 1) * 512], [w, u],
                         start=(c == 0), stop=(c == KC - 1))
                sg = vst.next()
                P.cp("act" if b % 2 == 0 else "dve", sg.t[:, :, 0:64], ps.t[:, 0:512].rearrange("p (h d) -> p h d", h=8), [ps], [sg])
                P.dma("pool", self.zV[t0 + b * 128:t0 + (b + 1) * 128, :], sg.t[:].rearrange("p h d -> p (h d)"), reads=[sg])
        S.close()

    def phaseC(self, l):
        P = self.P
        nc = self.nc
        TT, NS, T = self.TT, self.NS, self.T
        last = (l == DEPTH - 1)
        S = Scope(nc)
        x = S.sb([128, KC, TT], F32, "x")
        u = S.sb([128, KC, TT], BF16, "u")
        h = S.sb([128, FC, TT], BF16, "h")
        ym = S.sb([128, KC, TT], BF16, "ym")
        rstd = S.sb([128, TT], F32, "rstd")
        gts = Ring([S.sb([128, 3, TT], BF16, "gt") for _ in range(2)])
        slots = [S.sb([128, 4096], BF16, "wslot") for _ in range(4)]
        tmps = Ring([S.sb([128, 512], F32, "tmp") for _ in range(4)])
        stg = Ring([S.sb([128, 1024], F32, "stg") for _ in range(3)])
        pss = Ring([S.ps([128, 512], F32, "ps") for _ in range(8)])
        ntile = T // TT
        plan = []
        for it in range(ntile):
            plan += [(self.wb["wpabc"][l, 0:4], 4, D), (self.wb["wpabc"][l, 4:8], 4, D),
                     (self.wb["wout"][l, 0:4], 4, D), (self.wb["wout"][l, 4:8], 4, D)]
            plan += self.ffn_plan("wgu2", "wd2", l)
        ws = WStream(self, slots, plan)
        zGv = self.zG.rearrange("(g o) p t -> o p g t", g=3)
        for it in range(ntile):
            t0 = it * TT
            P.dma("sp", x.t[:], self.xT[:, :, t0:t0 + TT].rearrange("c p t -> p c t"), writes=[x])
            P.dma("sp", ym.t[:], self.yM[:, :, t0:t0 + TT].rearrange("c p t -> p c t"), writes=[ym])
            for op_ in range(2):
                w = ws.get()
                for oo in range(4):
                    o = op_ * 4 + oo
                    gt = gts.next()
                    P.dma("sp", gt.t[:], zGv[o, :, :, t0:t0 + TT], writes=[gt])
                    for s in range(NS):
                        sl = slice(s * 512, (s + 1) * 512)
                        pa, pb, pc = pss.next(), pss.next(), pss.next()
                        for (pp, c0, c1) in ((pa, 0, 4), (pb, 4, 6), (pc, 6, 8)):
                            for c in range(c0, c1):
                                P.mm(pp, pp.t[:, 0:512], w.t[:, oo * D + c * 128:oo * D + (c + 1) * 128], ym.t[:, c, sl],
                                     [w, ym], start=(c == c0), stop=(c == c1 - 1))
                        t1, t2, t3 = tmps.next(), tmps.next(), tmps.next()
                        P.tt("dve", t1.t[:, 0:512], pa.t[:, 0:512], gt.t[:, 0, sl], ALU.mult, [pa, gt], [t1])
                        P.tt("dve", t2.t[:, 0:512], pb.t[:, 0:512], gt.t[:, 1, sl], ALU.mult, [pb, gt], [t2])
                        P.tt("pool", t1.t[:, 0:512], t1.t[:, 0:512], t2.t[:, 0:512], ALU.add, [t1, t2], [t1])
                        P.tt("dve", t3.t[:, 0:512], pc.t[:, 0:512], gt.t[:, 2, sl], ALU.mult, [pc, gt], [t3])
                        P.tt("pool", u.t[:, o, sl], t1.t[:, 0:512], t3.t[:, 0:512], ALU.add, [t1, t3], [u])
            for op_ in range(2):
                w = ws.get()
                for oo in range(4):
                    o = op_ * 4 + oo
                    for s in range(NS):
                        sl = slice(s * 512, (s + 1) * 512)
                        pp = pss.next()
                        for c in range(KC):
                            P.mm(pp, pp.t[:, 0:512], w.t[:, oo * D + c * 128:oo * D + (c + 1) * 128], u.t[:, c, sl], [w, u],
                                 start=(c == 0), stop=(c == KC - 1))
                        P.tt("dve", x.t[:, o, sl], pp.t[:, 0:512], x.t[:, o, sl], ALU.add, [pp, x], [x])
            self.rmsnorm(x, h, u, (l * 3 + 2) * KC, pss, rstd)
            self.ffn(ws, x, u, h, pss, tmps)
            if not last:
                P.dma("pool", self.xT[:, :, t0:t0 + TT].rearrange("c p t -> p c t"), x.t[:], reads=[x])
            else:
                self.rmsnorm(x, h, x, 6 * KC, pss, rstd)
                for b in range(TT // 128):
                    sg = stg.next()
                    for half in range(2):
                        ps = pss.next()
                        for cc in range(4):
                            c = half * 4 + cc
                            P.tr(ps, ps.t[:, cc * 128:(cc + 1) * 128], x.t[:, c, b * 128:(b + 1) * 128], self.ident.t[:], [x, self.ident])
                        P.cp("act" if half == 0 else "dve", sg.t[:, half * 512:(half + 1) * 512], ps.t[:, 0:512], [ps], [sg])
                    P.dma("pool", self.yout[t0 + b * 128:t0 + (b + 1) * 128, :], sg.t[:, 0:D], reads=[sg])
        S.close()

    def mixers(self, l):
        P = self.P
        en = self.debug_en if hasattr(self, "debug_en") else ("rwkv", "na", "da")
        S = Scope(self.nc)
        z = S.sb([128, 2048], BF16, "zero")
        P.memset("dve", z, z.t[:], 0.0)
        for (name, c0, c1) in (("rwkv", 0, 4), ("na", 4, 6), ("da", 6, 8)):
            if name in en:
                continue
            for c in range(c0, c1):
                for t in range(0, self.T, 2048):
                    n = min(2048, self.T - t)
                    P.dma("sp", self.yM[c, :, t:t + n], z.t[:, 0:n], reads=[z])
        S.close()
        P.barrier()
        if "na" in en:
            self.mix_na(l)
            P.barrier()
        if "da" in en:
            self.mix_da(l)
            P.barrier()
        if "rwkv" in en:
            self.mix_rwkv(l)
            P.barrier()


NA_TYPES = [(0, 0), (-2, -2), (-4, -3), (-4, -4), (-6, -6)]
SLOPES = [2.0 ** (-8.0 * (h + 1) / 4) for h in range(4)]


def na_rs(r, rows):
    return min(max(r - 4, 0), rows - 8)


def na_tile_info(r, rows):
    a, b = na_rs(r, rows) - r, na_rs(r + 1, rows) - r
    ty = NA_TYPES.index((a, b))
    kr0 = r + a
    nk = (b + 8 - a + 1) // 2
    return ty, kr0, nk


def na_tables(rpb):
    tab = np.full((128, 5, 4, 5, 128), -30000.0, np.float32)
    pk = np.arange(128)
    pq = np.arange(128)
    for ti, (a, b) in enumerate(NA_TYPES):
        nk = (b + 8 - a + 1) // 2
        for j in range(nk):
            krow = a + 2 * j + pk // 64
            kcol = pk % 64
            qrow = pq // 64
            qcol = pq % 64
            rs_rel = np.where(qrow == 0, a, b)
            cs = np.clip(qcol - 8, 0, 64 - 16)
            okr = (krow[:, None] >= rs_rel[None, :]) & (krow[:, None] < rs_rel[None, :] + 8)
            okc = (kcol[:, None] >= cs[None, :]) & (kcol[:, None] < cs[None, :] + 16)
            dr = np.clip(krow[:, None] - qrow[None, :] + 7, 0, 14)
            dc = np.clip(kcol[:, None] - qcol[None, :] + 15, 0, 30)
            ok = okr & okc
            for h in range(4):
                g = rpb[h][dr, dc]
                t = tab[:, ti, h, j, :]
                t[ok] = g[ok]
    return tab


def da_consts():
    p = np.arange(128, dtype=np.float64)
    colL = np.zeros((128, 4, 32), np.float32)
    colR = np.zeros((128, 4, 32), np.float32)
    fLR = np.zeros((128, 4, 8), np.float32)
    biasD = np.zeros((128, 4, 4, 512), np.float32)
    q = np.arange(512, dtype=np.float64)
    for s_, sl in enumerate(SLOPES):
        for m in range(32):
            colL[:, s_, m] = -sl * (128 * m - p)
            colR[:, s_, m] = -sl * (128 * m + p - 511)
        for sub in range(4):
            fLR[:, s_, sub] = -sl * (128 * sub + p)
            fLR[:, s_, 4 + sub] = -sl * (511 - 128 * sub - p)
        for j in range(4):
            biasD[:, s_, j, :] = -sl * np.abs(q[None, :] - (128 * j + p[:, None]))
    return {"da_colL": colL, "da_colR": colR, "da_fLR": fLR, "da_biasD": biasD}


SMALL_SHAPES = {"na_tab": [DEPTH, 128, 5 * 4 * 5 * 128], "da_colL": [128, 4, 32], "da_colR": [128, 4, 32],
                "da_fLR": [128, 4, 8], "da_biasD": [128, 4, 4, 512], "da_lam": [DEPTH, 128, 128],
                "da_g": [DEPTH, 128, 256]}


def host_small(inp):
    f = lambda a: np.asarray(a, np.float32)
    out = {}
    out["na_tab"] = np.stack([na_tables(f(inp["na_rpb"][l])).reshape(128, -1) for l in range(DEPTH)])
    out.update(da_consts())
    out["da_lam"] = np.stack([np.broadcast_to(f(inp["diff_lam"][l]).reshape(1, 128), (128, 128)) for l in range(DEPTH)]).copy()
    out["da_g"] = np.stack([np.broadcast_to(np.tile(f(inp["diff_subln_g"][l]), 4).reshape(1, 256), (128, 256)) for l in range(DEPTH)]).copy()
    return out


def make_inputs(inp, seq_groups, ncores):
    shared = {}
    shared.update(host_weights(inp))
    shared.update({"c_" + k: v for k, v in host_consts().items()})
    shared.update(host_small(inp))
    in_maps = []
    for c in range(ncores):
        parts = []
        for (arr, n) in seq_groups:
            for b in range(n):
                parts.append(np.asarray(arr[c * n + b], np.float32))
        m = dict(shared)
        m["xin"] = np.ascontiguousarray(np.concatenate(parts, axis=0))
        in_maps.append(m)
    return in_maps


def run(inp, seq_groups, ncores, debug=None, en=None):
    seqs = []
    for (arr, n) in seq_groups:
        seqs += [arr.shape[1]] * n
    B = Builder(seqs, debug=debug)
    if en is not None:
        B.debug_en = en
    nc = B.build()
    in_maps = make_inputs(inp, seq_groups, ncores)
    res = run_bass_kernel_spmd(nc, in_maps, core_ids=list(range(ncores)))
    outs = []
    for (arr, n) in seq_groups:
        outs.append(np.zeros(arr.shape, np.float32))
    for c in range(ncores):
        y = res.results[c]["yout"]
        t = 0
        for gi, (arr, n) in enumerate(seq_groups):
            L = arr.shape[1]
            for b in range(n):
                outs[gi][c * n + b] = y[t:t + L]
                t += L
    return outs, res, B


def kernel(**inputs):
    xp = np.asarray(inputs["x_prompt"], np.float32)
    xs = np.asarray(inputs["x_sample"], np.float32)
    outs, _, _ = run(inputs, [(xp, xp.shape[0] // NCORES), (xs, xs.shape[0] // NCORES)], NCORES)
    return (outs[0], outs[1])


def mix_na(self, l):
    P = self.P
    S = Scope(self.nc)
    Lmax = max(self.seqs)
    qT = S.sb([128, 2, Lmax], BF16, "naq")
    kT = S.sb([128, 2, Lmax], BF16, "nak")
    V1 = S.sb([128, Lmax // 128, 260], BF16, "nav")
    yst = S.sb([128, 2, Lmax], BF16, "nay")
    tab = S.sb([128, 5, 4, 5 * 128], F32, "natab")
    sbr = Ring([S.sb([128, 640], F32, "nasb") for _ in range(2)])
    ptr = Ring([S.sb([128, 640], BF16, "napt") for _ in range(2)])
    yr = Ring([S.sb([128, 256], F32, "nayt") for _ in range(2)])
    rcr = Ring([S.sb([128, 4], F32, "narc") for _ in range(2)])
    pss = Ring([S.ps([128, 1024], F32, "naps") for _ in range(2)])
    pso = Ring([S.ps([128, 512], F32, "napo") for _ in range(2)])
    pst = Ring([S.ps([128, 512], F32, "napt") for _ in range(2)])
    P.dma("sp", tab.t[:].rearrange("p a b c -> p (a b c)"), self.small["na_tab"][l], writes=[tab])
    for si, L in enumerate(self.seqs):
        t0 = self.starts[si]
        rows = L // 64
        P.dma("sp", qT.t[:, :, 0:L], self.zNq[:, :, t0:t0 + L].rearrange("c p t -> p c t"), writes=[qT])
        P.dma("sp", kT.t[:, :, 0:L], self.zNk[:, :, t0:t0 + L].rearrange("c p t -> p c t"), writes=[kT])
        P.dma("sp", V1.t[:, 0:L // 128, :], self.zV[t0:t0 + L, 0:260].rearrange("(n p) x -> p n x", p=128), writes=[V1])
        for qi in range(L // 128):
            r = 2 * qi
            ty, kr0, nk = na_tile_info(r, rows)
            po = pso.next()
            for hd in range(4):
                cc, base = hd // 2, (hd % 2) * 64
                ps = pss.next()
                for j in range(nk):
                    kn = kr0 // 2 + j
                    P.mm(ps, ps.t[:, j * 128:(j + 1) * 128], kT.t[base:base + 64, cc, kn * 128:(kn + 1) * 128],
                         qT.t[base:base + 64, cc, qi * 128:(qi + 1) * 128], [kT, qT])
                sb = sbr.next()
                P.tt("dve", sb.t[:, 0:nk * 128], ps.t[:, 0:nk * 128], tab.t[:, ty, hd, 0:nk * 128], ALU.add, [ps, tab], [sb])
                pt = ptr.next()
                P.act(pt.t[:, 0:nk * 128], sb.t[:, 0:nk * 128], AF.Exp, [sb], [pt])
                for j in range(nk):
                    kn = kr0 // 2 + j
                    P.mm(po, po.t[:, hd * 65:(hd + 1) * 65], pt.t[:, j * 128:(j + 1) * 128], V1.t[:, kn, hd * 65:(hd + 1) * 65],
                         [pt, V1], start=(j == 0), stop=(j == nk - 1))
            rc = rcr.next()
            pov = po.t[:, 0:260].rearrange("p (h d) -> p h d", h=4)
            P.op("dve", lambda g, rc=rc, pov=pov: g.reciprocal(out=rc.t[:, :], in_=pov[:, :, 64]), [po], [rc])
            y = yr.next()
            for hd in range(4):
                P.ts("dve", y.t[:, hd * 64:(hd + 1) * 64], po.t[:, hd * 65:hd * 65 + 64], rc.t[:, hd:hd + 1], None, ALU.mult, None,
                     [po, rc], [y])
            pt_ = pst.next()
            for cc in range(2):
                P.tr(pt_, pt_.t[:, cc * 128:(cc + 1) * 128], y.t[:, cc * 128:(cc + 1) * 128], self.ident.t[:], [y, self.ident])
            P.cp("act", yst.t[:, :, qi * 128:(qi + 1) * 128], pt_.t[:, 0:256].rearrange("p (c t) -> p c t", c=2), [pt_], [yst])
        P.dma("pool", self.yM[4:6, :, t0:t0 + L].rearrange("c p t -> p c t"), yst.t[:, :, 0:L], reads=[yst])
    S.close()


def mix_da(self, l):
    P = self.P
    S = Scope(self.nc)
    Lmax = max(self.seqs)
    lam_init = 0.8 - 0.6 * math.exp(-0.3 * l)
    kT = S.sb([128, 2, Lmax], BF16, "dak")
    V1 = S.sb([128, Lmax // 128, 260], BF16, "dav")
    yst = S.sb([128, 2, Lmax], BF16, "day")
    qmr = Ring([S.sb([128, 8, 512], BF16, "daq") for _ in range(2)])
    colL = S.sb([128, 4, 32], F32, "colL")
    colR = S.sb([128, 4, 32], F32, "colR")
    fLR = S.sb([128, 4, 8], F32, "fLR")
    bD32 = S.sb([128, 4 * 4 * 512], F32, "bD32")
    bD = S.sb([128, 4, 4, 512], BF16, "bD")
    lamt = S.sb([128, 4, 32], F32, "lamt")
    lamw = S.sb([128, 2, 32], F32, "lamw")
    lams = S.sb([128, 4], F32, "lams")
    gt = S.sb([128, 4, 64], F32, "dag")
    att = S.sb([128, 4, 8, 64], F32, "att")
    ptr = Ring([S.sb([128, 512], BF16, "dapt") for _ in range(3)])
    totr = Ring([S.sb([128, 65], F32, "datot") for _ in range(3)])
    rcr = Ring([S.sb([128, 4], F32, "darc") for _ in range(3)])
    ar = Ring([S.sb([128, 4, 64], F32, "daa") for _ in range(2)])
    sqr = Ring([S.sb([128, 4, 64], F32, "dasq") for _ in range(2)])
    yr = Ring([S.sb([128, 256], F32, "dayt") for _ in range(2)])
    pss = Ring([S.ps([128, 512], F32, "daps") for _ in range(3)])
    pso = Ring([S.ps([128, 512], F32, "dapo") for _ in range(4)])
    pst = Ring([S.ps([128, 512], F32, "dapt") for _ in range(1)])
    sm = self.small
    P.dma("sp", colL.t[:], sm["da_colL"], writes=[colL])
    P.dma("sp", colR.t[:], sm["da_colR"], writes=[colR])
    P.dma("sp", fLR.t[:], sm["da_fLR"], writes=[fLR])
    P.dma("sp", bD32.t[:], sm["da_biasD"].rearrange("p a b c -> p (a b c)"), writes=[bD32])
    P.dma("sp", lamt.t[:].rearrange("p a b -> p (a b)"), sm["da_lam"][l], writes=[lamt])
    P.dma("sp", gt.t[:].rearrange("p a b -> p (a b)"), sm["da_g"][l], writes=[gt])
    P.cp("dve", bD.t[:].rearrange("p a b c -> p (a b c)"), bD32.t[:], [bD32], [bD])
    P.act(fLR.t[:], fLR.t[:], AF.Exp, [fLR], [fLR])
    P.tt("dve", lamw.t[:, 0, :], lamt.t[:, 0, :], lamt.t[:, 1, :], ALU.mult, [lamt], [lamw])
    P.tt("dve", lamw.t[:, 1, :], lamt.t[:, 2, :], lamt.t[:, 3, :], ALU.mult, [lamt], [lamw])
    P.op("dve", lambda g: g.tensor_reduce(out=lams.t[:, 0:2], in_=lamw.t[:], axis=AX.X, op=ALU.add), [lamw], [lams])
    P.act(lams.t[:, 0:2], lams.t[:, 0:2], AF.Exp, [lams], [lams])
    P.tt("dve", lams.t[:, 2:3], lams.t[:, 1:2], lams.t[:, 0:1], ALU.subtract, [lams], [lams])
    P.ts("dve", lams.t[:, 3:4], lams.t[:, 2:3], -lam_init, None, ALU.add, None, [lams], [lams])
    zero_col = self.epsr.t[:, 3:4]
    for si, L in enumerate(self.seqs):
        t0 = self.starts[si]
        NK = L // 128
        P.dma("sp", kT.t[:, :, 0:L], self.zDk[:, :, t0:t0 + L].rearrange("c p t -> p c t"), writes=[kT])
        P.dma("sp", V1.t[:, 0:NK, :], self.zV[t0:t0 + L, 260:520].rearrange("(n p) x -> p n x", p=128), writes=[V1])
        for qb in range(L // 512):
            q0 = qb * 512
            qm = qmr.next()
            P.dma("sp", qm.t[:], self.zDq[:, :, t0 + q0:t0 + q0 + 512].rearrange("g p t -> p g t"), writes=[qm])
            for g_ in range(8):
                s_ = g_ // 2
                hd = g_ // 2
                poA, poB = pso.next(), pso.next()
                cls_of = []
                for kt in range(NK):
                    k0 = kt * 128
                    cls_of.append(0 if k0 + 128 <= q0 else (2 if k0 >= q0 + 512 else 1))
                first = {c: cls_of.index(c) for c in set(cls_of)}
                lastk = {c: NK - 1 - cls_of[::-1].index(c) for c in set(cls_of)}
                for kt in range(NK):
                    k0 = kt * 128
                    cls = cls_of[kt]
                    ps = pss.next()
                    P.mm(ps, ps.t[:, 0:512], kT.t[:, g_ // 4, k0:k0 + 128], qm.t[:, g_, :], [kT, qm], start=True, stop=(cls != 1))
                    if cls == 1:
                        P.mm(ps, ps.t[:, 0:512], self.identb.t[:], bD.t[:, s_, (k0 - q0) // 128, :], [self.identb, bD], start=False, stop=True)
                        bias = zero_col
                        rd = [ps, self.epsr]
                    elif cls == 0:
                        bias = colL.t[:, s_, (q0 - k0) // 128:(q0 - k0) // 128 + 1]
                        rd = [ps, colL]
                    else:
                        bias = colR.t[:, s_, (k0 - q0) // 128:(k0 - q0) // 128 + 1]
                        rd = [ps, colR]
                    pt = ptr.next()
                    P.act(pt.t[:], ps.t[:, 0:512], AF.Exp, rd, [pt], bias=bias)
                    for sub in range(4):
                        po = poA if sub < 2 else poB
                        off = ((sub % 2) * 3 + cls) * 65
                        P.mm(po, po.t[:, off:off + 65], pt.t[:, sub * 128:(sub + 1) * 128], V1.t[:, kt, hd * 65:(hd + 1) * 65],
                             [pt, V1], start=(kt == 0 and sub % 2 == 0), stop=(kt == lastk[cls]), sgc=True)
                for sub in range(4):
                    po = poA if sub < 2 else poB
                    o0 = (sub % 2) * 3 * 65
                    tot = totr.next()
                    P.cp("act", tot.t[:], po.t[:, o0 + 65:o0 + 130], [po], [tot])
                    if 0 in first:
                        P.stt("dve", tot.t[:], po.t[:, o0:o0 + 65], fLR.t[:, s_, sub:sub + 1], tot.t[:], ALU.mult, ALU.add, [po, fLR, tot], [tot])
                    if 2 in first:
                        P.stt("dve", tot.t[:], po.t[:, o0 + 130:o0 + 195], fLR.t[:, s_, 4 + sub:5 + sub], tot.t[:], ALU.mult, ALU.add,
                              [po, fLR, tot], [tot])
                    rc = rcr.next()
                    P.op("dve", lambda g, rc=rc, tot=tot: g.reciprocal(out=rc.t[:, 0:1], in_=tot.t[:, 64:65]), [tot], [rc])
                    P.ts("dve", att.t[:, sub, g_, :], tot.t[:, 0:64], rc.t[:, 0:1], None, ALU.mult, None, [tot, rc], [att])
            if "att" in self.dbg and l == 0 and si == len(self.seqs) - 1 and qb == 0:
                P.dma("sp", self.dbg["att"], att.t[:].rearrange("p a b c -> p (a b c)"), reads=[att])
                P.dma("sp", self.dbg["lams"], lams.t[:], reads=[lams])
            for sub in range(4):
                a = ar.next()
                av = att.t[:, sub, :, :].rearrange("p (h two) d -> p h two d", two=2)
                P.stt("dve", a.t[:], av[:, :, 1, :], lams.t[:, 3:4], av[:, :, 0, :], ALU.mult, ALU.add, [att, lams], [a])
                sq = sqr.next()
                P.tt("pool", sq.t[:], a.t[:], a.t[:], ALU.mult, [a], [sq])
                rc = rcr.next()
                P.op("dve", lambda g, rc=rc, sq=sq: g.tensor_reduce(out=rc.t[:, 0:4], in_=sq.t[:], axis=AX.X, op=ALU.add), [sq], [rc])
                P.act(rc.t[:, 0:4], rc.t[:, 0:4], AF.Sqrt, [rc, self.epsr], [rc], bias=self.epsr.t[:, 1:2], scale=1.0 / 64)
                P.op("dve", lambda g, rc=rc: g.reciprocal(out=rc.t[:, 0:4], in_=rc.t[:, 0:4]), [rc], [rc])
                y = yr.next()
                yv = y.t[:].rearrange("p (h d) -> p h d", h=4)
                for hd in range(4):
                    P.ts("dve", yv[:, hd, :], a.t[:, hd, :], rc.t[:, hd:hd + 1], 1.0 - lam_init, ALU.mult, ALU.mult, [a, rc], [y])
                P.tt("pool", yv, yv, gt.t[:], ALU.mult, [y, gt], [y])
                pt_ = pst.next()
                for cc in range(2):
                    P.tr(pt_, pt_.t[:, cc * 128:(cc + 1) * 128], y.t[:, cc * 128:(cc + 1) * 128], self.ident.t[:], [y, self.ident])
                tq = q0 + sub * 128
                P.cp("act", yst.t[:, :, tq:tq + 128], pt_.t[:, 0:256].rearrange("p (c t) -> p c t", c=2), [pt_], [yst])
        P.dma("pool", self.yM[6:8, :, t0:t0 + L].rearrange("c p t -> p c t"), yst.t[:, :, 0:L], reads=[yst])
    S.close()


Builder.mix_na = mix_na
Builder.mix_da = mix_da


CDEC = math.exp(-0.5)


def rw_host(inp):
    f = lambda a: np.asarray(a, np.float32)
    out = {}
    mu_l, mulw_l, mulg_l, hp_l = [], [], [], []
    for l in range(DEPTH):
        mu = f(inp["rwkv_mu"][l])
        a = np.zeros((64, 8, 3, 2), np.float32)
        for h in range(8):
            for q in range(3):
                for m in range(2):
                    a[:, h, q, m] = mu[m, q * 512 + h * 64:q * 512 + h * 64 + 64]
        mu_l.append(a.reshape(64, 48))
        mulw_l.append(np.stack([mu[0, 1536:1600], mu[1, 1536:1600], mu[0, 1600:1664], mu[1, 1600:1664]], axis=1))
        mulg_l.append(np.stack([mu[0, 1664:1792], mu[1, 1664:1792]], axis=1))
        hp = np.zeros((64, 8, 11), np.float32)
        for h in range(8):
            sl = slice(h * 64, h * 64 + 64)
            hp[:, h, 0] = f(inp["rwkv_k_k"][l])[sl]
            hp[:, h, 1] = f(inp["rwkv_lnx_g"][l])[sl]
            hp[:, h, 2] = f(inp["rwkv_lnx_b"][l])[sl]
            for d in range(2):
                hp[:, h, 3 + 4 * d + 0] = f(inp["rwkv_w0"][l, d])[sl]
                hp[:, h, 3 + 4 * d + 1] = f(inp["rwkv_a0"][l, d])[sl]
                hp[:, h, 3 + 4 * d + 2] = f(inp["rwkv_k_a"][l, d])[sl]
                hp[:, h, 3 + 4 * d + 3] = f(inp["rwkv_r_k"][l, d]).reshape(-1)[sl]
        hp_l.append(hp.reshape(64, 88))
    out["rw_mu"] = np.stack(mu_l); out["rw_mulw"] = np.stack(mulw_l); out["rw_mulg"] = np.stack(mulg_l)
    out["rw_hp"] = np.stack(hp_l)
    out["rw_w2"] = np.ascontiguousarray(f(inp["rwkv_w2"])); out["rw_a2"] = np.ascontiguousarray(f(inp["rwkv_a2"]))
    out["rw_g2"] = np.ascontiguousarray(f(inp["rwkv_g2"]))
    i = np.arange(64)
    row, col = i[:, None], i[None, :]
    mk = np.zeros((2, 64, 320), np.float32)
    for d in range(2):
        st = (row < col) if d == 0 else (row > col)
        inc = (row <= col) if d == 0 else (row >= col)
        stT = (col < row) if d == 0 else (col > row)
        mk[d] = np.concatenate([st, inc, st, inc, stT], axis=1).astype(np.float32)
    out["rw_mask"] = mk
    rs = np.ones((64, 1024), np.float32)
    rs[:, ::64] = 0.0
    out["rw_reset"] = rs
    return out


SMALL_SHAPES.update({"rw_mu": [DEPTH, 64, 48], "rw_mulw": [DEPTH, 64, 4], "rw_mulg": [DEPTH, 128, 2], "rw_hp": [DEPTH, 64, 88],
                     "rw_w2": [DEPTH, 2, 64, 512], "rw_a2": [DEPTH, 2, 64, 512], "rw_g2": [DEPTH, 128, 512],
                     "rw_mask": [2, 64, 320], "rw_reset": [64, 1024]})
_host_small0 = host_small


def host_small(inp):
    o = _host_small0(inp)
    o.update(rw_host(inp))
    return o


def mix_rwkv(self, l):
    P = self.P
    nc = self.nc
    S = Scope(nc)
    sm = self.small
    T = self.T
    if not hasattr(self, "yF"):
        self.yF = nc.dram_tensor("yF", [2, 8, 64, T], F32).ap()
    yfb = Buf()
    SEG = min(1024, min(self.seqs))
    W = SEG
    sb = lambda shape, name: S.sb(shape, F32, name)
    mu = sb([64, 48], "mu"); c0 = sb([64, 24], "c0"); mulw = sb([64, 4], "mulw"); c0w = sb([64, 2], "c0w")
    mulg = sb([128, 2], "mulg"); c0g = sb([128, 1], "c0g"); hp = sb([64, 88], "hp"); omk = sb([64, 16], "omk")
    w2 = sb([64, 2, 512], "w2"); a2 = sb([64, 2, 512], "a2"); g2 = sb([128, 512], "g2")
    mk = sb([64, 2, 320], "mk"); rst = sb([64, 1024], "rst"); ones = sb([64, 64], "ones"); onesm = sb([64, 64], "onesm")
    P.dma("sp", mu.t[:], sm["rw_mu"][l], writes=[mu]); P.dma("sp", mulw.t[:], sm["rw_mulw"][l], writes=[mulw])
    P.dma("sp", mulg.t[:], sm["rw_mulg"][l], writes=[mulg]); P.dma("sp", hp.t[:], sm["rw_hp"][l], writes=[hp])
    P.dma("sp", w2.t[:], sm["rw_w2"][l].rearrange("d k n -> k d n"), writes=[w2])
    P.dma("sp", a2.t[:], sm["rw_a2"][l].rearrange("d k n -> k d n"), writes=[a2])
    P.dma("sp", g2.t[:], sm["rw_g2"][l], writes=[g2])
    P.dma("sp", mk.t[:], sm["rw_mask"].rearrange("d p n -> p d n"), writes=[mk])
    P.dma("sp", rst.t[:], sm["rw_reset"], writes=[rst])
    P.memset("dve", ones, ones.t[:], 1.0); P.memset("dve", onesm, onesm.t[:], 1.0 / 64)
    muv = mu.t[:].rearrange("p (a m) -> p a m", m=2)
    P.tt("dve", c0.t[:], muv[:, :, 0], muv[:, :, 1], ALU.add, [mu], [c0])
    P.ts("dve", c0.t[:], c0.t[:], -1.0, 1.0, ALU.mult, ALU.add, [c0], [c0])
    mwv = mulw.t[:].rearrange("p (a m) -> p a m", m=2)
    P.tt("dve", c0w.t[:], mwv[:, :, 0], mwv[:, :, 1], ALU.add, [mulw], [c0w])
    P.ts("dve", c0w.t[:], c0w.t[:], -1.0, 1.0, ALU.mult, ALU.add, [c0w], [c0w])
    P.tt("dve", c0g.t[:], mulg.t[:, 0:1], mulg.t[:, 1:2], ALU.add, [mulg], [c0g])
    P.ts("dve", c0g.t[:], c0g.t[:], -1.0, 1.0, ALU.mult, ALU.add, [c0g], [c0g])
    hpv = hp.t[:].rearrange("p (h c) -> p h c", c=11)
    for h in range(8):
        for d in range(2):
            P.ts("dve", omk.t[:, h * 2 + d:h * 2 + d + 1], hpv[:, h, 5 + 4 * d:6 + 4 * d], -1.0, 1.0, ALU.mult, ALU.add, [hp], [omk])
    zcol = self.epsr.t[0:64, 3:4]
    names = ["zr", "zk", "zv", "zw", "za"]
    Z = {n: sb([64, W + 2], n) for n in names}
    zg = sb([128, W + 2], "zg"); gl = sb([128, W], "gl")
    Tl = {n: sb([64, W], n) for n in ["r", "k", "v", "wl", "al", "kk", "sg", "a", "kd", "b", "t1", "bon", "Pf", "E", "Sf", "X",
                                      "eI", "eX", "eN", "eT", "at", "bt", "kt", "rt", "bh", "kh", "Y", "yf", "bf"]}
    tmr = Ring([sb([64, 256], "tm") for _ in range(2)])
    gmr = Ring([sb([64, 320], "gm") for _ in range(2)])
    p2r = Ring([sb([64, 128], "p2") for _ in range(3)])
    ttr = Ring([sb([64, 64], "tt") for _ in range(3)])
    x1r = Ring([sb([64, 64], "x1") for _ in range(2)])
    u0r = Ring([sb([64, 64], "u0") for _ in range(2)])
    ahr = Ring([sb([64, 64], "ah") for _ in range(2)])
    dgr = Ring([sb([64, 64], "dg") for _ in range(2)])
    ur = Ring([sb([64, 64], "u") for _ in range(2)])
    str_ = Ring([sb([64, 64], "st") for _ in range(3)])
    yo = S.sb([64, W], BF16, "yo")
    pss = Ring([S.ps([128, 512], F32, "rps") for _ in range(7)])
    idn = self.ident.t[0:64, 0:64]

    def shift(dst, src, c0c, m0c, m1c, np_=64):
        P.ts("dve", dst.t[:, :], src.t[:, 1:W + 1], c0c, None, ALU.mult, None, [src], [dst])
        P.stt("dve", dst.t[:, :], src.t[:, 0:W], m0c, dst.t[:, :], ALU.mult, ALU.add, [src, dst], [dst])
        P.stt("dve", dst.t[:, :], src.t[:, 2:W + 2], m1c, dst.t[:, :], ALU.mult, ALU.add, [src, dst], [dst])

    for si, L in enumerate(self.seqs):
        t0 = self.starts[si]
        nseg = L // SEG
        for h in range(8):
            cc, pb = h // 2, (h % 2) * 64
            for d in range(2):
                st = str_.next()
                P.memset("dve", st, st.t[:], 0.0)
                for sgi in (range(nseg) if d == 0 else range(nseg - 1, -1, -1)):
                    s0 = t0 + sgi * SEG
                    lo = 0 if sgi > 0 else 1
                    hi = W + 2 if sgi < nseg - 1 else W + 1
                    srcs = {"zr": (cc, pb), "zk": (4 + cc, pb), "zv": (8 + cc, pb), "zw": (12, 0), "za": (12, 64)}
                    for n in names:
                        if lo == 1 or hi == W + 1:
                            P.memset("dve", Z[n], Z[n].t[:], 0.0)
                        c_, p_ = srcs[n]
                        P.dma("sp", Z[n].t[:, lo:hi], self.zA[c_, p_:p_ + 64, s0 - 1 + lo:s0 - 1 + hi], writes=[Z[n]])
                    if lo == 1 or hi == W + 1:
                        P.memset("dve", zg, zg.t[:], 0.0)
                    P.dma("sp", zg.t[:, lo:hi], self.zA[13, :, s0 - 1 + lo:s0 - 1 + hi], writes=[zg])
                    for qi, (dn, sn) in enumerate((("r", "zr"), ("k", "zk"), ("v", "zv"))):
                        ix = h * 3 + qi
                        shift(Tl[dn], Z[sn], c0.t[:, ix:ix + 1], mu.t[:, 2 * ix:2 * ix + 1], mu.t[:, 2 * ix + 1:2 * ix + 2])
                    shift(Tl["wl"], Z["zw"], c0w.t[:, 0:1], mulw.t[:, 0:1], mulw.t[:, 1:2])
                    shift(Tl["al"], Z["za"], c0w.t[:, 1:2], mulw.t[:, 2:3], mulw.t[:, 3:4])
                    P.ts("dve", gl.t[:, :], zg.t[:, 1:W + 1], c0g.t[:, 0:1], None, ALU.mult, None, [zg], [gl])
                    P.stt("dve", gl.t[:, :], zg.t[:, 0:W], mulg.t[:, 0:1], gl.t[:, :], ALU.mult, ALU.add, [zg, gl], [gl])
                    P.stt("dve", gl.t[:, :], zg.t[:, 2:W + 2], mulg.t[:, 1:2], gl.t[:, :], ALU.mult, ALU.add, [zg, gl], [gl])
                    r, k, v, kk, sg, a, kd, b, t1 = (Tl[n] for n in ("r", "k", "v", "kk", "sg", "a", "kd", "b", "t1"))
                    P.ts("dve", kk.t[:], k.t[:], hpv[:, h, 0:1], None, ALU.mult, None, [k, hp], [kk])
                    P.tt("dve", t1.t[:], kk.t[:], kk.t[:], ALU.mult, [kk], [t1])
                    for blk in range(W // 512):
                        bs = slice(blk * 512, blk * 512 + 512)
                        ps = pss.next()
                        P.mm(ps, ps.t[0:64, 0:512], ones.t[:], t1.t[:, bs], [ones, t1])
                        P.act(Tl["X"].t[:, bs], ps.t[0:64, 0:512], AF.Sqrt, [ps, self.epsr], [Tl["X"]], bias=zcol)
                    P.ts("dve", Tl["X"].t[:], Tl["X"].t[:], 1e-12, None, ALU.max, None, [Tl["X"]], [Tl["X"]])
                    P.op("dve", lambda g: g.reciprocal(out=Tl["X"].t[:], in_=Tl["X"].t[:]), [Tl["X"]], [Tl["X"]])
                    P.tt("dve", kk.t[:], kk.t[:], Tl["X"].t[:], ALU.mult, [kk, Tl["X"]], [kk])
                    P.act(Tl["wl"].t[:], Tl["wl"].t[:], AF.Tanh, [Tl["wl"]], [Tl["wl"]])
                    for blk in range(W // 512):
                        bs = slice(blk * 512, blk * 512 + 512)
                        ps = pss.next()
                        P.mm(ps, ps.t[0:64, 0:512], w2.t[:, d, h * 64:h * 64 + 64], Tl["wl"].t[:, bs], [w2, Tl["wl"]])
                        P.act(sg.t[:, bs], ps.t[0:64, 0:512], AF.Sigmoid, [ps, hp], [sg], bias=hpv[:, h, 3 + 4 * d:4 + 4 * d])
                        ps = pss.next()
                        P.mm(ps, ps.t[0:64, 0:512], a2.t[:, d, h * 64:h * 64 + 64], Tl["al"].t[:, bs], [a2, Tl["al"]])
                        P.act(a.t[:, bs], ps.t[0:64, 0:512], AF.Sigmoid, [ps, hp], [a], bias=hpv[:, h, 4 + 4 * d:5 + 4 * d])
                    P.ts("dve", kd.t[:], a.t[:], hpv[:, h, 5 + 4 * d:6 + 4 * d], omk.t[:, h * 2 + d:h * 2 + d + 1], ALU.mult, ALU.add, [a, hp, omk], [kd])
                    P.tt("dve", kd.t[:], kd.t[:], k.t[:], ALU.mult, [kd, k], [kd])
                    P.tt("dve", b.t[:], kk.t[:], a.t[:], ALU.mult, [kk, a], [b])
                    P.stt("dve", t1.t[:], r.t[:], hpv[:, h, 6 + 4 * d:7 + 4 * d], kd.t[:], ALU.mult, ALU.mult, [r, hp, kd], [t1])
                    bon = Tl["bon"]
                    for blk in range(W // 512):
                        bs = slice(blk * 512, blk * 512 + 512)
                        ps = pss.next()
                        P.mm(ps, ps.t[0:64, 0:512], ones.t[:], t1.t[:, bs], [ones, t1])
                        P.tt("dve", bon.t[:, bs], ps.t[0:64, 0:512], v.t[:, bs], ALU.mult, [ps, v], [bon])
                    Pf, E, Sf, X = Tl["Pf"], Tl["E"], Tl["Sf"], Tl["X"]
                    P.op("dve", lambda g: g.tensor_tensor_scan(out=Pf.t[:], data0=rst.t[:, 0:W], data1=sg.t[:], initial=0.0,
                                                                op0=ALU.mult, op1=ALU.add), [rst, sg], [Pf])
                    P.tt("dve", E.t[:], Pf.t[:], sg.t[:], ALU.subtract, [Pf, sg], [E])
                    for j in range(W // 64):
                        P.ts("dve", Sf.t[:, 64 * j:64 * j + 64], E.t[:, 64 * j:64 * j + 64], -1.0, Pf.t[:, 64 * j + 63:64 * j + 64],
                             ALU.mult, ALU.add, [E, Pf], [Sf])
                    P.tt("dve", X.t[:], Sf.t[:], sg.t[:], ALU.subtract, [Sf, sg], [X])
                    Gi, Ge, Tm = (Pf, E, X) if d == 0 else (Sf, X, E)
                    eI, eX, eN, eT = Tl["eI"], Tl["eX"], Tl["eN"], Tl["eT"]
                    P.act(eI.t[:], Gi.t[:], AF.Exp, [Gi], [eI], scale=-CDEC)
                    P.act(eX.t[:], Ge.t[:], AF.Exp, [Ge], [eX], scale=-CDEC)
                    P.act(eN.t[:], Gi.t[:], AF.Exp, [Gi], [eN], scale=CDEC)
                    P.act(eT.t[:], Tm.t[:], AF.Exp, [Tm], [eT], scale=-CDEC)
                    at, bt, kt, rt, bh, kh = (Tl[n] for n in ("at", "bt", "kt", "rt", "bh", "kh"))
                    P.stt("dve", at.t[:], kk.t[:], -1.0, eX.t[:], ALU.mult, ALU.mult, [kk, eX], [at])
                    P.tt("dve", bt.t[:], b.t[:], eN.t[:], ALU.mult, [b, eN], [bt])
                    P.tt("dve", kt.t[:], kd.t[:], eN.t[:], ALU.mult, [kd, eN], [kt])
                    P.tt("dve", rt.t[:], r.t[:], eI.t[:], ALU.mult, [r, eI], [rt])
                    P.tt("dve", bh.t[:], b.t[:], eT.t[:], ALU.mult, [b, eT], [bh])
                    P.tt("dve", kh.t[:], kd.t[:], eT.t[:], ALU.mult, [kd, eT], [kh])
                    Y = Tl["Y"]
                    if self.debug.get("rw_skip_chunks") or self.debug.get("rw_lvl", 99) < 99:
                        P.memset("dve", Y, Y.t[:], 0.0)
                    for j in (range(W // 64) if d == 0 else range(W // 64 - 1, -1, -1)):
                        if self.debug.get("rw_skip_chunks"):
                            break
                        cs = slice(64 * j, 64 * j + 64)
                        ps = pss.next()
                        for qi, src in enumerate((at, bh, kh, v)):
                            P.tr(ps, ps.t[0:64, qi * 64:qi * 64 + 64], src.t[:, cs], idn, [src, self.ident])
                        tm = tmr.next()
                        P.cp("act", tm.t[:], ps.t[0:64, 0:256], [ps], [tm])
                        Atm, Bhtm, Khtm, Vtm = (tm.t[:, q * 64:q * 64 + 64] for q in range(4))
                        if self.debug.get("rw_lvl", 99) <= 1:
                            continue
                        ps = pss.next()
                        for qi, (lt, rh) in enumerate(((bt, at), (bt, rt), (kt, at), (kt, rt), (at, bt))):
                            P.mm(ps, ps.t[0:64, qi * 64:qi * 64 + 64], lt.t[:, cs], rh.t[:, cs], [lt, rh])
                        gm = gmr.next()
                        P.tt("dve", gm.t[:], ps.t[0:64, 0:320], mk.t[:, d, :], ALU.mult, [ps, mk], [gm])
                        Aab, Abr, Aak, Akr, NT = (gm.t[:, q * 64:q * 64 + 64] for q in range(5))
                        if self.debug.get("rw_lvl", 99) <= 2:
                            continue
                        Tt = ttr.next()
                        P.tt("dve", Tt.t[:], Aab, idn, ALU.add, [gm, self.ident], [Tt])
                        Pm, PTm, prd = Aab, NT, gm
                        for lev in range(5):
                            ps = pss.next()
                            P.mm(ps, ps.t[0:64, 0:64], PTm, Pm, [prd])
                            P.mm(ps, ps.t[0:64, 64:128], Pm, PTm, [prd])
                            p2 = p2r.next()
                            P.cp("act", p2.t[:], ps.t[0:64, 0:128], [ps], [p2])
                            Pm, PTm, prd = p2.t[:, 0:64], p2.t[:, 64:128], p2
                            ps = pss.next()
                            P.mm(ps, ps.t[0:64, 0:64], PTm, Tt.t[:], [p2, Tt])
                            Tn = ttr.next()
                            P.tt("dve", Tn.t[:], ps.t[0:64, 0:64], Tt.t[:], ALU.add, [ps, Tt], [Tn])
                            Tt = Tn
                        if self.debug.get("rw_lvl", 99) <= 3:
                            continue
                        ps = pss.next()
                        P.mm(ps, ps.t[0:64, 0:64], Aak, Vtm, [gm, tm])
                        x1 = x1r.next()
                        P.cp("act", x1.t[:], ps.t[0:64, 0:64], [ps], [x1])
                        if self.debug.get("rw_lvl", 99) <= 3.3:
                            continue
                        ps = pss.next()
                        P.mm(ps, ps.t[0:64, 0:64], Tt.t[:], x1.t[:], [Tt, x1])
                        P.mm(ps, ps.t[0:64, 64:128], Atm, Tt.t[:], [tm, Tt])
                        u0 = u0r.next(); ah = ahr.next()
                        P.cp("act", u0.t[:], ps.t[0:64, 0:64], [ps], [u0])
                        P.cp("dve", ah.t[:], ps.t[0:64, 64:128], [ps], [ah])
                        if self.debug.get("rw_lvl", 99) <= 3.6:
                            continue
                        dg = dgr.next()
                        gcol = 64 * j + 63 if d == 0 else 64 * j
                        P.ts("dve", dg.t[:], idn, eI.t[:, gcol:gcol + 1], None, ALU.mult, None, [self.ident, eI], [dg])
                        if self.debug.get("rw_lvl", 99) <= 4:
                            continue
                        ps = pss.next()
                        P.mm(ps, ps.t[0:64, 0:64], ah.t[:], st.t[:], [ah, st])
                        u = ur.next()
                        P.tt("dve", u.t[:], ps.t[0:64, 0:64], u0.t[:], ALU.add, [ps, u0], [u])
                        ps = pss.next()
                        P.mm(ps, ps.t[0:64, 0:64], st.t[:], rt.t[:, cs], [st, rt], start=True, stop=False)
                        P.mm(ps, ps.t[0:64, 0:64], u.t[:], Abr, [u, gm], start=False, stop=False)
                        P.mm(ps, ps.t[0:64, 0:64], Vtm, Akr, [tm, gm], start=False, stop=True)
                        P.cp("act", Y.t[:, cs], ps.t[0:64, 0:64], [ps], [Y])
                        ps = pss.next()
                        P.mm(ps, ps.t[0:64, 0:64], dg.t[:], st.t[:], [dg, st], start=True, stop=False)
                        P.mm(ps, ps.t[0:64, 0:64], Bhtm, u.t[:], [tm, u], start=False, stop=False)
                        P.mm(ps, ps.t[0:64, 0:64], Khtm, Vtm, [tm], start=False, stop=True)
                        st = str_.next()
                        P.cp("dve", st.t[:], ps.t[0:64, 0:64], [ps], [st])
                    if d == 0:
                        P.dma("pool", self.yF[0, h, :, s0:s0 + W], Y.t[:], reads=[Y], writes=[yfb])
                        P.dma("pool", self.yF[1, h, :, s0:s0 + W], bon.t[:], reads=[bon], writes=[yfb])
                    else:
                        yf, bf = Tl["yf"], Tl["bf"]
                        P.dma("sp", yf.t[:], self.yF[0, h, :, s0:s0 + W], reads=[yfb], writes=[yf])
                        P.dma("sp", bf.t[:], self.yF[1, h, :, s0:s0 + W], reads=[yfb], writes=[bf])
                        P.tt("dve", Y.t[:], Y.t[:], yf.t[:], ALU.add, [Y, yf], [Y])
                        P.tt("dve", bon.t[:], bon.t[:], bf.t[:], ALU.add, [bon, bf], [bon])
                        P.act(gl.t[:], gl.t[:], AF.Sigmoid, [gl], [gl])
                        for blk in range(W // 512):
                            bs = slice(blk * 512, blk * 512 + 512)
                            ps = pss.next()
                            P.mm(ps, ps.t[0:64, 0:512], onesm.t[:], Y.t[:, bs], [onesm, Y])
                            P.tt("dve", Y.t[:, bs], Y.t[:, bs], ps.t[0:64, 0:512], ALU.subtract, [Y, ps], [Y])
                            P.tt("dve", t1.t[:, bs], Y.t[:, bs], Y.t[:, bs], ALU.mult, [Y], [t1])
                            ps = pss.next()
                            P.mm(ps, ps.t[0:64, 0:512], onesm.t[:], t1.t[:, bs], [onesm, t1])
                            P.act(t1.t[:, bs], ps.t[0:64, 0:512], AF.Sqrt, [ps, self.epsr], [t1], bias=self.epsr.t[0:64, 2:3])
                            P.op("dve", lambda g, bs=bs: g.reciprocal(out=t1.t[:, bs], in_=t1.t[:, bs]), [t1], [t1])
                            P.tt("dve", Y.t[:, bs], Y.t[:, bs], t1.t[:, bs], ALU.mult, [Y, t1], [Y])
                            P.ts("dve", Y.t[:, bs], Y.t[:, bs], hpv[:, h, 1:2], hpv[:, h, 2:3], ALU.mult, ALU.add, [Y, hp], [Y])
                            P.tt("dve", Y.t[:, bs], Y.t[:, bs], bon.t[:, bs], ALU.add, [Y, bon], [Y])
                            ps = pss.next()
                            P.mm(ps, ps.t[0:64, 0:512], g2.t[:, h * 64:h * 64 + 64], gl.t[:, bs], [g2, gl])
                            P.tt("dve", yo.t[:, bs], Y.t[:, bs], ps.t[0:64, 0:512], ALU.mult, [Y, ps], [yo])
                        P.dma("pool", self.yM[cc, pb:pb + 64, s0:s0 + W], yo.t[:], reads=[yo])
    S.close()


Builder.mix_rwkv = mix_rwkv
```

```python
import math
from contextlib import ExitStack
import numpy as np
import concourse.bass as bass
import concourse.mybir as mybir
from concourse.bass_utils import run_bass_kernel_spmd

F32 = mybir.dt.float32
BF16 = mybir.dt.bfloat16
AF = mybir.ActivationFunctionType
ALU = mybir.AluOpType
AX = mybir.AxisListType

D = 1024
DFF = 2816
KC = 8
FC = 22
DEPTH = 2
NCORES = 8
RMS_EPS = 1e-6
SUBLN_EPS = 1e-5
LNX_EPS = 64e-5
NWIN = 52
CH = 64


class Buf:
    __slots__ = ("w", "r", "psum")

    def __init__(self):
        self.w = None
        self.r = {}
        self.psum = False


class Tile:
    def __init__(self, t, buf=None):
        self.t = t
        self.buf = buf if buf is not None else Buf()

    def __getitem__(self, k):
        return self.t[k]


class Prog:
    ENG = ("pe", "dve", "act", "pool", "sp")
    NDS = 24

    def __init__(self, nc):
        self.nc = nc
        self.eng = {"pe": nc.tensor, "dve": nc.vector, "act": nc.scalar, "pool": nc.gpsimd, "sp": nc.sync}
        self.sem = {}
        for e in self.ENG:
            self.sem[e] = nc.semaphore("s_" + e).__enter__()
        for i in range(self.NDS):
            self.sem[("d", i)] = nc.semaphore("d%d" % i).__enter__()
            self.sem[("g", i)] = nc.semaphore("g%d" % i).__enter__()
        self.gnext = 0
        self.cnt = {k: 0 for k in self.sem}
        self.waited = {e: {} for e in self.ENG}
        self.dnext = 0
        self.ninst = 0

    def _need(self, reads, writes, e=None):
        need = {}
        for b in reads:
            b = b.buf if isinstance(b, Tile) else b
            if b.w is not None and need.get(b.w[0], 0) < b.w[1]:
                need[b.w[0]] = b.w[1]
            if b.psum:
                for k, v in b.r.items():
                    if k != e and need.get(k, 0) < v:
                        need[k] = v
        for b in writes:
            b = b.buf if isinstance(b, Tile) else b
            if b.w is not None and need.get(b.w[0], 0) < b.w[1]:
                need[b.w[0]] = b.w[1]
            for k, v in b.r.items():
                if need.get(k, 0) < v:
                    need[k] = v
        return need

    def _wait(self, e, need, skip_self=False):
        eng = self.eng[e]
        wd = self.waited[e]
        for k, v in need.items():
            if skip_self and k == e:
                continue
            if wd.get(k, 0) >= v:
                continue
            eng.wait_ge(self.sem[k], v)
            wd[k] = v
            self.ninst += 1

    def _mark(self, ev, reads, writes):
        for b in reads:
            b = b.buf if isinstance(b, Tile) else b
            if b.r.get(ev[0], 0) < ev[1]:
                b.r[ev[0]] = ev[1]
        for b in writes:
            b = b.buf if isinstance(b, Tile) else b
            b.w = ev
            b.r = {}

    def op(self, e, fn, reads=(), writes=(), skip_self=False):
        self._wait(e, self._need(reads, writes, e), skip_self)
        ins = fn(self.eng[e])
        ins.then_inc(self.sem[e], 1)
        self.cnt[e] += 1
        self.ninst += 1
        self._mark((e, self.cnt[e]), reads, writes)

    def dma(self, q, out, in_, reads=(), writes=()):
        self._wait(q, self._need(reads, writes))
        if q == "pool":
            k = ("g", self.gnext)
            self.gnext = (self.gnext + 1) % self.NDS
        else:
            k = ("d", self.dnext)
            self.dnext = (self.dnext + 1) % self.NDS
        self.eng[q].dma_start(out=out, in_=in_).then_inc(self.sem[k], 16)
        self.cnt[k] += 16
        self.ninst += 1
        self._mark((k, self.cnt[k]), reads, writes)

    def barrier(self):
        for e in self.ENG:
            self._wait(e, dict(self.cnt))

    def mm(self, out_t, out_ap, lhsT_ap, rhs_ap, reads, start=True, stop=True, sgc=False):
        if sgc:
            self.op("pe", lambda g: g.matmul(out_ap, lhsT=lhsT_ap, rhs=rhs_ap, start=start, stop=stop, skip_group_check=True),
                    reads=reads, writes=[out_t], skip_self=True)
        else:
            self.op("pe", lambda g: g.matmul(out_ap, lhsT=lhsT_ap, rhs=rhs_ap, start=start, stop=stop),
                    reads=reads, writes=[out_t], skip_self=True)

    def tr(self, out_t, out_ap, in_ap, ident_ap, reads):
        self.op("pe", lambda g: g.transpose(out_ap, in_ap, ident_ap), reads=reads, writes=[out_t], skip_self=True)

    def act(self, out_ap, in_ap, func, reads, writes, bias=None, scale=1.0, accum=None):
        kw = {}
        if bias is not None:
            kw["bias"] = bias
        if accum is not None:
            kw["accum_out"] = accum
        self.op("act", lambda g: g.activation(out=out_ap, in_=in_ap, func=func, scale=scale, **kw),
                reads=reads, writes=writes)

    def tt(self, e, out_ap, a_ap, b_ap, op, reads, writes):
        self.op(e, lambda g: g.tensor_tensor(out=out_ap, in0=a_ap, in1=b_ap, op=op), reads=reads, writes=writes)

    def stt(self, e, out_ap, a_ap, scalar, b_ap, op0, op1, reads, writes):
        self.op(e, lambda g: g.scalar_tensor_tensor(out=out_ap, in0=a_ap, scalar=scalar, in1=b_ap, op0=op0, op1=op1),
                reads=reads, writes=writes)

    def ts(self, e, out_ap, a_ap, s1, s2, op0, op1, reads, writes):
        if s2 is None:
            self.op(e, lambda g: g.tensor_scalar(out=out_ap, in0=a_ap, scalar1=s1, scalar2=None, op0=op0),
                    reads=reads, writes=writes)
        else:
            self.op(e, lambda g: g.tensor_scalar(out=out_ap, in0=a_ap, scalar1=s1, scalar2=s2, op0=op0, op1=op1),
                    reads=reads, writes=writes)

    def cp(self, e, out_ap, in_ap, reads, writes):
        if e == "act":
            self.op(e, lambda g: g.copy(out=out_ap, in_=in_ap), reads=reads, writes=writes)
        else:
            self.op(e, lambda g: g.tensor_copy(out=out_ap, in_=in_ap), reads=reads, writes=writes)

    def memset(self, e, t, ap, val):
        self.op(e, lambda g: g.memset(ap, val), reads=(), writes=[t])


class Pool_:
    def __init__(self, nc):
        self.nc = nc
        self.st = ExitStack()
        self.n = 0

    def sb(self, shape, dt, name=None):
        self.n += 1
        return Tile(self.st.enter_context(self.nc.sbuf_tensor("%s_%d" % (name or "t", id(self) % 100000 * 1000 + self.n), list(shape), dt)))

    def ps(self, shape, dt=F32, name=None):
        self.n += 1
        return Tile(self.st.enter_context(self.nc.psum_tensor("%s_%d" % (name or "p", id(self) % 100000 * 1000 + self.n), list(shape), dt)))

    def close(self):
        self.st.close()


class Ring:
    def __init__(self, tiles):
        self.tiles = tiles
        self.i = 0

    def next(self):
        t = self.tiles[self.i % len(self.tiles)]
        self.i += 1
        return t


def fm_pieces(W):
    K, N = W.shape
    return np.ascontiguousarray(W.reshape(K // 128, 128, N // 128, 128).transpose(2, 1, 0, 3)).reshape(N // 128, 128, K)


def pcol(v, nchunk):
    return np.ascontiguousarray(np.asarray(v, np.float32).reshape(nchunk, 128).T)


def host_weights(inp):
    f = lambda a: np.asarray(a, np.float32)
    out = {}
    gu1, d1, gu2, d2, win, wv, pabc, wout = [], [], [], [], [], [], [], []
    for l in range(DEPTH):
        for (gl, dl, pre) in ((gu1, d1, "ffn1"), (gu2, d2, "ffn2")):
            g = fm_pieces(f(inp[pre + "_w_gate"][l]))
            u = fm_pieces(f(inp[pre + "_w_up"][l]))
            gl.append(np.stack([g, u], axis=1).reshape(2 * FC, 128, D))
            dl.append(fm_pieces(f(inp[pre + "_w_down"][l])))
        W = f(inp["w_in"][l])
        cols = [W[:, 0:1792], W[:, 1792:2304]]
        for g in range(8):
            blk = np.zeros((D, 128), np.float32)
            blk[:, (g % 4) * 32:(g % 4) * 32 + 32] = W[:, 2560 + g * 32:2560 + g * 32 + 32]
            cols.append(blk)
        cols += [W[:, 2816:3072], W[:, 3328:6400]]
        win.append(fm_pieces(np.concatenate(cols, axis=1)))
        Wv = np.concatenate([W[:, 2304:2560], W[:, 3072:3328]], axis=1)
        wv.append(np.ascontiguousarray(Wv.reshape(KC, 128, 512).transpose(1, 0, 2)).reshape(128, KC * 512))
        pabc.append(fm_pieces(np.concatenate([f(inp["p_a"][l]), f(inp["p_b"][l]), f(inp["p_c"][l])], axis=0)))
        wout.append(fm_pieces(f(inp["w_out"][l])))
    out["wgu1"] = np.stack(gu1); out["wd1"] = np.stack(d1)
    out["wgu2"] = np.stack(gu2); out["wd2"] = np.stack(d2)
    out["win"] = np.stack(win); out["wv"] = np.stack(wv)
    out["wpabc"] = np.stack(pabc); out["wout"] = np.stack(wout)
    gains = []
    for l in range(DEPTH):
        gains += [pcol(inp["ln_ffn1_g"][l], KC), pcol(inp["ln_mix_g"][l], KC), pcol(inp["ln_ffn2_g"][l], KC)]
    gains.append(pcol(inp["final_g"], KC))
    out["gains"] = np.concatenate(gains, axis=1)
    return out


WSHAPES = {"wgu1": [DEPTH, 2 * FC, 128, D], "wd1": [DEPTH, KC, 128, DFF], "wgu2": [DEPTH, 2 * FC, 128, D],
           "wd2": [DEPTH, KC, 128, DFF], "win": [DEPTH, NWIN, 128, D], "wv": [DEPTH, 128, KC * 512],
           "wpabc": [DEPTH, KC, 128, D], "wout": [DEPTH, KC, 128, D]}


def host_consts():
    c = {}
    c["ident"] = np.eye(128, dtype=np.float32)
    c["onesm"] = np.full((128, 128), 1.0 / D, np.float32)
    return c


CSHAPES = {"ident": [128, 128], "onesm": [128, 128]}


_UID = [0]


def _uid(prefix):
    _UID[0] += 1
    return "%s%d" % (prefix, _UID[0])


class Scope:
    def __init__(self, nc):
        self.nc = nc
        self.st = ExitStack()

    def sb(self, shape, dt, name="t"):
        return Tile(self.st.enter_context(self.nc.sbuf_tensor(_uid(name), list(shape), dt)))

    def ps(self, shape, dt=F32, name="p"):
        t = Tile(self.st.enter_context(self.nc.psum_tensor(_uid(name), list(shape), dt)))
        t.buf.psum = True
        return t

    def close(self):
        self.st.close()


class WStream:
    def __init__(self, B, slots, plan):
        self.B = B
        self.slots = slots
        self.plan = plan
        self.loaded = 0
        self.pos = 0

    def _load(self, i):
        src, G, X = self.plan[i]
        slot = self.slots[i % len(self.slots)]
        if G == 0:
            self.B.P.dma("sp", slot.t[:, 0:X], src, writes=[slot])
        else:
            self.B.P.dma("sp", slot.t[:, 0:G * X].rearrange("p (g x) -> p g x", g=G),
                         src.rearrange("g p x -> p g x"), writes=[slot])

    def get(self):
        while self.loaded < len(self.plan) and self.loaded < self.pos + len(self.slots):
            self._load(self.loaded)
            self.loaded += 1
        slot = self.slots[self.pos % len(self.slots)]
        self.pos += 1
        return slot


class Builder:
    def __init__(self, seqs, debug=None):
        self.seqs = list(seqs)
        self.T = sum(self.seqs)
        self.starts = [sum(self.seqs[:i]) for i in range(len(self.seqs))]
        self.TT = 1024 if self.T % 1024 == 0 else 512
        self.NS = self.TT // 512
        self.debug = debug or {}
        nc = bass.Bass("TRN2", target_bir_lowering=False)
        self.nc = nc
        self.P = Prog(nc)
        T = self.T
        dt_in = lambda name, shape: nc.dram_tensor(name, list(shape), F32, kind="ExternalInput").ap()
        self.xin = dt_in("xin", [T, D])
        self.yout = nc.dram_tensor("yout", [T, D], F32, kind="ExternalOutput").ap()
        self.wf = {k: dt_in(k, s) for k, s in WSHAPES.items()}
        self.wb = {k: nc.dram_tensor(k + "_b", list(s), BF16).ap() for k, s in WSHAPES.items()}
        self.cst = {k: dt_in("c_" + k, s) for k, s in CSHAPES.items()}
        self.gains_d = dt_in("gains", [128, 56])
        self.small = {k: dt_in(k, s) for k, s in SMALL_SHAPES.items()}
        scr = lambda name, shape, dt: nc.dram_tensor(name, list(shape), dt).ap()
        self.xT = scr("xT", [KC, 128, T], F32)
        self.zA = scr("zA", [14, 128, T], F32)
        self.zNq = scr("zNq", [2, 128, T], BF16)
        self.zNk = scr("zNk", [2, 128, T], BF16)
        self.zDq = scr("zDq", [8, 128, T], BF16)
        self.zDk = scr("zDk", [2, 128, T], BF16)
        self.zV = scr("zV", [T, 520], BF16)
        self.zG = scr("zG", [24, 128, T], BF16)
        self.yM = scr("yM", [KC, 128, T], BF16)
        self.dbg = {}
        for k, s in self.debug.items():
            if not isinstance(s, (list, tuple)):
                continue
            self.dbg[k] = nc.dram_tensor("dbg_" + k, list(s), F32, kind="ExternalOutput").ap()

    def build(self):
        P = self.P
        nc = self.nc
        G = Scope(nc)
        self.G = G
        self.ident = G.sb([128, 128], F32, "ident")
        self.identb = G.sb([128, 128], BF16, "identb")
        self.onesm = G.sb([128, 128], BF16, "onesm")
        self.gains = G.sb([128, 56], F32, "gains")
        self.epsr = G.sb([128, 4], F32, "eps")
        tmp = G.sb([128, 128], F32, "ctmp")
        P.dma("sp", self.ident.t[:], self.cst["ident"], writes=[self.ident])
        P.dma("sp", tmp.t[:], self.cst["onesm"], writes=[tmp])
        P.dma("sp", self.gains.t[:], self.gains_d, writes=[self.gains])
        P.cp("dve", self.identb.t[:], self.ident.t[:], [self.ident], [self.identb])
        P.cp("dve", self.onesm.t[:], tmp.t[:], [tmp], [self.onesm])
        P.memset("dve", self.epsr, self.epsr.t[:, 0:1], RMS_EPS)
        P.memset("dve", self.epsr, self.epsr.t[:, 1:2], SUBLN_EPS)
        P.memset("dve", self.epsr, self.epsr.t[:, 2:3], LNX_EPS)
        P.memset("dve", self.epsr, self.epsr.t[:, 3:4], 0.0)
        self.prep_weights()
        P.barrier()
        for l in range(DEPTH):
            self.phaseA(l)
            P.barrier()
            self.mixers(l)
            P.barrier()
            self.phaseC(l)
            P.barrier()
        G.close()
        return nc

    def prep_weights(self):
        P = self.P
        S = Scope(self.nc)
        CHK = 8192
        st32 = [S.sb([128, CHK], F32, "w32") for _ in range(3)]
        st16 = [S.sb([128, CHK], BF16, "w16") for _ in range(3)]
        i = 0
        for k, shp in WSHAPES.items():
            tot = int(np.prod(shp))
            per = tot // 128
            src = self.wf[k]
            dst = self.wb[k]
            X = shp[-1]
            s2 = src.rearrange("l j p x -> (l j p) x") if len(shp) == 4 else src.rearrange("l p x -> (l p) x")
            d2 = dst.rearrange("l j p x -> (l j p) x") if len(shp) == 4 else dst.rearrange("l p x -> (l p) x")
            rows = tot // X
            gmax = max(1, CHK // X)
            r = 0
            while r < rows:
                g = min(gmax, (rows - r) // 128)
                a32 = st32[i % 3]
                a16 = st16[i % 3]
                P.dma("sp", a32.t[:, 0:g * X].rearrange("p (g x) -> p g x", g=g),
                      s2[r:r + g * 128, :].rearrange("(g p) x -> p g x", p=128), writes=[a32])
                e = ("dve", "act", "pool")[i % 3]
                P.cp(e, a16.t[:, 0:g * X], a32.t[:, 0:g * X], [a32], [a16])
                P.dma("sp", d2[r:r + g * 128, :].rearrange("(g p) x -> p g x", p=128),
                      a16.t[:, 0:g * X].rearrange("p (g x) -> p g x", g=g), reads=[a16])
                r += g * 128
                i += 1
        S.close()

    def rmsnorm(self, x, sq, u, gcol, pss, rstd, out_f32=False):
        P = self.P
        NS = self.NS
        for s in range(NS):
            sl = slice(s * 512, (s + 1) * 512)
            for c in range(KC):
                P.tt("pool", sq.t[:, c, sl], x.t[:, c, sl], x.t[:, c, sl], ALU.mult, [x], [sq])
            ps = pss.next()
            for c in range(KC):
                P.mm(ps, ps.t[:, 0:512], self.onesm.t[:], sq.t[:, c, sl], [sq, self.onesm], start=(c == 0), stop=(c == KC - 1))
            P.act(rstd.t[:, sl], ps.t[:, 0:512], AF.Sqrt, [ps, self.epsr], [rstd], bias=self.epsr.t[:, 0:1])
            P.op("dve", lambda g, sl=sl: g.reciprocal(out=rstd.t[:, sl], in_=rstd.t[:, sl]), [rstd], [rstd])
            for c in range(KC):
                P.stt("dve", u.t[:, c, sl], x.t[:, c, sl], self.gains.t[:, gcol + c:gcol + c + 1], rstd.t[:, sl],
                      ALU.mult, ALU.mult, [x, rstd, self.gains], [u])

    def ffn(self, ws, x, u, h, pss, tmps):
        P = self.P
        NS = self.NS
        for jp in range(FC // 2):
            w = ws.get()
            for jj in range(2):
                j = jp * 2 + jj
                pg = [pss.next() for _ in range(NS)]
                pu = [pss.next() for _ in range(NS)]
                for gi, pp in ((0, pg), (1, pu)):
                    base = (jj * 2 + gi) * D
                    for c in range(KC):
                        for s in range(NS):
                            P.mm(pp[s], pp[s].t[:, 0:512], w.t[:, base + c * 128:base + (c + 1) * 128],
                                 u.t[:, c, s * 512:(s + 1) * 512], [w, u], start=(c == 0), stop=(c == KC - 1))
                for s in range(NS):
                    tm = tmps.next()
                    P.act(tm.t[:, 0:512], pg[s].t[:, 0:512], AF.Silu, [pg[s]], [tm])
                    P.tt("dve", h.t[:, j, s * 512:(s + 1) * 512], tm.t[:, 0:512], pu[s].t[:, 0:512], ALU.mult, [tm, pu[s]], [h])
        for o in range(KC):
            w = ws.get()
            po = [pss.next() for _ in range(NS)]
            for j in range(FC):
                for s in range(NS):
                    P.mm(po[s], po[s].t[:, 0:512], w.t[:, j * 128:(j + 1) * 128], h.t[:, j, s * 512:(s + 1) * 512],
                         [w, h], start=(j == 0), stop=(j == FC - 1))
            for s in range(NS):
                sl = slice(s * 512, (s + 1) * 512)
                P.stt("dve", x.t[:, o, sl], po[s].t[:, 0:512], 0.5, x.t[:, o, sl], ALU.mult, ALU.add, [po[s], x], [x])

    def ffn_plan(self, key_gu, key_d, l):
        plan = []
        for jp in range(FC // 2):
            plan.append((self.wb[key_gu][l, jp * 4:jp * 4 + 4], 4, D))
        for o in range(KC):
            plan.append((self.wb[key_d][l, o:o + 1], 1, DFF))
        return plan

    def phaseA(self, l):
        P = self.P
        nc = self.nc
        TT, NS, T = self.TT, self.NS, self.T
        S = Scope(nc)
        x = S.sb([128, KC, TT], F32, "x")
        u = S.sb([128, KC, TT], BF16, "u")
        h = S.sb([128, FC, TT], BF16, "h")
        rstd = S.sb([128, TT], F32, "rstd")
        slots = [S.sb([128, 4096], BF16, "wslot") for _ in range(4)]
        tmps = Ring([S.sb([128, 512], F32, "tmp") for _ in range(4)])
        stg = Ring([S.sb([128, 1024], F32, "stg") for _ in range(4)])
        pss = Ring([S.ps([128, 512], F32, "ps") for _ in range(8)])
        vst = Ring([S.sb([128, 8, 65], BF16, "vst") for _ in range(3)])
        for v_ in vst.tiles:
            P.memset("pool", v_, v_.t[:], 1.0)
        ntile = T // TT
        plan = []
        for it in range(ntile):
            plan += self.ffn_plan("wgu1", "wd1", l)
            for jp in range(NWIN // 4):
                plan.append((self.wb["win"][l, jp * 4:jp * 4 + 4], 4, D))
            plan.append((self.wb["wv"][l], 0, KC * 512))
        ws = WStream(self, slots, plan)
        for it in range(ntile):
            t0 = it * TT
            if l == 0:
                for b in range(TT // 128):
                    sg = stg.next()
                    P.dma("sp", sg.t[:, 0:D], self.xin[t0 + b * 128:t0 + (b + 1) * 128, :], writes=[sg])
                    for half in range(2):
                        ps = pss.next()
                        for cc in range(4):
                            c = half * 4 + cc
                            P.tr(ps, ps.t[:, cc * 128:(cc + 1) * 128], sg.t[:, c * 128:(c + 1) * 128], self.ident.t[:], [sg, self.ident])
                        e = "act" if half == 0 else "dve"
                        P.cp(e, x.t[:, half * 4:half * 4 + 4, b * 128:(b + 1) * 128],
                             ps.t[:, 0:512].rearrange("p (c t) -> p c t", c=4), [ps], [x])
            else:
                P.dma("sp", x.t[:], self.xT[:, :, t0:t0 + TT].rearrange("c p t -> p c t"), writes=[x])
            self.rmsnorm(x, h, u, (l * 3 + 0) * KC, pss, rstd)
            self.ffn(ws, x, u, h, pss, tmps)
            P.dma("pool", self.xT[:, :, t0:t0 + TT].rearrange("c p t -> p c t"), x.t[:], reads=[x])
            self.rmsnorm(x, h, u, (l * 3 + 1) * KC, pss, rstd)
            for jp in range(NWIN // 4):
                w = ws.get()
                for jj in range(4):
                    j = jp * 4 + jj
                    pp = [pss.next() for _ in range(NS)]
                    for c in range(KC):
                        for s in range(NS):
                            P.mm(pp[s], pp[s].t[:, 0:512], w.t[:, jj * D + c * 128:jj * D + (c + 1) * 128],
                                 u.t[:, c, s * 512:(s + 1) * 512], [w, u], start=(c == 0), stop=(c == KC - 1))
                    sg = stg.next()
                    if j < 14:
                        dst, view = self.zA[j, :, t0:t0 + TT], sg.t[:, 0:TT]
                        for s in range(NS):
                            P.cp("act" if s == 0 else "dve", view[:, s * 512:(s + 1) * 512], pp[s].t[:, 0:512], [pp[s]], [sg])
                    else:
                        view = sg.t[:].bitcast(BF16)[:, 0:TT]
                        if j < 16:
                            dst, sc, fn = self.zNq[j - 14, :, t0:t0 + TT], 0.125, AF.Copy
                        elif j < 18:
                            dst, sc, fn = self.zNk[j - 16, :, t0:t0 + TT], 1.0, AF.Copy
                        elif j < 26:
                            dst, sc, fn = self.zDq[j - 18, :, t0:t0 + TT], 32.0 ** -0.5, AF.Copy
                        elif j < 28:
                            dst, sc, fn = self.zDk[j - 26, :, t0:t0 + TT], 1.0, AF.Copy
                        else:
                            dst, sc, fn = self.zG[j - 28, :, t0:t0 + TT], 1.0, AF.Sigmoid
                        for s in range(NS):
                            if fn == AF.Sigmoid or s == 0:
                                P.act(view[:, s * 512:(s + 1) * 512], pp[s].t[:, 0:512], fn, [pp[s]], [sg], scale=sc)
                            else:
                                P.ts("dve", view[:, s * 512:(s + 1) * 512], pp[s].t[:, 0:512], sc, None, ALU.mult, None, [pp[s]], [sg])
                    P.dma("pool", dst, view, reads=[sg])
            w = ws.get()
            for b in range(TT // 128):
                ps = pss.next()
                for c in range(KC):
                    P.mm(ps, ps.t[:, 0:512], u.t[:, c, b * 128:(b + 1) * 128], w.t[:, c * 512:(c + 1) * 512], [w, u],
                         start=(c == 0), stop=(c == KC - 1))
                sg = vst.next()
                P.cp("act" if b % 2 == 0 else "dve", sg.t[:, :, 0:64], ps.t[:, 0:512].rearrange("p (h d) -> p h d", h=8), [ps], [sg])
                P.dma("pool", self.zV[t0 + b * 128:t0 + (b + 1) * 128, :], sg.t[:].rearrange("p h d -> p (h d)"), reads=[sg])
        S.close()

    def phaseC(self, l):
        P = self.P
        nc = self.nc
        TT, NS, T = self.TT, self.NS, self.T
        last = (l == DEPTH - 1)
        S = Scope(nc)
        x = S.sb([128, KC, TT], F32, "x")
        u = S.sb([128, KC, TT], BF16, "u")
        h = S.sb([128, FC, TT], BF16, "h")
        ym = S.sb([128, KC, TT], BF16, "ym")
        rstd = S.sb([128, TT], F32, "rstd")
        gts = Ring([S.sb([128, 3, TT], BF16, "gt") for _ in range(2)])
        slots = [S.sb([128, 4096], BF16, "wslot") for _ in range(4)]
        tmps = Ring([S.sb([128, 512], F32, "tmp") for _ in range(4)])
        stg = Ring([S.sb([128, 1024], F32, "stg") for _ in range(3)])
        pss = Ring([S.ps([128, 512], F32, "ps") for _ in range(8)])
        ntile = T // TT
        plan = []
        for it in range(ntile):
            plan += [(self.wb["wpabc"][l, 0:4], 4, D), (self.wb["wpabc"][l, 4:8], 4, D),
                     (self.wb["wout"][l, 0:4], 4, D), (self.wb["wout"][l, 4:8], 4, D)]
            plan += self.ffn_plan("wgu2", "wd2", l)
        ws = WStream(self, slots, plan)
        zGv = self.zG.rearrange("(g o) p t -> o p g t", g=3)
        for it in range(ntile):
            t0 = it * TT
            P.dma("sp", x.t[:], self.xT[:, :, t0:t0 + TT].rearrange("c p t -> p c t"), writes=[x])
            P.dma("sp", ym.t[:], self.yM[:, :, t0:t0 + TT].rearrange("c p t -> p c t"), writes=[ym])
            for op_ in range(2):
                w = ws.get()
                for oo in range(4):
                    o = op_ * 4 + oo
                    gt = gts.next()
                    P.dma("sp", gt.t[:], zGv[o, :, :, t0:t0 + TT], writes=[gt])
                    for s in range(NS):
                        sl = slice(s * 512, (s + 1) * 512)
                        pa, pb, pc = pss.next(), pss.next(), pss.next()
                        for (pp, c0, c1) in ((pa, 0, 4), (pb, 4, 6), (pc, 6, 8)):
                            for c in range(c0, c1):
                                P.mm(pp, pp.t[:, 0:512], w.t[:, oo * D + c * 128:oo * D + (c + 1) * 128], ym.t[:, c, sl],
                                     [w, ym], start=(c == c0), stop=(c == c1 - 1))
                        t1, t2, t3 = tmps.next(), tmps.next(), tmps.next()
                        P.tt("dve", t1.t[:, 0:512], pa.t[:, 0:512], gt.t[:, 0, sl], ALU.mult, [pa, gt], [t1])
                        P.tt("dve", t2.t[:, 0:512], pb.t[:, 0:512], gt.t[:, 1, sl], ALU.mult, [pb, gt], [t2])
                        P.tt("pool", t1.t[:, 0:512], t1.t[:, 0:512], t2.t[:, 0:512], ALU.add, [t1, t2], [t1])
                        P.tt("dve", t3.t[:, 0:512], pc.t[:, 0:512], gt.t[:, 2, sl], ALU.mult, [pc, gt], [t3])
                        P.tt("pool", u.t[:, o, sl], t1.t[:, 0:512], t3.t[:, 0:512], ALU.add, [t1, t3], [u])
            for op_ in range(2):
                w = ws.get()
                for oo in range(4):
                    o = op_ * 4 + oo
                    for s in range(NS):
                        sl = slice(s * 512, (s + 1) * 512)
                        pp = pss.next()
                        for c in range(KC):
                            P.mm(pp, pp.t[:, 0:512], w.t[:, oo * D + c * 128:oo * D + (c + 1) * 128], u.t[:, c, sl], [w, u],
                                 start=(c == 0), stop=(c == KC - 1))
                        P.tt("dve", x.t[:, o, sl], pp.t[:, 0:512], x.t[:, o, sl], ALU.add, [pp, x], [x])
            self.rmsnorm(x, h, u, (l * 3 + 2) * KC, pss, rstd)
            self.ffn(ws, x, u, h, pss, tmps)
            if not last:
                P.dma("pool", self.xT[:, :, t0:t0 + TT].rearrange("c p t -> p c t"), x.t[:], reads=[x])
            else:
                self.rmsnorm(x, h, x, 6 * KC, pss, rstd)
                for b in range(TT // 128):
                    sg = stg.next()
                    for half in range(2):
                        ps = pss.next()
                        for cc in range(4):
                            c = half * 4 + cc
                            P.tr(ps, ps.t[:, cc * 128:(cc + 1) * 128], x.t[:, c, b * 128:(b + 1) * 128], self.ident.t[:], [x, self.ident])
                        P.cp("act" if half == 0 else "dve", sg.t[:, half * 512:(half + 1) * 512], ps.t[:, 0:512], [ps], [sg])
                    P.dma("pool", self.yout[t0 + b * 128:t0 + (b + 1) * 128, :], sg.t[:, 0:D], reads=[sg])
        S.close()

    def mixers(self, l):
        P = self.P
        en = self.debug_en if hasattr(self, "debug_en") else ("rwkv", "na", "da")
        S = Scope(self.nc)
        z = S.sb([128, 2048], BF16, "zero")
        P.memset("dve", z, z.t[:], 0.0)
        for (name, c0, c1) in (("rwkv", 0, 4), ("na", 4, 6), ("da", 6, 8)):
            if name in en:
                continue
            for c in range(c0, c1):
                for t in range(0, self.T, 2048):
                    n = min(2048, self.T - t)
                    P.dma("sp", self.yM[c, :, t:t + n], z.t[:, 0:n], reads=[z])
        S.close()
        P.barrier()
        if "na" in en:
            self.mix_na(l)
            P.barrier()
        if "da" in en:
            self.mix_da(l)
            P.barrier()
        if "rwkv" in en:
            self.mix_rwkv(l)
            P.barrier()


NA_TYPES = [(0, 0), (-2, -2), (-4, -3), (-4, -4), (-6, -6)]
SLOPES = [2.0 ** (-8.0 * (h + 1) / 4) for h in range(4)]


def na_rs(r, rows):
    return min(max(r - 4, 0), rows - 8)


def na_tile_info(r, rows):
    a, b = na_rs(r, rows) - r, na_rs(r + 1, rows) - r
    ty = NA_TYPES.index((a, b))
    kr0 = r + a
    nk = (b + 8 - a + 1) // 2
    return ty, kr0, nk


def na_tables(rpb):
    tab = np.full((128, 5, 4, 5, 128), -30000.0, np.float32)
    pk = np.arange(128)
    pq = np.arange(128)
    for ti, (a, b) in enumerate(NA_TYPES):
        nk = (b + 8 - a + 1) // 2
        for j in range(nk):
            krow = a + 2 * j + pk // 64
            kcol = pk % 64
            qrow = pq // 64
            qcol = pq % 64
            rs_rel = np.where(qrow == 0, a, b)
            cs = np.clip(qcol - 8, 0, 64 - 16)
            okr = (krow[:, None] >= rs_rel[None, :]) & (krow[:, None] < rs_rel[None, :] + 8)
            okc = (kcol[:, None] >= cs[None, :]) & (kcol[:, None] < cs[None, :] + 16)
            dr = np.clip(krow[:, None] - qrow[None, :] + 7, 0, 14)
            dc = np.clip(kcol[:, None] - qcol[None, :] + 15, 0, 30)
            ok = okr & okc
            for h in range(4):
                g = rpb[h][dr, dc]
                t = tab[:, ti, h, j, :]
                t[ok] = g[ok]
    return tab


def da_consts():
    p = np.arange(128, dtype=np.float64)
    colL = np.zeros((128, 4, 32), np.float32)
    colR = np.zeros((128, 4, 32), np.float32)
    fLR = np.zeros((128, 4, 8), np.float32)
    biasD = np.zeros((128, 4, 4, 512), np.float32)
    q = np.arange(512, dtype=np.float64)
    for s_, sl in enumerate(SLOPES):
        for m in range(32):
            colL[:, s_, m] = -sl * (128 * m - p)
            colR[:, s_, m] = -sl * (128 * m + p - 511)
        for sub in range(4):
            fLR[:, s_, sub] = -sl * (128 * sub + p)
            fLR[:, s_, 4 + sub] = -sl * (511 - 128 * sub - p)
        for j in range(4):
            biasD[:, s_, j, :] = -sl * np.abs(q[None, :] - (128 * j + p[:, None]))
    return {"da_colL": colL, "da_colR": colR, "da_fLR": fLR, "da_biasD": biasD}


SMALL_SHAPES = {"na_tab": [DEPTH, 128, 5 * 4 * 5 * 128], "da_colL": [128, 4, 32], "da_colR": [128, 4, 32],
                "da_fLR": [128, 4, 8], "da_biasD": [128, 4, 4, 512], "da_lam": [DEPTH, 128, 128],
                "da_g": [DEPTH, 128, 256]}


def host_small(inp):
    f = lambda a: np.asarray(a, np.float32)
    out = {}
    out["na_tab"] = np.stack([na_tables(f(inp["na_rpb"][l])).reshape(128, -1) for l in range(DEPTH)])
    out.update(da_consts())
    out["da_lam"] = np.stack([np.broadcast_to(f(inp["diff_lam"][l]).reshape(1, 128), (128, 128)) for l in range(DEPTH)]).copy()
    out["da_g"] = np.stack([np.broadcast_to(np.tile(f(inp["diff_subln_g"][l]), 4).reshape(1, 256), (128, 256)) for l in range(DEPTH)]).copy()
    return out


def make_inputs(inp, seq_groups, ncores):
    shared = {}
    shared.update(host_weights(inp))
    shared.update({"c_" + k: v for k, v in host_consts().items()})
    shared.update(host_small(inp))
    in_maps = []
    for c in range(ncores):
        parts = []
        for (arr, n) in seq_groups:
            for b in range(n):
                parts.append(np.asarray(arr[c * n + b], np.float32))
        m = dict(shared)
        m["xin"] = np.ascontiguousarray(np.concatenate(parts, axis=0))
        in_maps.append(m)
    return in_maps


def run(inp, seq_groups, ncores, debug=None, en=None):
    seqs = []
    for (arr, n) in seq_groups:
        seqs += [arr.shape[1]] * n
    B = Builder(seqs, debug=debug)
    if en is not None:
        B.debug_en = en
    nc = B.build()
    in_maps = make_inputs(inp, seq_groups, ncores)
    res = run_bass_kernel_spmd(nc, in_maps, core_ids=list(range(ncores)))
    outs = []
    for (arr, n) in seq_groups:
        outs.append(np.zeros(arr.shape, np.float32))
    for c in range(ncores):
        y = res.results[c]["yout"]
        t = 0
        for gi, (arr, n) in enumerate(seq_groups):
            L = arr.shape[1]
            for b in range(n):
                outs[gi][c * n + b] = y[t:t + L]
                t += L
    return outs, res, B


def kernel(**inputs):
    xp = np.asarray(inputs["x_prompt"], np.float32)
    xs = np.asarray(inputs["x_sample"], np.float32)
    outs, _, _ = run(inputs, [(xp, xp.shape[0] // NCORES), (xs, xs.shape[0] // NCORES)], NCORES)
    return (outs[0], outs[1])


def mix_na(self, l):
    P = self.P
    S = Scope(self.nc)
    Lmax = max(self.seqs)
    qT = S.sb([128, 2, Lmax], BF16, "naq")
    kT = S.sb([128, 2, Lmax], BF16, "nak")
    V1 = S.sb([128, Lmax // 128, 260], BF16, "nav")
    yst = S.sb([128, 2, Lmax], BF16, "nay")
    tab = S.sb([128, 5, 4, 5 * 128], F32, "natab")
    sbr = Ring([S.sb([128, 640], F32, "nasb") for _ in range(2)])
    ptr = Ring([S.sb([128, 640], BF16, "napt") for _ in range(2)])
    yr = Ring([S.sb([128, 256], F32, "nayt") for _ in range(2)])
    rcr = Ring([S.sb([128, 4], F32, "narc") for _ in range(2)])
    pss = Ring([S.ps([128, 1024], F32, "naps") for _ in range(2)])
    pso = Ring([S.ps([128, 512], F32, "napo") for _ in range(2)])
    pst = Ring([S.ps([128, 512], F32, "napt") for _ in range(2)])
    P.dma("sp", tab.t[:].rearrange("p a b c -> p (a b c)"), self.small["na_tab"][l], writes=[tab])
    for si, L in enumerate(self.seqs):
        t0 = self.starts[si]
        rows = L // 64
        P.dma("sp", qT.t[:, :, 0:L], self.zNq[:, :, t0:t0 + L].rearrange("c p t -> p c t"), writes=[qT])
        P.dma("sp", kT.t[:, :, 0:L], self.zNk[:, :, t0:t0 + L].rearrange("c p t -> p c t"), writes=[kT])
        P.dma("sp", V1.t[:, 0:L // 128, :], self.zV[t0:t0 + L, 0:260].rearrange("(n p) x -> p n x", p=128), writes=[V1])
        for qi in range(L // 128):
            r = 2 * qi
            ty, kr0, nk = na_tile_info(r, rows)
            po = pso.next()
            for hd in range(4):
                cc, base = hd // 2, (hd % 2) * 64
                ps = pss.next()
                for j in range(nk):
                    kn = kr0 // 2 + j
                    P.mm(ps, ps.t[:, j * 128:(j + 1) * 128], kT.t[base:base + 64, cc, kn * 128:(kn + 1) * 128],
                         qT.t[base:base + 64, cc, qi * 128:(qi + 1) * 128], [kT, qT])
                sb = sbr.next()
                P.tt("dve", sb.t[:, 0:nk * 128], ps.t[:, 0:nk * 128], tab.t[:, ty, hd, 0:nk * 128], ALU.add, [ps, tab], [sb])
                pt = ptr.next()
                P.act(pt.t[:, 0:nk * 128], sb.t[:, 0:nk * 128], AF.Exp, [sb], [pt])
                for j in range(nk):
                    kn = kr0 // 2 + j
                    P.mm(po, po.t[:, hd * 65:(hd + 1) * 65], pt.t[:, j * 128:(j + 1) * 128], V1.t[:, kn, hd * 65:(hd + 1) * 65],
                         [pt, V1], start=(j == 0), stop=(j == nk - 1))
            rc = rcr.next()
            pov = po.t[:, 0:260].rearrange("p (h d) -> p h d", h=4)
            P.op("dve", lambda g, rc=rc, pov=pov: g.reciprocal(out=rc.t[:, :], in_=pov[:, :, 64]), [po], [rc])
            y = yr.next()
            for hd in range(4):
                P.ts("dve", y.t[:, hd * 64:(hd + 1) * 64], po.t[:, hd * 65:hd * 65 + 64], rc.t[:, hd:hd + 1], None, ALU.mult, None,
                     [po, rc], [y])
            pt_ = pst.next()
            for cc in range(2):
                P.tr(pt_, pt_.t[:, cc * 128:(cc + 1) * 128], y.t[:, cc * 128:(cc + 1) * 128], self.ident.t[:], [y, self.ident])
            P.cp("act", yst.t[:, :, qi * 128:(qi + 1) * 128], pt_.t[:, 0:256].rearrange("p (c t) -> p c t", c=2), [pt_], [yst])
        P.dma("pool", self.yM[4:6, :, t0:t0 + L].rearrange("c p t -> p c t"), yst.t[:, :, 0:L], reads=[yst])
    S.close()


def mix_da(self, l):
    P = self.P
    S = Scope(self.nc)
    Lmax = max(self.seqs)
    lam_init = 0.8 - 0.6 * math.exp(-0.3 * l)
    kT = S.sb([128, 2, Lmax], BF16, "dak")
    V1 = S.sb([128, Lmax // 128, 260], BF16, "dav")
    yst = S.sb([128, 2, Lmax], BF16, "day")
    qmr = Ring([S.sb([128, 8, 512], BF16, "daq") for _ in range(2)])
    colL = S.sb([128, 4, 32], F32, "colL")
    colR = S.sb([128, 4, 32], F32, "colR")
    fLR = S.sb([128, 4, 8], F32, "fLR")
    bD32 = S.sb([128, 4 * 4 * 512], F32, "bD32")
    bD = S.sb([128, 4, 4, 512], BF16, "bD")
    lamt = S.sb([128, 4, 32], F32, "lamt")
    lamw = S.sb([128, 2, 32], F32, "lamw")
    lams = S.sb([128, 4], F32, "lams")
    gt = S.sb([128, 4, 64], F32, "dag")
    att = S.sb([128, 4, 8, 64], F32, "att")
    ptr = Ring([S.sb([128, 512], BF16, "dapt") for _ in range(3)])
    totr = Ring([S.sb([128, 65], F32, "datot") for _ in range(3)])
    rcr = Ring([S.sb([128, 4], F32, "darc") for _ in range(3)])
    ar = Ring([S.sb([128, 4, 64], F32, "daa") for _ in range(2)])
    sqr = Ring([S.sb([128, 4, 64], F32, "dasq") for _ in range(2)])
    yr = Ring([S.sb([128, 256], F32, "dayt") for _ in range(2)])
    pss = Ring([S.ps([128, 512], F32, "daps") for _ in range(3)])
    pso = Ring([S.ps([128, 512], F32, "dapo") for _ in range(4)])
    pst = Ring([S.ps([128, 512], F32, "dapt") for _ in range(1)])
    sm = self.small
    P.dma("sp", colL.t[:], sm["da_colL"], writes=[colL])
    P.dma("sp", colR.t[:], sm["da_colR"], writes=[colR])
    P.dma("sp", fLR.t[:], sm["da_fLR"], writes=[fLR])
    P.dma("sp", bD32.t[:], sm["da_biasD"].rearrange("p a b c -> p (a b c)"), writes=[bD32])
    P.dma("sp", lamt.t[:].rearrange("p a b -> p (a b)"), sm["da_lam"][l], writes=[lamt])
    P.dma("sp", gt.t[:].rearrange("p a b -> p (a b)"), sm["da_g"][l], writes=[gt])
    P.cp("dve", bD.t[:].rearrange("p a b c -> p (a b c)"), bD32.t[:], [bD32], [bD])
    P.act(fLR.t[:], fLR.t[:], AF.Exp, [fLR], [fLR])
    P.tt("dve", lamw.t[:, 0, :], lamt.t[:, 0, :], lamt.t[:, 1, :], ALU.mult, [lamt], [lamw])
    P.tt("dve", lamw.t[:, 1, :], lamt.t[:, 2, :], lamt.t[:, 3, :], ALU.mult, [lamt], [lamw])
    P.op("dve", lambda g: g.tensor_reduce(out=lams.t[:, 0:2], in_=lamw.t[:], axis=AX.X, op=ALU.add), [lamw], [lams])
    P.act(lams.t[:, 0:2], lams.t[:, 0:2], AF.Exp, [lams], [lams])
    P.tt("dve", lams.t[:, 2:3], lams.t[:, 1:2], lams.t[:, 0:1], ALU.subtract, [lams], [lams])
    P.ts("dve", lams.t[:, 3:4], lams.t[:, 2:3], -lam_init, None, ALU.add, None, [lams], [lams])
    zero_col = self.epsr.t[:, 3:4]
    for si, L in enumerate(self.seqs):
        t0 = self.starts[si]
        NK = L // 128
        P.dma("sp", kT.t[:, :, 0:L], self.zDk[:, :, t0:t0 + L].rearrange("c p t -> p c t"), writes=[kT])
        P.dma("sp", V1.t[:, 0:NK, :], self.zV[t0:t0 + L, 260:520].rearrange("(n p) x -> p n x", p=128), writes=[V1])
        for qb in range(L // 512):
            q0 = qb * 512
            qm = qmr.next()
            P.dma("sp", qm.t[:], self.zDq[:, :, t0 + q0:t0 + q0 + 512].rearrange("g p t -> p g t"), writes=[qm])
            for g_ in range(8):
                s_ = g_ // 2
                hd = g_ // 2
                poA, poB = pso.next(), pso.next()
                cls_of = []
                for kt in range(NK):
                    k0 = kt * 128
                    cls_of.append(0 if k0 + 128 <= q0 else (2 if k0 >= q0 + 512 else 1))
                first = {c: cls_of.index(c) for c in set(cls_of)}
                lastk = {c: NK - 1 - cls_of[::-1].index(c) for c in set(cls_of)}
                for kt in range(NK):
                    k0 = kt * 128
                    cls = cls_of[kt]
                    ps = pss.next()
                    P.mm(ps, ps.t[:, 0:512], kT.t[:, g_ // 4, k0:k0 + 128], qm.t[:, g_, :], [kT, qm], start=True, stop=(cls != 1))
                    if cls == 1:
                        P.mm(ps, ps.t[:, 0:512], self.identb.t[:], bD.t[:, s_, (k0 - q0) // 128, :], [self.identb, bD], start=False, stop=True)
                        bias = zero_col
                        rd = [ps, self.epsr]
                    elif cls == 0:
                        bias = colL.t[:, s_, (q0 - k0) // 128:(q0 - k0) // 128 + 1]
                        rd = [ps, colL]
                    else:
                        bias = colR.t[:, s_, (k0 - q0) // 128:(k0 - q0) // 128 + 1]
                        rd = [ps, colR]
                    pt = ptr.next()
                    P.act(pt.t[:], ps.t[:, 0:512], AF.Exp, rd, [pt], bias=bias)
                    for sub in range(4):
                        po = poA if sub < 2 else poB
                        off = ((sub % 2) * 3 + cls) * 65
                        P.mm(po, po.t[:, off:off + 65], pt.t[:, sub * 128:(sub + 1) * 128], V1.t[:, kt, hd * 65:(hd + 1) * 65],
                             [pt, V1], start=(kt == 0 and sub % 2 == 0), stop=(kt == lastk[cls]), sgc=True)
                for sub in range(4):
                    po = poA if sub < 2 else poB
                    o0 = (sub % 2) * 3 * 65
                    tot = totr.next()
                    P.cp("act", tot.t[:], po.t[:, o0 + 65:o0 + 130], [po], [tot])
                    if 0 in first:
                        P.stt("dve", tot.t[:], po.t[:, o0:o0 + 65], fLR.t[:, s_, sub:sub + 1], tot.t[:], ALU.mult, ALU.add, [po, fLR, tot], [tot])
                    if 2 in first:
                        P.stt("dve", tot.t[:], po.t[:, o0 + 130:o0 + 195], fLR.t[:, s_, 4 + sub:5 + sub], tot.t[:], ALU.mult, ALU.add,
                              [po, fLR, tot], [tot])
                    rc = rcr.next()
                    P.op("dve", lambda g, rc=rc, tot=tot: g.reciprocal(out=rc.t[:, 0:1], in_=tot.t[:, 64:65]), [tot], [rc])
                    P.ts("dve", att.t[:, sub, g_, :], tot.t[:, 0:64], rc.t[:, 0:1], None, ALU.mult, None, [tot, rc], [att])
            if "att" in self.dbg and l == 0 and si == len(self.seqs) - 1 and qb == 0:
                P.dma("sp", self.dbg["att"], att.t[:].rearrange("p a b c -> p (a b c)"), reads=[att])
                P.dma("sp", self.dbg["lams"], lams.t[:], reads=[lams])
            for sub in range(4):
                a = ar.next()
                av = att.t[:, sub, :, :].rearrange("p (h two) d -> p h two d", two=2)
                P.stt("dve", a.t[:], av[:, :, 1, :], lams.t[:, 3:4], av[:, :, 0, :], ALU.mult, ALU.add, [att, lams], [a])
                sq = sqr.next()
                P.tt("pool", sq.t[:], a.t[:], a.t[:], ALU.mult, [a], [sq])
                rc = rcr.next()
                P.op("dve", lambda g, rc=rc, sq=sq: g.tensor_reduce(out=rc.t[:, 0:4], in_=sq.t[:], axis=AX.X, op=ALU.add), [sq], [rc])
                P.act(rc.t[:, 0:4], rc.t[:, 0:4], AF.Sqrt, [rc, self.epsr], [rc], bias=self.epsr.t[:, 1:2], scale=1.0 / 64)
                P.op("dve", lambda g, rc=rc: g.reciprocal(out=rc.t[:, 0:4], in_=rc.t[:, 0:4]), [rc], [rc])
                y = yr.next()
                yv = y.t[:].rearrange("p (h d) -> p h d", h=4)
                for hd in range(4):
                    P.ts("dve", yv[:, hd, :], a.t[:, hd, :], rc.t[:, hd:hd + 1], 1.0 - lam_init, ALU.mult, ALU.mult, [a, rc], [y])
                P.tt("pool", yv, yv, gt.t[:], ALU.mult, [y, gt], [y])
                pt_ = pst.next()
                for cc in range(2):
                    P.tr(pt_, pt_.t[:, cc * 128:(cc + 1) * 128], y.t[:, cc * 128:(cc + 1) * 128], self.ident.t[:], [y, self.ident])
                tq = q0 + sub * 128
                P.cp("act", yst.t[:, :, tq:tq + 128], pt_.t[:, 0:256].rearrange("p (c t) -> p c t", c=2), [pt_], [yst])
        P.dma("pool", self.yM[6:8, :, t0:t0 + L].rearrange("c p t -> p c t"), yst.t[:, :, 0:L], reads=[yst])
    S.close()


Builder.mix_na = mix_na
Builder.mix_da = mix_da


CDEC = math.exp(-0.5)


def rw_host(inp):
    f = lambda a: np.asarray(a, np.float32)
    out = {}
    mu_l, mulw_l, mulg_l, hp_l = [], [], [], []
    for l in range(DEPTH):
        mu = f(inp["rwkv_mu"][l])
        a = np.zeros((64, 8, 3, 2), np.float32)
        for h in range(8):
            for q in range(3):
                for m in range(2):
                    a[:, h, q, m] = mu[m, q * 512 + h * 64:q * 512 + h * 64 + 64]
        mu_l.append(a.reshape(64, 48))
        mulw_l.append(np.stack([mu[0, 1536:1600], mu[1, 1536:1600], mu[0, 1600:1664], mu[1, 1600:1664]], axis=1))
        mulg_l.append(np.stack([mu[0, 1664:1792], mu[1, 1664:1792]], axis=1))
        hp = np.zeros((64, 8, 11), np.float32)
        for h in range(8):
            sl = slice(h * 64, h * 64 + 64)
            hp[:, h, 0] = f(inp["rwkv_k_k"][l])[sl]
            hp[:, h, 1] = f(inp["rwkv_lnx_g"][l])[sl]
            hp[:, h, 2] = f(inp["rwkv_lnx_b"][l])[sl]
            for d in range(2):
                hp[:, h, 3 + 4 * d + 0] = f(inp["rwkv_w0"][l, d])[sl]
                hp[:, h, 3 + 4 * d + 1] = f(inp["rwkv_a0"][l, d])[sl]
                hp[:, h, 3 + 4 * d + 2] = f(inp["rwkv_k_a"][l, d])[sl]
                hp[:, h, 3 + 4 * d + 3] = f(inp["rwkv_r_k"][l, d]).reshape(-1)[sl]
        hp_l.append(hp.reshape(64, 88))
    out["rw_mu"] = np.stack(mu_l); out["rw_mulw"] = np.stack(mulw_l); out["rw_mulg"] = np.stack(mulg_l)
    out["rw_hp"] = np.stack(hp_l)
    out["rw_w2"] = np.ascontiguousarray(f(inp["rwkv_w2"])); out["rw_a2"] = np.ascontiguousarray(f(inp["rwkv_a2"]))
    out["rw_g2"] = np.ascontiguousarray(f(inp["rwkv_g2"]))
    i = np.arange(64)
    row, col = i[:, None], i[None, :]
    mk = np.zeros((2, 3, 64, 512), np.float32)
    for d in range(2):
        st = (row < col) if d == 0 else (row > col)
        inc = (row <= col) if d == 0 else (row >= col)
        stT = (col < row) if d == 0 else (col > row)
        for mi, m_ in enumerate((st, inc, stT)):
            mk[d, mi] = np.tile(m_.astype(np.float32), (1, 8))
    out["rw_mask"] = mk
    out["rw_irep"] = np.tile(np.eye(64, dtype=np.float32), (1, 8))
    rs = np.ones((64, 1024), np.float32)
    rs[:, ::64] = 0.0
    out["rw_reset"] = rs
    return out


SMALL_SHAPES.update({"rw_mu": [DEPTH, 64, 48], "rw_mulw": [DEPTH, 64, 4], "rw_mulg": [DEPTH, 128, 2], "rw_hp": [DEPTH, 64, 88],
                     "rw_w2": [DEPTH, 2, 64, 512], "rw_a2": [DEPTH, 2, 64, 512], "rw_g2": [DEPTH, 128, 512],
                     "rw_mask": [2, 3, 64, 512], "rw_irep": [64, 512], "rw_reset": [64, 1024]})
_host_small0 = host_small


def host_small(inp):
    o = _host_small0(inp)
    o.update(rw_host(inp))
    return o


def mix_rwkv(self, l):
    P = self.P
    nc = self.nc
    S = Scope(nc)
    sm = self.small
    T = self.T
    if not hasattr(self, "yF"):
        self.yF = nc.dram_tensor("yF", [2, 8, 64, T], F32).ap()
    yfb = Buf()
    SEG = min(512, min(self.seqs))
    W = SEG
    sb = lambda shape, name: S.sb(shape, F32, name)
    mu = sb([64, 48], "mu"); c0 = sb([64, 24], "c0"); mulw = sb([64, 4], "mulw"); c0w = sb([64, 2], "c0w")
    mulg = sb([128, 2], "mulg"); c0g = sb([128, 1], "c0g"); hp = sb([64, 88], "hp"); omk = sb([64, 16], "omk")
    w2 = sb([64, 2, 512], "w2"); a2 = sb([64, 2, 512], "a2"); g2 = sb([128, 512], "g2")
    mk = sb([64, 2, 3, 512], "mk"); irep = sb([64, 512], "irep"); rst = sb([64, 1024], "rst"); ones = sb([64, 64], "ones"); onesm = sb([64, 64], "onesm")
    P.dma("sp", mu.t[:], sm["rw_mu"][l], writes=[mu]); P.dma("sp", mulw.t[:], sm["rw_mulw"][l], writes=[mulw])
    P.dma("sp", mulg.t[:], sm["rw_mulg"][l], writes=[mulg]); P.dma("sp", hp.t[:], sm["rw_hp"][l], writes=[hp])
    P.dma("sp", w2.t[:], sm["rw_w2"][l].rearrange("d k n -> k d n"), writes=[w2])
    P.dma("sp", a2.t[:], sm["rw_a2"][l].rearrange("d k n -> k d n"), writes=[a2])
    P.dma("sp", g2.t[:], sm["rw_g2"][l], writes=[g2])
    P.dma("sp", mk.t[:], sm["rw_mask"].rearrange("d m p n -> p d m n"), writes=[mk])
    P.dma("sp", irep.t[:], sm["rw_irep"], writes=[irep])
    P.dma("sp", rst.t[:], sm["rw_reset"], writes=[rst])
    P.memset("dve", ones, ones.t[:], 1.0); P.memset("dve", onesm, onesm.t[:], 1.0 / 64)
    muv = mu.t[:].rearrange("p (a m) -> p a m", m=2)
    P.tt("dve", c0.t[:], muv[:, :, 0], muv[:, :, 1], ALU.add, [mu], [c0])
    P.ts("dve", c0.t[:], c0.t[:], -1.0, 1.0, ALU.mult, ALU.add, [c0], [c0])
    mwv = mulw.t[:].rearrange("p (a m) -> p a m", m=2)
    P.tt("dve", c0w.t[:], mwv[:, :, 0], mwv[:, :, 1], ALU.add, [mulw], [c0w])
    P.ts("dve", c0w.t[:], c0w.t[:], -1.0, 1.0, ALU.mult, ALU.add, [c0w], [c0w])
    P.tt("dve", c0g.t[:], mulg.t[:, 0:1], mulg.t[:, 1:2], ALU.add, [mulg], [c0g])
    P.ts("dve", c0g.t[:], c0g.t[:], -1.0, 1.0, ALU.mult, ALU.add, [c0g], [c0g])
    hpv = hp.t[:].rearrange("p (h c) -> p h c", c=11)
    for h in range(8):
        for d in range(2):
            P.ts("dve", omk.t[:, h * 2 + d:h * 2 + d + 1], hpv[:, h, 5 + 4 * d:6 + 4 * d], -1.0, 1.0, ALU.mult, ALU.add, [hp], [omk])
    zcol = self.epsr.t[0:64, 3:4]
    names = ["zr", "zk", "zv", "zw", "za"]
    Z = {n: sb([64, W + 2], n) for n in names}
    zg = sb([128, W + 2], "zg"); gl = sb([128, W], "gl")
    Tl = {n: sb([64, W], n) for n in ["r", "k", "v", "wl", "al", "kk", "sg", "a", "kd", "b", "t1", "bon", "Pf", "E", "Sf", "X",
                                      "eI", "eX", "eN", "eT", "at", "bt", "kt", "rt", "bh", "kh", "Y", "yf", "bf"]}
    TM = [sb([64, W], "TM%d" % i) for i in range(4)]
    GM = [sb([64, W], "GM%d" % i) for i in range(5)]
    TT_ = [sb([64, W], "TT%d" % i) for i in range(2)]
    PP = [(sb([64, W], "PPa%d" % i), sb([64, W], "PPb%d" % i)) for i in range(2)]
    X1 = sb([64, W], "X1"); AHT = sb([64, W], "AHT"); AHN = sb([64, W], "AHN"); U0 = sb([64, W], "U0")
    MT = sb([64, W], "MT"); CC = sb([64, W], "CC")
    tmr = Ring([sb([64, 256], "tm") for _ in range(2)])
    gmr = Ring([sb([64, 320], "gm") for _ in range(2)])
    p2r = Ring([sb([64, 128], "p2") for _ in range(3)])
    ttr = Ring([sb([64, 64], "tt") for _ in range(3)])
    x1r = Ring([sb([64, 64], "x1") for _ in range(2)])
    u0r = Ring([sb([64, 64], "u0") for _ in range(2)])
    ahr = Ring([sb([64, 64], "ah") for _ in range(2)])
    dgr = Ring([sb([64, 64], "dg") for _ in range(2)])
    ur = Ring([sb([64, 64], "u") for _ in range(2)])
    str_ = Ring([sb([64, 64], "st") for _ in range(3)])
    yo = S.sb([64, W], BF16, "yo")
    pss = Ring([S.ps([128, 512], F32, "rps") for _ in range(6)])
    psU_ = S.ps([128, 512], F32, "rpsU")
    psY_ = S.ps([128, 512], F32, "rpsY")
    idn = self.ident.t[0:64, 0:64]

    def shift(dst, src, c0c, m0c, m1c, np_=64):
        P.ts("dve", dst.t[:, :], src.t[:, 1:W + 1], c0c, None, ALU.mult, None, [src], [dst])
        P.stt("dve", dst.t[:, :], src.t[:, 0:W], m0c, dst.t[:, :], ALU.mult, ALU.add, [src, dst], [dst])
        P.stt("dve", dst.t[:, :], src.t[:, 2:W + 2], m1c, dst.t[:, :], ALU.mult, ALU.add, [src, dst], [dst])

    for si, L in enumerate(self.seqs):
        t0 = self.starts[si]
        nseg = L // SEG
        for h in range(8):
            cc, pb = h // 2, (h % 2) * 64
            for d in range(2):
                st = str_.next()
                P.memset("dve", st, st.t[:], 0.0)
                for sgi in (range(nseg) if d == 0 else range(nseg - 1, -1, -1)):
                    s0 = t0 + sgi * SEG
                    lo = 0 if sgi > 0 else 1
                    hi = W + 2 if sgi < nseg - 1 else W + 1
                    srcs = {"zr": (cc, pb), "zk": (4 + cc, pb), "zv": (8 + cc, pb), "zw": (12, 0), "za": (12, 64)}
                    for n in names:
                        if lo == 1 or hi == W + 1:
                            P.memset("dve", Z[n], Z[n].t[:], 0.0)
                        c_, p_ = srcs[n]
                        P.dma("sp", Z[n].t[:, lo:hi], self.zA[c_, p_:p_ + 64, s0 - 1 + lo:s0 - 1 + hi], writes=[Z[n]])
                    if lo == 1 or hi == W + 1:
                        P.memset("dve", zg, zg.t[:], 0.0)
                    P.dma("sp", zg.t[:, lo:hi], self.zA[13, :, s0 - 1 + lo:s0 - 1 + hi], writes=[zg])
                    for qi, (dn, sn) in enumerate((("r", "zr"), ("k", "zk"), ("v", "zv"))):
                        ix = h * 3 + qi
                        shift(Tl[dn], Z[sn], c0.t[:, ix:ix + 1], mu.t[:, 2 * ix:2 * ix + 1], mu.t[:, 2 * ix + 1:2 * ix + 2])
                    shift(Tl["wl"], Z["zw"], c0w.t[:, 0:1], mulw.t[:, 0:1], mulw.t[:, 1:2])
                    shift(Tl["al"], Z["za"], c0w.t[:, 1:2], mulw.t[:, 2:3], mulw.t[:, 3:4])
                    P.ts("dve", gl.t[:, :], zg.t[:, 1:W + 1], c0g.t[:, 0:1], None, ALU.mult, None, [zg], [gl])
                    P.stt("dve", gl.t[:, :], zg.t[:, 0:W], mulg.t[:, 0:1], gl.t[:, :], ALU.mult, ALU.add, [zg, gl], [gl])
                    P.stt("dve", gl.t[:, :], zg.t[:, 2:W + 2], mulg.t[:, 1:2], gl.t[:, :], ALU.mult, ALU.add, [zg, gl], [gl])
                    r, k, v, kk, sg, a, kd, b, t1 = (Tl[n] for n in ("r", "k", "v", "kk", "sg", "a", "kd", "b", "t1"))
                    P.ts("dve", kk.t[:], k.t[:], hpv[:, h, 0:1], None, ALU.mult, None, [k, hp], [kk])
                    P.tt("pool", t1.t[:], kk.t[:], kk.t[:], ALU.mult, [kk], [t1])
                    for blk in range(W // 512):
                        bs = slice(blk * 512, blk * 512 + 512)
                        ps = pss.next()
                        P.mm(ps, ps.t[0:64, 0:512], ones.t[:], t1.t[:, bs], [ones, t1])
                        P.act(Tl["X"].t[:, bs], ps.t[0:64, 0:512], AF.Sqrt, [ps, self.epsr], [Tl["X"]], bias=zcol)
                    P.ts("dve", Tl["X"].t[:], Tl["X"].t[:], 1e-12, None, ALU.max, None, [Tl["X"]], [Tl["X"]])
                    P.op("dve", lambda g: g.reciprocal(out=Tl["X"].t[:], in_=Tl["X"].t[:]), [Tl["X"]], [Tl["X"]])
                    P.tt("dve", kk.t[:], kk.t[:], Tl["X"].t[:], ALU.mult, [kk, Tl["X"]], [kk])
                    P.act(Tl["wl"].t[:], Tl["wl"].t[:], AF.Tanh, [Tl["wl"]], [Tl["wl"]])
                    for blk in range(W // 512):
                        bs = slice(blk * 512, blk * 512 + 512)
                        ps = pss.next()
                        P.mm(ps, ps.t[0:64, 0:512], w2.t[:, d, h * 64:h * 64 + 64], Tl["wl"].t[:, bs], [w2, Tl["wl"]])
                        P.act(sg.t[:, bs], ps.t[0:64, 0:512], AF.Sigmoid, [ps, hp], [sg], bias=hpv[:, h, 3 + 4 * d:4 + 4 * d])
                        ps = pss.next()
                        P.mm(ps, ps.t[0:64, 0:512], a2.t[:, d, h * 64:h * 64 + 64], Tl["al"].t[:, bs], [a2, Tl["al"]])
                        P.act(a.t[:, bs], ps.t[0:64, 0:512], AF.Sigmoid, [ps, hp], [a], bias=hpv[:, h, 4 + 4 * d:5 + 4 * d])
                    P.ts("dve", kd.t[:], a.t[:], hpv[:, h, 5 + 4 * d:6 + 4 * d], omk.t[:, h * 2 + d:h * 2 + d + 1], ALU.mult, ALU.add, [a, hp, omk], [kd])
                    P.tt("dve", kd.t[:], kd.t[:], k.t[:], ALU.mult, [kd, k], [kd])
                    P.tt("pool", b.t[:], kk.t[:], a.t[:], ALU.mult, [kk, a], [b])
                    P.stt("dve", t1.t[:], r.t[:], hpv[:, h, 6 + 4 * d:7 + 4 * d], kd.t[:], ALU.mult, ALU.mult, [r, hp, kd], [t1])
                    bon = Tl["bon"]
                    for blk in range(W // 512):
                        bs = slice(blk * 512, blk * 512 + 512)
                        ps = pss.next()
                        P.mm(ps, ps.t[0:64, 0:512], ones.t[:], t1.t[:, bs], [ones, t1])
                        P.tt("dve", bon.t[:, bs], ps.t[0:64, 0:512], v.t[:, bs], ALU.mult, [ps, v], [bon])
                    Pf, E, Sf, X = Tl["Pf"], Tl["E"], Tl["Sf"], Tl["X"]
                    P.op("dve", lambda g: g.tensor_tensor_scan(out=Pf.t[:], data0=rst.t[:, 0:W], data1=sg.t[:], initial=0.0,
                                                                op0=ALU.mult, op1=ALU.add), [rst, sg], [Pf])
                    P.tt("dve", E.t[:], Pf.t[:], sg.t[:], ALU.subtract, [Pf, sg], [E])
                    for j in range(W // 64):
                        P.ts("dve", Sf.t[:, 64 * j:64 * j + 64], E.t[:, 64 * j:64 * j + 64], -1.0, Pf.t[:, 64 * j + 63:64 * j + 64],
                             ALU.mult, ALU.add, [E, Pf], [Sf])
                    P.tt("pool", X.t[:], Sf.t[:], sg.t[:], ALU.subtract, [Sf, sg], [X])
                    Gi, Ge, Tm = (Pf, E, X) if d == 0 else (Sf, X, E)
                    eI, eX, eN, eT = Tl["eI"], Tl["eX"], Tl["eN"], Tl["eT"]
                    P.act(eI.t[:], Gi.t[:], AF.Exp, [Gi], [eI], scale=-CDEC)
                    P.act(eX.t[:], Ge.t[:], AF.Exp, [Ge], [eX], scale=-CDEC)
                    P.act(eN.t[:], Gi.t[:], AF.Exp, [Gi], [eN], scale=CDEC)
                    P.act(eT.t[:], Tm.t[:], AF.Exp, [Tm], [eT], scale=-CDEC)
                    at, bt, kt, rt, bh, kh = (Tl[n] for n in ("at", "bt", "kt", "rt", "bh", "kh"))
                    P.stt("dve", at.t[:], kk.t[:], -1.0, eX.t[:], ALU.mult, ALU.mult, [kk, eX], [at])
                    P.tt("pool", bt.t[:], b.t[:], eN.t[:], ALU.mult, [b, eN], [bt])
                    P.tt("dve", kt.t[:], kd.t[:], eN.t[:], ALU.mult, [kd, eN], [kt])
                    P.tt("pool", rt.t[:], r.t[:], eI.t[:], ALU.mult, [r, eI], [rt])
                    P.tt("dve", bh.t[:], b.t[:], eT.t[:], ALU.mult, [b, eT], [bh])
                    P.tt("pool", kh.t[:], kd.t[:], eT.t[:], ALU.mult, [kd, eT], [kh])
                    Y = Tl["Y"]
                    NCH = W // 64
                    order = list(range(NCH)) if d == 0 else list(range(NCH - 1, -1, -1))
                    cs_ = lambda j: slice(64 * j, 64 * j + 64)
                    tmq = []
                    for qi, src in enumerate((at, bh, kh, v)):
                        ps = pss.next()
                        for j in range(NCH):
                            P.tr(ps, ps.t[0:64, cs_(j)], src.t[:, cs_(j)], idn, [src, self.ident])
                        tq = TM[qi]
                        P.cp("act" if qi % 2 == 0 else "dve", tq.t[:], ps.t[0:64, 0:W], [ps], [tq])
                        tmq.append(tq)
                    Atm, Bhtm, Khtm, Vtm = tmq
                    gq = []
                    for qi, (lt, rh, mi) in enumerate(((bt, at, 0), (bt, rt, 1), (kt, at, 0), (kt, rt, 1), (at, bt, 2))):
                        ps = pss.next()
                        for j in range(NCH):
                            P.mm(ps, ps.t[0:64, cs_(j)], lt.t[:, cs_(j)], rh.t[:, cs_(j)], [lt, rh])
                        gt_ = GM[qi]
                        P.tt("dve", gt_.t[:], ps.t[0:64, 0:W], mk.t[:, d, mi, 0:W], ALU.mult, [ps, mk], [gt_])
                        gq.append(gt_)
                    Aab, Abr, Aak, Akr, NT = gq
                    Tt = TT_[0]
                    P.tt("pool", Tt.t[:], Aab.t[:], irep.t[:, 0:W], ALU.add, [Aab, irep], [Tt])
                    Pm, PTm = Aab, NT
                    for lev in range(5):
                        ps1 = pss.next()
                        for j in range(NCH):
                            P.mm(ps1, ps1.t[0:64, cs_(j)], PTm.t[:, cs_(j)], Pm.t[:, cs_(j)], [PTm, Pm])
                        ps2 = pss.next()
                        for j in range(NCH):
                            P.mm(ps2, ps2.t[0:64, cs_(j)], Pm.t[:, cs_(j)], PTm.t[:, cs_(j)], [PTm, Pm])
                        Pn, PTn = PP[lev % 2]
                        P.cp("act", Pn.t[:], ps1.t[0:64, 0:W], [ps1], [Pn])
                        P.cp("dve", PTn.t[:], ps2.t[0:64, 0:W], [ps2], [PTn])
                        Pm, PTm = Pn, PTn
                        ps3 = pss.next()
                        for j in range(NCH):
                            P.mm(ps3, ps3.t[0:64, cs_(j)], PTm.t[:, cs_(j)], Tt.t[:, cs_(j)], [PTm, Tt])
                        Tn = TT_[(lev + 1) % 2]
                        P.tt("dve", Tn.t[:], ps3.t[0:64, 0:W], Tt.t[:], ALU.add, [ps3, Tt], [Tn])
                        Tt = Tn
                    ps = pss.next()
                    for j in range(NCH):
                        P.mm(ps, ps.t[0:64, cs_(j)], Aak.t[:, cs_(j)], Vtm.t[:, cs_(j)], [Aak, Vtm])
                    P.cp("act", X1.t[:], ps.t[0:64, 0:W], [ps], [X1])
                    psa = pss.next()
                    for j in range(NCH):
                        P.mm(psa, psa.t[0:64, cs_(j)], Atm.t[:, cs_(j)], Tt.t[:, cs_(j)], [Atm, Tt])
                    P.cp("dve", AHT.t[:], psa.t[0:64, 0:W], [psa], [AHT])
                    psb = pss.next()
                    for j in range(NCH):
                        P.mm(psb, psb.t[0:64, cs_(j)], Tt.t[:, cs_(j)], Atm.t[:, cs_(j)], [Atm, Tt])
                    P.cp("act", AHN.t[:], psb.t[0:64, 0:W], [psb], [AHN])
                    ps = pss.next()
                    for j in range(NCH):
                        P.mm(ps, ps.t[0:64, cs_(j)], Tt.t[:, cs_(j)], X1.t[:, cs_(j)], [Tt, X1])
                    P.cp("dve", U0.t[:], ps.t[0:64, 0:W], [ps], [U0])
                    ps = pss.next()
                    for j in range(NCH):
                        P.mm(ps, ps.t[0:64, cs_(j)], AHN.t[:, cs_(j)], Bhtm.t[:, cs_(j)], [AHN, Bhtm])
                    for j in range(NCH):
                        gcol = 64 * j + 63 if d == 0 else 64 * j
                        P.stt("dve", MT.t[:, cs_(j)], idn, eI.t[:, gcol:gcol + 1], ps.t[0:64, cs_(j)], ALU.mult, ALU.add,
                              [self.ident, eI, ps], [MT])
                    ps = pss.next()
                    for j in range(NCH):
                        P.mm(ps, ps.t[0:64, cs_(j)], Bhtm.t[:, cs_(j)], U0.t[:, cs_(j)], [Bhtm, U0], start=(j == 0), stop=False, sgc=True)
                        P.mm(ps, ps.t[0:64, cs_(j)], Khtm.t[:, cs_(j)], Vtm.t[:, cs_(j)], [Khtm, Vtm], start=False, stop=(j == NCH - 1), sgc=True)
                    P.cp("act", CC.t[:], ps.t[0:64, 0:W], [ps], [CC])
                    psU = psU_
                    psY = psY_
                    for ji, j in enumerate(order):
                        P.mm(psU, psU.t[0:64, cs_(j)], AHT.t[:, cs_(j)], st.t[:], [AHT, st], start=(ji == 0), stop=False, sgc=True)
                        P.mm(psU, psU.t[0:64, cs_(j)], idn, U0.t[:, cs_(j)], [self.ident, U0], start=False, stop=True, sgc=True)
                        u = ur.next()
                        P.cp("act", u.t[:], psU.t[0:64, cs_(j)], [psU], [u])
                        P.mm(psY, psY.t[0:64, cs_(j)], st.t[:], rt.t[:, cs_(j)], [st, rt], start=(ji == 0), stop=False, sgc=True)
                        P.mm(psY, psY.t[0:64, cs_(j)], u.t[:], Abr.t[:, cs_(j)], [u, Abr], start=False, stop=False, sgc=True)
                        P.mm(psY, psY.t[0:64, cs_(j)], Vtm.t[:, cs_(j)], Akr.t[:, cs_(j)], [Vtm, Akr], start=False, stop=True, sgc=True)
                        psS = pss.next()
                        P.mm(psS, psS.t[0:64, 0:64], MT.t[:, cs_(j)], st.t[:], [MT, st])
                        stn = str_.next()
                        P.tt("dve", stn.t[:], psS.t[0:64, 0:64], CC.t[:, cs_(j)], ALU.add, [psS, CC], [stn])
                        st = stn
                    P.cp("act", Y.t[:], psY.t[0:64, 0:W], [psY], [Y])
                    if d == 0:
                        P.dma("pool", self.yF[0, h, :, s0:s0 + W], Y.t[:], reads=[Y], writes=[yfb])
                        P.dma("pool", self.yF[1, h, :, s0:s0 + W], bon.t[:], reads=[bon], writes=[yfb])
                    else:
                        yf, bf = Tl["yf"], Tl["bf"]
                        P.dma("sp", yf.t[:], self.yF[0, h, :, s0:s0 + W], reads=[yfb], writes=[yf])
                        P.dma("sp", bf.t[:], self.yF[1, h, :, s0:s0 + W], reads=[yfb], writes=[bf])
                        P.tt("dve", Y.t[:], Y.t[:], yf.t[:], ALU.add, [Y, yf], [Y])
                        P.tt("pool", bon.t[:], bon.t[:], bf.t[:], ALU.add, [bon, bf], [bon])
                        P.act(gl.t[:], gl.t[:], AF.Sigmoid, [gl], [gl])
                        for blk in range(W // 512):
                            bs = slice(blk * 512, blk * 512 + 512)
                            ps = pss.next()
                            P.mm(ps, ps.t[0:64, 0:512], onesm.t[:], Y.t[:, bs], [onesm, Y])
                            P.tt("dve", Y.t[:, bs], Y.t[:, bs], ps.t[0:64, 0:512], ALU.subtract, [Y, ps], [Y])
                            P.tt("pool", t1.t[:, bs], Y.t[:, bs], Y.t[:, bs], ALU.mult, [Y], [t1])
                            ps = pss.next()
                            P.mm(ps, ps.t[0:64, 0:512], onesm.t[:], t1.t[:, bs], [onesm, t1])
                            P.act(t1.t[:, bs], ps.t[0:64, 0:512], AF.Sqrt, [ps, self.epsr], [t1], bias=self.epsr.t[0:64, 2:3])
                            P.op("dve", lambda g, bs=bs: g.reciprocal(out=t1.t[:, bs], in_=t1.t[:, bs]), [t1], [t1])
                            P.tt("dve", Y.t[:, bs], Y.t[:, bs], t1.t[:, bs], ALU.mult, [Y, t1], [Y])
                            P.ts("dve", Y.t[:, bs], Y.t[:, bs], hpv[:, h, 1:2], hpv[:, h, 2:3], ALU.mult, ALU.add, [Y, hp], [Y])
                            P.tt("pool", Y.t[:, bs], Y.t[:, bs], bon.t[:, bs], ALU.add, [Y, bon], [Y])
                            ps = pss.next()
                            P.mm(ps, ps.t[0:64, 0:512], g2.t[:, h * 64:h * 64 + 64], gl.t[:, bs], [g2, gl])
                            P.tt("dve", yo.t[:, bs], Y.t[:, bs], ps.t[0:64, 0:512], ALU.mult, [Y, ps], [yo])
                        P.dma("pool", self.yM[cc, pb:pb + 64, s0:s0 + W], yo.t[:], reads=[yo])
    S.close()


Builder.mix_rwkv = mix_rwkv
```

```python
import math
from contextlib import ExitStack
import numpy as np
import concourse.bass as bass
import concourse.mybir as mybir
from concourse.bass_utils import run_bass_kernel_spmd

F32 = mybir.dt.float32
BF16 = mybir.dt.bfloat16
AF = mybir.ActivationFunctionType
ALU = mybir.AluOpType
AX = mybir.AxisListType

D = 1024
DFF = 2816
KC = 8
FC = 22
DEPTH = 2
NCORES = 8
RMS_EPS = 1e-6
SUBLN_EPS = 1e-5
LNX_EPS = 64e-5
NWIN = 52
CK = 128


class Buf:
    __slots__ = ("w", "r", "psum")

    def __init__(self):
        self.w = None
        self.r = {}
        self.psum = False


class Tile:
    def __init__(self, t, buf=None):
        self.t = t
        self.buf = buf if buf is not None else Buf()

    def __getitem__(self, k):
        return self.t[k]


class Prog:
    ENG = ("pe", "dve", "act", "pool", "sp")
    NDS = 24

    def __init__(self, nc):
        self.nc = nc
        self.eng = {"pe": nc.tensor, "dve": nc.vector, "act": nc.scalar, "pool": nc.gpsimd, "sp": nc.sync}
        self.sem = {}
        for e in self.ENG:
            self.sem[e] = nc.semaphore("s_" + e).__enter__()
        for i in range(self.NDS):
            self.sem[("d", i)] = nc.semaphore("d%d" % i).__enter__()
            self.sem[("g", i)] = nc.semaphore("g%d" % i).__enter__()
        self.gnext = 0
        self.cnt = {k: 0 for k in self.sem}
        self.waited = {e: {} for e in self.ENG}
        self.dnext = 0
        self.ninst = 0

    def _need(self, reads, writes, e=None):
        need = {}
        for b in reads:
            b = b.buf if isinstance(b, Tile) else b
            if b.w is not None and need.get(b.w[0], 0) < b.w[1]:
                need[b.w[0]] = b.w[1]
            if b.psum:
                for k, v in b.r.items():
                    if k != e and need.get(k, 0) < v:
                        need[k] = v
        for b in writes:
            b = b.buf if isinstance(b, Tile) else b
            if b.w is not None and need.get(b.w[0], 0) < b.w[1]:
                need[b.w[0]] = b.w[1]
            for k, v in b.r.items():
                if need.get(k, 0) < v:
                    need[k] = v
        return need

    SELF_SYNC = True

    def _wait(self, e, need, skip_self=False):
        eng = self.eng[e]
        wd = self.waited[e]
        for k, v in need.items():
            if (skip_self or not self.SELF_SYNC) and k == e:
                continue
            if wd.get(k, 0) >= v:
                continue
            eng.wait_ge(self.sem[k], v)
            wd[k] = v
            self.ninst += 1

    def _mark(self, ev, reads, writes):
        for b in reads:
            b = b.buf if isinstance(b, Tile) else b
            if b.r.get(ev[0], 0) < ev[1]:
                b.r[ev[0]] = ev[1]
        for b in writes:
            b = b.buf if isinstance(b, Tile) else b
            b.w = ev
            b.r = {}

    def op(self, e, fn, reads=(), writes=(), skip_self=False):
        self._wait(e, self._need(reads, writes, e), skip_self)
        ins = fn(self.eng[e])
        ins.then_inc(self.sem[e], 1)
        self.cnt[e] += 1
        self.ninst += 1
        self._mark((e, self.cnt[e]), reads, writes)

    def dma(self, q, out, in_, reads=(), writes=()):
        self._wait(q, self._need(reads, writes))
        if q == "pool":
            k = ("g", self.gnext)
            self.gnext = (self.gnext + 1) % self.NDS
        else:
            k = ("d", self.dnext)
            self.dnext = (self.dnext + 1) % self.NDS
        self.eng[q].dma_start(out=out, in_=in_).then_inc(self.sem[k], 16)
        self.cnt[k] += 16
        self.ninst += 1
        self._mark((k, self.cnt[k]), reads, writes)

    def barrier(self):
        for e in self.ENG:
            self._wait(e, dict(self.cnt))

    def mm(self, out_t, out_ap, lhsT_ap, rhs_ap, reads, start=True, stop=True, sgc=False):
        if sgc:
            self.op("pe", lambda g: g.matmul(out_ap, lhsT=lhsT_ap, rhs=rhs_ap, start=start, stop=stop, skip_group_check=True),
                    reads=reads, writes=[out_t], skip_self=True)
        else:
            self.op("pe", lambda g: g.matmul(out_ap, lhsT=lhsT_ap, rhs=rhs_ap, start=start, stop=stop),
                    reads=reads, writes=[out_t], skip_self=True)

    def tr(self, out_t, out_ap, in_ap, ident_ap, reads):
        self.op("pe", lambda g: g.transpose(out_ap, in_ap, ident_ap), reads=reads, writes=[out_t], skip_self=True)

    def act(self, out_ap, in_ap, func, reads, writes, bias=None, scale=1.0, accum=None):
        kw = {}
        if bias is not None:
            kw["bias"] = bias
        if accum is not None:
            kw["accum_out"] = accum
        self.op("act", lambda g: g.activation(out=out_ap, in_=in_ap, func=func, scale=scale, **kw),
                reads=reads, writes=writes)

    def tt(self, e, out_ap, a_ap, b_ap, op, reads, writes):
        self.op(e, lambda g: g.tensor_tensor(out=out_ap, in0=a_ap, in1=b_ap, op=op), reads=reads, writes=writes)

    def stt(self, e, out_ap, a_ap, scalar, b_ap, op0, op1, reads, writes):
        self.op(e, lambda g: g.scalar_tensor_tensor(out=out_ap, in0=a_ap, scalar=scalar, in1=b_ap, op0=op0, op1=op1),
                reads=reads, writes=writes)

    def ts(self, e, out_ap, a_ap, s1, s2, op0, op1, reads, writes):
        if s2 is None:
            self.op(e, lambda g: g.tensor_scalar(out=out_ap, in0=a_ap, scalar1=s1, scalar2=None, op0=op0),
                    reads=reads, writes=writes)
        else:
            self.op(e, lambda g: g.tensor_scalar(out=out_ap, in0=a_ap, scalar1=s1, scalar2=s2, op0=op0, op1=op1),
                    reads=reads, writes=writes)

    def cp(self, e, out_ap, in_ap, reads, writes):
        if e == "act":
            self.op(e, lambda g: g.copy(out=out_ap, in_=in_ap), reads=reads, writes=writes)
        else:
            self.op(e, lambda g: g.tensor_copy(out=out_ap, in_=in_ap), reads=reads, writes=writes)

    def memset(self, e, t, ap, val):
        self.op(e, lambda g: g.memset(ap, val), reads=(), writes=[t])


class Pool_:
    def __init__(self, nc):
        self.nc = nc
        self.st = ExitStack()
        self.n = 0

    def sb(self, shape, dt, name=None):
        self.n += 1
        return Tile(self.st.enter_context(self.nc.sbuf_tensor("%s_%d" % (name or "t", id(self) % 100000 * 1000 + self.n), list(shape), dt)))

    def ps(self, shape, dt=F32, name=None):
        self.n += 1
        return Tile(self.st.enter_context(self.nc.psum_tensor("%s_%d" % (name or "p", id(self) % 100000 * 1000 + self.n), list(shape), dt)))

    def close(self):
        self.st.close()


class Ring:
    def __init__(self, tiles):
        self.tiles = tiles
        self.i = 0

    def next(self):
        t = self.tiles[self.i % len(self.tiles)]
        self.i += 1
        return t


def fm_pieces(W):
    K, N = W.shape
    return np.ascontiguousarray(W.reshape(K // 128, 128, N // 128, 128).transpose(2, 1, 0, 3)).reshape(N // 128, 128, K)


def pcol(v, nchunk):
    return np.ascontiguousarray(np.asarray(v, np.float32).reshape(nchunk, 128).T)


def host_weights(inp):
    f = lambda a: np.asarray(a, np.float32)
    out = {}
    gu1, d1, gu2, d2, win, wv, pabc, wout = [], [], [], [], [], [], [], []
    for l in range(DEPTH):
        for (gl, dl, pre) in ((gu1, d1, "ffn1"), (gu2, d2, "ffn2")):
            g = fm_pieces(f(inp[pre + "_w_gate"][l]))
            u = fm_pieces(f(inp[pre + "_w_up"][l]))
            gl.append(np.stack([g, u], axis=1).reshape(2 * FC, 128, D))
            dl.append(fm_pieces(f(inp[pre + "_w_down"][l])))
        W = f(inp["w_in"][l])
        cols = [W[:, 0:1792], W[:, 1792:2304]]
        for g in range(8):
            blk = np.zeros((D, 128), np.float32)
            blk[:, (g % 4) * 32:(g % 4) * 32 + 32] = W[:, 2560 + g * 32:2560 + g * 32 + 32]
            cols.append(blk)
        cols += [W[:, 2816:3072], W[:, 3328:6400]]
        win.append(fm_pieces(np.concatenate(cols, axis=1)))
        Wv = np.concatenate([W[:, 2304:2560], W[:, 3072:3328]], axis=1)
        wv.append(np.ascontiguousarray(Wv.reshape(KC, 128, 512).transpose(1, 0, 2)).reshape(128, KC * 512))
        pabc.append(fm_pieces(np.concatenate([f(inp["p_a"][l]), f(inp["p_b"][l]), f(inp["p_c"][l])], axis=0)))
        wout.append(fm_pieces(f(inp["w_out"][l])))
    out["wgu1"] = np.stack(gu1); out["wd1"] = np.stack(d1)
    out["wgu2"] = np.stack(gu2); out["wd2"] = np.stack(d2)
    out["win"] = np.stack(win); out["wv"] = np.stack(wv)
    out["wpabc"] = np.stack(pabc); out["wout"] = np.stack(wout)
    gains = []
    for l in range(DEPTH):
        gains += [pcol(inp["ln_ffn1_g"][l], KC), pcol(inp["ln_mix_g"][l], KC), pcol(inp["ln_ffn2_g"][l], KC)]
    gains.append(pcol(inp["final_g"], KC))
    out["gains"] = np.concatenate(gains, axis=1)
    return out


WSHAPES = {"wgu1": [DEPTH, 2 * FC, 128, D], "wd1": [DEPTH, KC, 128, DFF], "wgu2": [DEPTH, 2 * FC, 128, D],
           "wd2": [DEPTH, KC, 128, DFF], "win": [DEPTH, NWIN, 128, D], "wv": [DEPTH, 128, KC * 512],
           "wpabc": [DEPTH, KC, 128, D], "wout": [DEPTH, KC, 128, D]}


def host_consts():
    c = {}
    c["ident"] = np.eye(128, dtype=np.float32)
    c["onesm"] = np.full((128, 128), 1.0 / D, np.float32)
    return c


CSHAPES = {"ident": [128, 128], "onesm": [128, 128]}


_UID = [0]


def _uid(prefix):
    _UID[0] += 1
    return "%s%d" % (prefix, _UID[0])


class Scope:
    def __init__(self, nc):
        self.nc = nc
        self.st = ExitStack()

    def sb(self, shape, dt, name="t"):
        return Tile(self.st.enter_context(self.nc.sbuf_tensor(_uid(name), list(shape), dt)))

    def ps(self, shape, dt=F32, name="p"):
        t = Tile(self.st.enter_context(self.nc.psum_tensor(_uid(name), list(shape), dt)))
        t.buf.psum = True
        return t

    def close(self):
        self.st.close()


class WStream:
    def __init__(self, B, slots, plan):
        self.B = B
        self.slots = slots
        self.plan = plan
        self.loaded = 0
        self.pos = 0

    def _load(self, i):
        src, G, X = self.plan[i]
        slot = self.slots[i % len(self.slots)]
        if G == 0:
            self.B.P.dma("sp", slot.t[:, 0:X], src, writes=[slot])
        else:
            self.B.P.dma("sp", slot.t[:, 0:G * X].rearrange("p (g x) -> p g x", g=G),
                         src.rearrange("g p x -> p g x"), writes=[slot])

    def get(self):
        while self.loaded < len(self.plan) and self.loaded < self.pos + len(self.slots):
            self._load(self.loaded)
            self.loaded += 1
        slot = self.slots[self.pos % len(self.slots)]
        self.pos += 1
        return slot


class Builder:
    def __init__(self, seqs, debug=None):
        self.seqs = list(seqs)
        self.T = sum(self.seqs)
        self.starts = [sum(self.seqs[:i]) for i in range(len(self.seqs))]
        self.TT = 1024 if self.T % 1024 == 0 else 512
        self.NS = self.TT // 512
        self.debug = debug or {}
        nc = bass.Bass("TRN2", target_bir_lowering=False)
        self.nc = nc
        self.P = Prog(nc)
        T = self.T
        dt_in = lambda name, shape: nc.dram_tensor(name, list(shape), F32, kind="ExternalInput").ap()
        self.xin = dt_in("xin", [T, D])
        self.yout = nc.dram_tensor("yout", [T, D], F32, kind="ExternalOutput").ap()
        self.wf = {k: dt_in(k, s) for k, s in WSHAPES.items()}
        self.wb = {k: nc.dram_tensor(k + "_b", list(s), BF16).ap() for k, s in WSHAPES.items()}
        self.cst = {k: dt_in("c_" + k, s) for k, s in CSHAPES.items()}
        self.gains_d = dt_in("gains", [128, 56])
        self.small = {k: dt_in(k, s) for k, s in SMALL_SHAPES.items()}
        scr = lambda name, shape, dt: nc.dram_tensor(name, list(shape), dt).ap()
        self.xT = scr("xT", [KC, 128, T], F32)
        self.zA = scr("zA", [14, 128, T], F32)
        self.zNq = scr("zNq", [2, 128, T], BF16)
        self.zNk = scr("zNk", [2, 128, T], BF16)
        self.zDq = scr("zDq", [8, 128, T], BF16)
        self.zDk = scr("zDk", [2, 128, T], BF16)
        self.zV = scr("zV", [T, 520], BF16)
        self.zG = scr("zG", [24, 128, T], BF16)
        self.yM = scr("yM", [KC, 128, T], BF16)
        self.dbg = {}
        for k, s in self.debug.items():
            if not isinstance(s, (list, tuple)):
                continue
            self.dbg[k] = nc.dram_tensor("dbg_" + k, list(s), F32, kind="ExternalOutput").ap()

    def build(self):
        P = self.P
        nc = self.nc
        G = Scope(nc)
        self.G = G
        self.ident = G.sb([128, 128], F32, "ident")
        self.identb = G.sb([128, 128], BF16, "identb")
        self.onesm = G.sb([128, 128], BF16, "onesm")
        self.gains = G.sb([128, 56], F32, "gains")
        self.epsr = G.sb([128, 4], F32, "eps")
        tmp = G.sb([128, 128], F32, "ctmp")
        P.dma("sp", self.ident.t[:], self.cst["ident"], writes=[self.ident])
        P.dma("sp", tmp.t[:], self.cst["onesm"], writes=[tmp])
        P.dma("sp", self.gains.t[:], self.gains_d, writes=[self.gains])
        P.cp("dve", self.identb.t[:], self.ident.t[:], [self.ident], [self.identb])
        P.cp("dve", self.onesm.t[:], tmp.t[:], [tmp], [self.onesm])
        P.memset("dve", self.epsr, self.epsr.t[:, 0:1], RMS_EPS)
        P.memset("dve", self.epsr, self.epsr.t[:, 1:2], SUBLN_EPS)
        P.memset("dve", self.epsr, self.epsr.t[:, 2:3], LNX_EPS)
        P.memset("dve", self.epsr, self.epsr.t[:, 3:4], 0.0)
        self.prep_weights()
        P.barrier()
        for l in range(DEPTH):
            self.phaseA(l)
            P.barrier()
            self.mixers(l)
            P.barrier()
            self.phaseC(l)
            P.barrier()
        G.close()
        return nc

    def prep_weights(self):
        P = self.P
        S = Scope(self.nc)
        CHK = 8192
        st32 = [S.sb([128, CHK], F32, "w32") for _ in range(3)]
        st16 = [S.sb([128, CHK], BF16, "w16") for _ in range(3)]
        i = 0
        for k, shp in WSHAPES.items():
            tot = int(np.prod(shp))
            per = tot // 128
            src = self.wf[k]
            dst = self.wb[k]
            X = shp[-1]
            s2 = src.rearrange("l j p x -> (l j p) x") if len(shp) == 4 else src.rearrange("l p x -> (l p) x")
            d2 = dst.rearrange("l j p x -> (l j p) x") if len(shp) == 4 else dst.rearrange("l p x -> (l p) x")
            rows = tot // X
            gmax = max(1, CHK // X)
            r = 0
            while r < rows:
                g = min(gmax, (rows - r) // 128)
                a32 = st32[i % 3]
                a16 = st16[i % 3]
                P.dma("sp", a32.t[:, 0:g * X].rearrange("p (g x) -> p g x", g=g),
                      s2[r:r + g * 128, :].rearrange("(g p) x -> p g x", p=128), writes=[a32])
                e = ("dve", "act", "pool")[i % 3]
                P.cp(e, a16.t[:, 0:g * X], a32.t[:, 0:g * X], [a32], [a16])
                P.dma("sp", d2[r:r + g * 128, :].rearrange("(g p) x -> p g x", p=128),
                      a16.t[:, 0:g * X].rearrange("p (g x) -> p g x", g=g), reads=[a16])
                r += g * 128
                i += 1
        S.close()

    def rmsnorm(self, x, sq, u, gcol, pss, rstd, out_f32=False):
        P = self.P
        NS = self.NS
        for s in range(NS):
            sl = slice(s * 512, (s + 1) * 512)
            for c in range(KC):
                P.tt("pool", sq.t[:, c, sl], x.t[:, c, sl], x.t[:, c, sl], ALU.mult, [x], [sq])
            ps = pss.next()
            for c in range(KC):
                P.mm(ps, ps.t[:, 0:512], self.onesm.t[:], sq.t[:, c, sl], [sq, self.onesm], start=(c == 0), stop=(c == KC - 1))
            P.act(rstd.t[:, sl], ps.t[:, 0:512], AF.Sqrt, [ps, self.epsr], [rstd], bias=self.epsr.t[:, 0:1])
            P.op("dve", lambda g, sl=sl: g.reciprocal(out=rstd.t[:, sl], in_=rstd.t[:, sl]), [rstd], [rstd])
            for c in range(KC):
                P.stt("dve", u.t[:, c, sl], x.t[:, c, sl], self.gains.t[:, gcol + c:gcol + c + 1], rstd.t[:, sl],
                      ALU.mult, ALU.mult, [x, rstd, self.gains], [u])

    def ffn(self, ws, x, u, h, pss, tmps):
        P = self.P
        NS = self.NS
        for jp in range(FC // 2):
            w = ws.get()
            for jj in range(2):
                j = jp * 2 + jj
                pg = [pss.next() for _ in range(NS)]
                pu = [pss.next() for _ in range(NS)]
                for gi, pp in ((0, pg), (1, pu)):
                    base = (jj * 2 + gi) * D
                    for c in range(KC):
                        for s in range(NS):
                            P.mm(pp[s], pp[s].t[:, 0:512], w.t[:, base + c * 128:base + (c + 1) * 128],
                                 u.t[:, c, s * 512:(s + 1) * 512], [w, u], start=(c == 0), stop=(c == KC - 1))
                for s in range(NS):
                    tm = tmps.next()
                    P.act(tm.t[:, 0:512], pg[s].t[:, 0:512], AF.Silu, [pg[s]], [tm])
                    P.tt("dve", h.t[:, j, s * 512:(s + 1) * 512], tm.t[:, 0:512], pu[s].t[:, 0:512], ALU.mult, [tm, pu[s]], [h])
        for o in range(KC):
            w = ws.get()
            po = [pss.next() for _ in range(NS)]
            for j in range(FC):
                for s in range(NS):
                    P.mm(po[s], po[s].t[:, 0:512], w.t[:, j * 128:(j + 1) * 128], h.t[:, j, s * 512:(s + 1) * 512],
                         [w, h], start=(j == 0), stop=(j == FC - 1))
            for s in range(NS):
                sl = slice(s * 512, (s + 1) * 512)
                P.stt("dve", x.t[:, o, sl], po[s].t[:, 0:512], 0.5, x.t[:, o, sl], ALU.mult, ALU.add, [po[s], x], [x])

    def ffn_plan(self, key_gu, key_d, l):
        plan = []
        for jp in range(FC // 2):
            plan.append((self.wb[key_gu][l, jp * 4:jp * 4 + 4], 4, D))
        for o in range(KC):
            plan.append((self.wb[key_d][l, o:o + 1], 1, DFF))
        return plan

    def phaseA(self, l):
        P = self.P
        nc = self.nc
        TT, NS, T = self.TT, self.NS, self.T
        S = Scope(nc)
        x = S.sb([128, KC, TT], F32, "x")
        u = S.sb([128, KC, TT], BF16, "u")
        h = S.sb([128, FC, TT], BF16, "h")
        rstd = S.sb([128, TT], F32, "rstd")
        slots = [S.sb([128, 4096], BF16, "wslot") for _ in range(4)]
        tmps = Ring([S.sb([128, 512], F32, "tmp") for _ in range(4)])
        stg = Ring([S.sb([128, 1024], F32, "stg") for _ in range(4)])
        pss = Ring([S.ps([128, 512], F32, "ps") for _ in range(8)])
        vst = Ring([S.sb([128, 8, 65], BF16, "vst") for _ in range(3)])
        for v_ in vst.tiles:
            P.memset("pool", v_, v_.t[:], 1.0)
        ntile = T // TT
        plan = []
        for it in range(ntile):
            plan += self.ffn_plan("wgu1", "wd1", l)
            for jp in range(NWIN // 4):
                plan.append((self.wb["win"][l, jp * 4:jp * 4 + 4], 4, D))
            plan.append((self.wb["wv"][l], 0, KC * 512))
        ws = WStream(self, slots, plan)
        for it in range(ntile):
            t0 = it * TT
            if l == 0:
                for b in range(TT // 128):
                    sg = stg.next()
                    P.dma("sp", sg.t[:, 0:D], self.xin[t0 + b * 128:t0 + (b + 1) * 128, :], writes=[sg])
                    for half in range(2):
                        ps = pss.next()
                        for cc in range(4):
                            c = half * 4 + cc
                            P.tr(ps, ps.t[:, cc * 128:(cc + 1) * 128], sg.t[:, c * 128:(c + 1) * 128], self.ident.t[:], [sg, self.ident])
                        e = "act" if half == 0 else "dve"
                        P.cp(e, x.t[:, half * 4:half * 4 + 4, b * 128:(b + 1) * 128],
                             ps.t[:, 0:512].rearrange("p (c t) -> p c t", c=4), [ps], [x])
            else:
                P.dma("sp", x.t[:], self.xT[:, :, t0:t0 + TT].rearrange("c p t -> p c t"), writes=[x])
            self.rmsnorm(x, h, u, (l * 3 + 0) * KC, pss, rstd)
            self.ffn(ws, x, u, h, pss, tmps)
            P.dma("pool", self.xT[:, :, t0:t0 + TT].rearrange("c p t -> p c t"), x.t[:], reads=[x])
            self.rmsnorm(x, h, u, (l * 3 + 1) * KC, pss, rstd)
            for jp in range(NWIN // 4):
                w = ws.get()
                for jj in range(4):
                    j = jp * 4 + jj
                    pp = [pss.next() for _ in range(NS)]
                    for c in range(KC):
                        for s in range(NS):
                            P.mm(pp[s], pp[s].t[:, 0:512], w.t[:, jj * D + c * 128:jj * D + (c + 1) * 128],
                                 u.t[:, c, s * 512:(s + 1) * 512], [w, u], start=(c == 0), stop=(c == KC - 1))
                    sg = stg.next()
                    if j < 14:
                        dst, view = self.zA[j, :, t0:t0 + TT], sg.t[:, 0:TT]
                        for s in range(NS):
                            P.cp("act" if s == 0 else "dve", view[:, s * 512:(s + 1) * 512], pp[s].t[:, 0:512], [pp[s]], [sg])
                    else:
                        view = sg.t[:].bitcast(BF16)[:, 0:TT]
                        if j < 16:
                            dst, sc, fn = self.zNq[j - 14, :, t0:t0 + TT], 0.125, AF.Copy
                        elif j < 18:
                            dst, sc, fn = self.zNk[j - 16, :, t0:t0 + TT], 1.0, AF.Copy
                        elif j < 26:
                            dst, sc, fn = self.zDq[j - 18, :, t0:t0 + TT], 32.0 ** -0.5, AF.Copy
                        elif j < 28:
                            dst, sc, fn = self.zDk[j - 26, :, t0:t0 + TT], 1.0, AF.Copy
                        else:
                            dst, sc, fn = self.zG[j - 28, :, t0:t0 + TT], 1.0, AF.Sigmoid
                        for s in range(NS):
                            if fn == AF.Sigmoid or s == 0:
                                P.act(view[:, s * 512:(s + 1) * 512], pp[s].t[:, 0:512], fn, [pp[s]], [sg], scale=sc)
                            else:
                                P.ts("dve", view[:, s * 512:(s + 1) * 512], pp[s].t[:, 0:512], sc, None, ALU.mult, None, [pp[s]], [sg])
                    P.dma("pool", dst, view, reads=[sg])
            w = ws.get()
            for b in range(TT // 128):
                ps = pss.next()
                for c in range(KC):
                    P.mm(ps, ps.t[:, 0:512], u.t[:, c, b * 128:(b + 1) * 128], w.t[:, c * 512:(c + 1) * 512], [w, u],
                         start=(c == 0), stop=(c == KC - 1))
                sg = vst.next()
                P.cp("act" if b % 2 == 0 else "dve", sg.t[:, :, 0:64], ps.t[:, 0:512].rearrange("p (h d) -> p h d", h=8), [ps], [sg])
                P.dma("pool", self.zV[t0 + b * 128:t0 + (b + 1) * 128, :], sg.t[:].rearrange("p h d -> p (h d)"), reads=[sg])
        S.close()

    def phaseC(self, l):
        P = self.P
        nc = self.nc
        TT, NS, T = self.TT, self.NS, self.T
        last = (l == DEPTH - 1)
        S = Scope(nc)
        x = S.sb([128, KC, TT], F32, "x")
        u = S.sb([128, KC, TT], BF16, "u")
        h = S.sb([128, FC, TT], BF16, "h")
        ym = S.sb([128, KC, TT], BF16, "ym")
        rstd = S.sb([128, TT], F32, "rstd")
        gts = Ring([S.sb([128, 3, TT], BF16, "gt") for _ in range(2)])
        slots = [S.sb([128, 4096], BF16, "wslot") for _ in range(4)]
        tmps = Ring([S.sb([128, 512], F32, "tmp") for _ in range(4)])
        stg = Ring([S.sb([128, 1024], F32, "stg") for _ in range(3)])
        pss = Ring([S.ps([128, 512], F32, "ps") for _ in range(8)])
        ntile = T // TT
        plan = []
        for it in range(ntile):
            plan += [(self.wb["wpabc"][l, 0:4], 4, D), (self.wb["wpabc"][l, 4:8], 4, D),
                     (self.wb["wout"][l, 0:4], 4, D), (self.wb["wout"][l, 4:8], 4, D)]
            plan += self.ffn_plan("wgu2", "wd2", l)
        ws = WStream(self, slots, plan)
        zGv = self.zG.rearrange("(g o) p t -> o p g t", g=3)
        for it in range(ntile):
            t0 = it * TT
            P.dma("sp", x.t[:], self.xT[:, :, t0:t0 + TT].rearrange("c p t -> p c t"), writes=[x])
            P.dma("sp", ym.t[:], self.yM[:, :, t0:t0 + TT].rearrange("c p t -> p c t"), writes=[ym])
            for op_ in range(2):
                w = ws.get()
                for oo in range(4):
                    o = op_ * 4 + oo
                    gt = gts.next()
                    P.dma("sp", gt.t[:], zGv[o, :, :, t0:t0 + TT], writes=[gt])
                    for s in range(NS):
                        sl = slice(s * 512, (s + 1) * 512)
                        pa, pb, pc = pss.next(), pss.next(), pss.next()
                        for (pp, c0, c1) in ((pa, 0, 4), (pb, 4, 6), (pc, 6, 8)):
                            for c in range(c0, c1):
                                P.mm(pp, pp.t[:, 0:512], w.t[:, oo * D + c * 128:oo * D + (c + 1) * 128], ym.t[:, c, sl],
                                     [w, ym], start=(c == c0), stop=(c == c1 - 1))
                        t1, t2, t3 = tmps.next(), tmps.next(), tmps.next()
                        P.tt("dve", t1.t[:, 0:512], pa.t[:, 0:512], gt.t[:, 0, sl], ALU.mult, [pa, gt], [t1])
                        P.tt("dve", t2.t[:, 0:512], pb.t[:, 0:512], gt.t[:, 1, sl], ALU.mult, [pb, gt], [t2])
                        P.tt("pool", t1.t[:, 0:512], t1.t[:, 0:512], t2.t[:, 0:512], ALU.add, [t1, t2], [t1])
                        P.tt("dve", t3.t[:, 0:512], pc.t[:, 0:512], gt.t[:, 2, sl], ALU.mult, [pc, gt], [t3])
                        P.tt("pool", u.t[:, o, sl], t1.t[:, 0:512], t3.t[:, 0:512], ALU.add, [t1, t3], [u])
            for op_ in range(2):
                w = ws.get()
                for oo in range(4):
                    o = op_ * 4 + oo
                    for s in range(NS):
                        sl = slice(s * 512, (s + 1) * 512)
                        pp = pss.next()
                        for c in range(KC):
                            P.mm(pp, pp.t[:, 0:512], w.t[:, oo * D + c * 128:oo * D + (c + 1) * 128], u.t[:, c, sl], [w, u],
                                 start=(c == 0), stop=(c == KC - 1))
                        P.tt("dve", x.t[:, o, sl], pp.t[:, 0:512], x.t[:, o, sl], ALU.add, [pp, x], [x])
            self.rmsnorm(x, h, u, (l * 3 + 2) * KC, pss, rstd)
            self.ffn(ws, x, u, h, pss, tmps)
            if not last:
                P.dma("pool", self.xT[:, :, t0:t0 + TT].rearrange("c p t -> p c t"), x.t[:], reads=[x])
            else:
                self.rmsnorm(x, h, x, 6 * KC, pss, rstd)
                for b in range(TT // 128):
                    sg = stg.next()
                    for half in range(2):
                        ps = pss.next()
                        for cc in range(4):
                            c = half * 4 + cc
                            P.tr(ps, ps.t[:, cc * 128:(cc + 1) * 128], x.t[:, c, b * 128:(b + 1) * 128], self.ident.t[:], [x, self.ident])
                        P.cp("act" if half == 0 else "dve", sg.t[:, half * 512:(half + 1) * 512], ps.t[:, 0:512], [ps], [sg])
                    P.dma("pool", self.yout[t0 + b * 128:t0 + (b + 1) * 128, :], sg.t[:, 0:D], reads=[sg])
        S.close()

    def mixers(self, l):
        P = self.P
        en = self.debug_en if hasattr(self, "debug_en") else ("rwkv", "na", "da")
        S = Scope(self.nc)
        z = S.sb([128, 2048], BF16, "zero")
        P.memset("dve", z, z.t[:], 0.0)
        for (name, c0, c1) in (("rwkv", 0, 4), ("na", 4, 6), ("da", 6, 8)):
            if name in en:
                continue
            for c in range(c0, c1):
                for t in range(0, self.T, 2048):
                    n = min(2048, self.T - t)
                    P.dma("sp", self.yM[c, :, t:t + n], z.t[:, 0:n], reads=[z])
        S.close()
        P.barrier()
        if "na" in en:
            self.mix_na(l)
            P.barrier()
        if "da" in en:
            self.mix_da(l)
            P.barrier()
        if "rwkv" in en:
            self.mix_rwkv(l)
            P.barrier()


NA_TYPES = [(0, 0), (-2, -2), (-4, -3), (-4, -4), (-6, -6)]
SLOPES = [2.0 ** (-8.0 * (h + 1) / 4) for h in range(4)]


def na_rs(r, rows):
    return min(max(r - 4, 0), rows - 8)


def na_tile_info(r, rows):
    a, b = na_rs(r, rows) - r, na_rs(r + 1, rows) - r
    ty = NA_TYPES.index((a, b))
    kr0 = r + a
    nk = (b + 8 - a + 1) // 2
    return ty, kr0, nk


def na_tables(rpb):
    tab = np.full((128, 5, 4, 5, 128), -30000.0, np.float32)
    pk = np.arange(128)
    pq = np.arange(128)
    for ti, (a, b) in enumerate(NA_TYPES):
        nk = (b + 8 - a + 1) // 2
        for j in range(nk):
            krow = a + 2 * j + pk // 64
            kcol = pk % 64
            qrow = pq // 64
            qcol = pq % 64
            rs_rel = np.where(qrow == 0, a, b)
            cs = np.clip(qcol - 8, 0, 64 - 16)
            okr = (krow[:, None] >= rs_rel[None, :]) & (krow[:, None] < rs_rel[None, :] + 8)
            okc = (kcol[:, None] >= cs[None, :]) & (kcol[:, None] < cs[None, :] + 16)
            dr = np.clip(krow[:, None] - qrow[None, :] + 7, 0, 14)
            dc = np.clip(kcol[:, None] - qcol[None, :] + 15, 0, 30)
            ok = okr & okc
            for h in range(4):
                g = rpb[h][dr, dc]
                t = tab[:, ti, h, j, :]
                t[ok] = g[ok]
    return tab


def da_consts():
    p = np.arange(128, dtype=np.float64)
    colL = np.zeros((128, 4, 32), np.float32)
    colR = np.zeros((128, 4, 32), np.float32)
    fLR = np.zeros((128, 4, 8), np.float32)
    biasD = np.zeros((128, 4, 4, 512), np.float32)
    q = np.arange(512, dtype=np.float64)
    for s_, sl in enumerate(SLOPES):
        for m in range(32):
            colL[:, s_, m] = -sl * (128 * m - p)
            colR[:, s_, m] = -sl * (128 * m + p - 511)
        for sub in range(4):
            fLR[:, s_, sub] = -sl * (128 * sub + p)
            fLR[:, s_, 4 + sub] = -sl * (511 - 128 * sub - p)
        for j in range(4):
            biasD[:, s_, j, :] = -sl * np.abs(q[None, :] - (128 * j + p[:, None]))
    return {"da_colL": colL, "da_colR": colR, "da_fLR": fLR, "da_biasD": biasD}


SMALL_SHAPES = {"na_tab": [DEPTH, 128, 5 * 4 * 5 * 128], "da_colL": [128, 4, 32], "da_colR": [128, 4, 32],
                "da_fLR": [128, 4, 8], "da_biasD": [128, 4, 4, 512], "da_lam": [DEPTH, 128, 128],
                "da_g": [DEPTH, 128, 256]}


def host_small(inp):
    f = lambda a: np.asarray(a, np.float32)
    out = {}
    out["na_tab"] = np.stack([na_tables(f(inp["na_rpb"][l])).reshape(128, -1) for l in range(DEPTH)])
    out.update(da_consts())
    out["da_lam"] = np.stack([np.broadcast_to(f(inp["diff_lam"][l]).reshape(1, 128), (128, 128)) for l in range(DEPTH)]).copy()
    out["da_g"] = np.stack([np.broadcast_to(np.tile(f(inp["diff_subln_g"][l]), 4).reshape(1, 256), (128, 256)) for l in range(DEPTH)]).copy()
    return out


def make_inputs(inp, seq_groups, ncores):
    shared = {}
    shared.update(host_weights(inp))
    shared.update({"c_" + k: v for k, v in host_consts().items()})
    shared.update(host_small(inp))
    in_maps = []
    for c in range(ncores):
        parts = []
        for (arr, n) in seq_groups:
            for b in range(n):
                parts.append(np.asarray(arr[c * n + b], np.float32))
        m = dict(shared)
        m["xin"] = np.ascontiguousarray(np.concatenate(parts, axis=0))
        in_maps.append(m)
    return in_maps


def run(inp, seq_groups, ncores, debug=None, en=None):
    seqs = []
    for (arr, n) in seq_groups:
        seqs += [arr.shape[1]] * n
    B = Builder(seqs, debug=debug)
    if en is not None:
        B.debug_en = en
    nc = B.build()
    in_maps = make_inputs(inp, seq_groups, ncores)
    res = run_bass_kernel_spmd(nc, in_maps, core_ids=list(range(ncores)))
    outs = []
    for (arr, n) in seq_groups:
        outs.append(np.zeros(arr.shape, np.float32))
    for c in range(ncores):
        y = res.results[c]["yout"]
        t = 0
        for gi, (arr, n) in enumerate(seq_groups):
            L = arr.shape[1]
            for b in range(n):
                outs[gi][c * n + b] = y[t:t + L]
                t += L
    return outs, res, B


def kernel(**inputs):
    xp = np.asarray(inputs["x_prompt"], np.float32)
    xs = np.asarray(inputs["x_sample"], np.float32)
    outs, _, _ = run(inputs, [(xp, xp.shape[0] // NCORES), (xs, xs.shape[0] // NCORES)], NCORES)
    return (outs[0], outs[1])


def mix_na(self, l):
    P = self.P
    S = Scope(self.nc)
    Lmax = max(self.seqs)
    qT = S.sb([128, 2, Lmax], BF16, "naq")
    kT = S.sb([128, 2, Lmax], BF16, "nak")
    V1 = S.sb([128, Lmax // 128, 260], BF16, "nav")
    yst = S.sb([128, 2, Lmax], BF16, "nay")
    tab = S.sb([128, 5, 4, 5 * 128], F32, "natab")
    sbr = Ring([S.sb([128, 640], F32, "nasb") for _ in range(2)])
    ptr = Ring([S.sb([128, 640], BF16, "napt") for _ in range(2)])
    yr = Ring([S.sb([128, 256], F32, "nayt") for _ in range(2)])
    rcr = Ring([S.sb([128, 4], F32, "narc") for _ in range(2)])
    pss = Ring([S.ps([128, 1024], F32, "naps") for _ in range(2)])
    pso = Ring([S.ps([128, 512], F32, "napo") for _ in range(2)])
    pst = Ring([S.ps([128, 512], F32, "napt") for _ in range(2)])
    P.dma("sp", tab.t[:].rearrange("p a b c -> p (a b c)"), self.small["na_tab"][l], writes=[tab])
    for si, L in enumerate(self.seqs):
        t0 = self.starts[si]
        rows = L // 64
        P.dma("sp", qT.t[:, :, 0:L], self.zNq[:, :, t0:t0 + L].rearrange("c p t -> p c t"), writes=[qT])
        P.dma("sp", kT.t[:, :, 0:L], self.zNk[:, :, t0:t0 + L].rearrange("c p t -> p c t"), writes=[kT])
        P.dma("sp", V1.t[:, 0:L // 128, :], self.zV[t0:t0 + L, 0:260].rearrange("(n p) x -> p n x", p=128), writes=[V1])
        for qi in range(L // 128):
            r = 2 * qi
            ty, kr0, nk = na_tile_info(r, rows)
            po = pso.next()
            for hd in range(4):
                cc, base = hd // 2, (hd % 2) * 64
                ps = pss.next()
                for j in range(nk):
                    kn = kr0 // 2 + j
                    P.mm(ps, ps.t[:, j * 128:(j + 1) * 128], kT.t[base:base + 64, cc, kn * 128:(kn + 1) * 128],
                         qT.t[base:base + 64, cc, qi * 128:(qi + 1) * 128], [kT, qT])
                sb = sbr.next()
                P.tt("dve", sb.t[:, 0:nk * 128], ps.t[:, 0:nk * 128], tab.t[:, ty, hd, 0:nk * 128], ALU.add, [ps, tab], [sb])
                pt = ptr.next()
                P.act(pt.t[:, 0:nk * 128], sb.t[:, 0:nk * 128], AF.Exp, [sb], [pt])
                for j in range(nk):
                    kn = kr0 // 2 + j
                    P.mm(po, po.t[:, hd * 65:(hd + 1) * 65], pt.t[:, j * 128:(j + 1) * 128], V1.t[:, kn, hd * 65:(hd + 1) * 65],
                         [pt, V1], start=(j == 0), stop=(j == nk - 1))
            rc = rcr.next()
            pov = po.t[:, 0:260].rearrange("p (h d) -> p h d", h=4)
            P.op("dve", lambda g, rc=rc, pov=pov: g.reciprocal(out=rc.t[:, :], in_=pov[:, :, 64]), [po], [rc])
            y = yr.next()
            for hd in range(4):
                P.ts("dve", y.t[:, hd * 64:(hd + 1) * 64], po.t[:, hd * 65:hd * 65 + 64], rc.t[:, hd:hd + 1], None, ALU.mult, None,
                     [po, rc], [y])
            pt_ = pst.next()
            for cc in range(2):
                P.tr(pt_, pt_.t[:, cc * 128:(cc + 1) * 128], y.t[:, cc * 128:(cc + 1) * 128], self.ident.t[:], [y, self.ident])
            P.cp("act", yst.t[:, :, qi * 128:(qi + 1) * 128], pt_.t[:, 0:256].rearrange("p (c t) -> p c t", c=2), [pt_], [yst])
        P.dma("pool", self.yM[4:6, :, t0:t0 + L].rearrange("c p t -> p c t"), yst.t[:, :, 0:L], reads=[yst])
    S.close()


def mix_da(self, l):
    P = self.P
    S = Scope(self.nc)
    Lmax = max(self.seqs)
    lam_init = 0.8 - 0.6 * math.exp(-0.3 * l)
    kT = S.sb([128, 2, Lmax], BF16, "dak")
    V1 = S.sb([128, Lmax // 128, 260], BF16, "dav")
    yst = S.sb([128, 2, Lmax], BF16, "day")
    qmr = Ring([S.sb([128, 8, 512], BF16, "daq") for _ in range(2)])
    colL = S.sb([128, 4, 32], F32, "colL")
    colR = S.sb([128, 4, 32], F32, "colR")
    fLR = S.sb([128, 4, 8], F32, "fLR")
    bD32 = S.sb([128, 4 * 4 * 512], F32, "bD32")
    bD = S.sb([128, 4, 4, 512], BF16, "bD")
    lamt = S.sb([128, 4, 32], F32, "lamt")
    lamw = S.sb([128, 2, 32], F32, "lamw")
    lams = S.sb([128, 4], F32, "lams")
    gt = S.sb([128, 4, 64], F32, "dag")
    att = S.sb([128, 4, 8, 64], F32, "att")
    ptr = Ring([S.sb([128, 512], BF16, "dapt") for _ in range(3)])
    totr = Ring([S.sb([128, 65], F32, "datot") for _ in range(3)])
    rcr = Ring([S.sb([128, 4], F32, "darc") for _ in range(3)])
    ar = Ring([S.sb([128, 4, 64], F32, "daa") for _ in range(2)])
    sqr = Ring([S.sb([128, 4, 64], F32, "dasq") for _ in range(2)])
    yr = Ring([S.sb([128, 256], F32, "dayt") for _ in range(2)])
    pss = Ring([S.ps([128, 512], F32, "daps") for _ in range(3)])
    pso = Ring([S.ps([128, 512], F32, "dapo") for _ in range(4)])
    pst = Ring([S.ps([128, 512], F32, "dapt") for _ in range(1)])
    sm = self.small
    P.dma("sp", colL.t[:], sm["da_colL"], writes=[colL])
    P.dma("sp", colR.t[:], sm["da_colR"], writes=[colR])
    P.dma("sp", fLR.t[:], sm["da_fLR"], writes=[fLR])
    P.dma("sp", bD32.t[:], sm["da_biasD"].rearrange("p a b c -> p (a b c)"), writes=[bD32])
    P.dma("sp", lamt.t[:].rearrange("p a b -> p (a b)"), sm["da_lam"][l], writes=[lamt])
    P.dma("sp", gt.t[:].rearrange("p a b -> p (a b)"), sm["da_g"][l], writes=[gt])
    P.cp("dve", bD.t[:].rearrange("p a b c -> p (a b c)"), bD32.t[:], [bD32], [bD])
    P.act(fLR.t[:], fLR.t[:], AF.Exp, [fLR], [fLR])
    P.tt("dve", lamw.t[:, 0, :], lamt.t[:, 0, :], lamt.t[:, 1, :], ALU.mult, [lamt], [lamw])
    P.tt("dve", lamw.t[:, 1, :], lamt.t[:, 2, :], lamt.t[:, 3, :], ALU.mult, [lamt], [lamw])
    P.op("dve", lambda g: g.tensor_reduce(out=lams.t[:, 0:2], in_=lamw.t[:], axis=AX.X, op=ALU.add), [lamw], [lams])
    P.act(lams.t[:, 0:2], lams.t[:, 0:2], AF.Exp, [lams], [lams])
    P.tt("dve", lams.t[:, 2:3], lams.t[:, 1:2], lams.t[:, 0:1], ALU.subtract, [lams], [lams])
    P.ts("dve", lams.t[:, 3:4], lams.t[:, 2:3], -lam_init, None, ALU.add, None, [lams], [lams])
    zero_col = self.epsr.t[:, 3:4]
    for si, L in enumerate(self.seqs):
        t0 = self.starts[si]
        NK = L // 128
        P.dma("sp", kT.t[:, :, 0:L], self.zDk[:, :, t0:t0 + L].rearrange("c p t -> p c t"), writes=[kT])
        P.dma("sp", V1.t[:, 0:NK, :], self.zV[t0:t0 + L, 260:520].rearrange("(n p) x -> p n x", p=128), writes=[V1])
        for qb in range(L // 512):
            q0 = qb * 512
            qm = qmr.next()
            P.dma("sp", qm.t[:], self.zDq[:, :, t0 + q0:t0 + q0 + 512].rearrange("g p t -> p g t"), writes=[qm])
            for g_ in range(8):
                s_ = g_ // 2
                hd = g_ // 2
                poA, poB = pso.next(), pso.next()
                cls_of = {}
                for kt in range(NK):
                    k0 = kt * 128
                    c_ = 0 if k0 + 128 <= q0 else (2 if k0 >= q0 + 512 else 1)
                    mind = (q0 - (k0 + 127)) if c_ == 0 else ((k0 - (q0 + 511)) if c_ == 2 else 0)
                    if SLOPES[s_] * mind >= 80.0:
                        continue
                    cls_of[kt] = c_
                kts = sorted(cls_of)
                first = {}
                lastk = {}
                for kt in kts:
                    first.setdefault(cls_of[kt], kt)
                    lastk[cls_of[kt]] = kt
                for kt in kts:
                    k0 = kt * 128
                    cls = cls_of[kt]
                    ps = pss.next()
                    P.mm(ps, ps.t[:, 0:512], kT.t[:, g_ // 4, k0:k0 + 128], qm.t[:, g_, :], [kT, qm], start=True, stop=(cls != 1))
                    if cls == 1:
                        P.mm(ps, ps.t[:, 0:512], self.identb.t[:], bD.t[:, s_, (k0 - q0) // 128, :], [self.identb, bD], start=False, stop=True)
                        bias = zero_col
                        rd = [ps, self.epsr]
                    elif cls == 0:
                        bias = colL.t[:, s_, (q0 - k0) // 128:(q0 - k0) // 128 + 1]
                        rd = [ps, colL]
                    else:
                        bias = colR.t[:, s_, (k0 - q0) // 128:(k0 - q0) // 128 + 1]
                        rd = [ps, colR]
                    pt = ptr.next()
                    P.act(pt.t[:], ps.t[:, 0:512], AF.Exp, rd, [pt], bias=bias)
                    for sub in range(4):
                        po = poA if sub < 2 else poB
                        off = ((sub % 2) * 3 + cls) * 65
                        P.mm(po, po.t[:, off:off + 65], pt.t[:, sub * 128:(sub + 1) * 128], V1.t[:, kt, hd * 65:(hd + 1) * 65],
                             [pt, V1], start=(kt == kts[0] and sub % 2 == 0), stop=(kt == lastk[cls]), sgc=True)
                for sub in range(4):
                    po = poA if sub < 2 else poB
                    o0 = (sub % 2) * 3 * 65
                    tot = totr.next()
                    P.cp("act", tot.t[:], po.t[:, o0 + 65:o0 + 130], [po], [tot])
                    if 0 in first:
                        P.stt("dve", tot.t[:], po.t[:, o0:o0 + 65], fLR.t[:, s_, sub:sub + 1], tot.t[:], ALU.mult, ALU.add, [po, fLR, tot], [tot])
                    if 2 in first:
                        P.stt("dve", tot.t[:], po.t[:, o0 + 130:o0 + 195], fLR.t[:, s_, 4 + sub:5 + sub], tot.t[:], ALU.mult, ALU.add,
                              [po, fLR, tot], [tot])
                    rc = rcr.next()
                    P.op("dve", lambda g, rc=rc, tot=tot: g.reciprocal(out=rc.t[:, 0:1], in_=tot.t[:, 64:65]), [tot], [rc])
                    P.ts("dve", att.t[:, sub, g_, :], tot.t[:, 0:64], rc.t[:, 0:1], None, ALU.mult, None, [tot, rc], [att])
            if "att" in self.dbg and l == 0 and si == len(self.seqs) - 1 and qb == 0:
                P.dma("sp", self.dbg["att"], att.t[:].rearrange("p a b c -> p (a b c)"), reads=[att])
                P.dma("sp", self.dbg["lams"], lams.t[:], reads=[lams])
            for sub in range(4):
                a = ar.next()
                av = att.t[:, sub, :, :].rearrange("p (h two) d -> p h two d", two=2)
                P.stt("dve", a.t[:], av[:, :, 1, :], lams.t[:, 3:4], av[:, :, 0, :], ALU.mult, ALU.add, [att, lams], [a])
                sq = sqr.next()
                P.tt("pool", sq.t[:], a.t[:], a.t[:], ALU.mult, [a], [sq])
                rc = rcr.next()
                P.op("dve", lambda g, rc=rc, sq=sq: g.tensor_reduce(out=rc.t[:, 0:4], in_=sq.t[:], axis=AX.X, op=ALU.add), [sq], [rc])
                P.act(rc.t[:, 0:4], rc.t[:, 0:4], AF.Sqrt, [rc, self.epsr], [rc], bias=self.epsr.t[:, 1:2], scale=1.0 / 64)
                P.op("dve", lambda g, rc=rc: g.reciprocal(out=rc.t[:, 0:4], in_=rc.t[:, 0:4]), [rc], [rc])
                y = yr.next()
                yv = y.t[:].rearrange("p (h d) -> p h d", h=4)
                for hd in range(4):
                    P.ts("dve", yv[:, hd, :], a.t[:, hd, :], rc.t[:, hd:hd + 1], 1.0 - lam_init, ALU.mult, ALU.mult, [a, rc], [y])
                P.tt("pool", yv, yv, gt.t[:], ALU.mult, [y, gt], [y])
                pt_ = pst.next()
                for cc in range(2):
                    P.tr(pt_, pt_.t[:, cc * 128:(cc + 1) * 128], y.t[:, cc * 128:(cc + 1) * 128], self.ident.t[:], [y, self.ident])
                tq = q0 + sub * 128
                P.cp("act", yst.t[:, :, tq:tq + 128], pt_.t[:, 0:256].rearrange("p (c t) -> p c t", c=2), [pt_], [yst])
        P.dma("pool", self.yM[6:8, :, t0:t0 + L].rearrange("c p t -> p c t"), yst.t[:, :, 0:L], reads=[yst])
    S.close()


Builder.mix_na = mix_na
Builder.mix_da = mix_da


CDEC = math.exp(-0.5)


def rw_host(inp):
    f = lambda a: np.asarray(a, np.float32)
    out = {}
    mu_l, mulw_l, mulg_l, hp_l = [], [], [], []
    for l in range(DEPTH):
        mu = f(inp["rwkv_mu"][l])
        a = np.zeros((64, 8, 3, 2), np.float32)
        for h in range(8):
            for q in range(3):
                for m in range(2):
                    a[:, h, q, m] = mu[m, q * 512 + h * 64:q * 512 + h * 64 + 64]
        mu_l.append(a.reshape(64, 48))
        mulw_l.append(np.stack([mu[0, 1536:1600], mu[1, 1536:1600], mu[0, 1600:1664], mu[1, 1600:1664]], axis=1))
        mulg_l.append(np.stack([mu[0, 1664:1792], mu[1, 1664:1792]], axis=1))
        hp = np.zeros((64, 8, 11), np.float32)
        for h in range(8):
            sl = slice(h * 64, h * 64 + 64)
            hp[:, h, 0] = f(inp["rwkv_k_k"][l])[sl]
            hp[:, h, 1] = f(inp["rwkv_lnx_g"][l])[sl]
            hp[:, h, 2] = f(inp["rwkv_lnx_b"][l])[sl]
            for d in range(2):
                hp[:, h, 3 + 4 * d + 0] = f(inp["rwkv_w0"][l, d])[sl]
                hp[:, h, 3 + 4 * d + 1] = f(inp["rwkv_a0"][l, d])[sl]
                hp[:, h, 3 + 4 * d + 2] = f(inp["rwkv_k_a"][l, d])[sl]
                hp[:, h, 3 + 4 * d + 3] = f(inp["rwkv_r_k"][l, d]).reshape(-1)[sl]
        hp_l.append(hp.reshape(64, 88))
    out["rw_mu"] = np.stack(mu_l); out["rw_mulw"] = np.stack(mulw_l); out["rw_mulg"] = np.stack(mulg_l)
    out["rw_hp"] = np.stack(hp_l)
    out["rw_w2"] = np.ascontiguousarray(f(inp["rwkv_w2"])); out["rw_a2"] = np.ascontiguousarray(f(inp["rwkv_a2"]))
    out["rw_g2"] = np.ascontiguousarray(f(inp["rwkv_g2"]))
    i = np.arange(CK)
    row, col = i[:, None], i[None, :]
    mk = np.zeros((2, 3, CK, 512), np.float32)
    for d in range(2):
        st = (row < col) if d == 0 else (row > col)
        inc = (row <= col) if d == 0 else (row >= col)
        stT = (col < row) if d == 0 else (col > row)
        for mi, m_ in enumerate((st, inc, stT)):
            mk[d, mi] = np.tile(m_.astype(np.float32), (1, 512 // CK))
    out["rw_mask"] = mk
    out["rw_irep"] = np.tile(np.eye(CK, dtype=np.float32), (1, 512 // CK))
    rs = np.ones((64, 1024), np.float32)
    rs[:, ::CK] = 0.0
    out["rw_reset"] = rs
    return out


SMALL_SHAPES.update({"rw_mu": [DEPTH, 64, 48], "rw_mulw": [DEPTH, 64, 4], "rw_mulg": [DEPTH, 128, 2], "rw_hp": [DEPTH, 64, 88],
                     "rw_w2": [DEPTH, 2, 64, 512], "rw_a2": [DEPTH, 2, 64, 512], "rw_g2": [DEPTH, 128, 512],
                     "rw_mask": [2, 3, CK, 512], "rw_irep": [CK, 512], "rw_reset": [64, 1024]})
_host_small0 = host_small


def host_small(inp):
    o = _host_small0(inp)
    o.update(rw_host(inp))
    return o


def mix_rwkv(self, l):
    P = self.P
    nc = self.nc
    S = Scope(nc)
    sm = self.small
    T = self.T
    if not hasattr(self, "yF"):
        self.yF = nc.dram_tensor("yF", [2, 8, 64, T], F32).ap()
    yfb = Buf()
    SEG = min(512, min(self.seqs))
    W = SEG
    sb = lambda shape, name: S.sb(shape, F32, name)
    mu = sb([64, 48], "mu"); c0 = sb([64, 24], "c0"); mulw = sb([64, 4], "mulw"); c0w = sb([64, 2], "c0w")
    mulg = sb([128, 2], "mulg"); c0g = sb([128, 1], "c0g"); hp = sb([64, 88], "hp"); omk = sb([64, 16], "omk")
    w2 = sb([64, 2, 512], "w2"); a2 = sb([64, 2, 512], "a2"); g2 = sb([128, 512], "g2")
    mk = sb([CK, 2, 3, 512], "mk"); irep = sb([CK, 512], "irep"); rst = sb([64, 1024], "rst"); ones = sb([64, 64], "ones"); onesm = sb([64, 64], "onesm")
    P.dma("sp", mu.t[:], sm["rw_mu"][l], writes=[mu]); P.dma("sp", mulw.t[:], sm["rw_mulw"][l], writes=[mulw])
    P.dma("sp", mulg.t[:], sm["rw_mulg"][l], writes=[mulg]); P.dma("sp", hp.t[:], sm["rw_hp"][l], writes=[hp])
    P.dma("sp", w2.t[:], sm["rw_w2"][l].rearrange("d k n -> k d n"), writes=[w2])
    P.dma("sp", a2.t[:], sm["rw_a2"][l].rearrange("d k n -> k d n"), writes=[a2])
    P.dma("sp", g2.t[:], sm["rw_g2"][l], writes=[g2])
    P.dma("sp", mk.t[:], sm["rw_mask"].rearrange("d m p n -> p d m n"), writes=[mk])
    P.dma("sp", irep.t[:], sm["rw_irep"], writes=[irep])
    P.dma("sp", rst.t[:], sm["rw_reset"], writes=[rst])
    P.memset("dve", ones, ones.t[:], 1.0); P.memset("dve", onesm, onesm.t[:], 1.0 / 64)
    muv = mu.t[:].rearrange("p (a m) -> p a m", m=2)
    P.tt("dve", c0.t[:], muv[:, :, 0], muv[:, :, 1], ALU.add, [mu], [c0])
    P.ts("dve", c0.t[:], c0.t[:], -1.0, 1.0, ALU.mult, ALU.add, [c0], [c0])
    mwv = mulw.t[:].rearrange("p (a m) -> p a m", m=2)
    P.tt("dve", c0w.t[:], mwv[:, :, 0], mwv[:, :, 1], ALU.add, [mulw], [c0w])
    P.ts("dve", c0w.t[:], c0w.t[:], -1.0, 1.0, ALU.mult, ALU.add, [c0w], [c0w])
    P.tt("dve", c0g.t[:], mulg.t[:, 0:1], mulg.t[:, 1:2], ALU.add, [mulg], [c0g])
    P.ts("dve", c0g.t[:], c0g.t[:], -1.0, 1.0, ALU.mult, ALU.add, [c0g], [c0g])
    hpv = hp.t[:].rearrange("p (h c) -> p h c", c=11)
    for h in range(8):
        for d in range(2):
            P.ts("dve", omk.t[:, h * 2 + d:h * 2 + d + 1], hpv[:, h, 5 + 4 * d:6 + 4 * d], -1.0, 1.0, ALU.mult, ALU.add, [hp], [omk])
    zcol = self.epsr.t[0:64, 3:4]
    names = ["zr", "zk", "zv", "zw", "za"]
    Z = {n: sb([64, W + 2], n) for n in names}
    zg = sb([128, W + 2], "zg"); gl = sb([128, W], "gl")
    Tl = {n: sb([64, W], n) for n in ["r", "k", "v", "wl", "al", "kk", "sg", "a", "kd", "b", "t1", "bon", "Pf", "E", "Sf", "X",
                                      "eI", "eX", "eN", "eT", "at", "bt", "kt", "rt", "bh", "kh", "Y", "yf", "bf"]}
    NCH_ = W // CK
    TM = [sb([CK, NCH_ * 64], "TM%d" % i) for i in range(4)]
    GM = [sb([CK, W], "GM%d" % i) for i in range(5)]
    TT_ = [sb([CK, W], "TT%d" % i) for i in range(2)]
    PP = [(sb([CK, W], "PPa%d" % i), sb([CK, W], "PPb%d" % i)) for i in range(2)]
    X1 = sb([CK, NCH_ * 64], "X1"); AHT = sb([64, W], "AHT"); AHN = sb([CK, NCH_ * 64], "AHN"); U0 = sb([CK, NCH_ * 64], "U0")
    MT = sb([64, NCH_ * 64], "MT"); CC = sb([64, NCH_ * 64], "CC")
    tmr = Ring([sb([64, 256], "tm") for _ in range(2)])
    gmr = Ring([sb([64, 320], "gm") for _ in range(2)])
    p2r = Ring([sb([64, 128], "p2") for _ in range(3)])
    ttr = Ring([sb([64, 64], "tt") for _ in range(3)])
    x1r = Ring([sb([64, 64], "x1") for _ in range(2)])
    u0r = Ring([sb([64, 64], "u0") for _ in range(2)])
    ahr = Ring([sb([64, 64], "ah") for _ in range(2)])
    dgr = Ring([sb([64, 64], "dg") for _ in range(2)])
    ur = Ring([sb([CK, 64], "u") for _ in range(2)])
    str_ = Ring([sb([64, 64], "st") for _ in range(3)])
    yo = S.sb([64, W], BF16, "yo")
    pss = Ring([S.ps([128, 512], F32, "rps") for _ in range(6)])
    psU_ = S.ps([128, 512], F32, "rpsU")
    psY_ = S.ps([128, 512], F32, "rpsY")
    idn = self.ident.t[0:64, 0:64]

    def shift(dst, src, c0c, m0c, m1c, np_=64):
        P.ts("dve", dst.t[:, :], src.t[:, 1:W + 1], c0c, None, ALU.mult, None, [src], [dst])
        P.stt("dve", dst.t[:, :], src.t[:, 0:W], m0c, dst.t[:, :], ALU.mult, ALU.add, [src, dst], [dst])
        P.stt("dve", dst.t[:, :], src.t[:, 2:W + 2], m1c, dst.t[:, :], ALU.mult, ALU.add, [src, dst], [dst])

    for si, L in enumerate(self.seqs):
        t0 = self.starts[si]
        nseg = L // SEG
        for h in range(8):
            cc, pb = h // 2, (h % 2) * 64
            for d in range(2):
                st = str_.next()
                P.memset("dve", st, st.t[:], 0.0)
                for sgi in (range(nseg) if d == 0 else range(nseg - 1, -1, -1)):
                    s0 = t0 + sgi * SEG
                    lo = 0 if sgi > 0 else 1
                    hi = W + 2 if sgi < nseg - 1 else W + 1
                    srcs = {"zr": (cc, pb), "zk": (4 + cc, pb), "zv": (8 + cc, pb), "zw": (12, 0), "za": (12, 64)}
                    for n in names:
                        if lo == 1 or hi == W + 1:
                            P.memset("dve", Z[n], Z[n].t[:], 0.0)
                        c_, p_ = srcs[n]
                        P.dma("sp", Z[n].t[:, lo:hi], self.zA[c_, p_:p_ + 64, s0 - 1 + lo:s0 - 1 + hi], writes=[Z[n]])
                    if lo == 1 or hi == W + 1:
                        P.memset("dve", zg, zg.t[:], 0.0)
                    P.dma("sp", zg.t[:, lo:hi], self.zA[13, :, s0 - 1 + lo:s0 - 1 + hi], writes=[zg])
                    for qi, (dn, sn) in enumerate((("r", "zr"), ("k", "zk"), ("v", "zv"))):
                        ix = h * 3 + qi
                        shift(Tl[dn], Z[sn], c0.t[:, ix:ix + 1], mu.t[:, 2 * ix:2 * ix + 1], mu.t[:, 2 * ix + 1:2 * ix + 2])
                    shift(Tl["wl"], Z["zw"], c0w.t[:, 0:1], mulw.t[:, 0:1], mulw.t[:, 1:2])
                    shift(Tl["al"], Z["za"], c0w.t[:, 1:2], mulw.t[:, 2:3], mulw.t[:, 3:4])
                    P.ts("dve", gl.t[:, :], zg.t[:, 1:W + 1], c0g.t[:, 0:1], None, ALU.mult, None, [zg], [gl])
                    P.stt("dve", gl.t[:, :], zg.t[:, 0:W], mulg.t[:, 0:1], gl.t[:, :], ALU.mult, ALU.add, [zg, gl], [gl])
                    P.stt("dve", gl.t[:, :], zg.t[:, 2:W + 2], mulg.t[:, 1:2], gl.t[:, :], ALU.mult, ALU.add, [zg, gl], [gl])
                    r, k, v, kk, sg, a, kd, b, t1 = (Tl[n] for n in ("r", "k", "v", "kk", "sg", "a", "kd", "b", "t1"))
                    P.ts("dve", kk.t[:], k.t[:], hpv[:, h, 0:1], None, ALU.mult, None, [k, hp], [kk])
                    P.tt("pool", t1.t[:], kk.t[:], kk.t[:], ALU.mult, [kk], [t1])
                    for blk in range(W // 512):
                        bs = slice(blk * 512, blk * 512 + 512)
                        ps = pss.next()
                        P.mm(ps, ps.t[0:64, 0:512], ones.t[:], t1.t[:, bs], [ones, t1])
                        P.act(Tl["X"].t[:, bs], ps.t[0:64, 0:512], AF.Sqrt, [ps, self.epsr], [Tl["X"]], bias=zcol)
                    P.ts("dve", Tl["X"].t[:], Tl["X"].t[:], 1e-12, None, ALU.max, None, [Tl["X"]], [Tl["X"]])
                    P.op("dve", lambda g: g.reciprocal(out=Tl["X"].t[:], in_=Tl["X"].t[:]), [Tl["X"]], [Tl["X"]])
                    P.tt("dve", kk.t[:], kk.t[:], Tl["X"].t[:], ALU.mult, [kk, Tl["X"]], [kk])
                    P.act(Tl["wl"].t[:], Tl["wl"].t[:], AF.Tanh, [Tl["wl"]], [Tl["wl"]])
                    for blk in range(W // 512):
                        bs = slice(blk * 512, blk * 512 + 512)
                        ps = pss.next()
                        P.mm(ps, ps.t[0:64, 0:512], w2.t[:, d, h * 64:h * 64 + 64], Tl["wl"].t[:, bs], [w2, Tl["wl"]])
                        P.act(sg.t[:, bs], ps.t[0:64, 0:512], AF.Sigmoid, [ps, hp], [sg], bias=hpv[:, h, 3 + 4 * d:4 + 4 * d])
                        ps = pss.next()
                        P.mm(ps, ps.t[0:64, 0:512], a2.t[:, d, h * 64:h * 64 + 64], Tl["al"].t[:, bs], [a2, Tl["al"]])
                        P.act(a.t[:, bs], ps.t[0:64, 0:512], AF.Sigmoid, [ps, hp], [a], bias=hpv[:, h, 4 + 4 * d:5 + 4 * d])
                    P.ts("dve", kd.t[:], a.t[:], hpv[:, h, 5 + 4 * d:6 + 4 * d], omk.t[:, h * 2 + d:h * 2 + d + 1], ALU.mult, ALU.add, [a, hp, omk], [kd])
                    P.tt("dve", kd.t[:], kd.t[:], k.t[:], ALU.mult, [kd, k], [kd])
                    P.tt("pool", b.t[:], kk.t[:], a.t[:], ALU.mult, [kk, a], [b])
                    P.stt("dve", t1.t[:], r.t[:], hpv[:, h, 6 + 4 * d:7 + 4 * d], kd.t[:], ALU.mult, ALU.mult, [r, hp, kd], [t1])
                    bon = Tl["bon"]
                    for blk in range(W // 512):
                        bs = slice(blk * 512, blk * 512 + 512)
                        ps = pss.next()
                        P.mm(ps, ps.t[0:64, 0:512], ones.t[:], t1.t[:, bs], [ones, t1])
                        P.tt("dve", bon.t[:, bs], ps.t[0:64, 0:512], v.t[:, bs], ALU.mult, [ps, v], [bon])
                    Pf, E, Sf, X = Tl["Pf"], Tl["E"], Tl["Sf"], Tl["X"]
                    P.op("dve", lambda g: g.tensor_tensor_scan(out=Pf.t[:], data0=rst.t[:, 0:W], data1=sg.t[:], initial=0.0,
                                                                op0=ALU.mult, op1=ALU.add), [rst, sg], [Pf])
                    P.tt("dve", E.t[:], Pf.t[:], sg.t[:], ALU.subtract, [Pf, sg], [E])
                    for j in range(W // CK):
                        P.ts("dve", Sf.t[:, CK * j:CK * j + CK], E.t[:, CK * j:CK * j + CK], -1.0, Pf.t[:, CK * j + CK - 1:CK * j + CK],
                             ALU.mult, ALU.add, [E, Pf], [Sf])
                    P.tt("pool", X.t[:], Sf.t[:], sg.t[:], ALU.subtract, [Sf, sg], [X])
                    Gi, Ge, Tm = (Pf, E, X) if d == 0 else (Sf, X, E)
                    eI, eX, eN, eT = Tl["eI"], Tl["eX"], Tl["eN"], Tl["eT"]
                    P.act(eI.t[:], Gi.t[:], AF.Exp, [Gi], [eI], scale=-CDEC)
                    P.act(eX.t[:], Ge.t[:], AF.Exp, [Ge], [eX], scale=-CDEC)
                    P.act(eN.t[:], Gi.t[:], AF.Exp, [Gi], [eN], scale=CDEC)
                    P.act(eT.t[:], Tm.t[:], AF.Exp, [Tm], [eT], scale=-CDEC)
                    at, bt, kt, rt, bh, kh = (Tl[n] for n in ("at", "bt", "kt", "rt", "bh", "kh"))
                    P.stt("dve", at.t[:], kk.t[:], -1.0, eX.t[:], ALU.mult, ALU.mult, [kk, eX], [at])
                    P.tt("pool", bt.t[:], b.t[:], eN.t[:], ALU.mult, [b, eN], [bt])
                    P.tt("dve", kt.t[:], kd.t[:], eN.t[:], ALU.mult, [kd, eN], [kt])
                    P.tt("pool", rt.t[:], r.t[:], eI.t[:], ALU.mult, [r, eI], [rt])
                    P.tt("dve", bh.t[:], b.t[:], eT.t[:], ALU.mult, [b, eT], [bh])
                    P.tt("pool", kh.t[:], kd.t[:], eT.t[:], ALU.mult, [kd, eT], [kh])
                    Y = Tl["Y"]
                    NCH = W // CK
                    order = list(range(NCH)) if d == 0 else list(range(NCH - 1, -1, -1))
                    cs_ = lambda j: slice(CK * j, CK * j + CK)
                    ks_ = lambda j: slice(64 * j, 64 * j + 64)
                    idf = self.ident.t[:]
                    tmq = []
                    for qi, src in enumerate((at, bh, kh, v)):
                        ps = pss.next()
                        for j in range(NCH):
                            P.tr(ps, ps.t[0:CK, ks_(j)], src.t[:, cs_(j)], idn, [src, self.ident])
                        tq = TM[qi]
                        P.cp("act" if qi % 2 == 0 else "dve", tq.t[:], ps.t[0:CK, 0:NCH * 64], [ps], [tq])
                        tmq.append(tq)
                    Atm, Bhtm, Khtm, Vtm = tmq
                    gq = []
                    for qi, (lt, rh, mi) in enumerate(((bt, at, 0), (bt, rt, 1), (kt, at, 0), (kt, rt, 1), (at, bt, 2))):
                        ps = pss.next()
                        for j in range(NCH):
                            P.mm(ps, ps.t[0:CK, cs_(j)], lt.t[:, cs_(j)], rh.t[:, cs_(j)], [lt, rh])
                        gt_ = GM[qi]
                        P.tt("dve", gt_.t[:], ps.t[0:CK, 0:W], mk.t[:, d, mi, 0:W], ALU.mult, [ps, mk], [gt_])
                        gq.append(gt_)
                    Aab, Abr, Aak, Akr, NT = gq
                    Tt = TT_[0]
                    P.tt("pool", Tt.t[:], Aab.t[:], irep.t[:, 0:W], ALU.add, [Aab, irep], [Tt])
                    Pm, PTm = Aab, NT
                    NLEV = 6 if CK == 128 else 5
                    for lev in range(NLEV):
                        Pn, PTn = PP[lev % 2]
                        ps2 = pss.next()
                        for j in range(NCH):
                            P.mm(ps2, ps2.t[0:CK, cs_(j)], Pm.t[:, cs_(j)], PTm.t[:, cs_(j)], [PTm, Pm])
                        if lev < NLEV - 1:
                            ps1 = pss.next()
                            for j in range(NCH):
                                P.mm(ps1, ps1.t[0:CK, cs_(j)], PTm.t[:, cs_(j)], Pm.t[:, cs_(j)], [PTm, Pm])
                            P.cp("act", Pn.t[:], ps1.t[0:CK, 0:W], [ps1], [Pn])
                        P.cp("dve", PTn.t[:], ps2.t[0:CK, 0:W], [ps2], [PTn])
                        Pm, PTm = Pn, PTn
                        ps3 = pss.next()
                        for j in range(NCH):
                            P.mm(ps3, ps3.t[0:CK, cs_(j)], PTm.t[:, cs_(j)], Tt.t[:, cs_(j)], [PTm, Tt])
                        Tn = TT_[(lev + 1) % 2]
                        P.tt("dve", Tn.t[:], ps3.t[0:CK, 0:W], Tt.t[:], ALU.add, [ps3, Tt], [Tn])
                        Tt = Tn
                    ps = pss.next()
                    for j in range(NCH):
                        P.mm(ps, ps.t[0:CK, ks_(j)], Aak.t[:, cs_(j)], Vtm.t[:, ks_(j)], [Aak, Vtm])
                    P.cp("act", X1.t[:], ps.t[0:CK, 0:NCH * 64], [ps], [X1])
                    psa = pss.next()
                    for j in range(NCH):
                        P.mm(psa, psa.t[0:64, cs_(j)], Atm.t[:, ks_(j)], Tt.t[:, cs_(j)], [Atm, Tt])
                    P.cp("dve", AHT.t[:], psa.t[0:64, 0:W], [psa], [AHT])
                    psb = pss.next()
                    for j in range(NCH):
                        P.mm(psb, psb.t[0:CK, ks_(j)], Tt.t[:, cs_(j)], Atm.t[:, ks_(j)], [Atm, Tt])
                    P.cp("act", AHN.t[:], psb.t[0:CK, 0:NCH * 64], [psb], [AHN])
                    ps = pss.next()
                    for j in range(NCH):
                        P.mm(ps, ps.t[0:CK, ks_(j)], Tt.t[:, cs_(j)], X1.t[:, ks_(j)], [Tt, X1])
                    P.cp("dve", U0.t[:], ps.t[0:CK, 0:NCH * 64], [ps], [U0])
                    ps = pss.next()
                    for j in range(NCH):
                        P.mm(ps, ps.t[0:64, ks_(j)], AHN.t[:, ks_(j)], Bhtm.t[:, ks_(j)], [AHN, Bhtm])
                    for j in range(NCH):
                        gcol = CK * j + CK - 1 if d == 0 else CK * j
                        P.stt("dve", MT.t[:, ks_(j)], idn, eI.t[:, gcol:gcol + 1], ps.t[0:64, ks_(j)], ALU.mult, ALU.add,
                              [self.ident, eI, ps], [MT])
                    ps = pss.next()
                    for j in range(NCH):
                        P.mm(ps, ps.t[0:64, ks_(j)], Bhtm.t[:, ks_(j)], U0.t[:, ks_(j)], [Bhtm, U0], start=(j == 0), stop=False, sgc=True)
                        P.mm(ps, ps.t[0:64, ks_(j)], Khtm.t[:, ks_(j)], Vtm.t[:, ks_(j)], [Khtm, Vtm], start=False, stop=(j == NCH - 1), sgc=True)
                    P.cp("act", CC.t[:], ps.t[0:64, 0:NCH * 64], [ps], [CC])
                    psU = psU_
                    psY = psY_
                    for ji, j in enumerate(order):
                        P.mm(psU, psU.t[0:CK, ks_(j)], AHT.t[:, cs_(j)], st.t[:], [AHT, st], start=(ji == 0), stop=False, sgc=True)
                        P.mm(psU, psU.t[0:CK, ks_(j)], idf[0:CK, 0:CK], U0.t[:, ks_(j)], [self.ident, U0], start=False, stop=True, sgc=True)
                        u = ur.next()
                        P.cp("act", u.t[:], psU.t[0:CK, ks_(j)], [psU], [u])
                        P.mm(psY, psY.t[0:64, cs_(j)], st.t[:], rt.t[:, cs_(j)], [st, rt], start=(ji == 0), stop=False, sgc=True)
                        P.mm(psY, psY.t[0:64, cs_(j)], u.t[:], Abr.t[:, cs_(j)], [u, Abr], start=False, stop=False, sgc=True)
                        P.mm(psY, psY.t[0:64, cs_(j)], Vtm.t[:, ks_(j)], Akr.t[:, cs_(j)], [Vtm, Akr], start=False, stop=True, sgc=True)
                        psS = pss.next()
                        P.mm(psS, psS.t[0:64, 0:64], MT.t[:, ks_(j)], st.t[:], [MT, st])
                        stn = str_.next()
                        P.tt("dve", stn.t[:], psS.t[0:64, 0:64], CC.t[:, ks_(j)], ALU.add, [psS, CC], [stn])
                        st = stn
                    P.cp("act", Y.t[:], psY.t[0:64, 0:W], [psY], [Y])
                    if d == 0:
                        P.dma("pool", self.yF[0, h, :, s0:s0 + W], Y.t[:], reads=[Y], writes=[yfb])
                        P.dma("pool", self.yF[1, h, :, s0:s0 + W], bon.t[:], reads=[bon], writes=[yfb])
                    else:
                        yf, bf = Tl["yf"], Tl["bf"]
                        P.dma("sp", yf.t[:], self.yF[0, h, :, s0:s0 + W], reads=[yfb], writes=[yf])
                        P.dma("sp", bf.t[:], self.yF[1, h, :, s0:s0 + W], reads=[yfb], writes=[bf])
                        P.tt("dve", Y.t[:], Y.t[:], yf.t[:], ALU.add, [Y, yf], [Y])
                        P.tt("pool", bon.t[:], bon.t[:], bf.t[:], ALU.add, [bon, bf], [bon])
                        P.act(gl.t[:], gl.t[:], AF.Sigmoid, [gl], [gl])
                        for blk in range(W // 512):
                            bs = slice(blk * 512, blk * 512 + 512)
                            ps = pss.next()
                            P.mm(ps, ps.t[0:64, 0:512], onesm.t[:], Y.t[:, bs], [onesm, Y])
                            P.tt("dve", Y.t[:, bs], Y.t[:, bs], ps.t[0:64, 0:512], ALU.subtract, [Y, ps], [Y])
                            P.tt("pool", t1.t[:, bs], Y.t[:, bs], Y.t[:, bs], ALU.mult, [Y], [t1])
                            ps = pss.next()
                            P.mm(ps, ps.t[0:64, 0:512], onesm.t[:], t1.t[:, bs], [onesm, t1])
                            P.act(t1.t[:, bs], ps.t[0:64, 0:512], AF.Sqrt, [ps, self.epsr], [t1], bias=self.epsr.t[0:64, 2:3])
                            P.op("dve", lambda g, bs=bs: g.reciprocal(out=t1.t[:, bs], in_=t1.t[:, bs]), [t1], [t1])
                            P.tt("dve", Y.t[:, bs], Y.t[:, bs], t1.t[:, bs], ALU.mult, [Y, t1], [Y])
                            P.ts("dve", Y.t[:, bs], Y.t[:, bs], hpv[:, h, 1:2], hpv[:, h, 2:3], ALU.mult, ALU.add, [Y, hp], [Y])
                            P.tt("pool", Y.t[:, bs], Y.t[:, bs], bon.t[:, bs], ALU.add, [Y, bon], [Y])
                            ps = pss.next()
                            P.mm(ps, ps.t[0:64, 0:512], g2.t[:, h * 64:h * 64 + 64], gl.t[:, bs], [g2, gl])
                            P.tt("dve", yo.t[:, bs], Y.t[:, bs], ps.t[0:64, 0:512], ALU.mult, [Y, ps], [yo])
                        P.dma("pool", self.yM[cc, pb:pb + 64, s0:s0 + W], yo.t[:], reads=[yo])
    S.close()


Builder.mix_rwkv = mix_rwkv
```

```python
import math
from contextlib import ExitStack
import numpy as np
import concourse.bass as bass
import concourse.mybir as mybir
from concourse.bass_utils import run_bass_kernel_spmd

F32 = mybir.dt.float32
BF16 = mybir.dt.bfloat16
AF = mybir.ActivationFunctionType
ALU = mybir.AluOpType
AX = mybir.AxisListType

D = 1024
DFF = 2816
KC = 8
FC = 22
DEPTH = 2
NCORES = 8
RMS_EPS = 1e-6
SUBLN_EPS = 1e-5
LNX_EPS = 64e-5
NWIN = 52
CK = 128


class Buf:
    __slots__ = ("w", "r", "psum")

    def __init__(self):
        self.w = None
        self.r = {}
        self.psum = False


class Tile:
    def __init__(self, t, buf=None):
        self.t = t
        self.buf = buf if buf is not None else Buf()

    def __getitem__(self, k):
        return self.t[k]


class Prog:
    ENG = ("pe", "dve", "act", "pool", "sp")
    NDS = 24

    def __init__(self, nc):
        self.nc = nc
        self.eng = {"pe": nc.tensor, "dve": nc.vector, "act": nc.scalar, "pool": nc.gpsimd, "sp": nc.sync}
        self.sem = {}
        for e in self.ENG:
            self.sem[e] = nc.semaphore("s_" + e).__enter__()
        for i in range(self.NDS):
            self.sem[("d", i)] = nc.semaphore("d%d" % i).__enter__()
            self.sem[("g", i)] = nc.semaphore("g%d" % i).__enter__()
        self.gnext = 0
        self.cnt = {k: 0 for k in self.sem}
        self.waited = {e: {} for e in self.ENG}
        self.dnext = 0
        self.ninst = 0

    def _need(self, reads, writes, e=None):
        need = {}
        for b in reads:
            b = b.buf if isinstance(b, Tile) else b
            if b.w is not None and need.get(b.w[0], 0) < b.w[1]:
                need[b.w[0]] = b.w[1]
            if b.psum:
                for k, v in b.r.items():
                    if k != e and need.get(k, 0) < v:
                        need[k] = v
        for b in writes:
            b = b.buf if isinstance(b, Tile) else b
            if b.w is not None and need.get(b.w[0], 0) < b.w[1]:
                need[b.w[0]] = b.w[1]
            for k, v in b.r.items():
                if need.get(k, 0) < v:
                    need[k] = v
        return need

    SELF_SYNC = True

    def _wait(self, e, need, skip_self=False):
        eng = self.eng[e]
        wd = self.waited[e]
        for k, v in need.items():
            if (skip_self or not self.SELF_SYNC) and k == e:
                continue
            if wd.get(k, 0) >= v:
                continue
            eng.wait_ge(self.sem[k], v)
            wd[k] = v
            self.ninst += 1

    def _mark(self, ev, reads, writes):
        for b in reads:
            b = b.buf if isinstance(b, Tile) else b
            if b.r.get(ev[0], 0) < ev[1]:
                b.r[ev[0]] = ev[1]
        for b in writes:
            b = b.buf if isinstance(b, Tile) else b
            b.w = ev
            b.r = {}

    def op(self, e, fn, reads=(), writes=(), skip_self=False):
        self._wait(e, self._need(reads, writes, e), skip_self)
        ins = fn(self.eng[e])
        ins.then_inc(self.sem[e], 1)
        self.cnt[e] += 1
        self.ninst += 1
        self._mark((e, self.cnt[e]), reads, writes)

    def dma(self, q, out, in_, reads=(), writes=()):
        self._wait(q, self._need(reads, writes))
        if q == "pool":
            k = ("g", self.gnext)
            self.gnext = (self.gnext + 1) % self.NDS
        else:
            k = ("d", self.dnext)
            self.dnext = (self.dnext + 1) % self.NDS
        self.eng[q].dma_start(out=out, in_=in_).then_inc(self.sem[k], 16)
        self.cnt[k] += 16
        self.ninst += 1
        self._mark((k, self.cnt[k]), reads, writes)

    def barrier(self):
        for e in self.ENG:
            self._wait(e, dict(self.cnt))

    def mm(self, out_t, out_ap, lhsT_ap, rhs_ap, reads, start=True, stop=True, sgc=False):
        if sgc:
            self.op("pe", lambda g: g.matmul(out_ap, lhsT=lhsT_ap, rhs=rhs_ap, start=start, stop=stop, skip_group_check=True),
                    reads=reads, writes=[out_t], skip_self=True)
        else:
            self.op("pe", lambda g: g.matmul(out_ap, lhsT=lhsT_ap, rhs=rhs_ap, start=start, stop=stop),
                    reads=reads, writes=[out_t], skip_self=True)

    def tr(self, out_t, out_ap, in_ap, ident_ap, reads):
        self.op("pe", lambda g: g.transpose(out_ap, in_ap, ident_ap), reads=reads, writes=[out_t], skip_self=True)

    def act(self, out_ap, in_ap, func, reads, writes, bias=None, scale=1.0, accum=None):
        kw = {}
        if bias is not None:
            kw["bias"] = bias
        if accum is not None:
            kw["accum_out"] = accum
        self.op("act", lambda g: g.activation(out=out_ap, in_=in_ap, func=func, scale=scale, **kw),
                reads=reads, writes=writes)

    def tt(self, e, out_ap, a_ap, b_ap, op, reads, writes):
        self.op(e, lambda g: g.tensor_tensor(out=out_ap, in0=a_ap, in1=b_ap, op=op), reads=reads, writes=writes)

    def stt(self, e, out_ap, a_ap, scalar, b_ap, op0, op1, reads, writes):
        self.op(e, lambda g: g.scalar_tensor_tensor(out=out_ap, in0=a_ap, scalar=scalar, in1=b_ap, op0=op0, op1=op1),
                reads=reads, writes=writes)

    def ts(self, e, out_ap, a_ap, s1, s2, op0, op1, reads, writes):
        if s2 is None:
            self.op(e, lambda g: g.tensor_scalar(out=out_ap, in0=a_ap, scalar1=s1, scalar2=None, op0=op0),
                    reads=reads, writes=writes)
        else:
            self.op(e, lambda g: g.tensor_scalar(out=out_ap, in0=a_ap, scalar1=s1, scalar2=s2, op0=op0, op1=op1),
                    reads=reads, writes=writes)

    def cp(self, e, out_ap, in_ap, reads, writes):
        if e == "act":
            self.op(e, lambda g: g.copy(out=out_ap, in_=in_ap), reads=reads, writes=writes)
        else:
            self.op(e, lambda g: g.tensor_copy(out=out_ap, in_=in_ap), reads=reads, writes=writes)

    def memset(self, e, t, ap, val):
        self.op(e, lambda g: g.memset(ap, val), reads=(), writes=[t])


class Pool_:
    def __init__(self, nc):
        self.nc = nc
        self.st = ExitStack()
        self.n = 0

    def sb(self, shape, dt, name=None):
        self.n += 1
        return Tile(self.st.enter_context(self.nc.sbuf_tensor("%s_%d" % (name or "t", id(self) % 100000 * 1000 + self.n), list(shape), dt)))

    def ps(self, shape, dt=F32, name=None):
        self.n += 1
        return Tile(self.st.enter_context(self.nc.psum_tensor("%s_%d" % (name or "p", id(self) % 100000 * 1000 + self.n), list(shape), dt)))

    def close(self):
        self.st.close()


class Ring:
    def __init__(self, tiles):
        self.tiles = tiles
        self.i = 0

    def next(self):
        t = self.tiles[self.i % len(self.tiles)]
        self.i += 1
        return t


def fm_pieces(W):
    K, N = W.shape
    return np.ascontiguousarray(W.reshape(K // 128, 128, N // 128, 128).transpose(2, 1, 0, 3)).reshape(N // 128, 128, K)


def pcol(v, nchunk):
    return np.ascontiguousarray(np.asarray(v, np.float32).reshape(nchunk, 128).T)


def host_weights(inp):
    f = lambda a: np.asarray(a, np.float32)
    out = {}
    gu1, d1, gu2, d2, win, wv, pabc, wout = [], [], [], [], [], [], [], []
    for l in range(DEPTH):
        for (gl, dl, pre) in ((gu1, d1, "ffn1"), (gu2, d2, "ffn2")):
            g = fm_pieces(f(inp[pre + "_w_gate"][l]))
            u = fm_pieces(f(inp[pre + "_w_up"][l]))
            gl.append(np.stack([g, u], axis=1).reshape(2 * FC, 128, D))
            dl.append(fm_pieces(f(inp[pre + "_w_down"][l])))
        W = f(inp["w_in"][l])
        cols = [W[:, 0:1792], W[:, 1792:2304]]
        for g in range(8):
            blk = np.zeros((D, 128), np.float32)
            blk[:, (g % 4) * 32:(g % 4) * 32 + 32] = W[:, 2560 + g * 32:2560 + g * 32 + 32]
            cols.append(blk)
        cols += [W[:, 2816:3072], W[:, 3328:6400]]
        win.append(fm_pieces(np.concatenate(cols, axis=1)))
        Wv = np.concatenate([W[:, 2304:2560], W[:, 3072:3328]], axis=1)
        wv.append(np.ascontiguousarray(Wv.reshape(KC, 128, 512).transpose(1, 0, 2)).reshape(128, KC * 512))
        pabc.append(fm_pieces(np.concatenate([f(inp["p_a"][l]), f(inp["p_b"][l]), f(inp["p_c"][l])], axis=0)))
        wout.append(fm_pieces(f(inp["w_out"][l])))
    out["wgu1"] = np.stack(gu1); out["wd1"] = np.stack(d1)
    out["wgu2"] = np.stack(gu2); out["wd2"] = np.stack(d2)
    out["win"] = np.stack(win); out["wv"] = np.stack(wv)
    out["wpabc"] = np.stack(pabc); out["wout"] = np.stack(wout)
    gains = []
    for l in range(DEPTH):
        gains += [pcol(inp["ln_ffn1_g"][l], KC), pcol(inp["ln_mix_g"][l], KC), pcol(inp["ln_ffn2_g"][l], KC)]
    gains.append(pcol(inp["final_g"], KC))
    out["gains"] = np.concatenate(gains, axis=1)
    return out


WSHAPES = {"wgu1": [DEPTH, 2 * FC, 128, D], "wd1": [DEPTH, KC, 128, DFF], "wgu2": [DEPTH, 2 * FC, 128, D],
           "wd2": [DEPTH, KC, 128, DFF], "win": [DEPTH, NWIN, 128, D], "wv": [DEPTH, 128, KC * 512],
           "wpabc": [DEPTH, KC, 128, D], "wout": [DEPTH, KC, 128, D]}


def host_consts():
    c = {}
    c["ident"] = np.eye(128, dtype=np.float32)
    c["onesm"] = np.full((128, 128), 1.0 / D, np.float32)
    return c


CSHAPES = {"ident": [128, 128], "onesm": [128, 128]}


_UID = [0]


def _uid(prefix):
    _UID[0] += 1
    return "%s%d" % (prefix, _UID[0])


class Scope:
    def __init__(self, nc):
        self.nc = nc
        self.st = ExitStack()

    def sb(self, shape, dt, name="t"):
        return Tile(self.st.enter_context(self.nc.sbuf_tensor(_uid(name), list(shape), dt)))

    def ps(self, shape, dt=F32, name="p"):
        t = Tile(self.st.enter_context(self.nc.psum_tensor(_uid(name), list(shape), dt)))
        t.buf.psum = True
        return t

    def close(self):
        self.st.close()


class WStream:
    def __init__(self, B, slots, plan):
        self.B = B
        self.slots = slots
        self.plan = plan
        self.loaded = 0
        self.pos = 0

    def _load(self, i):
        src, G, X = self.plan[i]
        slot = self.slots[i % len(self.slots)]
        if G == 0:
            self.B.P.dma("sp", slot.t[:, 0:X], src, writes=[slot])
        else:
            self.B.P.dma("sp", slot.t[:, 0:G * X].rearrange("p (g x) -> p g x", g=G),
                         src.rearrange("g p x -> p g x"), writes=[slot])

    def get(self):
        while self.loaded < len(self.plan) and self.loaded < self.pos + len(self.slots):
            self._load(self.loaded)
            self.loaded += 1
        slot = self.slots[self.pos % len(self.slots)]
        self.pos += 1
        return slot


class Builder:
    def __init__(self, seqs, debug=None):
        self.seqs = list(seqs)
        self.T = sum(self.seqs)
        self.starts = [sum(self.seqs[:i]) for i in range(len(self.seqs))]
        self.TT = 1024 if self.T % 1024 == 0 else 512
        self.NS = self.TT // 512
        self.debug = debug or {}
        nc = bass.Bass("TRN2", target_bir_lowering=False)
        self.nc = nc
        self.P = Prog(nc)
        T = self.T
        dt_in = lambda name, shape: nc.dram_tensor(name, list(shape), F32, kind="ExternalInput").ap()
        self.xin = dt_in("xin", [T, D])
        self.yout = nc.dram_tensor("yout", [T, D], F32, kind="ExternalOutput").ap()
        self.wf = {k: dt_in(k, s) for k, s in WSHAPES.items()}
        self.wb = {k: nc.dram_tensor(k + "_b", list(s), BF16).ap() for k, s in WSHAPES.items()}
        self.cst = {k: dt_in("c_" + k, s) for k, s in CSHAPES.items()}
        self.gains_d = dt_in("gains", [128, 56])
        self.small = {k: dt_in(k, s) for k, s in SMALL_SHAPES.items()}
        scr = lambda name, shape, dt: nc.dram_tensor(name, list(shape), dt).ap()
        self.xT = scr("xT", [KC, 128, T], F32)
        self.zA = scr("zA", [14, 128, T], F32)
        self.zNq = scr("zNq", [2, 128, T], BF16)
        self.zNk = scr("zNk", [2, 128, T], BF16)
        self.zDq = scr("zDq", [8, 128, T], BF16)
        self.zDk = scr("zDk", [2, 128, T], BF16)
        self.zV = scr("zV", [T, 520], BF16)
        self.zG = scr("zG", [24, 128, T], BF16)
        self.yM = scr("yM", [KC, 128, T], BF16)
        self.dbg = {}
        for k, s in self.debug.items():
            if not isinstance(s, (list, tuple)):
                continue
            self.dbg[k] = nc.dram_tensor("dbg_" + k, list(s), F32, kind="ExternalOutput").ap()

    def build(self):
        P = self.P
        nc = self.nc
        G = Scope(nc)
        self.G = G
        self.ident = G.sb([128, 128], F32, "ident")
        self.identb = G.sb([128, 128], BF16, "identb")
        self.onesm = G.sb([128, 128], BF16, "onesm")
        self.gains = G.sb([128, 56], F32, "gains")
        self.epsr = G.sb([128, 4], F32, "eps")
        tmp = G.sb([128, 128], F32, "ctmp")
        P.dma("sp", self.ident.t[:], self.cst["ident"], writes=[self.ident])
        P.dma("sp", tmp.t[:], self.cst["onesm"], writes=[tmp])
        P.dma("sp", self.gains.t[:], self.gains_d, writes=[self.gains])
        P.cp("dve", self.identb.t[:], self.ident.t[:], [self.ident], [self.identb])
        P.cp("dve", self.onesm.t[:], tmp.t[:], [tmp], [self.onesm])
        P.memset("dve", self.epsr, self.epsr.t[:, 0:1], RMS_EPS)
        P.memset("dve", self.epsr, self.epsr.t[:, 1:2], SUBLN_EPS)
        P.memset("dve", self.epsr, self.epsr.t[:, 2:3], LNX_EPS)
        P.memset("dve", self.epsr, self.epsr.t[:, 3:4], 0.0)
        self.prep_weights()
        P.barrier()
        for l in range(DEPTH):
            self.phaseA(l)
            P.barrier()
            self.mixers(l)
            P.barrier()
            self.phaseC(l)
            P.barrier()
        G.close()
        return nc

    def prep_weights(self):
        P = self.P
        S = Scope(self.nc)
        CHK = 8192
        st32 = [S.sb([128, CHK], F32, "w32") for _ in range(3)]
        st16 = [S.sb([128, CHK], BF16, "w16") for _ in range(3)]
        i = 0
        for k, shp in WSHAPES.items():
            tot = int(np.prod(shp))
            per = tot // 128
            src = self.wf[k]
            dst = self.wb[k]
            X = shp[-1]
            s2 = src.rearrange("l j p x -> (l j p) x") if len(shp) == 4 else src.rearrange("l p x -> (l p) x")
            d2 = dst.rearrange("l j p x -> (l j p) x") if len(shp) == 4 else dst.rearrange("l p x -> (l p) x")
            rows = tot // X
            gmax = max(1, CHK // X)
            r = 0
            while r < rows:
                g = min(gmax, (rows - r) // 128)
                a32 = st32[i % 3]
                a16 = st16[i % 3]
                P.dma("sp", a32.t[:, 0:g * X].rearrange("p (g x) -> p g x", g=g),
                      s2[r:r + g * 128, :].rearrange("(g p) x -> p g x", p=128), writes=[a32])
                e = ("dve", "act", "pool")[i % 3]
                P.cp(e, a16.t[:, 0:g * X], a32.t[:, 0:g * X], [a32], [a16])
                P.dma("sp", d2[r:r + g * 128, :].rearrange("(g p) x -> p g x", p=128),
                      a16.t[:, 0:g * X].rearrange("p (g x) -> p g x", g=g), reads=[a16])
                r += g * 128
                i += 1
        S.close()

    def rmsnorm(self, x, sq, u, gcol, pss, rstd, out_f32=False):
        P = self.P
        NS = self.NS
        for s in range(NS):
            sl = slice(s * 512, (s + 1) * 512)
            for c in range(KC):
                P.tt("pool", sq.t[:, c, sl], x.t[:, c, sl], x.t[:, c, sl], ALU.mult, [x], [sq])
            ps = pss.next()
            for c in range(KC):
                P.mm(ps, ps.t[:, 0:512], self.onesm.t[:], sq.t[:, c, sl], [sq, self.onesm], start=(c == 0), stop=(c == KC - 1))
            P.act(rstd.t[:, sl], ps.t[:, 0:512], AF.Sqrt, [ps, self.epsr], [rstd], bias=self.epsr.t[:, 0:1])
            P.op("dve", lambda g, sl=sl: g.reciprocal(out=rstd.t[:, sl], in_=rstd.t[:, sl]), [rstd], [rstd])
            for c in range(KC):
                P.stt("dve", u.t[:, c, sl], x.t[:, c, sl], self.gains.t[:, gcol + c:gcol + c + 1], rstd.t[:, sl],
                      ALU.mult, ALU.mult, [x, rstd, self.gains], [u])

    def ffn(self, ws, x, u, h, pss, tmps):
        P = self.P
        NS = self.NS
        for jp in range(FC // 2):
            w = ws.get()
            for jj in range(2):
                j = jp * 2 + jj
                pg = [pss.next() for _ in range(NS)]
                pu = [pss.next() for _ in range(NS)]
                for gi, pp in ((0, pg), (1, pu)):
                    base = (jj * 2 + gi) * D
                    for c in range(KC):
                        for s in range(NS):
                            P.mm(pp[s], pp[s].t[:, 0:512], w.t[:, base + c * 128:base + (c + 1) * 128],
                                 u.t[:, c, s * 512:(s + 1) * 512], [w, u], start=(c == 0), stop=(c == KC - 1))
                for s in range(NS):
                    tm = tmps.next()
                    P.act(tm.t[:, 0:512], pg[s].t[:, 0:512], AF.Silu, [pg[s]], [tm])
                    P.tt("dve", h.t[:, j, s * 512:(s + 1) * 512], tm.t[:, 0:512], pu[s].t[:, 0:512], ALU.mult, [tm, pu[s]], [h])
        for o in range(KC):
            w = ws.get()
            po = [pss.next() for _ in range(NS)]
            for j in range(FC):
                for s in range(NS):
                    P.mm(po[s], po[s].t[:, 0:512], w.t[:, j * 128:(j + 1) * 128], h.t[:, j, s * 512:(s + 1) * 512],
                         [w, h], start=(j == 0), stop=(j == FC - 1))
            for s in range(NS):
                sl = slice(s * 512, (s + 1) * 512)
                P.stt("dve", x.t[:, o, sl], po[s].t[:, 0:512], 0.5, x.t[:, o, sl], ALU.mult, ALU.add, [po[s], x], [x])

    def ffn_plan(self, key_gu, key_d, l):
        plan = []
        for jp in range(FC // 2):
            plan.append((self.wb[key_gu][l, jp * 4:jp * 4 + 4], 4, D))
        for o in range(KC):
            plan.append((self.wb[key_d][l, o:o + 1], 1, DFF))
        return plan

    def phaseA(self, l):
        P = self.P
        nc = self.nc
        TT, NS, T = self.TT, self.NS, self.T
        S = Scope(nc)
        x = S.sb([128, KC, TT], F32, "x")
        u = S.sb([128, KC, TT], BF16, "u")
        h = S.sb([128, FC, TT], BF16, "h")
        rstd = S.sb([128, TT], F32, "rstd")
        slots = [S.sb([128, 4096], BF16, "wslot") for _ in range(4)]
        tmps = Ring([S.sb([128, 512], F32, "tmp") for _ in range(4)])
        stg = Ring([S.sb([128, 1024], F32, "stg") for _ in range(4)])
        pss = Ring([S.ps([128, 512], F32, "ps") for _ in range(8)])
        vst = Ring([S.sb([128, 8, 65], BF16, "vst") for _ in range(3)])
        for v_ in vst.tiles:
            P.memset("pool", v_, v_.t[:], 1.0)
        ntile = T // TT
        plan = []
        for it in range(ntile):
            plan += self.ffn_plan("wgu1", "wd1", l)
            for jp in range(NWIN // 4):
                plan.append((self.wb["win"][l, jp * 4:jp * 4 + 4], 4, D))
            plan.append((self.wb["wv"][l], 0, KC * 512))
        ws = WStream(self, slots, plan)
        for it in range(ntile):
            t0 = it * TT
            if l == 0:
                for b in range(TT // 128):
                    sg = stg.next()
                    P.dma("sp", sg.t[:, 0:D], self.xin[t0 + b * 128:t0 + (b + 1) * 128, :], writes=[sg])
                    for half in range(2):
                        ps = pss.next()
                        for cc in range(4):
                            c = half * 4 + cc
                            P.tr(ps, ps.t[:, cc * 128:(cc + 1) * 128], sg.t[:, c * 128:(c + 1) * 128], self.ident.t[:], [sg, self.ident])
                        e = "act" if half == 0 else "dve"
                        P.cp(e, x.t[:, half * 4:half * 4 + 4, b * 128:(b + 1) * 128],
                             ps.t[:, 0:512].rearrange("p (c t) -> p c t", c=4), [ps], [x])
            else:
                P.dma("sp", x.t[:], self.xT[:, :, t0:t0 + TT].rearrange("c p t -> p c t"), writes=[x])
            self.rmsnorm(x, h, u, (l * 3 + 0) * KC, pss, rstd)
            self.ffn(ws, x, u, h, pss, tmps)
            P.dma("pool", self.xT[:, :, t0:t0 + TT].rearrange("c p t -> p c t"), x.t[:], reads=[x])
            self.rmsnorm(x, h, u, (l * 3 + 1) * KC, pss, rstd)
            for jp in range(NWIN // 4):
                w = ws.get()
                for jj in range(4):
                    j = jp * 4 + jj
                    pp = [pss.next() for _ in range(NS)]
                    for c in range(KC):
                        for s in range(NS):
                            P.mm(pp[s], pp[s].t[:, 0:512], w.t[:, jj * D + c * 128:jj * D + (c + 1) * 128],
                                 u.t[:, c, s * 512:(s + 1) * 512], [w, u], start=(c == 0), stop=(c == KC - 1))
                    sg = stg.next()
                    if j < 14:
                        dst, view = self.zA[j, :, t0:t0 + TT], sg.t[:, 0:TT]
                        for s in range(NS):
                            P.cp("act" if s == 0 else "dve", view[:, s * 512:(s + 1) * 512], pp[s].t[:, 0:512], [pp[s]], [sg])
                    else:
                        view = sg.t[:].bitcast(BF16)[:, 0:TT]
                        if j < 16:
                            dst, sc, fn = self.zNq[j - 14, :, t0:t0 + TT], 0.125, AF.Copy
                        elif j < 18:
                            dst, sc, fn = self.zNk[j - 16, :, t0:t0 + TT], 1.0, AF.Copy
                        elif j < 26:
                            dst, sc, fn = self.zDq[j - 18, :, t0:t0 + TT], 32.0 ** -0.5, AF.Copy
                        elif j < 28:
                            dst, sc, fn = self.zDk[j - 26, :, t0:t0 + TT], 1.0, AF.Copy
                        else:
                            dst, sc, fn = self.zG[j - 28, :, t0:t0 + TT], 1.0, AF.Sigmoid
                        for s in range(NS):
                            if fn == AF.Sigmoid or s == 0:
                                P.act(view[:, s * 512:(s + 1) * 512], pp[s].t[:, 0:512], fn, [pp[s]], [sg], scale=sc)
                            else:
                                P.ts("dve", view[:, s * 512:(s + 1) * 512], pp[s].t[:, 0:512], sc, None, ALU.mult, None, [pp[s]], [sg])
                    P.dma("pool", dst, view, reads=[sg])
            w = ws.get()
            for b in range(TT // 128):
                ps = pss.next()
                for c in range(KC):
                    P.mm(ps, ps.t[:, 0:512], u.t[:, c, b * 128:(b + 1) * 128], w.t[:, c * 512:(c + 1) * 512], [w, u],
                         start=(c == 0), stop=(c == KC - 1))
                sg = vst.next()
                P.cp("act" if b % 2 == 0 else "dve", sg.t[:, :, 0:64], ps.t[:, 0:512].rearrange("p (h d) -> p h d", h=8), [ps], [sg])
                P.dma("pool", self.zV[t0 + b * 128:t0 + (b + 1) * 128, :], sg.t[:].rearrange("p h d -> p (h d)"), reads=[sg])
        S.close()

    def phaseC(self, l):
        P = self.P
        nc = self.nc
        TT, NS, T = self.TT, self.NS, self.T
        last = (l == DEPTH - 1)
        S = Scope(nc)
        x = S.sb([128, KC, TT], F32, "x")
        u = S.sb([128, KC, TT], BF16, "u")
        h = S.sb([128, FC, TT], BF16, "h")
        ym = S.sb([128, KC, TT], BF16, "ym")
        rstd = S.sb([128, TT], F32, "rstd")
        gts = Ring([S.sb([128, 3, TT], BF16, "gt") for _ in range(2)])
        slots = [S.sb([128, 4096], BF16, "wslot") for _ in range(4)]
        tmps = Ring([S.sb([128, 512], F32, "tmp") for _ in range(4)])
        stg = Ring([S.sb([128, 1024], F32, "stg") for _ in range(3)])
        pss = Ring([S.ps([128, 512], F32, "ps") for _ in range(8)])
        ntile = T // TT
        plan = []
        for it in range(ntile):
            plan += [(self.wb["wpabc"][l, 0:4], 4, D), (self.wb["wpabc"][l, 4:8], 4, D),
                     (self.wb["wout"][l, 0:4], 4, D), (self.wb["wout"][l, 4:8], 4, D)]
            plan += self.ffn_plan("wgu2", "wd2", l)
        ws = WStream(self, slots, plan)
        zGv = self.zG.rearrange("(g o) p t -> o p g t", g=3)
        for it in range(ntile):
            t0 = it * TT
            P.dma("sp", x.t[:], self.xT[:, :, t0:t0 + TT].rearrange("c p t -> p c t"), writes=[x])
            P.dma("sp", ym.t[:], self.yM[:, :, t0:t0 + TT].rearrange("c p t -> p c t"), writes=[ym])
            for op_ in range(2):
                w = ws.get()
                for oo in range(4):
                    o = op_ * 4 + oo
                    gt = gts.next()
                    P.dma("sp", gt.t[:], zGv[o, :, :, t0:t0 + TT], writes=[gt])
                    for s in range(NS):
                        sl = slice(s * 512, (s + 1) * 512)
                        pa, pb, pc = pss.next(), pss.next(), pss.next()
                        for (pp, c0, c1) in ((pa, 0, 4), (pb, 4, 6), (pc, 6, 8)):
                            for c in range(c0, c1):
                                P.mm(pp, pp.t[:, 0:512], w.t[:, oo * D + c * 128:oo * D + (c + 1) * 128], ym.t[:, c, sl],
                                     [w, ym], start=(c == c0), stop=(c == c1 - 1))
                        t1, t2, t3 = tmps.next(), tmps.next(), tmps.next()
                        P.tt("dve", t1.t[:, 0:512], pa.t[:, 0:512], gt.t[:, 0, sl], ALU.mult, [pa, gt], [t1])
                        P.tt("dve", t2.t[:, 0:512], pb.t[:, 0:512], gt.t[:, 1, sl], ALU.mult, [pb, gt], [t2])
                        P.tt("pool", t1.t[:, 0:512], t1.t[:, 0:512], t2.t[:, 0:512], ALU.add, [t1, t2], [t1])
                        P.tt("dve", t3.t[:, 0:512], pc.t[:, 0:512], gt.t[:, 2, sl], ALU.mult, [pc, gt], [t3])
                        P.tt("pool", u.t[:, o, sl], t1.t[:, 0:512], t3.t[:, 0:512], ALU.add, [t1, t3], [u])
            for op_ in range(2):
                w = ws.get()
                for oo in range(4):
                    o = op_ * 4 + oo
                    for s in range(NS):
                        sl = slice(s * 512, (s + 1) * 512)
                        pp = pss.next()
                        for c in range(KC):
                            P.mm(pp, pp.t[:, 0:512], w.t[:, oo * D + c * 128:oo * D + (c + 1) * 128], u.t[:, c, sl], [w, u],
                                 start=(c == 0), stop=(c == KC - 1))
                        P.tt("dve", x.t[:, o, sl], pp.t[:, 0:512], x.t[:, o, sl], ALU.add, [pp, x], [x])
            self.rmsnorm(x, h, u, (l * 3 + 2) * KC, pss, rstd)
            self.ffn(ws, x, u, h, pss, tmps)
            if not last:
                P.dma("pool", self.xT[:, :, t0:t0 + TT].rearrange("c p t -> p c t"), x.t[:], reads=[x])
            else:
                self.rmsnorm(x, h, x, 6 * KC, pss, rstd)
                for b in range(TT // 128):
                    sg = stg.next()
                    for half in range(2):
                        ps = pss.next()
                        for cc in range(4):
                            c = half * 4 + cc
                            P.tr(ps, ps.t[:, cc * 128:(cc + 1) * 128], x.t[:, c, b * 128:(b + 1) * 128], self.ident.t[:], [x, self.ident])
                        P.cp("act" if half == 0 else "dve", sg.t[:, half * 512:(half + 1) * 512], ps.t[:, 0:512], [ps], [sg])
                    P.dma("pool", self.yout[t0 + b * 128:t0 + (b + 1) * 128, :], sg.t[:, 0:D], reads=[sg])
        S.close()

    def mixers(self, l):
        P = self.P
        en = self.debug_en if hasattr(self, "debug_en") else ("rwkv", "na", "da")
        S = Scope(self.nc)
        z = S.sb([128, 2048], BF16, "zero")
        P.memset("dve", z, z.t[:], 0.0)
        for (name, c0, c1) in (("rwkv", 0, 4), ("na", 4, 6), ("da", 6, 8)):
            if name in en:
                continue
            for c in range(c0, c1):
                for t in range(0, self.T, 2048):
                    n = min(2048, self.T - t)
                    P.dma("sp", self.yM[c, :, t:t + n], z.t[:, 0:n], reads=[z])
        S.close()
        P.barrier()
        if "na" in en:
            self.mix_na(l)
            P.barrier()
        if "da" in en:
            self.mix_da(l)
            P.barrier()
        if "rwkv" in en:
            self.mix_rwkv(l)
            P.barrier()


NA_TYPES = [(0, 0), (-2, -2), (-4, -3), (-4, -4), (-6, -6)]
SLOPES = [2.0 ** (-8.0 * (h + 1) / 4) for h in range(4)]


def na_rs(r, rows):
    return min(max(r - 4, 0), rows - 8)


def na_tile_info(r, rows):
    a, b = na_rs(r, rows) - r, na_rs(r + 1, rows) - r
    ty = NA_TYPES.index((a, b))
    kr0 = r + a
    nk = (b + 8 - a + 1) // 2
    return ty, kr0, nk


def na_tables(rpb):
    tab = np.full((128, 5, 4, 5, 128), -30000.0, np.float32)
    pk = np.arange(128)
    pq = np.arange(128)
    for ti, (a, b) in enumerate(NA_TYPES):
        nk = (b + 8 - a + 1) // 2
        for j in range(nk):
            krow = a + 2 * j + pk // 64
            kcol = pk % 64
            qrow = pq // 64
            qcol = pq % 64
            rs_rel = np.where(qrow == 0, a, b)
            cs = np.clip(qcol - 8, 0, 64 - 16)
            okr = (krow[:, None] >= rs_rel[None, :]) & (krow[:, None] < rs_rel[None, :] + 8)
            okc = (kcol[:, None] >= cs[None, :]) & (kcol[:, None] < cs[None, :] + 16)
            dr = np.clip(krow[:, None] - qrow[None, :] + 7, 0, 14)
            dc = np.clip(kcol[:, None] - qcol[None, :] + 15, 0, 30)
            ok = okr & okc
            for h in range(4):
                g = rpb[h][dr, dc]
                t = tab[:, ti, h, j, :]
                t[ok] = g[ok]
    return tab


def da_consts():
    p = np.arange(128, dtype=np.float64)
    colL = np.zeros((128, 4, 32), np.float32)
    colR = np.zeros((128, 4, 32), np.float32)
    fLR = np.zeros((128, 4, 8), np.float32)
    biasD = np.zeros((128, 4, 4, 512), np.float32)
    q = np.arange(512, dtype=np.float64)
    for s_, sl in enumerate(SLOPES):
        for m in range(32):
            colL[:, s_, m] = -sl * (128 * m - p)
            colR[:, s_, m] = -sl * (128 * m + p - 511)
        for sub in range(4):
            fLR[:, s_, sub] = -sl * (128 * sub + p)
            fLR[:, s_, 4 + sub] = -sl * (511 - 128 * sub - p)
        for j in range(4):
            biasD[:, s_, j, :] = -sl * np.abs(q[None, :] - (128 * j + p[:, None]))
    return {"da_colL": colL, "da_colR": colR, "da_fLR": fLR, "da_biasD": biasD}


SMALL_SHAPES = {"na_tab": [DEPTH, 128, 5 * 4 * 5 * 128], "da_colL": [128, 4, 32], "da_colR": [128, 4, 32],
                "da_fLR": [128, 4, 8], "da_biasD": [128, 4, 4, 512], "da_lam": [DEPTH, 128, 128],
                "da_g": [DEPTH, 128, 256]}


def host_small(inp):
    f = lambda a: np.asarray(a, np.float32)
    out = {}
    out["na_tab"] = np.stack([na_tables(f(inp["na_rpb"][l])).reshape(128, -1) for l in range(DEPTH)])
    out.update(da_consts())
    out["da_lam"] = np.stack([np.broadcast_to(f(inp["diff_lam"][l]).reshape(1, 128), (128, 128)) for l in range(DEPTH)]).copy()
    out["da_g"] = np.stack([np.broadcast_to(np.tile(f(inp["diff_subln_g"][l]), 4).reshape(1, 256), (128, 256)) for l in range(DEPTH)]).copy()
    return out


def make_inputs(inp, seq_groups, ncores):
    shared = {}
    shared.update(host_weights(inp))
    shared.update({"c_" + k: v for k, v in host_consts().items()})
    shared.update(host_small(inp))
    in_maps = []
    for c in range(ncores):
        parts = []
        for (arr, n) in seq_groups:
            for b in range(n):
                parts.append(np.asarray(arr[c * n + b], np.float32))
        m = dict(shared)
        m["xin"] = np.ascontiguousarray(np.concatenate(parts, axis=0))
        in_maps.append(m)
    return in_maps


def run(inp, seq_groups, ncores, debug=None, en=None):
    seqs = []
    for (arr, n) in seq_groups:
        seqs += [arr.shape[1]] * n
    B = Builder(seqs, debug=debug)
    if en is not None:
        B.debug_en = en
    nc = B.build()
    in_maps = make_inputs(inp, seq_groups, ncores)
    res = run_bass_kernel_spmd(nc, in_maps, core_ids=list(range(ncores)))
    outs = []
    for (arr, n) in seq_groups:
        outs.append(np.zeros(arr.shape, np.float32))
    for c in range(ncores):
        y = res.results[c]["yout"]
        t = 0
        for gi, (arr, n) in enumerate(seq_groups):
            L = arr.shape[1]
            for b in range(n):
                outs[gi][c * n + b] = y[t:t + L]
                t += L
    return outs, res, B


def kernel(**inputs):
    xp = np.asarray(inputs["x_prompt"], np.float32)
    xs = np.asarray(inputs["x_sample"], np.float32)
    outs, _, _ = run(inputs, [(xp, xp.shape[0] // NCORES), (xs, xs.shape[0] // NCORES)], NCORES)
    return (outs[0], outs[1])


def mix_na(self, l):
    P = self.P
    S = Scope(self.nc)
    Lmax = max(self.seqs)
    qT = S.sb([128, 2, Lmax], BF16, "naq")
    kT = S.sb([128, 2, Lmax], BF16, "nak")
    V1 = S.sb([128, Lmax // 128, 260], BF16, "nav")
    yst = S.sb([128, 2, Lmax], BF16, "nay")
    tab = S.sb([128, 5, 4, 5 * 128], F32, "natab")
    sbr = Ring([S.sb([128, 640], F32, "nasb") for _ in range(2)])
    ptr = Ring([S.sb([128, 640], BF16, "napt") for _ in range(2)])
    yr = Ring([S.sb([128, 256], F32, "nayt") for _ in range(2)])
    rcr = Ring([S.sb([128, 4], F32, "narc") for _ in range(2)])
    pss = Ring([S.ps([128, 1024], F32, "naps") for _ in range(2)])
    pso = Ring([S.ps([128, 512], F32, "napo") for _ in range(2)])
    pst = Ring([S.ps([128, 512], F32, "napt") for _ in range(2)])
    P.dma("sp", tab.t[:].rearrange("p a b c -> p (a b c)"), self.small["na_tab"][l], writes=[tab])
    for si, L in enumerate(self.seqs):
        t0 = self.starts[si]
        rows = L // 64
        P.dma("sp", qT.t[:, :, 0:L], self.zNq[:, :, t0:t0 + L].rearrange("c p t -> p c t"), writes=[qT])
        P.dma("sp", kT.t[:, :, 0:L], self.zNk[:, :, t0:t0 + L].rearrange("c p t -> p c t"), writes=[kT])
        P.dma("sp", V1.t[:, 0:L // 128, :], self.zV[t0:t0 + L, 0:260].rearrange("(n p) x -> p n x", p=128), writes=[V1])
        for qi in range(L // 128):
            r = 2 * qi
            ty, kr0, nk = na_tile_info(r, rows)
            po = pso.next()
            for hd in range(4):
                cc, base = hd // 2, (hd % 2) * 64
                ps = pss.next()
                for j in range(nk):
                    kn = kr0 // 2 + j
                    P.mm(ps, ps.t[:, j * 128:(j + 1) * 128], kT.t[base:base + 64, cc, kn * 128:(kn + 1) * 128],
                         qT.t[base:base + 64, cc, qi * 128:(qi + 1) * 128], [kT, qT])
                sb = sbr.next()
                P.tt("dve", sb.t[:, 0:nk * 128], ps.t[:, 0:nk * 128], tab.t[:, ty, hd, 0:nk * 128], ALU.add, [ps, tab], [sb])
                pt = ptr.next()
                P.act(pt.t[:, 0:nk * 128], sb.t[:, 0:nk * 128], AF.Exp, [sb], [pt])
                for j in range(nk):
                    kn = kr0 // 2 + j
                    P.mm(po, po.t[:, hd * 65:(hd + 1) * 65], pt.t[:, j * 128:(j + 1) * 128], V1.t[:, kn, hd * 65:(hd + 1) * 65],
                         [pt, V1], start=(j == 0), stop=(j == nk - 1))
            rc = rcr.next()
            pov = po.t[:, 0:260].rearrange("p (h d) -> p h d", h=4)
            P.op("dve", lambda g, rc=rc, pov=pov: g.reciprocal(out=rc.t[:, :], in_=pov[:, :, 64]), [po], [rc])
            y = yr.next()
            for hd in range(4):
                P.ts("dve", y.t[:, hd * 64:(hd + 1) * 64], po.t[:, hd * 65:hd * 65 + 64], rc.t[:, hd:hd + 1], None, ALU.mult, None,
                     [po, rc], [y])
            pt_ = pst.next()
            for cc in range(2):
                P.tr(pt_, pt_.t[:, cc * 128:(cc + 1) * 128], y.t[:, cc * 128:(cc + 1) * 128], self.ident.t[:], [y, self.ident])
            P.cp("act", yst.t[:, :, qi * 128:(qi + 1) * 128], pt_.t[:, 0:256].rearrange("p (c t) -> p c t", c=2), [pt_], [yst])
        P.dma("pool", self.yM[4:6, :, t0:t0 + L].rearrange("c p t -> p c t"), yst.t[:, :, 0:L], reads=[yst])
    S.close()


def mix_da(self, l):
    P = self.P
    S = Scope(self.nc)
    Lmax = max(self.seqs)
    lam_init = 0.8 - 0.6 * math.exp(-0.3 * l)
    kT = S.sb([128, 2, Lmax], BF16, "dak")
    V1 = S.sb([128, Lmax // 128, 260], BF16, "dav")
    yst = S.sb([128, 2, Lmax], BF16, "day")
    qmr = Ring([S.sb([128, 8, 512], BF16, "daq") for _ in range(2)])
    colL = S.sb([128, 4, 32], F32, "colL")
    colR = S.sb([128, 4, 32], F32, "colR")
    fLR = S.sb([128, 4, 8], F32, "fLR")
    bD32 = S.sb([128, 4 * 4 * 512], F32, "bD32")
    bD = S.sb([128, 4, 4, 512], BF16, "bD")
    lamt = S.sb([128, 4, 32], F32, "lamt")
    lamw = S.sb([128, 2, 32], F32, "lamw")
    lams = S.sb([128, 4], F32, "lams")
    gt = S.sb([128, 4, 64], F32, "dag")
    att = S.sb([128, 4, 8, 64], F32, "att")
    ptr = Ring([S.sb([128, 512], BF16, "dapt") for _ in range(3)])
    totr = Ring([S.sb([128, 65], F32, "datot") for _ in range(3)])
    rcr = Ring([S.sb([128, 4], F32, "darc") for _ in range(3)])
    ar = Ring([S.sb([128, 4, 64], F32, "daa") for _ in range(2)])
    sqr = Ring([S.sb([128, 4, 64], F32, "dasq") for _ in range(2)])
    yr = Ring([S.sb([128, 256], F32, "dayt") for _ in range(2)])
    pss = Ring([S.ps([128, 512], F32, "daps") for _ in range(3)])
    pso = Ring([S.ps([128, 512], F32, "dapo") for _ in range(4)])
    pst = Ring([S.ps([128, 512], F32, "dapt") for _ in range(1)])
    sm = self.small
    P.dma("sp", colL.t[:], sm["da_colL"], writes=[colL])
    P.dma("sp", colR.t[:], sm["da_colR"], writes=[colR])
    P.dma("sp", fLR.t[:], sm["da_fLR"], writes=[fLR])
    P.dma("sp", bD32.t[:], sm["da_biasD"].rearrange("p a b c -> p (a b c)"), writes=[bD32])
    P.dma("sp", lamt.t[:].rearrange("p a b -> p (a b)"), sm["da_lam"][l], writes=[lamt])
    P.dma("sp", gt.t[:].rearrange("p a b -> p (a b)"), sm["da_g"][l], writes=[gt])
    P.cp("dve", bD.t[:].rearrange("p a b c -> p (a b c)"), bD32.t[:], [bD32], [bD])
    P.act(fLR.t[:], fLR.t[:], AF.Exp, [fLR], [fLR])
    P.tt("dve", lamw.t[:, 0, :], lamt.t[:, 0, :], lamt.t[:, 1, :], ALU.mult, [lamt], [lamw])
    P.tt("dve", lamw.t[:, 1, :], lamt.t[:, 2, :], lamt.t[:, 3, :], ALU.mult, [lamt], [lamw])
    P.op("dve", lambda g: g.tensor_reduce(out=lams.t[:, 0:2], in_=lamw.t[:], axis=AX.X, op=ALU.add), [lamw], [lams])
    P.act(lams.t[:, 0:2], lams.t[:, 0:2], AF.Exp, [lams], [lams])
    P.tt("dve", lams.t[:, 2:3], lams.t[:, 1:2], lams.t[:, 0:1], ALU.subtract, [lams], [lams])
    P.ts("dve", lams.t[:, 3:4], lams.t[:, 2:3], -lam_init, None, ALU.add, None, [lams], [lams])
    zero_col = self.epsr.t[:, 3:4]
    for si, L in enumerate(self.seqs):
        t0 = self.starts[si]
        NK = L // 128
        P.dma("sp", kT.t[:, :, 0:L], self.zDk[:, :, t0:t0 + L].rearrange("c p t -> p c t"), writes=[kT])
        P.dma("sp", V1.t[:, 0:NK, :], self.zV[t0:t0 + L, 260:520].rearrange("(n p) x -> p n x", p=128), writes=[V1])
        for qb in range(L // 512):
            q0 = qb * 512
            qm = qmr.next()
            P.dma("sp", qm.t[:], self.zDq[:, :, t0 + q0:t0 + q0 + 512].rearrange("g p t -> p g t"), writes=[qm])
            for g_ in range(8):
                s_ = g_ // 2
                hd = g_ // 2
                poA, poB = pso.next(), pso.next()
                cls_of = {}
                for kt in range(NK):
                    k0 = kt * 128
                    c_ = 0 if k0 + 128 <= q0 else (2 if k0 >= q0 + 512 else 1)
                    mind = (q0 - (k0 + 127)) if c_ == 0 else ((k0 - (q0 + 511)) if c_ == 2 else 0)
                    if SLOPES[s_] * mind >= 80.0:
                        continue
                    cls_of[kt] = c_
                kts = sorted(cls_of)
                first = {}
                lastk = {}
                for kt in kts:
                    first.setdefault(cls_of[kt], kt)
                    lastk[cls_of[kt]] = kt
                for kt in kts:
                    k0 = kt * 128
                    cls = cls_of[kt]
                    ps = pss.next()
                    P.mm(ps, ps.t[:, 0:512], kT.t[:, g_ // 4, k0:k0 + 128], qm.t[:, g_, :], [kT, qm], start=True, stop=(cls != 1))
                    if cls == 1:
                        P.mm(ps, ps.t[:, 0:512], self.identb.t[:], bD.t[:, s_, (k0 - q0) // 128, :], [self.identb, bD], start=False, stop=True)
                        bias = zero_col
                        rd = [ps, self.epsr]
                    elif cls == 0:
                        bias = colL.t[:, s_, (q0 - k0) // 128:(q0 - k0) // 128 + 1]
                        rd = [ps, colL]
                    else:
                        bias = colR.t[:, s_, (k0 - q0) // 128:(k0 - q0) // 128 + 1]
                        rd = [ps, colR]
                    pt = ptr.next()
                    P.act(pt.t[:], ps.t[:, 0:512], AF.Exp, rd, [pt], bias=bias)
                    for sub in range(4):
                        po = poA if sub < 2 else poB
                        off = ((sub % 2) * 3 + cls) * 65
                        P.mm(po, po.t[:, off:off + 65], pt.t[:, sub * 128:(sub + 1) * 128], V1.t[:, kt, hd * 65:(hd + 1) * 65],
                             [pt, V1], start=(kt == kts[0] and sub % 2 == 0), stop=(kt == lastk[cls]), sgc=True)
                for sub in range(4):
                    po = poA if sub < 2 else poB
                    o0 = (sub % 2) * 3 * 65
                    tot = totr.next()
                    P.cp("act", tot.t[:], po.t[:, o0 + 65:o0 + 130], [po], [tot])
                    if 0 in first:
                        P.stt("dve", tot.t[:], po.t[:, o0:o0 + 65], fLR.t[:, s_, sub:sub + 1], tot.t[:], ALU.mult, ALU.add, [po, fLR, tot], [tot])
                    if 2 in first:
                        P.stt("dve", tot.t[:], po.t[:, o0 + 130:o0 + 195], fLR.t[:, s_, 4 + sub:5 + sub], tot.t[:], ALU.mult, ALU.add,
                              [po, fLR, tot], [tot])
                    rc = rcr.next()
                    P.op("dve", lambda g, rc=rc, tot=tot: g.reciprocal(out=rc.t[:, 0:1], in_=tot.t[:, 64:65]), [tot], [rc])
                    P.ts("dve", att.t[:, sub, g_, :], tot.t[:, 0:64], rc.t[:, 0:1], None, ALU.mult, None, [tot, rc], [att])
            if "att" in self.dbg and l == 0 and si == len(self.seqs) - 1 and qb == 0:
                P.dma("sp", self.dbg["att"], att.t[:].rearrange("p a b c -> p (a b c)"), reads=[att])
                P.dma("sp", self.dbg["lams"], lams.t[:], reads=[lams])
            for sub in range(4):
                a = ar.next()
                av = att.t[:, sub, :, :].rearrange("p (h two) d -> p h two d", two=2)
                P.stt("dve", a.t[:], av[:, :, 1, :], lams.t[:, 3:4], av[:, :, 0, :], ALU.mult, ALU.add, [att, lams], [a])
                sq = sqr.next()
                P.tt("pool", sq.t[:], a.t[:], a.t[:], ALU.mult, [a], [sq])
                rc = rcr.next()
                P.op("dve", lambda g, rc=rc, sq=sq: g.tensor_reduce(out=rc.t[:, 0:4], in_=sq.t[:], axis=AX.X, op=ALU.add), [sq], [rc])
                P.act(rc.t[:, 0:4], rc.t[:, 0:4], AF.Sqrt, [rc, self.epsr], [rc], bias=self.epsr.t[:, 1:2], scale=1.0 / 64)
                P.op("dve", lambda g, rc=rc: g.reciprocal(out=rc.t[:, 0:4], in_=rc.t[:, 0:4]), [rc], [rc])
                y = yr.next()
                yv = y.t[:].rearrange("p (h d) -> p h d", h=4)
                for hd in range(4):
                    P.ts("dve", yv[:, hd, :], a.t[:, hd, :], rc.t[:, hd:hd + 1], 1.0 - lam_init, ALU.mult, ALU.mult, [a, rc], [y])
                P.tt("pool", yv, yv, gt.t[:], ALU.mult, [y, gt], [y])
                pt_ = pst.next()
                for cc in range(2):
                    P.tr(pt_, pt_.t[:, cc * 128:(cc + 1) * 128], y.t[:, cc * 128:(cc + 1) * 128], self.ident.t[:], [y, self.ident])
                tq = q0 + sub * 128
                P.cp("act", yst.t[:, :, tq:tq + 128], pt_.t[:, 0:256].rearrange("p (c t) -> p c t", c=2), [pt_], [yst])
        P.dma("pool", self.yM[6:8, :, t0:t0 + L].rearrange("c p t -> p c t"), yst.t[:, :, 0:L], reads=[yst])
    S.close()


Builder.mix_na = mix_na
Builder.mix_da = mix_da


CDEC = math.exp(-0.5)


def rw_host(inp):
    f = lambda a: np.asarray(a, np.float32)
    out = {}
    mu_l, mulw_l, mulg_l, hp_l = [], [], [], []
    for l in range(DEPTH):
        mu = f(inp["rwkv_mu"][l])
        a = np.zeros((64, 8, 3, 2), np.float32)
        for h in range(8):
            for q in range(3):
                for m in range(2):
                    a[:, h, q, m] = mu[m, q * 512 + h * 64:q * 512 + h * 64 + 64]
        mu_l.append(a.reshape(64, 48))
        mulw_l.append(np.stack([mu[0, 1536:1600], mu[1, 1536:1600], mu[0, 1600:1664], mu[1, 1600:1664]], axis=1))
        mulg_l.append(np.stack([mu[0, 1664:1792], mu[1, 1664:1792]], axis=1))
        hp = np.zeros((64, 8, 11), np.float32)
        for h in range(8):
            sl = slice(h * 64, h * 64 + 64)
            hp[:, h, 0] = f(inp["rwkv_k_k"][l])[sl]
            hp[:, h, 1] = f(inp["rwkv_lnx_g"][l])[sl]
            hp[:, h, 2] = f(inp["rwkv_lnx_b"][l])[sl]
            for d in range(2):
                hp[:, h, 3 + 4 * d + 0] = f(inp["rwkv_w0"][l, d])[sl]
                hp[:, h, 3 + 4 * d + 1] = f(inp["rwkv_a0"][l, d])[sl]
                hp[:, h, 3 + 4 * d + 2] = f(inp["rwkv_k_a"][l, d])[sl]
                hp[:, h, 3 + 4 * d + 3] = f(inp["rwkv_r_k"][l, d]).reshape(-1)[sl]
        hp_l.append(hp.reshape(64, 88))
    out["rw_mu"] = np.stack(mu_l); out["rw_mulw"] = np.stack(mulw_l); out["rw_mulg"] = np.stack(mulg_l)
    out["rw_hp"] = np.stack(hp_l)
    out["rw_w2"] = np.ascontiguousarray(f(inp["rwkv_w2"])); out["rw_a2"] = np.ascontiguousarray(f(inp["rwkv_a2"]))
    out["rw_g2"] = np.ascontiguousarray(f(inp["rwkv_g2"]))
    i = np.arange(CK)
    row, col = i[:, None], i[None, :]
    mk = np.zeros((2, 3, CK, 512), np.float32)
    for d in range(2):
        st = (row < col) if d == 0 else (row > col)
        inc = (row <= col) if d == 0 else (row >= col)
        stT = (col < row) if d == 0 else (col > row)
        for mi, m_ in enumerate((st, inc, stT)):
            mk[d, mi] = np.tile(m_.astype(np.float32), (1, 512 // CK))
    out["rw_mask"] = mk
    out["rw_irep"] = np.tile(np.eye(CK, dtype=np.float32), (1, 512 // CK))
    rs = np.ones((64, 1024), np.float32)
    rs[:, ::CK] = 0.0
    out["rw_reset"] = rs
    return out


SMALL_SHAPES.update({"rw_mu": [DEPTH, 64, 48], "rw_mulw": [DEPTH, 64, 4], "rw_mulg": [DEPTH, 128, 2], "rw_hp": [DEPTH, 64, 88],
                     "rw_w2": [DEPTH, 2, 64, 512], "rw_a2": [DEPTH, 2, 64, 512], "rw_g2": [DEPTH, 128, 512],
                     "rw_mask": [2, 3, CK, 512], "rw_irep": [CK, 512], "rw_reset": [64, 1024]})
_host_small0 = host_small


def host_small(inp):
    o = _host_small0(inp)
    o.update(rw_host(inp))
    return o


def mix_rwkv(self, l):
    P = self.P
    nc = self.nc
    S = Scope(nc)
    sm = self.small
    T = self.T
    if not hasattr(self, "yF"):
        self.yF = nc.dram_tensor("yF", [2, 8, 64, T], F32).ap()
    yfb = Buf()
    SEG = min(512, min(self.seqs))
    W = SEG
    sb = lambda shape, name: S.sb(shape, F32, name)
    mu = sb([64, 48], "mu"); c0 = sb([64, 24], "c0"); mulw = sb([64, 4], "mulw"); c0w = sb([64, 2], "c0w")
    mulg = sb([128, 2], "mulg"); c0g = sb([128, 1], "c0g"); hp = sb([64, 88], "hp"); omk = sb([64, 16], "omk")
    w2 = sb([64, 2, 512], "w2"); a2 = sb([64, 2, 512], "a2"); g2 = sb([128, 512], "g2")
    mk = sb([CK, 2, 3, 512], "mk"); irep = sb([CK, 512], "irep"); rst = sb([64, 1024], "rst"); ones = sb([64, 64], "ones"); onesm = sb([64, 64], "onesm")
    P.dma("sp", mu.t[:], sm["rw_mu"][l], writes=[mu]); P.dma("sp", mulw.t[:], sm["rw_mulw"][l], writes=[mulw])
    P.dma("sp", mulg.t[:], sm["rw_mulg"][l], writes=[mulg]); P.dma("sp", hp.t[:], sm["rw_hp"][l], writes=[hp])
    P.dma("sp", w2.t[:], sm["rw_w2"][l].rearrange("d k n -> k d n"), writes=[w2])
    P.dma("sp", a2.t[:], sm["rw_a2"][l].rearrange("d k n -> k d n"), writes=[a2])
    P.dma("sp", g2.t[:], sm["rw_g2"][l], writes=[g2])
    P.dma("sp", mk.t[:], sm["rw_mask"].rearrange("d m p n -> p d m n"), writes=[mk])
    P.dma("sp", irep.t[:], sm["rw_irep"], writes=[irep])
    P.dma("sp", rst.t[:], sm["rw_reset"], writes=[rst])
    P.memset("dve", ones, ones.t[:], 1.0); P.memset("dve", onesm, onesm.t[:], 1.0 / 64)
    muv = mu.t[:].rearrange("p (a m) -> p a m", m=2)
    P.tt("dve", c0.t[:], muv[:, :, 0], muv[:, :, 1], ALU.add, [mu], [c0])
    P.ts("dve", c0.t[:], c0.t[:], -1.0, 1.0, ALU.mult, ALU.add, [c0], [c0])
    mwv = mulw.t[:].rearrange("p (a m) -> p a m", m=2)
    P.tt("dve", c0w.t[:], mwv[:, :, 0], mwv[:, :, 1], ALU.add, [mulw], [c0w])
    P.ts("dve", c0w.t[:], c0w.t[:], -1.0, 1.0, ALU.mult, ALU.add, [c0w], [c0w])
    P.tt("dve", c0g.t[:], mulg.t[:, 0:1], mulg.t[:, 1:2], ALU.add, [mulg], [c0g])
    P.ts("dve", c0g.t[:], c0g.t[:], -1.0, 1.0, ALU.mult, ALU.add, [c0g], [c0g])
    hpv = hp.t[:].rearrange("p (h c) -> p h c", c=11)
    for h in range(8):
        for d in range(2):
            P.ts("dve", omk.t[:, h * 2 + d:h * 2 + d + 1], hpv[:, h, 5 + 4 * d:6 + 4 * d], -1.0, 1.0, ALU.mult, ALU.add, [hp], [omk])
    zcol = self.epsr.t[0:64, 3:4]
    names = ["zr", "zk", "zv", "zw", "za"]
    Zs = [{n: sb([64, W + 2], n) for n in names} for _ in range(2)]
    zgs = [sb([128, W + 2], "zg") for _ in range(2)]
    gl = sb([128, W], "gl")
    unit = [0]
    Tl = {n: sb([64, W], n) for n in ["r", "k", "v", "wl", "al", "kk", "sg", "a", "kd", "b", "t1", "bon", "Pf", "E", "Sf", "X",
                                      "eI", "eX", "eN", "eT", "at", "bt", "kt", "rt", "bh", "kh", "Y", "yf", "bf"]}
    NCH_ = W // CK
    TM = [sb([CK, NCH_ * 64], "TM%d" % i) for i in range(4)]
    GM = [sb([CK, W], "GM%d" % i) for i in range(5)]
    TT_ = [sb([CK, W], "TT%d" % i) for i in range(2)]
    PP = [(sb([CK, W], "PPa%d" % i), sb([CK, W], "PPb%d" % i)) for i in range(2)]
    X1 = sb([CK, NCH_ * 64], "X1"); AHT = sb([64, W], "AHT"); AHN = sb([CK, NCH_ * 64], "AHN"); U0 = sb([CK, NCH_ * 64], "U0")
    MT = sb([64, NCH_ * 64], "MT"); CC = sb([64, NCH_ * 64], "CC")
    tmr = Ring([sb([64, 256], "tm") for _ in range(2)])
    gmr = Ring([sb([64, 320], "gm") for _ in range(2)])
    p2r = Ring([sb([64, 128], "p2") for _ in range(3)])
    ttr = Ring([sb([64, 64], "tt") for _ in range(3)])
    x1r = Ring([sb([64, 64], "x1") for _ in range(2)])
    u0r = Ring([sb([64, 64], "u0") for _ in range(2)])
    ahr = Ring([sb([64, 64], "ah") for _ in range(2)])
    dgr = Ring([sb([64, 64], "dg") for _ in range(2)])
    ur = Ring([sb([CK, 64], "u") for _ in range(2)])
    str_ = Ring([sb([64, 64], "st") for _ in range(3)])
    yo = S.sb([64, W], BF16, "yo")
    pss = Ring([S.ps([128, 512], F32, "rps") for _ in range(6)])
    psU_ = S.ps([128, 512], F32, "rpsU")
    psY_ = S.ps([128, 512], F32, "rpsY")
    idn = self.ident.t[0:64, 0:64]

    def shift(dst, src, c0c, m0c, m1c, np_=64):
        P.act(dst.t[:, :], src.t[:, 1:W + 1], AF.Copy, [src], [dst], scale=c0c)
        P.stt("dve", dst.t[:, :], src.t[:, 0:W], m0c, dst.t[:, :], ALU.mult, ALU.add, [src, dst], [dst])
        P.stt("dve", dst.t[:, :], src.t[:, 2:W + 2], m1c, dst.t[:, :], ALU.mult, ALU.add, [src, dst], [dst])

    for si, L in enumerate(self.seqs):
        t0 = self.starts[si]
        nseg = L // SEG
        for h in range(8):
            cc, pb = h // 2, (h % 2) * 64
            for d in range(2):
                st = str_.next()
                P.memset("dve", st, st.t[:], 0.0)
                for sgi in (range(nseg) if d == 0 else range(nseg - 1, -1, -1)):
                    s0 = t0 + sgi * SEG
                    Z = Zs[unit[0] % 2]
                    zg = zgs[unit[0] % 2]
                    unit[0] += 1
                    lo = 0 if sgi > 0 else 1
                    hi = W + 2 if sgi < nseg - 1 else W + 1
                    srcs = {"zr": (cc, pb), "zk": (4 + cc, pb), "zv": (8 + cc, pb), "zw": (12, 0), "za": (12, 64)}
                    for n in names:
                        if lo == 1 or hi == W + 1:
                            P.memset("dve", Z[n], Z[n].t[:], 0.0)
                        c_, p_ = srcs[n]
                        P.dma("sp", Z[n].t[:, lo:hi], self.zA[c_, p_:p_ + 64, s0 - 1 + lo:s0 - 1 + hi], writes=[Z[n]])
                    if lo == 1 or hi == W + 1:
                        P.memset("dve", zg, zg.t[:], 0.0)
                    P.dma("sp", zg.t[:, lo:hi], self.zA[13, :, s0 - 1 + lo:s0 - 1 + hi], writes=[zg])
                    for qi, (dn, sn) in enumerate((("r", "zr"), ("k", "zk"), ("v", "zv"))):
                        ix = h * 3 + qi
                        shift(Tl[dn], Z[sn], c0.t[:, ix:ix + 1], mu.t[:, 2 * ix:2 * ix + 1], mu.t[:, 2 * ix + 1:2 * ix + 2])
                    shift(Tl["wl"], Z["zw"], c0w.t[:, 0:1], mulw.t[:, 0:1], mulw.t[:, 1:2])
                    shift(Tl["al"], Z["za"], c0w.t[:, 1:2], mulw.t[:, 2:3], mulw.t[:, 3:4])
                    P.ts("dve", gl.t[:, :], zg.t[:, 1:W + 1], c0g.t[:, 0:1], None, ALU.mult, None, [zg], [gl])
                    P.stt("dve", gl.t[:, :], zg.t[:, 0:W], mulg.t[:, 0:1], gl.t[:, :], ALU.mult, ALU.add, [zg, gl], [gl])
                    P.stt("dve", gl.t[:, :], zg.t[:, 2:W + 2], mulg.t[:, 1:2], gl.t[:, :], ALU.mult, ALU.add, [zg, gl], [gl])
                    r, k, v, kk, sg, a, kd, b, t1 = (Tl[n] for n in ("r", "k", "v", "kk", "sg", "a", "kd", "b", "t1"))
                    P.act(kk.t[:], k.t[:], AF.Copy, [k, hp], [kk], scale=hpv[:, h, 0:1])
                    P.tt("pool", t1.t[:], kk.t[:], kk.t[:], ALU.mult, [kk], [t1])
                    for blk in range(W // 512):
                        bs = slice(blk * 512, blk * 512 + 512)
                        ps = pss.next()
                        P.mm(ps, ps.t[0:64, 0:512], ones.t[:], t1.t[:, bs], [ones, t1])
                        P.act(Tl["X"].t[:, bs], ps.t[0:64, 0:512], AF.Sqrt, [ps, self.epsr], [Tl["X"]], bias=zcol)
                    P.ts("dve", Tl["X"].t[:], Tl["X"].t[:], 1e-12, None, ALU.max, None, [Tl["X"]], [Tl["X"]])
                    P.op("dve", lambda g: g.reciprocal(out=Tl["X"].t[:], in_=Tl["X"].t[:]), [Tl["X"]], [Tl["X"]])
                    P.tt("dve", kk.t[:], kk.t[:], Tl["X"].t[:], ALU.mult, [kk, Tl["X"]], [kk])
                    P.act(Tl["wl"].t[:], Tl["wl"].t[:], AF.Tanh, [Tl["wl"]], [Tl["wl"]])
                    for blk in range(W // 512):
                        bs = slice(blk * 512, blk * 512 + 512)
                        ps = pss.next()
                        P.mm(ps, ps.t[0:64, 0:512], w2.t[:, d, h * 64:h * 64 + 64], Tl["wl"].t[:, bs], [w2, Tl["wl"]])
                        P.act(sg.t[:, bs], ps.t[0:64, 0:512], AF.Sigmoid, [ps, hp], [sg], bias=hpv[:, h, 3 + 4 * d:4 + 4 * d])
                        ps = pss.next()
                        P.mm(ps, ps.t[0:64, 0:512], a2.t[:, d, h * 64:h * 64 + 64], Tl["al"].t[:, bs], [a2, Tl["al"]])
                        P.act(a.t[:, bs], ps.t[0:64, 0:512], AF.Sigmoid, [ps, hp], [a], bias=hpv[:, h, 4 + 4 * d:5 + 4 * d])
                    P.ts("dve", kd.t[:], a.t[:], hpv[:, h, 5 + 4 * d:6 + 4 * d], omk.t[:, h * 2 + d:h * 2 + d + 1], ALU.mult, ALU.add, [a, hp, omk], [kd])
                    P.tt("pool", kd.t[:], kd.t[:], k.t[:], ALU.mult, [kd, k], [kd])
                    P.tt("pool", b.t[:], kk.t[:], a.t[:], ALU.mult, [kk, a], [b])
                    P.stt("dve", t1.t[:], r.t[:], hpv[:, h, 6 + 4 * d:7 + 4 * d], kd.t[:], ALU.mult, ALU.mult, [r, hp, kd], [t1])
                    bon = Tl["bon"]
                    for blk in range(W // 512):
                        bs = slice(blk * 512, blk * 512 + 512)
                        ps = pss.next()
                        P.mm(ps, ps.t[0:64, 0:512], ones.t[:], t1.t[:, bs], [ones, t1])
                        P.tt("dve", bon.t[:, bs], ps.t[0:64, 0:512], v.t[:, bs], ALU.mult, [ps, v], [bon])
                    Pf, E, Sf, X = Tl["Pf"], Tl["E"], Tl["Sf"], Tl["X"]
                    P.op("dve", lambda g: g.tensor_tensor_scan(out=Pf.t[:], data0=rst.t[:, 0:W], data1=sg.t[:], initial=0.0,
                                                                op0=ALU.mult, op1=ALU.add), [rst, sg], [Pf])
                    P.tt("pool", E.t[:], Pf.t[:], sg.t[:], ALU.subtract, [Pf, sg], [E])
                    for j in range(W // CK):
                        P.ts("dve", Sf.t[:, CK * j:CK * j + CK], E.t[:, CK * j:CK * j + CK], -1.0, Pf.t[:, CK * j + CK - 1:CK * j + CK],
                             ALU.mult, ALU.add, [E, Pf], [Sf])
                    P.tt("pool", X.t[:], Sf.t[:], sg.t[:], ALU.subtract, [Sf, sg], [X])
                    Gi, Ge, Tm = (Pf, E, X) if d == 0 else (Sf, X, E)
                    eI, eX, eN, eT = Tl["eI"], Tl["eX"], Tl["eN"], Tl["eT"]
                    P.act(eI.t[:], Gi.t[:], AF.Exp, [Gi], [eI], scale=-CDEC)
                    P.act(eX.t[:], Ge.t[:], AF.Exp, [Ge], [eX], scale=-CDEC)
                    P.act(eN.t[:], Gi.t[:], AF.Exp, [Gi], [eN], scale=CDEC)
                    P.act(eT.t[:], Tm.t[:], AF.Exp, [Tm], [eT], scale=-CDEC)
                    at, bt, kt, rt, bh, kh = (Tl[n] for n in ("at", "bt", "kt", "rt", "bh", "kh"))
                    P.stt("dve", at.t[:], kk.t[:], -1.0, eX.t[:], ALU.mult, ALU.mult, [kk, eX], [at])
                    P.tt("pool", bt.t[:], b.t[:], eN.t[:], ALU.mult, [b, eN], [bt])
                    P.tt("dve", kt.t[:], kd.t[:], eN.t[:], ALU.mult, [kd, eN], [kt])
                    P.tt("pool", rt.t[:], r.t[:], eI.t[:], ALU.mult, [r, eI], [rt])
                    P.tt("dve", bh.t[:], b.t[:], eT.t[:], ALU.mult, [b, eT], [bh])
                    P.tt("pool", kh.t[:], kd.t[:], eT.t[:], ALU.mult, [kd, eT], [kh])
                    Y = Tl["Y"]
                    NCH = W // CK
                    order = list(range(NCH)) if d == 0 else list(range(NCH - 1, -1, -1))
                    cs_ = lambda j: slice(CK * j, CK * j + CK)
                    ks_ = lambda j: slice(64 * j, 64 * j + 64)
                    idf = self.ident.t[:]
                    tmq = []
                    for qi, src in enumerate((at, bh, kh, v)):
                        ps = pss.next()
                        for j in range(NCH):
                            P.tr(ps, ps.t[0:CK, ks_(j)], src.t[:, cs_(j)], idn, [src, self.ident])
                        tq = TM[qi]
                        P.cp("act" if qi % 2 == 0 else "dve", tq.t[:], ps.t[0:CK, 0:NCH * 64], [ps], [tq])
                        tmq.append(tq)
                    Atm, Bhtm, Khtm, Vtm = tmq
                    gq = []
                    for qi, (lt, rh, mi) in enumerate(((bt, at, 0), (bt, rt, 1), (kt, at, 0), (kt, rt, 1), (at, bt, 2))):
                        ps = pss.next()
                        for j in range(NCH):
                            P.mm(ps, ps.t[0:CK, cs_(j)], lt.t[:, cs_(j)], rh.t[:, cs_(j)], [lt, rh])
                        gt_ = GM[qi]
                        P.tt("dve", gt_.t[:], ps.t[0:CK, 0:W], mk.t[:, d, mi, 0:W], ALU.mult, [ps, mk], [gt_])
                        gq.append(gt_)
                    Aab, Abr, Aak, Akr, NT = gq
                    Tt = TT_[0]
                    P.tt("pool", Tt.t[:], Aab.t[:], irep.t[:, 0:W], ALU.add, [Aab, irep], [Tt])
                    Pm, PTm = Aab, NT
                    NLEV = 6 if CK == 128 else 5
                    for lev in range(NLEV):
                        Pn, PTn = PP[lev % 2]
                        ps2 = pss.next()
                        for j in range(NCH):
                            P.mm(ps2, ps2.t[0:CK, cs_(j)], Pm.t[:, cs_(j)], PTm.t[:, cs_(j)], [PTm, Pm])
                        if lev < NLEV - 1:
                            ps1 = pss.next()
                            for j in range(NCH):
                                P.mm(ps1, ps1.t[0:CK, cs_(j)], PTm.t[:, cs_(j)], Pm.t[:, cs_(j)], [PTm, Pm])
                            P.cp("act", Pn.t[:], ps1.t[0:CK, 0:W], [ps1], [Pn])
                        P.cp("dve", PTn.t[:], ps2.t[0:CK, 0:W], [ps2], [PTn])
                        Pm, PTm = Pn, PTn
                        ps3 = pss.next()
                        for j in range(NCH):
                            P.mm(ps3, ps3.t[0:CK, cs_(j)], PTm.t[:, cs_(j)], Tt.t[:, cs_(j)], [PTm, Tt])
                        Tn = TT_[(lev + 1) % 2]
                        P.tt("dve", Tn.t[:], ps3.t[0:CK, 0:W], Tt.t[:], ALU.add, [ps3, Tt], [Tn])
                        Tt = Tn
                    ps = pss.next()
                    for j in range(NCH):
                        P.mm(ps, ps.t[0:CK, ks_(j)], Aak.t[:, cs_(j)], Vtm.t[:, ks_(j)], [Aak, Vtm])
                    P.cp("act", X1.t[:], ps.t[0:CK, 0:NCH * 64], [ps], [X1])
                    psa = pss.next()
                    for j in range(NCH):
                        P.mm(psa, psa.t[0:64, cs_(j)], Atm.t[:, ks_(j)], Tt.t[:, cs_(j)], [Atm, Tt])
                    P.cp("dve", AHT.t[:], psa.t[0:64, 0:W], [psa], [AHT])
                    psb = pss.next()
                    for j in range(NCH):
                        P.mm(psb, psb.t[0:CK, ks_(j)], Tt.t[:, cs_(j)], Atm.t[:, ks_(j)], [Atm, Tt])
                    P.cp("act", AHN.t[:], psb.t[0:CK, 0:NCH * 64], [psb], [AHN])
                    ps = pss.next()
                    for j in range(NCH):
                        P.mm(ps, ps.t[0:CK, ks_(j)], Tt.t[:, cs_(j)], X1.t[:, ks_(j)], [Tt, X1])
                    P.cp("dve", U0.t[:], ps.t[0:CK, 0:NCH * 64], [ps], [U0])
                    ps = pss.next()
                    for j in range(NCH):
                        P.mm(ps, ps.t[0:64, ks_(j)], AHN.t[:, ks_(j)], Bhtm.t[:, ks_(j)], [AHN, Bhtm])
                    for j in range(NCH):
                        gcol = CK * j + CK - 1 if d == 0 else CK * j
                        P.stt("dve", MT.t[:, ks_(j)], idn, eI.t[:, gcol:gcol + 1], ps.t[0:64, ks_(j)], ALU.mult, ALU.add,
                              [self.ident, eI, ps], [MT])
                    ps = pss.next()
                    for j in range(NCH):
                        P.mm(ps, ps.t[0:64, ks_(j)], Bhtm.t[:, ks_(j)], U0.t[:, ks_(j)], [Bhtm, U0], start=(j == 0), stop=False, sgc=True)
                        P.mm(ps, ps.t[0:64, ks_(j)], Khtm.t[:, ks_(j)], Vtm.t[:, ks_(j)], [Khtm, Vtm], start=False, stop=(j == NCH - 1), sgc=True)
                    P.cp("act", CC.t[:], ps.t[0:64, 0:NCH * 64], [ps], [CC])
                    psU = psU_
                    psY = psY_
                    for ji, j in enumerate(order):
                        P.mm(psU, psU.t[0:CK, ks_(j)], AHT.t[:, cs_(j)], st.t[:], [AHT, st], start=(ji == 0), stop=False, sgc=True)
                        P.mm(psU, psU.t[0:CK, ks_(j)], idf[0:CK, 0:CK], U0.t[:, ks_(j)], [self.ident, U0], start=False, stop=True, sgc=True)
                        u = ur.next()
                        P.cp("act", u.t[:], psU.t[0:CK, ks_(j)], [psU], [u])
                        P.mm(psY, psY.t[0:64, cs_(j)], st.t[:], rt.t[:, cs_(j)], [st, rt], start=(ji == 0), stop=False, sgc=True)
                        P.mm(psY, psY.t[0:64, cs_(j)], u.t[:], Abr.t[:, cs_(j)], [u, Abr], start=False, stop=False, sgc=True)
                        P.mm(psY, psY.t[0:64, cs_(j)], Vtm.t[:, ks_(j)], Akr.t[:, cs_(j)], [Vtm, Akr], start=False, stop=True, sgc=True)
                        psS = pss.next()
                        P.mm(psS, psS.t[0:64, 0:64], MT.t[:, ks_(j)], st.t[:], [MT, st])
                        stn = str_.next()
                        P.tt("dve", stn.t[:], psS.t[0:64, 0:64], CC.t[:, ks_(j)], ALU.add, [psS, CC], [stn])
                        st = stn
                    P.cp("act", Y.t[:], psY.t[0:64, 0:W], [psY], [Y])
                    if d == 0:
                        P.dma("pool", self.yF[0, h, :, s0:s0 + W], Y.t[:], reads=[Y], writes=[yfb])
                        P.dma("pool", self.yF[1, h, :, s0:s0 + W], bon.t[:], reads=[bon], writes=[yfb])
                    else:
                        yf, bf = Tl["yf"], Tl["bf"]
                        P.dma("sp", yf.t[:], self.yF[0, h, :, s0:s0 + W], reads=[yfb], writes=[yf])
                        P.dma("sp", bf.t[:], self.yF[1, h, :, s0:s0 + W], reads=[yfb], writes=[bf])
                        P.tt("dve", Y.t[:], Y.t[:], yf.t[:], ALU.add, [Y, yf], [Y])
                        P.tt("pool", bon.t[:], bon.t[:], bf.t[:], ALU.add, [bon, bf], [bon])
                        P.act(gl.t[:], gl.t[:], AF.Sigmoid, [gl], [gl])
                        for blk in range(W // 512):
                            bs = slice(blk * 512, blk * 512 + 512)
                            ps = pss.next()
                            P.mm(ps, ps.t[0:64, 0:512], onesm.t[:], Y.t[:, bs], [onesm, Y])
                            P.tt("dve", Y.t[:, bs], Y.t[:, bs], ps.t[0:64, 0:512], ALU.subtract, [Y, ps], [Y])
                            P.tt("pool", t1.t[:, bs], Y.t[:, bs], Y.t[:, bs], ALU.mult, [Y], [t1])
                            ps = pss.next()
                            P.mm(ps, ps.t[0:64, 0:512], onesm.t[:], t1.t[:, bs], [onesm, t1])
                            P.act(t1.t[:, bs], ps.t[0:64, 0:512], AF.Sqrt, [ps, self.epsr], [t1], bias=self.epsr.t[0:64, 2:3])
                            P.op("dve", lambda g, bs=bs: g.reciprocal(out=t1.t[:, bs], in_=t1.t[:, bs]), [t1], [t1])
                            P.tt("dve", Y.t[:, bs], Y.t[:, bs], t1.t[:, bs], ALU.mult, [Y, t1], [Y])
                            P.ts("dve", Y.t[:, bs], Y.t[:, bs], hpv[:, h, 1:2], hpv[:, h, 2:3], ALU.mult, ALU.add, [Y, hp], [Y])
                            P.tt("pool", Y.t[:, bs], Y.t[:, bs], bon.t[:, bs], ALU.add, [Y, bon], [Y])
                            ps = pss.next()
                            P.mm(ps, ps.t[0:64, 0:512], g2.t[:, h * 64:h * 64 + 64], gl.t[:, bs], [g2, gl])
                            P.tt("dve", yo.t[:, bs], Y.t[:, bs], ps.t[0:64, 0:512], ALU.mult, [Y, ps], [yo])
                        P.dma("pool", self.yM[cc, pb:pb + 64, s0:s0 + W], yo.t[:], reads=[yo])
    S.close()


Builder.mix_rwkv = mix_rwkv
```

```python
import math
from contextlib import ExitStack
import numpy as np
import concourse.bass as bass
import concourse.mybir as mybir
from concourse.bass_utils import run_bass_kernel_spmd

F32 = mybir.dt.float32
BF16 = mybir.dt.bfloat16
AF = mybir.ActivationFunctionType
ALU = mybir.AluOpType
AX = mybir.AxisListType

D = 1024
DFF = 2816
KC = 8
FC = 22
DEPTH = 2
NCORES = 8
RMS_EPS = 1e-6
SUBLN_EPS = 1e-5
LNX_EPS = 64e-5
NWIN = 52
CK = 128


class Buf:
    __slots__ = ("w", "r", "psum")

    def __init__(self):
        self.w = None
        self.r = {}
        self.psum = False


class Tile:
    def __init__(self, t, buf=None):
        self.t = t
        self.buf = buf if buf is not None else Buf()

    def __getitem__(self, k):
        return self.t[k]


class Prog:
    ENG = ("pe", "dve", "act", "pool", "sp")
    NDS = 24

    def __init__(self, nc):
        self.nc = nc
        self.eng = {"pe": nc.tensor, "dve": nc.vector, "act": nc.scalar, "pool": nc.gpsimd, "sp": nc.sync}
        self.sem = {}
        for e in self.ENG:
            self.sem[e] = nc.semaphore("s_" + e).__enter__()
        for i in range(self.NDS):
            self.sem[("d", i)] = nc.semaphore("d%d" % i).__enter__()
            self.sem[("g", i)] = nc.semaphore("g%d" % i).__enter__()
        self.gnext = 0
        self.cnt = {k: 0 for k in self.sem}
        self.waited = {e: {} for e in self.ENG}
        self.dnext = 0
        self.ninst = 0

    def _need(self, reads, writes, e=None):
        need = {}
        for b in reads:
            b = b.buf if isinstance(b, Tile) else b
            if b.w is not None and need.get(b.w[0], 0) < b.w[1]:
                need[b.w[0]] = b.w[1]
            if b.psum:
                for k, v in b.r.items():
                    if k != e and need.get(k, 0) < v:
                        need[k] = v
        for b in writes:
            b = b.buf if isinstance(b, Tile) else b
            if b.w is not None and need.get(b.w[0], 0) < b.w[1] and not (self.RELAX and b.w[0] == e):
                need[b.w[0]] = b.w[1]
            for k, v in b.r.items():
                if need.get(k, 0) < v and not (self.RELAX and k == e):
                    need[k] = v
        return need

    RELAX = True

    SELF_SYNC = True

    def _wait(self, e, need, skip_self=False):
        eng = self.eng[e]
        wd = self.waited[e]
        for k, v in need.items():
            if (skip_self or not self.SELF_SYNC) and k == e:
                continue
            if wd.get(k, 0) >= v:
                continue
            eng.wait_ge(self.sem[k], v)
            wd[k] = v
            self.ninst += 1

    def _mark(self, ev, reads, writes):
        for b in reads:
            b = b.buf if isinstance(b, Tile) else b
            if b.r.get(ev[0], 0) < ev[1]:
                b.r[ev[0]] = ev[1]
        for b in writes:
            b = b.buf if isinstance(b, Tile) else b
            b.w = ev
            b.r = {}

    def op(self, e, fn, reads=(), writes=(), skip_self=False):
        self._wait(e, self._need(reads, writes, e), skip_self)
        ins = fn(self.eng[e])
        ins.then_inc(self.sem[e], 1)
        self.cnt[e] += 1
        self.ninst += 1
        self._mark((e, self.cnt[e]), reads, writes)

    def dma(self, q, out, in_, reads=(), writes=()):
        self._wait(q, self._need(reads, writes, None))
        if q == "pool":
            k = ("g", self.gnext)
            self.gnext = (self.gnext + 1) % self.NDS
        else:
            k = ("d", self.dnext)
            self.dnext = (self.dnext + 1) % self.NDS
        self.eng[q].dma_start(out=out, in_=in_).then_inc(self.sem[k], 16)
        self.cnt[k] += 16
        self.ninst += 1
        self._mark((k, self.cnt[k]), reads, writes)

    def barrier(self):
        for e in self.ENG:
            self._wait(e, dict(self.cnt))

    def mm(self, out_t, out_ap, lhsT_ap, rhs_ap, reads, start=True, stop=True, sgc=False):
        if sgc:
            self.op("pe", lambda g: g.matmul(out_ap, lhsT=lhsT_ap, rhs=rhs_ap, start=start, stop=stop, skip_group_check=True),
                    reads=reads, writes=[out_t], skip_self=True)
        else:
            self.op("pe", lambda g: g.matmul(out_ap, lhsT=lhsT_ap, rhs=rhs_ap, start=start, stop=stop),
                    reads=reads, writes=[out_t], skip_self=True)

    def tr(self, out_t, out_ap, in_ap, ident_ap, reads):
        self.op("pe", lambda g: g.transpose(out_ap, in_ap, ident_ap), reads=reads, writes=[out_t], skip_self=True)

    def act(self, out_ap, in_ap, func, reads, writes, bias=None, scale=1.0, accum=None):
        kw = {}
        if bias is not None:
            kw["bias"] = bias
        if accum is not None:
            kw["accum_out"] = accum
        self.op("act", lambda g: g.activation(out=out_ap, in_=in_ap, func=func, scale=scale, **kw),
                reads=reads, writes=writes)

    def tt(self, e, out_ap, a_ap, b_ap, op, reads, writes):
        self.op(e, lambda g: g.tensor_tensor(out=out_ap, in0=a_ap, in1=b_ap, op=op), reads=reads, writes=writes)

    def stt(self, e, out_ap, a_ap, scalar, b_ap, op0, op1, reads, writes):
        self.op(e, lambda g: g.scalar_tensor_tensor(out=out_ap, in0=a_ap, scalar=scalar, in1=b_ap, op0=op0, op1=op1),
                reads=reads, writes=writes)

    def ts(self, e, out_ap, a_ap, s1, s2, op0, op1, reads, writes):
        if s2 is None:
            self.op(e, lambda g: g.tensor_scalar(out=out_ap, in0=a_ap, scalar1=s1, scalar2=None, op0=op0),
                    reads=reads, writes=writes)
        else:
            self.op(e, lambda g: g.tensor_scalar(out=out_ap, in0=a_ap, scalar1=s1, scalar2=s2, op0=op0, op1=op1),
                    reads=reads, writes=writes)

    def cp(self, e, out_ap, in_ap, reads, writes):
        if e == "act":
            self.op(e, lambda g: g.copy(out=out_ap, in_=in_ap), reads=reads, writes=writes)
        else:
            self.op(e, lambda g: g.tensor_copy(out=out_ap, in_=in_ap), reads=reads, writes=writes)

    def memset(self, e, t, ap, val):
        self.op(e, lambda g: g.memset(ap, val), reads=(), writes=[t])


class Pool_:
    def __init__(self, nc):
        self.nc = nc
        self.st = ExitStack()
        self.n = 0

    def sb(self, shape, dt, name=None):
        self.n += 1
        return Tile(self.st.enter_context(self.nc.sbuf_tensor("%s_%d" % (name or "t", id(self) % 100000 * 1000 + self.n), list(shape), dt)))

    def ps(self, shape, dt=F32, name=None):
        self.n += 1
        return Tile(self.st.enter_context(self.nc.psum_tensor("%s_%d" % (name or "p", id(self) % 100000 * 1000 + self.n), list(shape), dt)))

    def close(self):
        self.st.close()


class Ring:
    def __init__(self, tiles):
        self.tiles = tiles
        self.i = 0

    def next(self):
        t = self.tiles[self.i % len(self.tiles)]
        self.i += 1
        return t


def fm_pieces(W):
    K, N = W.shape
    return np.ascontiguousarray(W.reshape(K // 128, 128, N // 128, 128).transpose(2, 1, 0, 3)).reshape(N // 128, 128, K)


def pcol(v, nchunk):
    return np.ascontiguousarray(np.asarray(v, np.float32).reshape(nchunk, 128).T)


def host_weights(inp):
    f = lambda a: np.asarray(a, np.float32)
    out = {}
    gu1, d1, gu2, d2, win, wv, pabc, wout = [], [], [], [], [], [], [], []
    for l in range(DEPTH):
        for (gl, dl, pre) in ((gu1, d1, "ffn1"), (gu2, d2, "ffn2")):
            g = fm_pieces(f(inp[pre + "_w_gate"][l]))
            u = fm_pieces(f(inp[pre + "_w_up"][l]))
            gl.append(np.stack([g, u], axis=1).reshape(2 * FC, 128, D))
            dl.append(fm_pieces(f(inp[pre + "_w_down"][l])))
        W = f(inp["w_in"][l])
        cols = [W[:, 0:1792], W[:, 1792:2304]]
        for g in range(8):
            blk = np.zeros((D, 128), np.float32)
            blk[:, (g % 4) * 32:(g % 4) * 32 + 32] = W[:, 2560 + g * 32:2560 + g * 32 + 32]
            cols.append(blk)
        cols += [W[:, 2816:3072], W[:, 3328:6400]]
        win.append(fm_pieces(np.concatenate(cols, axis=1)))
        Wv = np.concatenate([W[:, 2304:2560], W[:, 3072:3328]], axis=1)
        wv.append(np.ascontiguousarray(Wv.reshape(KC, 128, 512).transpose(1, 0, 2)).reshape(128, KC * 512))
        pabc.append(fm_pieces(np.concatenate([f(inp["p_a"][l]), f(inp["p_b"][l]), f(inp["p_c"][l])], axis=0)))
        wout.append(fm_pieces(f(inp["w_out"][l])))
    out["wgu1"] = np.stack(gu1); out["wd1"] = np.stack(d1)
    out["wgu2"] = np.stack(gu2); out["wd2"] = np.stack(d2)
    out["win"] = np.stack(win); out["wv"] = np.stack(wv)
    out["wpabc"] = np.stack(pabc); out["wout"] = np.stack(wout)
    gains = []
    for l in range(DEPTH):
        gains += [pcol(inp["ln_ffn1_g"][l], KC), pcol(inp["ln_mix_g"][l], KC), pcol(inp["ln_ffn2_g"][l], KC)]
    gains.append(pcol(inp["final_g"], KC))
    out["gains"] = np.concatenate(gains, axis=1)
    return out


WSHAPES = {"wgu1": [DEPTH, 2 * FC, 128, D], "wd1": [DEPTH, KC, 128, DFF], "wgu2": [DEPTH, 2 * FC, 128, D],
           "wd2": [DEPTH, KC, 128, DFF], "win": [DEPTH, NWIN, 128, D], "wv": [DEPTH, 128, KC * 512],
           "wpabc": [DEPTH, KC, 128, D], "wout": [DEPTH, KC, 128, D]}


def host_consts():
    c = {}
    c["ident"] = np.eye(128, dtype=np.float32)
    c["onesm"] = np.full((128, 128), 1.0 / D, np.float32)
    return c


CSHAPES = {"ident": [128, 128], "onesm": [128, 128]}


_UID = [0]


def _uid(prefix):
    _UID[0] += 1
    return "%s%d" % (prefix, _UID[0])


class Scope:
    def __init__(self, nc):
        self.nc = nc
        self.st = ExitStack()

    def sb(self, shape, dt, name="t"):
        return Tile(self.st.enter_context(self.nc.sbuf_tensor(_uid(name), list(shape), dt)))

    def ps(self, shape, dt=F32, name="p"):
        t = Tile(self.st.enter_context(self.nc.psum_tensor(_uid(name), list(shape), dt)))
        t.buf.psum = True
        return t

    def close(self):
        self.st.close()


class WStream:
    def __init__(self, B, slots, plan):
        self.B = B
        self.slots = slots
        self.plan = plan
        self.loaded = 0
        self.pos = 0

    def _load(self, i):
        src, G, X = self.plan[i]
        slot = self.slots[i % len(self.slots)]
        if G == 0:
            self.B.P.dma("sp", slot.t[:, 0:X], src, writes=[slot])
        else:
            self.B.P.dma("sp", slot.t[:, 0:G * X].rearrange("p (g x) -> p g x", g=G),
                         src.rearrange("g p x -> p g x"), writes=[slot])

    def get(self):
        while self.loaded < len(self.plan) and self.loaded < self.pos + len(self.slots):
            self._load(self.loaded)
            self.loaded += 1
        slot = self.slots[self.pos % len(self.slots)]
        self.pos += 1
        return slot


class Builder:
    def __init__(self, seqs, debug=None):
        self.seqs = list(seqs)
        self.T = sum(self.seqs)
        self.starts = [sum(self.seqs[:i]) for i in range(len(self.seqs))]
        self.TT = 1024 if self.T % 1024 == 0 else 512
        self.NS = self.TT // 512
        self.debug = debug or {}
        nc = bass.Bass("TRN2", target_bir_lowering=False)
        self.nc = nc
        self.P = Prog(nc)
        T = self.T
        dt_in = lambda name, shape: nc.dram_tensor(name, list(shape), F32, kind="ExternalInput").ap()
        self.xin = dt_in("xin", [T, D])
        self.yout = nc.dram_tensor("yout", [T, D], F32, kind="ExternalOutput").ap()
        self.wf = {k: dt_in(k, s) for k, s in WSHAPES.items()}
        self.wb = {k: nc.dram_tensor(k + "_b", list(s), BF16).ap() for k, s in WSHAPES.items()}
        self.cst = {k: dt_in("c_" + k, s) for k, s in CSHAPES.items()}
        self.gains_d = dt_in("gains", [128, 56])
        self.small = {k: dt_in(k, s) for k, s in SMALL_SHAPES.items()}
        scr = lambda name, shape, dt: nc.dram_tensor(name, list(shape), dt).ap()
        self.xT = scr("xT", [KC, 128, T], F32)
        self.zA = scr("zA", [14, 128, T], F32)
        self.zNq = scr("zNq", [2, 128, T], BF16)
        self.zNk = scr("zNk", [2, 128, T], BF16)
        self.zDq = scr("zDq", [8, 128, T], BF16)
        self.zDk = scr("zDk", [2, 128, T], BF16)
        self.zV = scr("zV", [T, 520], BF16)
        self.zG = scr("zG", [24, 128, T], BF16)
        self.yM = scr("yM", [KC, 128, T], BF16)
        self.dbg = {}
        for k, s in self.debug.items():
            if not isinstance(s, (list, tuple)):
                continue
            self.dbg[k] = nc.dram_tensor("dbg_" + k, list(s), F32, kind="ExternalOutput").ap()

    def build(self):
        P = self.P
        nc = self.nc
        G = Scope(nc)
        self.G = G
        self.ident = G.sb([128, 128], F32, "ident")
        self.identb = G.sb([128, 128], BF16, "identb")
        self.onesm = G.sb([128, 128], BF16, "onesm")
        self.gains = G.sb([128, 56], F32, "gains")
        self.epsr = G.sb([128, 4], F32, "eps")
        tmp = G.sb([128, 128], F32, "ctmp")
        P.dma("sp", self.ident.t[:], self.cst["ident"], writes=[self.ident])
        P.dma("sp", tmp.t[:], self.cst["onesm"], writes=[tmp])
        P.dma("sp", self.gains.t[:], self.gains_d, writes=[self.gains])
        P.cp("dve", self.identb.t[:], self.ident.t[:], [self.ident], [self.identb])
        P.cp("dve", self.onesm.t[:], tmp.t[:], [tmp], [self.onesm])
        P.memset("dve", self.epsr, self.epsr.t[:, 0:1], RMS_EPS)
        P.memset("dve", self.epsr, self.epsr.t[:, 1:2], SUBLN_EPS)
        P.memset("dve", self.epsr, self.epsr.t[:, 2:3], LNX_EPS)
        P.memset("dve", self.epsr, self.epsr.t[:, 3:4], 0.0)
        self.prep_weights()
        P.barrier()
        for l in range(DEPTH):
            self.phaseA(l)
            P.barrier()
            self.mixers(l)
            P.barrier()
            self.phaseC(l)
            P.barrier()
        G.close()
        return nc

    def prep_weights(self):
        P = self.P
        S = Scope(self.nc)
        CHK = 8192
        st32 = [S.sb([128, CHK], F32, "w32") for _ in range(3)]
        st16 = [S.sb([128, CHK], BF16, "w16") for _ in range(3)]
        i = 0
        for k, shp in WSHAPES.items():
            tot = int(np.prod(shp))
            per = tot // 128
            src = self.wf[k]
            dst = self.wb[k]
            X = shp[-1]
            s2 = src.rearrange("l j p x -> (l j p) x") if len(shp) == 4 else src.rearrange("l p x -> (l p) x")
            d2 = dst.rearrange("l j p x -> (l j p) x") if len(shp) == 4 else dst.rearrange("l p x -> (l p) x")
            rows = tot // X
            gmax = max(1, CHK // X)
            r = 0
            while r < rows:
                g = min(gmax, (rows - r) // 128)
                a32 = st32[i % 3]
                a16 = st16[i % 3]
                P.dma("sp", a32.t[:, 0:g * X].rearrange("p (g x) -> p g x", g=g),
                      s2[r:r + g * 128, :].rearrange("(g p) x -> p g x", p=128), writes=[a32])
                e = ("dve", "act", "pool")[i % 3]
                P.cp(e, a16.t[:, 0:g * X], a32.t[:, 0:g * X], [a32], [a16])
                P.dma("sp", d2[r:r + g * 128, :].rearrange("(g p) x -> p g x", p=128),
                      a16.t[:, 0:g * X].rearrange("p (g x) -> p g x", g=g), reads=[a16])
                r += g * 128
                i += 1
        S.close()

    def rmsnorm(self, x, sq, u, gcol, pss, rstd, out_f32=False):
        P = self.P
        NS = self.NS
        for s in range(NS):
            sl = slice(s * 512, (s + 1) * 512)
            for c in range(KC):
                P.tt("pool", sq.t[:, c, sl], x.t[:, c, sl], x.t[:, c, sl], ALU.mult, [x], [sq])
            ps = pss.next()
            for c in range(KC):
                P.mm(ps, ps.t[:, 0:512], self.onesm.t[:], sq.t[:, c, sl], [sq, self.onesm], start=(c == 0), stop=(c == KC - 1))
            P.act(rstd.t[:, sl], ps.t[:, 0:512], AF.Sqrt, [ps, self.epsr], [rstd], bias=self.epsr.t[:, 0:1])
            P.op("dve", lambda g, sl=sl: g.reciprocal(out=rstd.t[:, sl], in_=rstd.t[:, sl]), [rstd], [rstd])
            for c in range(KC):
                P.stt("dve", u.t[:, c, sl], x.t[:, c, sl], self.gains.t[:, gcol + c:gcol + c + 1], rstd.t[:, sl],
                      ALU.mult, ALU.mult, [x, rstd, self.gains], [u])

    def ffn(self, ws, x, u, h, pss, tmps):
        P = self.P
        NS = self.NS
        for jp in range(FC // 2):
            w = ws.get()
            for jj in range(2):
                j = jp * 2 + jj
                pg = [pss.next() for _ in range(NS)]
                pu = [pss.next() for _ in range(NS)]
                for gi, pp in ((0, pg), (1, pu)):
                    base = (jj * 2 + gi) * D
                    for c in range(KC):
                        for s in range(NS):
                            P.mm(pp[s], pp[s].t[:, 0:512], w.t[:, base + c * 128:base + (c + 1) * 128],
                                 u.t[:, c, s * 512:(s + 1) * 512], [w, u], start=(c == 0), stop=(c == KC - 1))
                for s in range(NS):
                    tm = tmps.next()
                    P.act(tm.t[:, 0:512], pg[s].t[:, 0:512], AF.Silu, [pg[s]], [tm])
                    P.tt("dve", h.t[:, j, s * 512:(s + 1) * 512], tm.t[:, 0:512], pu[s].t[:, 0:512], ALU.mult, [tm, pu[s]], [h])
        for o in range(KC):
            w = ws.get()
            po = [pss.next() for _ in range(NS)]
            for j in range(FC):
                for s in range(NS):
                    P.mm(po[s], po[s].t[:, 0:512], w.t[:, j * 128:(j + 1) * 128], h.t[:, j, s * 512:(s + 1) * 512],
                         [w, h], start=(j == 0), stop=(j == FC - 1))
            for s in range(NS):
                sl = slice(s * 512, (s + 1) * 512)
                P.stt("dve", x.t[:, o, sl], po[s].t[:, 0:512], 0.5, x.t[:, o, sl], ALU.mult, ALU.add, [po[s], x], [x])

    def ffn_plan(self, key_gu, key_d, l):
        plan = []
        for jp in range(FC // 2):
            plan.append((self.wb[key_gu][l, jp * 4:jp * 4 + 4], 4, D))
        for o in range(KC):
            plan.append((self.wb[key_d][l, o:o + 1], 1, DFF))
        return plan

    def phaseA(self, l):
        P = self.P
        nc = self.nc
        TT, NS, T = self.TT, self.NS, self.T
        S = Scope(nc)
        x = S.sb([128, KC, TT], F32, "x")
        u = S.sb([128, KC, TT], BF16, "u")
        h = S.sb([128, FC, TT], BF16, "h")
        rstd = S.sb([128, TT], F32, "rstd")
        slots = [S.sb([128, 4096], BF16, "wslot") for _ in range(4)]
        tmps = Ring([S.sb([128, 512], F32, "tmp") for _ in range(4)])
        stg = Ring([S.sb([128, 1024], F32, "stg") for _ in range(4)])
        pss = Ring([S.ps([128, 512], F32, "ps") for _ in range(8)])
        vst = Ring([S.sb([128, 8, 65], BF16, "vst") for _ in range(3)])
        for v_ in vst.tiles:
            P.memset("pool", v_, v_.t[:], 1.0)
        ntile = T // TT
        plan = []
        for it in range(ntile):
            plan += self.ffn_plan("wgu1", "wd1", l)
            for jp in range(NWIN // 4):
                plan.append((self.wb["win"][l, jp * 4:jp * 4 + 4], 4, D))
            plan.append((self.wb["wv"][l], 0, KC * 512))
        ws = WStream(self, slots, plan)
        for it in range(ntile):
            t0 = it * TT
            if l == 0:
                for b in range(TT // 128):
                    sg = stg.next()
                    P.dma("sp", sg.t[:, 0:D], self.xin[t0 + b * 128:t0 + (b + 1) * 128, :], writes=[sg])
                    for half in range(2):
                        ps = pss.next()
                        for cc in range(4):
                            c = half * 4 + cc
                            P.tr(ps, ps.t[:, cc * 128:(cc + 1) * 128], sg.t[:, c * 128:(c + 1) * 128], self.ident.t[:], [sg, self.ident])
                        e = "act" if half == 0 else "dve"
                        P.cp(e, x.t[:, half * 4:half * 4 + 4, b * 128:(b + 1) * 128],
                             ps.t[:, 0:512].rearrange("p (c t) -> p c t", c=4), [ps], [x])
            else:
                P.dma("sp", x.t[:], self.xT[:, :, t0:t0 + TT].rearrange("c p t -> p c t"), writes=[x])
            self.rmsnorm(x, h, u, (l * 3 + 0) * KC, pss, rstd)
            self.ffn(ws, x, u, h, pss, tmps)
            P.dma("pool", self.xT[:, :, t0:t0 + TT].rearrange("c p t -> p c t"), x.t[:], reads=[x])
            self.rmsnorm(x, h, u, (l * 3 + 1) * KC, pss, rstd)
            for jp in range(NWIN // 4):
                w = ws.get()
                for jj in range(4):
                    j = jp * 4 + jj
                    pp = [pss.next() for _ in range(NS)]
                    for c in range(KC):
                        for s in range(NS):
                            P.mm(pp[s], pp[s].t[:, 0:512], w.t[:, jj * D + c * 128:jj * D + (c + 1) * 128],
                                 u.t[:, c, s * 512:(s + 1) * 512], [w, u], start=(c == 0), stop=(c == KC - 1))
                    sg = stg.next()
                    if j < 14:
                        dst, view = self.zA[j, :, t0:t0 + TT], sg.t[:, 0:TT]
                        for s in range(NS):
                            P.cp("act" if s == 0 else "dve", view[:, s * 512:(s + 1) * 512], pp[s].t[:, 0:512], [pp[s]], [sg])
                    else:
                        view = sg.t[:].bitcast(BF16)[:, 0:TT]
                        if j < 16:
                            dst, sc, fn = self.zNq[j - 14, :, t0:t0 + TT], 0.125, AF.Copy
                        elif j < 18:
                            dst, sc, fn = self.zNk[j - 16, :, t0:t0 + TT], 1.0, AF.Copy
                        elif j < 26:
                            dst, sc, fn = self.zDq[j - 18, :, t0:t0 + TT], 32.0 ** -0.5, AF.Copy
                        elif j < 28:
                            dst, sc, fn = self.zDk[j - 26, :, t0:t0 + TT], 1.0, AF.Copy
                        else:
                            dst, sc, fn = self.zG[j - 28, :, t0:t0 + TT], 1.0, AF.Sigmoid
                        for s in range(NS):
                            if fn == AF.Sigmoid or s == 0:
                                P.act(view[:, s * 512:(s + 1) * 512], pp[s].t[:, 0:512], fn, [pp[s]], [sg], scale=sc)
                            else:
                                P.ts("dve", view[:, s * 512:(s + 1) * 512], pp[s].t[:, 0:512], sc, None, ALU.mult, None, [pp[s]], [sg])
                    P.dma("pool", dst, view, reads=[sg])
            w = ws.get()
            for b in range(TT // 128):
                ps = pss.next()
                for c in range(KC):
                    P.mm(ps, ps.t[:, 0:512], u.t[:, c, b * 128:(b + 1) * 128], w.t[:, c * 512:(c + 1) * 512], [w, u],
                         start=(c == 0), stop=(c == KC - 1))
                sg = vst.next()
                P.cp("act" if b % 2 == 0 else "dve", sg.t[:, :, 0:64], ps.t[:, 0:512].rearrange("p (h d) -> p h d", h=8), [ps], [sg])
                P.dma("pool", self.zV[t0 + b * 128:t0 + (b + 1) * 128, :], sg.t[:].rearrange("p h d -> p (h d)"), reads=[sg])
        S.close()

    def phaseC(self, l):
        P = self.P
        nc = self.nc
        TT, NS, T = self.TT, self.NS, self.T
        last = (l == DEPTH - 1)
        S = Scope(nc)
        x = S.sb([128, KC, TT], F32, "x")
        u = S.sb([128, KC, TT], BF16, "u")
        h = S.sb([128, FC, TT], BF16, "h")
        ym = S.sb([128, KC, TT], BF16, "ym")
        rstd = S.sb([128, TT], F32, "rstd")
        gts = Ring([S.sb([128, 3, TT], BF16, "gt") for _ in range(2)])
        slots = [S.sb([128, 4096], BF16, "wslot") for _ in range(4)]
        tmps = Ring([S.sb([128, 512], F32, "tmp") for _ in range(4)])
        stg = Ring([S.sb([128, 1024], F32, "stg") for _ in range(3)])
        pss = Ring([S.ps([128, 512], F32, "ps") for _ in range(8)])
        ntile = T // TT
        plan = []
        for it in range(ntile):
            plan += [(self.wb["wpabc"][l, 0:4], 4, D), (self.wb["wpabc"][l, 4:8], 4, D),
                     (self.wb["wout"][l, 0:4], 4, D), (self.wb["wout"][l, 4:8], 4, D)]
            plan += self.ffn_plan("wgu2", "wd2", l)
        ws = WStream(self, slots, plan)
        zGv = self.zG.rearrange("(g o) p t -> o p g t", g=3)
        for it in range(ntile):
            t0 = it * TT
            P.dma("sp", x.t[:], self.xT[:, :, t0:t0 + TT].rearrange("c p t -> p c t"), writes=[x])
            P.dma("sp", ym.t[:], self.yM[:, :, t0:t0 + TT].rearrange("c p t -> p c t"), writes=[ym])
            for op_ in range(2):
                w = ws.get()
                for oo in range(4):
                    o = op_ * 4 + oo
                    gt = gts.next()
                    P.dma("sp", gt.t[:], zGv[o, :, :, t0:t0 + TT], writes=[gt])
                    for s in range(NS):
                        sl = slice(s * 512, (s + 1) * 512)
                        pa, pb, pc = pss.next(), pss.next(), pss.next()
                        for (pp, c0, c1) in ((pa, 0, 4), (pb, 4, 6), (pc, 6, 8)):
                            for c in range(c0, c1):
                                P.mm(pp, pp.t[:, 0:512], w.t[:, oo * D + c * 128:oo * D + (c + 1) * 128], ym.t[:, c, sl],
                                     [w, ym], start=(c == c0), stop=(c == c1 - 1))
                        t1, t2, t3 = tmps.next(), tmps.next(), tmps.next()
                        P.tt("dve", t1.t[:, 0:512], pa.t[:, 0:512], gt.t[:, 0, sl], ALU.mult, [pa, gt], [t1])
                        P.tt("dve", t2.t[:, 0:512], pb.t[:, 0:512], gt.t[:, 1, sl], ALU.mult, [pb, gt], [t2])
                        P.tt("pool", t1.t[:, 0:512], t1.t[:, 0:512], t2.t[:, 0:512], ALU.add, [t1, t2], [t1])
                        P.tt("dve", t3.t[:, 0:512], pc.t[:, 0:512], gt.t[:, 2, sl], ALU.mult, [pc, gt], [t3])
                        P.tt("pool", u.t[:, o, sl], t1.t[:, 0:512], t3.t[:, 0:512], ALU.add, [t1, t3], [u])
            for op_ in range(2):
                w = ws.get()
                for oo in range(4):
                    o = op_ * 4 + oo
                    for s in range(NS):
                        sl = slice(s * 512, (s + 1) * 512)
                        pp = pss.next()
                        for c in range(KC):
                            P.mm(pp, pp.t[:, 0:512], w.t[:, oo * D + c * 128:oo * D + (c + 1) * 128], u.t[:, c, sl], [w, u],
                                 start=(c == 0), stop=(c == KC - 1))
                        P.tt("dve", x.t[:, o, sl], pp.t[:, 0:512], x.t[:, o, sl], ALU.add, [pp, x], [x])
            self.rmsnorm(x, h, u, (l * 3 + 2) * KC, pss, rstd)
            self.ffn(ws, x, u, h, pss, tmps)
            if not last:
                P.dma("pool", self.xT[:, :, t0:t0 + TT].rearrange("c p t -> p c t"), x.t[:], reads=[x])
            else:
                self.rmsnorm(x, h, x, 6 * KC, pss, rstd)
                for b in range(TT // 128):
                    sg = stg.next()
                    for half in range(2):
                        ps = pss.next()
                        for cc in range(4):
                            c = half * 4 + cc
                            P.tr(ps, ps.t[:, cc * 128:(cc + 1) * 128], x.t[:, c, b * 128:(b + 1) * 128], self.ident.t[:], [x, self.ident])
                        P.cp("act" if half == 0 else "dve", sg.t[:, half * 512:(half + 1) * 512], ps.t[:, 0:512], [ps], [sg])
                    P.dma("pool", self.yout[t0 + b * 128:t0 + (b + 1) * 128, :], sg.t[:, 0:D], reads=[sg])
        S.close()

    def mixers(self, l):
        P = self.P
        en = self.debug_en if hasattr(self, "debug_en") else ("rwkv", "na", "da")
        S = Scope(self.nc)
        z = S.sb([128, 2048], BF16, "zero")
        P.memset("dve", z, z.t[:], 0.0)
        for (name, c0, c1) in (("rwkv", 0, 4), ("na", 4, 6), ("da", 6, 8)):
            if name in en:
                continue
            for c in range(c0, c1):
                for t in range(0, self.T, 2048):
                    n = min(2048, self.T - t)
                    P.dma("sp", self.yM[c, :, t:t + n], z.t[:, 0:n], reads=[z])
        S.close()
        P.barrier()
        if "na" in en:
            self.mix_na(l)
            P.barrier()
        if "da" in en:
            self.mix_da(l)
            P.barrier()
        if "rwkv" in en:
            self.mix_rwkv(l)
            P.barrier()


NA_TYPES = [(0, 0), (-2, -2), (-4, -3), (-4, -4), (-6, -6)]
SLOPES = [2.0 ** (-8.0 * (h + 1) / 4) for h in range(4)]


def na_rs(r, rows):
    return min(max(r - 4, 0), rows - 8)


def na_tile_info(r, rows):
    a, b = na_rs(r, rows) - r, na_rs(r + 1, rows) - r
    ty = NA_TYPES.index((a, b))
    kr0 = r + a
    nk = (b + 8 - a + 1) // 2
    return ty, kr0, nk


def na_tables(rpb):
    tab = np.full((128, 5, 4, 5, 128), -30000.0, np.float32)
    pk = np.arange(128)
    pq = np.arange(128)
    for ti, (a, b) in enumerate(NA_TYPES):
        nk = (b + 8 - a + 1) // 2
        for j in range(nk):
            krow = a + 2 * j + pk // 64
            kcol = pk % 64
            qrow = pq // 64
            qcol = pq % 64
            rs_rel = np.where(qrow == 0, a, b)
            cs = np.clip(qcol - 8, 0, 64 - 16)
            okr = (krow[:, None] >= rs_rel[None, :]) & (krow[:, None] < rs_rel[None, :] + 8)
            okc = (kcol[:, None] >= cs[None, :]) & (kcol[:, None] < cs[None, :] + 16)
            dr = np.clip(krow[:, None] - qrow[None, :] + 7, 0, 14)
            dc = np.clip(kcol[:, None] - qcol[None, :] + 15, 0, 30)
            ok = okr & okc
            for h in range(4):
                g = rpb[h][dr, dc]
                t = tab[:, ti, h, j, :]
                t[ok] = g[ok]
    return tab


def da_consts():
    p = np.arange(128, dtype=np.float64)
    colL = np.zeros((128, 4, 32), np.float32)
    colR = np.zeros((128, 4, 32), np.float32)
    fLR = np.zeros((128, 4, 8), np.float32)
    biasD = np.zeros((128, 4, 4, 512), np.float32)
    q = np.arange(512, dtype=np.float64)
    for s_, sl in enumerate(SLOPES):
        for m in range(32):
            colL[:, s_, m] = -sl * (128 * m - p)
            colR[:, s_, m] = -sl * (128 * m + p - 511)
        for sub in range(4):
            fLR[:, s_, sub] = -sl * (128 * sub + p)
            fLR[:, s_, 4 + sub] = -sl * (511 - 128 * sub - p)
        for j in range(4):
            biasD[:, s_, j, :] = -sl * np.abs(q[None, :] - (128 * j + p[:, None]))
    return {"da_colL": colL, "da_colR": colR, "da_fLR": fLR, "da_biasD": biasD}


SMALL_SHAPES = {"na_tab": [DEPTH, 128, 5 * 4 * 5 * 128], "da_colL": [128, 4, 32], "da_colR": [128, 4, 32],
                "da_fLR": [128, 4, 8], "da_biasD": [128, 4, 4, 512], "da_lam": [DEPTH, 128, 128],
                "da_g": [DEPTH, 128, 256]}


def host_small(inp):
    f = lambda a: np.asarray(a, np.float32)
    out = {}
    out["na_tab"] = np.stack([na_tables(f(inp["na_rpb"][l])).reshape(128, -1) for l in range(DEPTH)])
    out.update(da_consts())
    out["da_lam"] = np.stack([np.broadcast_to(f(inp["diff_lam"][l]).reshape(1, 128), (128, 128)) for l in range(DEPTH)]).copy()
    out["da_g"] = np.stack([np.broadcast_to(np.tile(f(inp["diff_subln_g"][l]), 4).reshape(1, 256), (128, 256)) for l in range(DEPTH)]).copy()
    return out


def make_inputs(inp, seq_groups, ncores):
    shared = {}
    shared.update(host_weights(inp))
    shared.update({"c_" + k: v for k, v in host_consts().items()})
    shared.update(host_small(inp))
    in_maps = []
    for c in range(ncores):
        parts = []
        for (arr, n) in seq_groups:
            for b in range(n):
                parts.append(np.asarray(arr[c * n + b], np.float32))
        m = dict(shared)
        m["xin"] = np.ascontiguousarray(np.concatenate(parts, axis=0))
        in_maps.append(m)
    return in_maps


def run(inp, seq_groups, ncores, debug=None, en=None):
    seqs = []
    for (arr, n) in seq_groups:
        seqs += [arr.shape[1]] * n
    B = Builder(seqs, debug=debug)
    if en is not None:
        B.debug_en = en
    nc = B.build()
    in_maps = make_inputs(inp, seq_groups, ncores)
    res = run_bass_kernel_spmd(nc, in_maps, core_ids=list(range(ncores)))
    outs = []
    for (arr, n) in seq_groups:
        outs.append(np.zeros(arr.shape, np.float32))
    for c in range(ncores):
        y = res.results[c]["yout"]
        t = 0
        for gi, (arr, n) in enumerate(seq_groups):
            L = arr.shape[1]
            for b in range(n):
                outs[gi][c * n + b] = y[t:t + L]
                t += L
    return outs, res, B


def kernel(**inputs):
    xp = np.asarray(inputs["x_prompt"], np.float32)
    xs = np.asarray(inputs["x_sample"], np.float32)
    outs, _, _ = run(inputs, [(xp, xp.shape[0] // NCORES), (xs, xs.shape[0] // NCORES)], NCORES)
    return (outs[0], outs[1])


def mix_na(self, l):
    P = self.P
    S = Scope(self.nc)
    Lmax = max(self.seqs)
    qT = S.sb([128, 2, Lmax], BF16, "naq")
    kT = S.sb([128, 2, Lmax], BF16, "nak")
    V1 = S.sb([128, Lmax // 128, 260], BF16, "nav")
    yst = S.sb([128, 2, Lmax], BF16, "nay")
    tab = S.sb([128, 5, 4, 5 * 128], F32, "natab")
    sbr = Ring([S.sb([128, 640], F32, "nasb") for _ in range(2)])
    ptr = Ring([S.sb([128, 640], BF16, "napt") for _ in range(2)])
    yr = Ring([S.sb([128, 256], F32, "nayt") for _ in range(2)])
    rcr = Ring([S.sb([128, 4], F32, "narc") for _ in range(2)])
    pss = Ring([S.ps([128, 1024], F32, "naps") for _ in range(2)])
    pso = Ring([S.ps([128, 512], F32, "napo") for _ in range(2)])
    pst = Ring([S.ps([128, 512], F32, "napt") for _ in range(2)])
    P.dma("sp", tab.t[:].rearrange("p a b c -> p (a b c)"), self.small["na_tab"][l], writes=[tab])
    for si, L in enumerate(self.seqs):
        t0 = self.starts[si]
        rows = L // 64
        P.dma("sp", qT.t[:, :, 0:L], self.zNq[:, :, t0:t0 + L].rearrange("c p t -> p c t"), writes=[qT])
        P.dma("sp", kT.t[:, :, 0:L], self.zNk[:, :, t0:t0 + L].rearrange("c p t -> p c t"), writes=[kT])
        P.dma("sp", V1.t[:, 0:L // 128, :], self.zV[t0:t0 + L, 0:260].rearrange("(n p) x -> p n x", p=128), writes=[V1])
        for qi in range(L // 128):
            r = 2 * qi
            ty, kr0, nk = na_tile_info(r, rows)
            po = pso.next()
            for hd in range(4):
                cc, base = hd // 2, (hd % 2) * 64
                ps = pss.next()
                for j in range(nk):
                    kn = kr0 // 2 + j
                    P.mm(ps, ps.t[:, j * 128:(j + 1) * 128], kT.t[base:base + 64, cc, kn * 128:(kn + 1) * 128],
                         qT.t[base:base + 64, cc, qi * 128:(qi + 1) * 128], [kT, qT])
                sb = sbr.next()
                P.tt("dve", sb.t[:, 0:nk * 128], ps.t[:, 0:nk * 128], tab.t[:, ty, hd, 0:nk * 128], ALU.add, [ps, tab], [sb])
                pt = ptr.next()
                P.act(pt.t[:, 0:nk * 128], sb.t[:, 0:nk * 128], AF.Exp, [sb], [pt])
                for j in range(nk):
                    kn = kr0 // 2 + j
                    P.mm(po, po.t[:, hd * 65:(hd + 1) * 65], pt.t[:, j * 128:(j + 1) * 128], V1.t[:, kn, hd * 65:(hd + 1) * 65],
                         [pt, V1], start=(j == 0), stop=(j == nk - 1))
            rc = rcr.next()
            pov = po.t[:, 0:260].rearrange("p (h d) -> p h d", h=4)
            P.op("dve", lambda g, rc=rc, pov=pov: g.reciprocal(out=rc.t[:, :], in_=pov[:, :, 64]), [po], [rc])
            y = yr.next()
            for hd in range(4):
                P.ts("dve", y.t[:, hd * 64:(hd + 1) * 64], po.t[:, hd * 65:hd * 65 + 64], rc.t[:, hd:hd + 1], None, ALU.mult, None,
                     [po, rc], [y])
            pt_ = pst.next()
            for cc in range(2):
                P.tr(pt_, pt_.t[:, cc * 128:(cc + 1) * 128], y.t[:, cc * 128:(cc + 1) * 128], self.ident.t[:], [y, self.ident])
            P.cp("act", yst.t[:, :, qi * 128:(qi + 1) * 128], pt_.t[:, 0:256].rearrange("p (c t) -> p c t", c=2), [pt_], [yst])
        P.dma("pool", self.yM[4:6, :, t0:t0 + L].rearrange("c p t -> p c t"), yst.t[:, :, 0:L], reads=[yst])
    S.close()


def mix_da(self, l):
    P = self.P
    S = Scope(self.nc)
    Lmax = max(self.seqs)
    lam_init = 0.8 - 0.6 * math.exp(-0.3 * l)
    kT = S.sb([128, 2, Lmax], BF16, "dak")
    V1 = S.sb([128, Lmax // 128, 260], BF16, "dav")
    yst = S.sb([128, 2, Lmax], BF16, "day")
    qmr = Ring([S.sb([128, 8, 512], BF16, "daq") for _ in range(2)])
    colL = S.sb([128, 4, 32], F32, "colL")
    colR = S.sb([128, 4, 32], F32, "colR")
    fLR = S.sb([128, 4, 8], F32, "fLR")
    bD32 = S.sb([128, 4 * 4 * 512], F32, "bD32")
    bD = S.sb([128, 4, 4, 512], BF16, "bD")
    lamt = S.sb([128, 4, 32], F32, "lamt")
    lamw = S.sb([128, 2, 32], F32, "lamw")
    lams = S.sb([128, 4], F32, "lams")
    gt = S.sb([128, 4, 64], F32, "dag")
    att = S.sb([128, 4, 8, 64], F32, "att")
    ptr = Ring([S.sb([128, 512], BF16, "dapt") for _ in range(6)])
    totr = Ring([S.sb([128, 65], F32, "datot") for _ in range(3)])
    rcr = Ring([S.sb([128, 4], F32, "darc") for _ in range(3)])
    ar = Ring([S.sb([128, 4, 64], F32, "daa") for _ in range(2)])
    sqr = Ring([S.sb([128, 4, 64], F32, "dasq") for _ in range(2)])
    yr = Ring([S.sb([128, 256], F32, "dayt") for _ in range(2)])
    pss = Ring([S.ps([128, 512], F32, "daps") for _ in range(4)])
    pso = Ring([S.ps([128, 512], F32, "dapo") for _ in range(4)])
    pst = pss
    sm = self.small
    P.dma("sp", colL.t[:], sm["da_colL"], writes=[colL])
    P.dma("sp", colR.t[:], sm["da_colR"], writes=[colR])
    P.dma("sp", fLR.t[:], sm["da_fLR"], writes=[fLR])
    P.dma("sp", bD32.t[:], sm["da_biasD"].rearrange("p a b c -> p (a b c)"), writes=[bD32])
    P.dma("sp", lamt.t[:].rearrange("p a b -> p (a b)"), sm["da_lam"][l], writes=[lamt])
    P.dma("sp", gt.t[:].rearrange("p a b -> p (a b)"), sm["da_g"][l], writes=[gt])
    P.cp("dve", bD.t[:].rearrange("p a b c -> p (a b c)"), bD32.t[:], [bD32], [bD])
    P.act(fLR.t[:], fLR.t[:], AF.Exp, [fLR], [fLR])
    P.tt("dve", lamw.t[:, 0, :], lamt.t[:, 0, :], lamt.t[:, 1, :], ALU.mult, [lamt], [lamw])
    P.tt("dve", lamw.t[:, 1, :], lamt.t[:, 2, :], lamt.t[:, 3, :], ALU.mult, [lamt], [lamw])
    P.op("dve", lambda g: g.tensor_reduce(out=lams.t[:, 0:2], in_=lamw.t[:], axis=AX.X, op=ALU.add), [lamw], [lams])
    P.act(lams.t[:, 0:2], lams.t[:, 0:2], AF.Exp, [lams], [lams])
    P.tt("dve", lams.t[:, 2:3], lams.t[:, 1:2], lams.t[:, 0:1], ALU.subtract, [lams], [lams])
    P.ts("dve", lams.t[:, 3:4], lams.t[:, 2:3], -lam_init, None, ALU.add, None, [lams], [lams])
    zero_col = self.epsr.t[:, 3:4]
    for si, L in enumerate(self.seqs):
        t0 = self.starts[si]
        NK = L // 128
        P.dma("sp", kT.t[:, :, 0:L], self.zDk[:, :, t0:t0 + L].rearrange("c p t -> p c t"), writes=[kT])
        P.dma("sp", V1.t[:, 0:NK, :], self.zV[t0:t0 + L, 260:520].rearrange("(n p) x -> p n x", p=128), writes=[V1])
        for qb in range(L // 512):
            q0 = qb * 512
            qm = qmr.next()
            P.dma("sp", qm.t[:], self.zDq[:, :, t0 + q0:t0 + q0 + 512].rearrange("g p t -> p g t"), writes=[qm])
            for g_ in range(8):
                s_ = g_ // 2
                hd = g_ // 2
                poA, poB = pso.next(), pso.next()
                cls_of = {}
                for kt in range(NK):
                    k0 = kt * 128
                    c_ = 0 if k0 + 128 <= q0 else (2 if k0 >= q0 + 512 else 1)
                    mind = (q0 - (k0 + 127)) if c_ == 0 else ((k0 - (q0 + 511)) if c_ == 2 else 0)
                    if SLOPES[s_] * mind >= 80.0:
                        continue
                    cls_of[kt] = c_
                kts = sorted(cls_of)
                first = {}
                lastk = {}
                for kt in kts:
                    first.setdefault(cls_of[kt], kt)
                    lastk[cls_of[kt]] = kt
                for kt in kts:
                    k0 = kt * 128
                    cls = cls_of[kt]
                    ps = pss.next()
                    P.mm(ps, ps.t[:, 0:512], kT.t[:, g_ // 4, k0:k0 + 128], qm.t[:, g_, :], [kT, qm], start=True, stop=(cls != 1))
                    if cls == 1:
                        P.mm(ps, ps.t[:, 0:512], self.identb.t[:], bD.t[:, s_, (k0 - q0) // 128, :], [self.identb, bD], start=False, stop=True)
                        bias = zero_col
                        rd = [ps, self.epsr]
                    elif cls == 0:
                        bias = colL.t[:, s_, (q0 - k0) // 128:(q0 - k0) // 128 + 1]
                        rd = [ps, colL]
                    else:
                        bias = colR.t[:, s_, (k0 - q0) // 128:(k0 - q0) // 128 + 1]
                        rd = [ps, colR]
                    pt = ptr.next()
                    P.act(pt.t[:], ps.t[:, 0:512], AF.Exp, rd, [pt], bias=bias)
                    for sub in range(4):
                        po = poA if sub < 2 else poB
                        off = ((sub % 2) * 3 + cls) * 65
                        P.mm(po, po.t[:, off:off + 65], pt.t[:, sub * 128:(sub + 1) * 128], V1.t[:, kt, hd * 65:(hd + 1) * 65],
                             [pt, V1], start=(kt == kts[0] and sub % 2 == 0), stop=(kt == lastk[cls]), sgc=True)
                for sub in range(4):
                    po = poA if sub < 2 else poB
                    o0 = (sub % 2) * 3 * 65
                    tot = totr.next()
                    P.cp("act", tot.t[:], po.t[:, o0 + 65:o0 + 130], [po], [tot])
                    if 0 in first:
                        P.stt("dve", tot.t[:], po.t[:, o0:o0 + 65], fLR.t[:, s_, sub:sub + 1], tot.t[:], ALU.mult, ALU.add, [po, fLR, tot], [tot])
                    if 2 in first:
                        P.stt("dve", tot.t[:], po.t[:, o0 + 130:o0 + 195], fLR.t[:, s_, 4 + sub:5 + sub], tot.t[:], ALU.mult, ALU.add,
                              [po, fLR, tot], [tot])
                    rc = rcr.next()
                    P.op("dve", lambda g, rc=rc, tot=tot: g.reciprocal(out=rc.t[:, 0:1], in_=tot.t[:, 64:65]), [tot], [rc])
                    P.ts("dve", att.t[:, sub, g_, :], tot.t[:, 0:64], rc.t[:, 0:1], None, ALU.mult, None, [tot, rc], [att])
            if "att" in self.dbg and l == 0 and si == len(self.seqs) - 1 and qb == 0:
                P.dma("sp", self.dbg["att"], att.t[:].rearrange("p a b c -> p (a b c)"), reads=[att])
                P.dma("sp", self.dbg["lams"], lams.t[:], reads=[lams])
            for sub in range(4):
                a = ar.next()
                av = att.t[:, sub, :, :].rearrange("p (h two) d -> p h two d", two=2)
                P.stt("dve", a.t[:], av[:, :, 1, :], lams.t[:, 3:4], av[:, :, 0, :], ALU.mult, ALU.add, [att, lams], [a])
                sq = sqr.next()
                P.tt("pool", sq.t[:], a.t[:], a.t[:], ALU.mult, [a], [sq])
                rc = rcr.next()
                P.op("dve", lambda g, rc=rc, sq=sq: g.tensor_reduce(out=rc.t[:, 0:4], in_=sq.t[:], axis=AX.X, op=ALU.add), [sq], [rc])
                P.act(rc.t[:, 0:4], rc.t[:, 0:4], AF.Sqrt, [rc, self.epsr], [rc], bias=self.epsr.t[:, 1:2], scale=1.0 / 64)
                P.op("dve", lambda g, rc=rc: g.reciprocal(out=rc.t[:, 0:4], in_=rc.t[:, 0:4]), [rc], [rc])
                y = yr.next()
                yv = y.t[:].rearrange("p (h d) -> p h d", h=4)
                for hd in range(4):
                    P.ts("dve", yv[:, hd, :], a.t[:, hd, :], rc.t[:, hd:hd + 1], 1.0 - lam_init, ALU.mult, ALU.mult, [a, rc], [y])
                P.tt("pool", yv, yv, gt.t[:], ALU.mult, [y, gt], [y])
                pt_ = pst.next()
                for cc in range(2):
                    P.tr(pt_, pt_.t[:, cc * 128:(cc + 1) * 128], y.t[:, cc * 128:(cc + 1) * 128], self.ident.t[:], [y, self.ident])
                tq = q0 + sub * 128
                P.cp("act", yst.t[:, :, tq:tq + 128], pt_.t[:, 0:256].rearrange("p (c t) -> p c t", c=2), [pt_], [yst])
        P.dma("pool", self.yM[6:8, :, t0:t0 + L].rearrange("c p t -> p c t"), yst.t[:, :, 0:L], reads=[yst])
    S.close()


Builder.mix_na = mix_na
Builder.mix_da = mix_da


CDEC = math.exp(-0.5)


def rw_host(inp):
    f = lambda a: np.asarray(a, np.float32)
    out = {}
    mu_l, mulw_l, mulg_l, hp_l = [], [], [], []
    for l in range(DEPTH):
        mu = f(inp["rwkv_mu"][l])
        a = np.zeros((64, 8, 3, 2), np.float32)
        for h in range(8):
            for q in range(3):
                for m in range(2):
                    a[:, h, q, m] = mu[m, q * 512 + h * 64:q * 512 + h * 64 + 64]
        mu_l.append(a.reshape(64, 48))
        mulw_l.append(np.stack([mu[0, 1536:1600], mu[1, 1536:1600], mu[0, 1600:1664], mu[1, 1600:1664]], axis=1))
        mulg_l.append(np.stack([mu[0, 1664:1792], mu[1, 1664:1792]], axis=1))
        hp = np.zeros((64, 8, 11), np.float32)
        for h in range(8):
            sl = slice(h * 64, h * 64 + 64)
            hp[:, h, 0] = f(inp["rwkv_k_k"][l])[sl]
            hp[:, h, 1] = f(inp["rwkv_lnx_g"][l])[sl]
            hp[:, h, 2] = f(inp["rwkv_lnx_b"][l])[sl]
            for d in range(2):
                hp[:, h, 3 + 4 * d + 0] = f(inp["rwkv_w0"][l, d])[sl]
                hp[:, h, 3 + 4 * d + 1] = f(inp["rwkv_a0"][l, d])[sl]
                hp[:, h, 3 + 4 * d + 2] = f(inp["rwkv_k_a"][l, d])[sl]
                hp[:, h, 3 + 4 * d + 3] = f(inp["rwkv_r_k"][l, d]).reshape(-1)[sl]
        hp_l.append(hp.reshape(64, 88))
    out["rw_mu"] = np.stack(mu_l); out["rw_mulw"] = np.stack(mulw_l); out["rw_mulg"] = np.stack(mulg_l)
    out["rw_hp"] = np.stack(hp_l)
    out["rw_w2"] = np.ascontiguousarray(f(inp["rwkv_w2"])); out["rw_a2"] = np.ascontiguousarray(f(inp["rwkv_a2"]))
    out["rw_g2"] = np.ascontiguousarray(f(inp["rwkv_g2"]))
    i = np.arange(CK)
    row, col = i[:, None], i[None, :]
    mk = np.zeros((2, 3, CK, 512), np.float32)
    for d in range(2):
        st = (row < col) if d == 0 else (row > col)
        inc = (row <= col) if d == 0 else (row >= col)
        stT = (col < row) if d == 0 else (col > row)
        for mi, m_ in enumerate((st, inc, stT)):
            mk[d, mi] = np.tile(m_.astype(np.float32), (1, 512 // CK))
    out["rw_mask"] = mk
    out["rw_irep"] = np.tile(np.eye(CK, dtype=np.float32), (1, 512 // CK))
    rs = np.ones((64, 1024), np.float32)
    rs[:, ::CK] = 0.0
    out["rw_reset"] = rs
    return out


SMALL_SHAPES.update({"rw_mu": [DEPTH, 64, 48], "rw_mulw": [DEPTH, 64, 4], "rw_mulg": [DEPTH, 128, 2], "rw_hp": [DEPTH, 64, 88],
                     "rw_w2": [DEPTH, 2, 64, 512], "rw_a2": [DEPTH, 2, 64, 512], "rw_g2": [DEPTH, 128, 512],
                     "rw_mask": [2, 3, CK, 512], "rw_irep": [CK, 512], "rw_reset": [64, 1024]})
_host_small0 = host_small


def host_small(inp):
    o = _host_small0(inp)
    o.update(rw_host(inp))
    return o


def mix_rwkv(self, l):
    P = self.P
    nc = self.nc
    S = Scope(nc)
    sm = self.small
    T = self.T
    if not hasattr(self, "yF"):
        self.yF = nc.dram_tensor("yF", [2, 8, 64, T], F32).ap()
    yfb = Buf()
    SEG = min(512, min(self.seqs))
    W = SEG
    sb = lambda shape, name: S.sb(shape, F32, name)
    mu = sb([64, 48], "mu"); c0 = sb([64, 24], "c0"); mulw = sb([64, 4], "mulw"); c0w = sb([64, 2], "c0w")
    mulg = sb([128, 2], "mulg"); c0g = sb([128, 1], "c0g"); hp = sb([64, 88], "hp"); omk = sb([64, 16], "omk")
    w2 = sb([64, 2, 512], "w2"); a2 = sb([64, 2, 512], "a2"); g2 = sb([128, 512], "g2")
    mk = sb([CK, 2, 3, 512], "mk"); irep = sb([CK, 512], "irep"); rst = sb([64, 1024], "rst"); ones = sb([64, 64], "ones"); onesm = sb([64, 64], "onesm")
    P.dma("sp", mu.t[:], sm["rw_mu"][l], writes=[mu]); P.dma("sp", mulw.t[:], sm["rw_mulw"][l], writes=[mulw])
    P.dma("sp", mulg.t[:], sm["rw_mulg"][l], writes=[mulg]); P.dma("sp", hp.t[:], sm["rw_hp"][l], writes=[hp])
    P.dma("sp", w2.t[:], sm["rw_w2"][l].rearrange("d k n -> k d n"), writes=[w2])
    P.dma("sp", a2.t[:], sm["rw_a2"][l].rearrange("d k n -> k d n"), writes=[a2])
    P.dma("sp", g2.t[:], sm["rw_g2"][l], writes=[g2])
    P.dma("sp", mk.t[:], sm["rw_mask"].rearrange("d m p n -> p d m n"), writes=[mk])
    P.dma("sp", irep.t[:], sm["rw_irep"], writes=[irep])
    P.dma("sp", rst.t[:], sm["rw_reset"], writes=[rst])
    P.memset("dve", ones, ones.t[:], 1.0); P.memset("dve", onesm, onesm.t[:], 1.0 / 64)
    muv = mu.t[:].rearrange("p (a m) -> p a m", m=2)
    P.tt("dve", c0.t[:], muv[:, :, 0], muv[:, :, 1], ALU.add, [mu], [c0])
    P.ts("dve", c0.t[:], c0.t[:], -1.0, 1.0, ALU.mult, ALU.add, [c0], [c0])
    mwv = mulw.t[:].rearrange("p (a m) -> p a m", m=2)
    P.tt("dve", c0w.t[:], mwv[:, :, 0], mwv[:, :, 1], ALU.add, [mulw], [c0w])
    P.ts("dve", c0w.t[:], c0w.t[:], -1.0, 1.0, ALU.mult, ALU.add, [c0w], [c0w])
    P.tt("dve", c0g.t[:], mulg.t[:, 0:1], mulg.t[:, 1:2], ALU.add, [mulg], [c0g])
    P.ts("dve", c0g.t[:], c0g.t[:], -1.0, 1.0, ALU.mult, ALU.add, [c0g], [c0g])
    hpv = hp.t[:].rearrange("p (h c) -> p h c", c=11)
    for h in range(8):
        for d in range(2):
            P.ts("dve", omk.t[:, h * 2 + d:h * 2 + d + 1], hpv[:, h, 5 + 4 * d:6 + 4 * d], -1.0, 1.0, ALU.mult, ALU.add, [hp], [omk])
    zcol = self.epsr.t[0:64, 3:4]
    names = ["zr", "zk", "zv", "zw", "za"]
    Zs = [{n: sb([64, W + 2], n) for n in names} for _ in range(2)]
    zgs = [sb([128, W + 2], "zg") for _ in range(2)]
    gl = sb([128, W], "gl")
    unit = [0]
    Tl = {n: sb([64, W], n) for n in ["r", "k", "v", "wl", "al", "kk", "sg", "a", "kd", "b", "t1", "bon", "Pf", "E", "Sf", "X",
                                      "eI", "eX", "eN", "eT", "at", "bt", "kt", "rt", "bh", "kh", "Y", "yf", "bf"]}
    NCH_ = W // CK
    TM = [sb([CK, NCH_ * 64], "TM%d" % i) for i in range(4)]
    GM = [sb([CK, W], "GM%d" % i) for i in range(5)]
    TT_ = [sb([CK, W], "TT%d" % i) for i in range(2)]
    PP = [(sb([CK, W], "PPa%d" % i), sb([CK, W], "PPb%d" % i)) for i in range(2)]
    X1 = sb([CK, NCH_ * 64], "X1"); AHT = sb([64, W], "AHT"); AHN = sb([CK, NCH_ * 64], "AHN"); U0 = sb([CK, NCH_ * 64], "U0")
    MT = sb([64, NCH_ * 64], "MT"); CC = sb([64, NCH_ * 64], "CC")
    tmr = Ring([sb([64, 256], "tm") for _ in range(2)])
    gmr = Ring([sb([64, 320], "gm") for _ in range(2)])
    p2r = Ring([sb([64, 128], "p2") for _ in range(3)])
    ttr = Ring([sb([64, 64], "tt") for _ in range(3)])
    x1r = Ring([sb([64, 64], "x1") for _ in range(2)])
    u0r = Ring([sb([64, 64], "u0") for _ in range(2)])
    ahr = Ring([sb([64, 64], "ah") for _ in range(2)])
    dgr = Ring([sb([64, 64], "dg") for _ in range(2)])
    ur = Ring([sb([CK, 64], "u") for _ in range(2)])
    str_ = Ring([sb([64, 64], "st") for _ in range(3)])
    yo = S.sb([64, W], BF16, "yo")
    pss = Ring([S.ps([128, 512], F32, "rps") for _ in range(6)])
    psU_ = S.ps([128, 512], F32, "rpsU")
    psY_ = S.ps([128, 512], F32, "rpsY")
    idn = self.ident.t[0:64, 0:64]

    def shift(dst, src, c0c, m0c, m1c, np_=64):
        P.act(dst.t[:, :], src.t[:, 1:W + 1], AF.Copy, [src], [dst], scale=c0c)
        P.stt("dve", dst.t[:, :], src.t[:, 0:W], m0c, dst.t[:, :], ALU.mult, ALU.add, [src, dst], [dst])
        P.stt("dve", dst.t[:, :], src.t[:, 2:W + 2], m1c, dst.t[:, :], ALU.mult, ALU.add, [src, dst], [dst])

    for si, L in enumerate(self.seqs):
        t0 = self.starts[si]
        nseg = L // SEG
        for h in range(8):
            cc, pb = h // 2, (h % 2) * 64
            for d in range(2):
                st = str_.next()
                P.memset("dve", st, st.t[:], 0.0)
                for sgi in (range(nseg) if d == 0 else range(nseg - 1, -1, -1)):
                    s0 = t0 + sgi * SEG
                    Z = Zs[unit[0] % 2]
                    zg = zgs[unit[0] % 2]
                    unit[0] += 1
                    lo = 0 if sgi > 0 else 1
                    hi = W + 2 if sgi < nseg - 1 else W + 1
                    srcs = {"zr": (cc, pb), "zk": (4 + cc, pb), "zv": (8 + cc, pb), "zw": (12, 0), "za": (12, 64)}
                    for n in names:
                        if lo == 1 or hi == W + 1:
                            P.memset("dve", Z[n], Z[n].t[:], 0.0)
                        c_, p_ = srcs[n]
                        P.dma("sp", Z[n].t[:, lo:hi], self.zA[c_, p_:p_ + 64, s0 - 1 + lo:s0 - 1 + hi], writes=[Z[n]])
                    if lo == 1 or hi == W + 1:
                        P.memset("dve", zg, zg.t[:], 0.0)
                    P.dma("sp", zg.t[:, lo:hi], self.zA[13, :, s0 - 1 + lo:s0 - 1 + hi], writes=[zg])
                    for qi, (dn, sn) in enumerate((("r", "zr"), ("k", "zk"), ("v", "zv"))):
                        ix = h * 3 + qi
                        shift(Tl[dn], Z[sn], c0.t[:, ix:ix + 1], mu.t[:, 2 * ix:2 * ix + 1], mu.t[:, 2 * ix + 1:2 * ix + 2])
                    shift(Tl["wl"], Z["zw"], c0w.t[:, 0:1], mulw.t[:, 0:1], mulw.t[:, 1:2])
                    shift(Tl["al"], Z["za"], c0w.t[:, 1:2], mulw.t[:, 2:3], mulw.t[:, 3:4])
                    P.ts("dve", gl.t[:, :], zg.t[:, 1:W + 1], c0g.t[:, 0:1], None, ALU.mult, None, [zg], [gl])
                    P.stt("dve", gl.t[:, :], zg.t[:, 0:W], mulg.t[:, 0:1], gl.t[:, :], ALU.mult, ALU.add, [zg, gl], [gl])
                    P.stt("dve", gl.t[:, :], zg.t[:, 2:W + 2], mulg.t[:, 1:2], gl.t[:, :], ALU.mult, ALU.add, [zg, gl], [gl])
                    r, k, v, kk, sg, a, kd, b, t1 = (Tl[n] for n in ("r", "k", "v", "kk", "sg", "a", "kd", "b", "t1"))
                    P.act(kk.t[:], k.t[:], AF.Copy, [k, hp], [kk], scale=hpv[:, h, 0:1])
                    P.tt("pool", t1.t[:], kk.t[:], kk.t[:], ALU.mult, [kk], [t1])
                    for blk in range(W // 512):
                        bs = slice(blk * 512, blk * 512 + 512)
                        ps = pss.next()
                        P.mm(ps, ps.t[0:64, 0:512], ones.t[:], t1.t[:, bs], [ones, t1])
                        P.act(Tl["X"].t[:, bs], ps.t[0:64, 0:512], AF.Sqrt, [ps, self.epsr], [Tl["X"]], bias=zcol)
                    P.ts("dve", Tl["X"].t[:], Tl["X"].t[:], 1e-12, None, ALU.max, None, [Tl["X"]], [Tl["X"]])
                    P.op("dve", lambda g: g.reciprocal(out=Tl["X"].t[:], in_=Tl["X"].t[:]), [Tl["X"]], [Tl["X"]])
                    P.tt("dve", kk.t[:], kk.t[:], Tl["X"].t[:], ALU.mult, [kk, Tl["X"]], [kk])
                    P.act(Tl["wl"].t[:], Tl["wl"].t[:], AF.Tanh, [Tl["wl"]], [Tl["wl"]])
                    for blk in range(W // 512):
                        bs = slice(blk * 512, blk * 512 + 512)
                        ps = pss.next()
                        P.mm(ps, ps.t[0:64, 0:512], w2.t[:, d, h * 64:h * 64 + 64], Tl["wl"].t[:, bs], [w2, Tl["wl"]])
                        P.act(sg.t[:, bs], ps.t[0:64, 0:512], AF.Sigmoid, [ps, hp], [sg], bias=hpv[:, h, 3 + 4 * d:4 + 4 * d])
                        ps = pss.next()
                        P.mm(ps, ps.t[0:64, 0:512], a2.t[:, d, h * 64:h * 64 + 64], Tl["al"].t[:, bs], [a2, Tl["al"]])
                        P.act(a.t[:, bs], ps.t[0:64, 0:512], AF.Sigmoid, [ps, hp], [a], bias=hpv[:, h, 4 + 4 * d:5 + 4 * d])
                    P.ts("dve", kd.t[:], a.t[:], hpv[:, h, 5 + 4 * d:6 + 4 * d], omk.t[:, h * 2 + d:h * 2 + d + 1], ALU.mult, ALU.add, [a, hp, omk], [kd])
                    P.tt("pool", kd.t[:], kd.t[:], k.t[:], ALU.mult, [kd, k], [kd])
                    P.tt("pool", b.t[:], kk.t[:], a.t[:], ALU.mult, [kk, a], [b])
                    P.stt("dve", t1.t[:], r.t[:], hpv[:, h, 6 + 4 * d:7 + 4 * d], kd.t[:], ALU.mult, ALU.mult, [r, hp, kd], [t1])
                    bon = Tl["bon"]
                    for blk in range(W // 512):
                        bs = slice(blk * 512, blk * 512 + 512)
                        ps = pss.next()
                        P.mm(ps, ps.t[0:64, 0:512], ones.t[:], t1.t[:, bs], [ones, t1])
                        P.tt("dve", bon.t[:, bs], ps.t[0:64, 0:512], v.t[:, bs], ALU.mult, [ps, v], [bon])
                    Pf, E, Sf, X = Tl["Pf"], Tl["E"], Tl["Sf"], Tl["X"]
                    P.op("dve", lambda g: g.tensor_tensor_scan(out=Pf.t[:], data0=rst.t[:, 0:W], data1=sg.t[:], initial=0.0,
                                                                op0=ALU.mult, op1=ALU.add), [rst, sg], [Pf])
                    P.tt("pool", E.t[:], Pf.t[:], sg.t[:], ALU.subtract, [Pf, sg], [E])
                    for j in range(W // CK):
                        P.ts("dve", Sf.t[:, CK * j:CK * j + CK], E.t[:, CK * j:CK * j + CK], -1.0, Pf.t[:, CK * j + CK - 1:CK * j + CK],
                             ALU.mult, ALU.add, [E, Pf], [Sf])
                    P.tt("pool", X.t[:], Sf.t[:], sg.t[:], ALU.subtract, [Sf, sg], [X])
                    Gi, Ge, Tm = (Pf, E, X) if d == 0 else (Sf, X, E)
                    eI, eX, eN, eT = Tl["eI"], Tl["eX"], Tl["eN"], Tl["eT"]
                    P.act(eI.t[:], Gi.t[:], AF.Exp, [Gi], [eI], scale=-CDEC)
                    P.act(eX.t[:], Ge.t[:], AF.Exp, [Ge], [eX], scale=-CDEC)
                    P.act(eN.t[:], Gi.t[:], AF.Exp, [Gi], [eN], scale=CDEC)
                    P.act(eT.t[:], Tm.t[:], AF.Exp, [Tm], [eT], scale=-CDEC)
                    at, bt, kt, rt, bh, kh = (Tl[n] for n in ("at", "bt", "kt", "rt", "bh", "kh"))
                    P.stt("dve", at.t[:], kk.t[:], -1.0, eX.t[:], ALU.mult, ALU.mult, [kk, eX], [at])
                    P.tt("pool", bt.t[:], b.t[:], eN.t[:], ALU.mult, [b, eN], [bt])
                    P.tt("dve", kt.t[:], kd.t[:], eN.t[:], ALU.mult, [kd, eN], [kt])
                    P.tt("pool", rt.t[:], r.t[:], eI.t[:], ALU.mult, [r, eI], [rt])
                    P.tt("dve", bh.t[:], b.t[:], eT.t[:], ALU.mult, [b, eT], [bh])
                    P.tt("pool", kh.t[:], kd.t[:], eT.t[:], ALU.mult, [kd, eT], [kh])
                    Y = Tl["Y"]
                    NCH = W // CK
                    order = list(range(NCH)) if d == 0 else list(range(NCH - 1, -1, -1))
                    cs_ = lambda j: slice(CK * j, CK * j + CK)
                    ks_ = lambda j: slice(64 * j, 64 * j + 64)
                    idf = self.ident.t[:]
                    tmq = []
                    for qi, src in enumerate((at, bh, kh, v)):
                        ps = pss.next()
                        for j in range(NCH):
                            P.tr(ps, ps.t[0:CK, ks_(j)], src.t[:, cs_(j)], idn, [src, self.ident])
                        tq = TM[qi]
                        P.cp("act" if qi % 2 == 0 else "dve", tq.t[:], ps.t[0:CK, 0:NCH * 64], [ps], [tq])
                        tmq.append(tq)
                    Atm, Bhtm, Khtm, Vtm = tmq
                    gq = []
                    for qi, (lt, rh, mi) in enumerate(((bt, at, 0), (bt, rt, 1), (kt, at, 0), (kt, rt, 1), (at, bt, 2))):
                        ps = pss.next()
                        for j in range(NCH):
                            P.mm(ps, ps.t[0:CK, cs_(j)], lt.t[:, cs_(j)], rh.t[:, cs_(j)], [lt, rh])
                        gt_ = GM[qi]
                        P.tt("dve", gt_.t[:], ps.t[0:CK, 0:W], mk.t[:, d, mi, 0:W], ALU.mult, [ps, mk], [gt_])
                        gq.append(gt_)
                    Aab, Abr, Aak, Akr, NT = gq
                    Tt = TT_[0]
                    P.tt("pool", Tt.t[:], Aab.t[:], irep.t[:, 0:W], ALU.add, [Aab, irep], [Tt])
                    Pm, PTm = Aab, NT
                    NLEV = 6 if CK == 128 else 5
                    for lev in range(NLEV):
                        Pn, PTn = PP[lev % 2]
                        ps2 = pss.next()
                        for j in range(NCH):
                            P.mm(ps2, ps2.t[0:CK, cs_(j)], Pm.t[:, cs_(j)], PTm.t[:, cs_(j)], [PTm, Pm])
                        if lev < NLEV - 1:
                            ps1 = pss.next()
                            for j in range(NCH):
                                P.mm(ps1, ps1.t[0:CK, cs_(j)], PTm.t[:, cs_(j)], Pm.t[:, cs_(j)], [PTm, Pm])
                            P.cp("act", Pn.t[:], ps1.t[0:CK, 0:W], [ps1], [Pn])
                        P.cp("dve", PTn.t[:], ps2.t[0:CK, 0:W], [ps2], [PTn])
                        Pm, PTm = Pn, PTn
                        ps3 = pss.next()
                        for j in range(NCH):
                            P.mm(ps3, ps3.t[0:CK, cs_(j)], PTm.t[:, cs_(j)], Tt.t[:, cs_(j)], [PTm, Tt])
                        Tn = TT_[(lev + 1) % 2]
                        P.tt("dve", Tn.t[:], ps3.t[0:CK, 0:W], Tt.t[:], ALU.add, [ps3, Tt], [Tn])
                        Tt = Tn
                    ps = pss.next()
                    for j in range(NCH):
                        P.mm(ps, ps.t[0:CK, ks_(j)], Aak.t[:, cs_(j)], Vtm.t[:, ks_(j)], [Aak, Vtm])
                    P.cp("act", X1.t[:], ps.t[0:CK, 0:NCH * 64], [ps], [X1])
                    psa = pss.next()
                    for j in range(NCH):
                        P.mm(psa, psa.t[0:64, cs_(j)], Atm.t[:, ks_(j)], Tt.t[:, cs_(j)], [Atm, Tt])
                    P.cp("dve", AHT.t[:], psa.t[0:64, 0:W], [psa], [AHT])
                    psb = pss.next()
                    for j in range(NCH):
                        P.mm(psb, psb.t[0:CK, ks_(j)], Tt.t[:, cs_(j)], Atm.t[:, ks_(j)], [Atm, Tt])
                    P.cp("act", AHN.t[:], psb.t[0:CK, 0:NCH * 64], [psb], [AHN])
                    ps = pss.next()
                    for j in range(NCH):
                        P.mm(ps, ps.t[0:CK, ks_(j)], Tt.t[:, cs_(j)], X1.t[:, ks_(j)], [Tt, X1])
                    P.cp("dve", U0.t[:], ps.t[0:CK, 0:NCH * 64], [ps], [U0])
                    ps = pss.next()
                    for j in range(NCH):
                        P.mm(ps, ps.t[0:64, ks_(j)], AHN.t[:, ks_(j)], Bhtm.t[:, ks_(j)], [AHN, Bhtm])
                    for j in range(NCH):
                        gcol = CK * j + CK - 1 if d == 0 else CK * j
                        P.stt("dve", MT.t[:, ks_(j)], idn, eI.t[:, gcol:gcol + 1], ps.t[0:64, ks_(j)], ALU.mult, ALU.add,
                              [self.ident, eI, ps], [MT])
                    ps = pss.next()
                    for j in range(NCH):
                        P.mm(ps, ps.t[0:64, ks_(j)], Bhtm.t[:, ks_(j)], U0.t[:, ks_(j)], [Bhtm, U0], start=(j == 0), stop=False, sgc=True)
                        P.mm(ps, ps.t[0:64, ks_(j)], Khtm.t[:, ks_(j)], Vtm.t[:, ks_(j)], [Khtm, Vtm], start=False, stop=(j == NCH - 1), sgc=True)
                    P.cp("act", CC.t[:], ps.t[0:64, 0:NCH * 64], [ps], [CC])
                    psU = psU_
                    psY = psY_
                    for ji, j in enumerate(order):
                        P.mm(psU, psU.t[0:CK, ks_(j)], AHT.t[:, cs_(j)], st.t[:], [AHT, st], start=(ji == 0), stop=False, sgc=True)
                        P.mm(psU, psU.t[0:CK, ks_(j)], idf[0:CK, 0:CK], U0.t[:, ks_(j)], [self.ident, U0], start=False, stop=True, sgc=True)
                        u = ur.next()
                        P.cp("act", u.t[:], psU.t[0:CK, ks_(j)], [psU], [u])
                        P.mm(psY, psY.t[0:64, cs_(j)], st.t[:], rt.t[:, cs_(j)], [st, rt], start=(ji == 0), stop=False, sgc=True)
                        P.mm(psY, psY.t[0:64, cs_(j)], u.t[:], Abr.t[:, cs_(j)], [u, Abr], start=False, stop=False, sgc=True)
                        P.mm(psY, psY.t[0:64, cs_(j)], Vtm.t[:, ks_(j)], Akr.t[:, cs_(j)], [Vtm, Akr], start=False, stop=True, sgc=True)
                        psS = pss.next()
                        P.mm(psS, psS.t[0:64, 0:64], MT.t[:, ks_(j)], st.t[:], [MT, st])
                        stn = str_.next()
                        P.tt("dve", stn.t[:], psS.t[0:64, 0:64], CC.t[:, ks_(j)], ALU.add, [psS, CC], [stn])
                        st = stn
                    P.cp("act", Y.t[:], psY.t[0:64, 0:W], [psY], [Y])
                    if d == 0:
                        P.dma("pool", self.yF[0, h, :, s0:s0 + W], Y.t[:], reads=[Y], writes=[yfb])
                        P.dma("pool", self.yF[1, h, :, s0:s0 + W], bon.t[:], reads=[bon], writes=[yfb])
                    else:
                        yf, bf = Tl["yf"], Tl["bf"]
                        P.dma("sp", yf.t[:], self.yF[0, h, :, s0:s0 + W], reads=[yfb], writes=[yf])
                        P.dma("sp", bf.t[:], self.yF[1, h, :, s0:s0 + W], reads=[yfb], writes=[bf])
                        P.tt("dve", Y.t[:], Y.t[:], yf.t[:], ALU.add, [Y, yf], [Y])
                        P.tt("pool", bon.t[:], bon.t[:], bf.t[:], ALU.add, [bon, bf], [bon])
                        P.act(gl.t[:], gl.t[:], AF.Sigmoid, [gl], [gl])
                        for blk in range(W // 512):
                            bs = slice(blk * 512, blk * 512 + 512)
                            ps = pss.next()
                            P.mm(ps, ps.t[0:64, 0:512], onesm.t[:], Y.t[:, bs], [onesm, Y])
                            P.tt("dve", Y.t[:, bs], Y.t[:, bs], ps.t[0:64, 0:512], ALU.subtract, [Y, ps], [Y])
                            P.tt("pool", t1.t[:, bs], Y.t[:, bs], Y.t[:, bs], ALU.mult, [Y], [t1])
                            ps = pss.next()
                            P.mm(ps, ps.t[0:64, 0:512], onesm.t[:], t1.t[:, bs], [onesm, t1])
                            P.act(t1.t[:, bs], ps.t[0:64, 0:512], AF.Sqrt, [ps, self.epsr], [t1], bias=self.epsr.t[0:64, 2:3])
                            P.op("dve", lambda g, bs=bs: g.reciprocal(out=t1.t[:, bs], in_=t1.t[:, bs]), [t1], [t1])
                            P.tt("dve", Y.t[:, bs], Y.t[:, bs], t1.t[:, bs], ALU.mult, [Y, t1], [Y])
                            P.ts("dve", Y.t[:, bs], Y.t[:, bs], hpv[:, h, 1:2], hpv[:, h, 2:3], ALU.mult, ALU.add, [Y, hp], [Y])
                            P.tt("pool", Y.t[:, bs], Y.t[:, bs], bon.t[:, bs], ALU.add, [Y, bon], [Y])
                            ps = pss.next()
                            P.mm(ps, ps.t[0:64, 0:512], g2.t[:, h * 64:h * 64 + 64], gl.t[:, bs], [g2, gl])
                            P.tt("dve", yo.t[:, bs], Y.t[:, bs], ps.t[0:64, 0:512], ALU.mult, [Y, ps], [yo])
                        P.dma("pool", self.yM[cc, pb:pb + 64, s0:s0 + W], yo.t[:], reads=[yo])
    S.close()


Builder.mix_rwkv = mix_rwkv
```
